# Optimizing a Trainium2 kernel written in Bass

```python
import math
import jax, jax.numpy as jnp
from jax import lax
import numpy as np

D_MODEL = 4096
BATCH = 2
SEQ = 8192
DEPTH = 1

HEAD_DIM = 128
DN_HEADS = 16
DN_DK = HEAD_DIM
DN_DV = HEAD_DIM
DN_QK = DN_HEADS * DN_DK
DN_WIDTH = DN_HEADS * DN_DV
CONV_K = 4
CHUNK = 64
DF_HEADS = 8
DF_DH = HEAD_DIM
DF_DV = 2 * DF_DH
DF_QK = 2 * DF_HEADS * DF_DH
DF_WIDTH = DF_HEADS * DF_DV
MIX_WIDTH = DN_WIDTH + DF_WIDTH
ROPE_THETA = 500000.0
ROPE_DIM = DF_DH // 4
Q_BLOCK = 128
FFN_HIDDEN = -(-8 * D_MODEL // (3 * 256)) * 256
EPS = 1e-6
PROJ_SIZES = (DN_QK, DN_QK, DN_WIDTH, DN_WIDTH, DN_HEADS, DN_HEADS, DF_QK, DF_QK, DF_WIDTH)
PROJ_TOTAL = sum(PROJ_SIZES)
PROJ_SPLIT = tuple(int(v) for v in np.cumsum(PROJ_SIZES)[:-1])
CONV_CH = 2 * DN_QK + DN_WIDTH

kernel_name = "hybrid_gdn_diffattn_parallel_heads"


def rmsnorm(x, w):
    xf = x.astype(jnp.float32)
    xf = xf * lax.rsqrt(jnp.mean(xf * xf, axis=-1, keepdims=True) + EPS)
    return (xf * w.astype(jnp.float32)).astype(x.dtype)


def l2norm(x):
    return x * lax.rsqrt(jnp.sum(x * x, axis=-1, keepdims=True) + EPS)


def causal_depthwise_conv(x, w):
    C = x.shape[-1]
    return lax.conv_general_dilated(
        x, w.astype(x.dtype)[:, None, :], window_strides=(1,),
        padding=[(w.shape[0] - 1, 0)],
        dimension_numbers=("NWC", "WIO", "NWC"), feature_group_count=C)


def rotary_tables(S):
    pos = jnp.arange(S, dtype=jnp.float32)
    inv = ROPE_THETA ** (-jnp.arange(0, ROPE_DIM, 2, dtype=jnp.float32) / ROPE_DIM)
    ang = pos[:, None] * inv[None, :]
    return jnp.cos(ang), jnp.sin(ang)


def partial_rotary(x, cos, sin):
    half = ROPE_DIM // 2
    xr = x[..., :ROPE_DIM].astype(jnp.float32)
    x1, x2 = xr[..., :half], xr[..., half:]
    c, s = cos[None, :, None, :], sin[None, :, None, :]
    rot = jnp.concatenate([x1 * c - x2 * s, x2 * c + x1 * s], axis=-1)
    return jnp.concatenate([rot.astype(x.dtype), x[..., ROPE_DIM:]], axis=-1)


def gated_delta_rule(q, k, v, g, beta):
    B, S, H, dk = q.shape
    dv = v.shape[-1]
    N = S // CHUNK
    q = l2norm(q.astype(jnp.float32)) * (dk ** -0.5)
    k = l2norm(k.astype(jnp.float32))
    v = v.astype(jnp.float32)

    def to_chunks(t):
        return t.reshape((B, N, CHUNK, H) + t.shape[3:]).transpose((0, 3, 1, 2) + tuple(range(4, t.ndim + 1)))

    q, k, v = to_chunks(q), to_chunks(k), to_chunks(v)
    g, beta = to_chunks(g), to_chunks(beta)
    gc = jnp.cumsum(g, axis=-1)
    tril = jnp.tril(jnp.ones((CHUNK, CHUNK), dtype=bool))
    strict = jnp.tril(jnp.ones((CHUNK, CHUNK), dtype=bool), k=-1)
    decay = jnp.exp(jnp.where(tril, gc[..., :, None] - gc[..., None, :], -jnp.inf))
    kb = k * beta[..., None]
    vb = v * beta[..., None]
    A = jnp.where(strict, jnp.einsum('bhnid,bhnjd->bhnij', kb, k) * decay, 0.0)
    eye = jnp.eye(CHUNK, dtype=jnp.float32)
    rhs = jnp.concatenate([vb, kb * jnp.exp(gc)[..., None]], axis=-1)
    sol = lax.linalg.triangular_solve(eye + A, rhs, left_side=True, lower=True)
    u, w = sol[..., :dv], sol[..., dv:]
    attn = jnp.einsum('bhnid,bhnjd->bhnij', q, k) * decay

    def step(state, inp):
        qc, kc, uc, wc, gcc, ac = inp
        v_new = uc - jnp.einsum('bhck,bhkv->bhcv', wc, state)
        o = (jnp.einsum('bhck,bhkv->bhcv', qc * jnp.exp(gcc)[..., None], state)
             + jnp.einsum('bhij,bhjv->bhiv', ac, v_new))
        g_last = gcc[..., -1]
        state = (state * jnp.exp(g_last)[..., None, None]
                 + jnp.einsum('bhck,bhcv->bhkv', kc * jnp.exp(g_last[..., None] - gcc)[..., None], v_new))
        return state, o

    xs = tuple(jnp.moveaxis(t, 2, 0) for t in (q, k, u, w, gc, attn))
    state0 = jnp.zeros((B, H, dk, dv), jnp.float32)
    _, o = lax.scan(step, state0, xs)
    return o.transpose(1, 0, 3, 2, 4).reshape(B, S, H, dv)


def diff_attention(q, k, v, lam):
    B, S, H2, dh = q.shape
    H = H2 // 2
    nq = S // Q_BLOCK
    scale = dh ** -0.5
    qb = q.reshape(B, nq, Q_BLOCK, H2, dh).transpose(1, 0, 2, 3, 4)
    kf = k.astype(jnp.float32)
    vf = v.astype(jnp.float32)
    kpos = jnp.arange(S)

    def block(args):
        q_blk, idx = args
        s = jnp.einsum('bqhd,bkhd->bhqk', q_blk.astype(jnp.float32), kf) * scale
        qpos = idx * Q_BLOCK + jnp.arange(Q_BLOCK)
        s = jnp.where(kpos[None, :] <= qpos[:, None], s, -jnp.inf)
        p = jax.nn.softmax(s, axis=-1).reshape(B, H, 2, Q_BLOCK, S)
        a = p[:, :, 0] - lam * p[:, :, 1]
        return jnp.einsum('bhqk,bkhe->bqhe', a, vf)

    o = lax.map(block, (qb, jnp.arange(nq)))
    return o.transpose(1, 0, 2, 3, 4).reshape(B, S, H, 2 * dh)


def setup_inputs(seed: int = 0) -> dict:
    key = jax.random.key(seed)
    ks = jax.random.split(key, 20)
    f32 = jnp.float32
    L = DEPTH

    def nrm(k, shape, scale):
        return jax.random.normal(k, shape, f32) * scale

    def gain(k, n):
        return 1.0 + 0.02 * jax.random.normal(k, (L, n), f32)

    dt = jnp.exp(jax.random.uniform(ks[5], (L, DN_HEADS), f32, math.log(1e-3), math.log(1e-1)))
    return {
        "x": jax.random.normal(ks[0], (BATCH, SEQ, D_MODEL), f32),
        "ln_mix_w": gain(ks[1], D_MODEL),
        "w_in": nrm(ks[2], (L, D_MODEL, PROJ_TOTAL), D_MODEL ** -0.5),
        "conv_w": nrm(ks[3], (L, CONV_K, CONV_CH), CONV_K ** -0.5),
        "a_log": jnp.log(jax.random.uniform(ks[4], (L, DN_HEADS), f32, 1.0, 16.0)),
        "dt_bias": dt + jnp.log(-jnp.expm1(-dt)),
        "dn_norm_w": gain(ks[6], DN_DV),
        "lambda_q1": nrm(ks[7], (L, DF_DH), 0.1),
        "lambda_k1": nrm(ks[8], (L, DF_DH), 0.1),
        "lambda_q2": nrm(ks[9], (L, DF_DH), 0.1),
        "lambda_k2": nrm(ks[10], (L, DF_DH), 0.1),
        "df_norm_w": gain(ks[11], DF_DV),
        "w_out": nrm(ks[12], (L, MIX_WIDTH, D_MODEL), MIX_WIDTH ** -0.5),
        "ln_ffn_w": gain(ks[13], D_MODEL),
        "w_gate": nrm(ks[14], (L, D_MODEL, FFN_HIDDEN), D_MODEL ** -0.5),
        "w_up": nrm(ks[15], (L, D_MODEL, FFN_HIDDEN), D_MODEL ** -0.5),
        "w_down": nrm(ks[16], (L, FFN_HIDDEN, D_MODEL), FFN_HIDDEN ** -0.5),
        "ln_final_w": 1.0 + 0.02 * jax.random.normal(ks[17], (D_MODEL,), f32),
    }


def reference(x, ln_mix_w, w_in, conv_w, a_log, dt_bias, dn_norm_w,
              lambda_q1, lambda_k1, lambda_q2, lambda_k2, df_norm_w, w_out,
              ln_ffn_w, w_gate, w_up, w_down, ln_final_w):
    B, S, _ = x.shape
    cos, sin = rotary_tables(S)
    for l in range(DEPTH):
        h = rmsnorm(x, ln_mix_w[l])
        proj = h @ w_in[l]
        dq, dk, dv, dz, db, da, fq, fk, fv = jnp.split(proj, PROJ_SPLIT, axis=-1)

        qkv = jax.nn.silu(causal_depthwise_conv(jnp.concatenate([dq, dk, dv], axis=-1), conv_w[l]))
        dq, dk, dv = jnp.split(qkv, (DN_QK, 2 * DN_QK), axis=-1)
        beta = jax.nn.sigmoid(db.astype(jnp.float32))
        g = -jnp.exp(a_log[l].astype(jnp.float32)) * jax.nn.softplus(
            da.astype(jnp.float32) + dt_bias[l].astype(jnp.float32))
        o_dn = gated_delta_rule(dq.reshape(B, S, DN_HEADS, DN_DK), dk.reshape(B, S, DN_HEADS, DN_DK),
                                dv.reshape(B, S, DN_HEADS, DN_DV), g, beta)
        o_dn = rmsnorm(o_dn, dn_norm_w[l]) * jax.nn.silu(dz.reshape(B, S, DN_HEADS, DN_DV).astype(jnp.float32))

        lambda_init = 0.8 - 0.6 * math.exp(-0.3 * l)
        lam = (jnp.exp(jnp.sum(lambda_q1[l].astype(jnp.float32) * lambda_k1[l].astype(jnp.float32)))
               - jnp.exp(jnp.sum(lambda_q2[l].astype(jnp.float32) * lambda_k2[l].astype(jnp.float32)))
               + lambda_init)
        fq = partial_rotary(fq.reshape(B, S, 2 * DF_HEADS, DF_DH), cos, sin)
        fk = partial_rotary(fk.reshape(B, S, 2 * DF_HEADS, DF_DH), cos, sin)
        o_df = diff_attention(fq, fk, fv.reshape(B, S, DF_HEADS, DF_DV), lam)
        o_df = rmsnorm(o_df, df_norm_w[l]) * (1.0 - lambda_init)

        mix = jnp.concatenate([o_dn.reshape(B, S, DN_WIDTH), o_df.reshape(B, S, DF_WIDTH)],
                              axis=-1).astype(x.dtype)
        x = x + mix @ w_out[l]

        h = rmsnorm(x, ln_ffn_w[l])
        x = x + (jax.nn.silu(h @ w_gate[l]) * (h @ w_up[l])) @ w_down[l]
    return rmsnorm(x, ln_final_w)
```

```python
import math
from contextlib import ExitStack
import numpy as np
import concourse.bass as bass
import concourse.mybir as mybir
from concourse.bass_utils import run_bass_kernel_spmd

F32 = mybir.dt.float32
BF16 = mybir.dt.bfloat16
AF = mybir.ActivationFunctionType
ALU = mybir.AluOpType
AX = mybir.AxisListType

SAME_ENGINE_SYNC = True
import os
DNSTOP = int(os.environ.get("DNSTOP", "0"))
DNHEADS = int(os.environ.get("DNHEADS", "16"))


class Sem:
    def __init__(self, h):
        self.h = h
        self.cnt = 0


class Track:
    def __init__(self):
        self.writers = {}
        self.readers = {}


class Obj:
    def __init__(self, t, name, dsem=None, tr=None, psum=False):
        self.t = t
        self.name = name
        self.tr = tr if tr is not None else Track()
        self.dsem = dsem
        self.psum = psum

    @property
    def writers(self):
        return self.tr.writers

    @writers.setter
    def writers(self, v):
        self.tr.writers = v

    @property
    def readers(self):
        return self.tr.readers

    @readers.setter
    def readers(self, v):
        self.tr.readers = v

    def view(self, ap, name=None):
        return Obj(ap, name or self.name, self.dsem, self.tr, self.psum)

    def __getitem__(self, k):
        return self.t[k]


class FW:
    def __init__(self, nc, stack):
        self.nc = nc
        self.stack = stack
        self.engs = ["pe", "act", "dve", "pool", "sp"]
        self.esem = {k: Sem(stack.enter_context(nc.semaphore("es_" + k))) for k in self.engs}
        self.thunks = {k: [] for k in self.engs}
        self.waited = {k: {} for k in self.engs}
        self.allsems = list(self.esem.values())
        self.ninst = 0

    def new_sem(self, name):
        s = Sem(self.stack.enter_context(self.nc.semaphore(name)))
        self.allsems.append(s)
        return s

    def sbuf(self, st, name, shape, dt, dma=False):
        self.uid = getattr(self, "uid", 0) + 1
        name = f"{name}_{self.uid}"
        t = st.enter_context(self.nc.sbuf_tensor(name, shape, dt))
        sem = None
        if dma:
            pool = self.__dict__.setdefault("sem_pool_" + str(dma), [])
            sem = pool.pop() if pool else self.new_sem("d_" + name)
            st.callback(lambda: pool.append(sem))
        return Obj(t, name, sem)

    def psum(self, st, name, shape, dt=F32):
        self.uid = getattr(self, "uid", 0) + 1
        name = f"{name}_{self.uid}"
        t = st.enter_context(self.nc.psum_tensor(name, shape, dt))
        return Obj(t, name, psum=True)

    def dram(self, name, shape, dt, kind="Internal"):
        t = self.nc.dram_tensor(name, shape, dt, kind=kind).ap()
        return Obj(t, name)

    def _deps(self, reads, writes):
        deps = {}
        for o in reads:
            for s, v in o.writers.items():
                if deps.get(s, 0) < v:
                    deps[s] = v
            if o.psum:
                for s, v in o.readers.items():
                    if deps.get(s, 0) < v:
                        deps[s] = v
        for o in writes:
            for d in (o.writers, o.readers):
                for s, v in d.items():
                    if deps.get(s, 0) < v:
                        deps[s] = v
        return deps

    def _waits(self, ek, deps):
        w = self.waited[ek]
        own = self.esem[ek]
        out = []
        for s, v in deps.items():
            if s is own and not SAME_ENGINE_SYNC:
                continue
            if w.get(s, 0) >= v:
                continue
            w[s] = v
            out.append((s.h, v))
        return out

    def op(self, ek, fn, reads=(), writes=()):
        waits = self._waits(ek, self._deps(reads, writes))
        sem = self.esem[ek]
        sem.cnt += 1
        val = sem.cnt
        h = sem.h

        def thunk(e):
            for sh, v in waits:
                e.wait_ge(sh, v)
            r = fn(e)
            if isinstance(r, (list, tuple)):
                r = r[-1]
            r.then_inc(h, 1)

        self.thunks[ek].append(thunk)
        self.ninst += 1
        for o in reads:
            if o.readers.get(sem, 0) < val:
                o.readers[sem] = val
        for o in writes:
            o.writers = {sem: val}
            o.readers = {}

    def dma(self, qk, out_o, out_ap, in_o, in_ap, sem_obj=None, group=False, **kw):
        if sem_obj is None:
            sem_obj = out_o if out_o.dsem is not None else in_o
        sem = sem_obj.dsem
        assert sem is not None, (out_o.name, in_o.name)
        if group:
            deps = self._deps([in_o], [])
            for s, v in out_o.readers.items():
                deps[s] = max(deps.get(s, 0), v)
            for s, v in out_o.writers.items():
                if s is not sem:
                    deps[s] = max(deps.get(s, 0), v)
        else:
            deps = self._deps([in_o], [out_o])
        waits = self._waits(qk, deps)
        sem.cnt += 16
        val = sem.cnt
        h = sem.h

        def thunk(e):
            for sh, v in waits:
                e.wait_ge(sh, v)
            e.dma_start(out=out_ap, in_=in_ap, **kw).then_inc(h, 16)

        self.thunks[qk].append(thunk)
        self.ninst += 1
        if in_o.readers.get(sem, 0) < val:
            in_o.readers[sem] = val
        if group:
            out_o.writers[sem] = val
        else:
            out_o.writers = {sem: val}
        out_o.readers = {}

    def barrier(self):
        snap = [(s, s.cnt) for s in self.allsems if s.cnt > 0]
        for ek in self.engs:
            waits = self._waits(ek, dict(snap))

            def thunk(e, waits=waits):
                for sh, v in waits:
                    e.wait_ge(sh, v)

            self.thunks[ek].append(thunk)

    def flush(self):
        lists = self.thunks
        with self.nc.Block() as block:
            @block.tensor
            def _(e):
                for t in lists["pe"]:
                    t(e)

            @block.scalar
            def _(e):
                for t in lists["act"]:
                    t(e)

            @block.vector
            def _(e):
                for t in lists["dve"]:
                    t(e)

            @block.gpsimd
            def _(e):
                for t in lists["pool"]:
                    t(e)

            @block.sync
            def _(e):
                for t in lists["sp"]:
                    t(e)
        self.thunks = {k: [] for k in self.engs}
        if max(s.cnt for s in self.esem.values()) > 20000:
            self.epoch = getattr(self, "epoch", 0) + 1
            for k in self.engs:
                ns = Sem(self.stack.enter_context(self.nc.semaphore(f"es_{k}_{self.epoch}")))
                self.esem[k] = ns
                self.allsems.append(ns)

D = 4096
KC = 32
NH_DN = 16
NSUB = 16
NH_DF = 8
FFN = 11008
HB = FFN // 128
PROJ = 14368
OFF_DQ, OFF_DK, OFF_DV, OFF_DZ, OFF_DB, OFF_DA, OFF_FQ, OFF_FK, OFF_FV = 0, 2048, 4096, 6144, 8192, 8208, 8224, 10272, 12320
HALO = 4
EPS = 1e-6
ROPE_THETA = 500000.0
LAMBDA_INIT = 0.8 - 0.6 * math.exp(-0.3 * 0)
PI = math.pi


def finish(nc, fw, top, cst, dbg, dbg_o, env):
    if dbg is not None:
        for i, d in enumerate(dbg):
            src = env[d[0]]
            dbg_o[i].dsem = fw.new_sem(f"dbgsem{i}")
            fw.dma("sp", dbg_o[i], dbg_o[i][:], src, d[3](src), sem_obj=dbg_o[i])
    fw.barrier()
    fw.flush()
    c2 = env.get("cst2")
    if c2 is not None:
        c2.close()
    cst.close()
    top.close()
    return nc


def build_program(S, phases=99, dbg=None):
    QT = S // 4
    TT = min(512, QT)
    NCH = S // 128
    NCO = QT // 128
    nc = bass.Bass("TRN2", target_bir_lowering=False)
    top = ExitStack()
    fw = FW(nc, top)

    def ein(name, shape, dt=F32):
        return fw.dram(name, shape, dt, kind="ExternalInput")

    xTf = ein("xTf", [128, KC, S])
    xTo = ein("xTo", [128, KC, HALO + QT])
    w_in = ein("w_in", [D, PROJ])
    if phases >= 5:
        w_out = ein("w_out", [D, D])
        w_gate = ein("w_gate", [D, FFN])
        w_up = ein("w_up", [D, FFN])
        w_down = ein("w_down", [FFN, D])
    lnw_d = ein("lnw", [128, 3, KC])
    convw_d = ein("convw", [128, 48, 4])
    hv_d = ein("hv", [128, 32])
    nw_d = ein("nw", [128, 3])
    lamv_d = ein("lamv", [128, 4])
    cm_d = ein("cmask", [128, 5, 128])
    flags_d = ein("flags", [128, NCH])
    qpos_d = ein("qpos", [128, HALO + QT])
    kpos_d = ein("kpos", [128, NCH])
    posf_d = ein("posf", [128, S])
    invf_d = ein("invf", [128, 1])
    outT = fw.dram("outT", [128, KC, QT], F32, kind="ExternalOutput")

    xn_f = fw.dram("xn_f", [128, KC, S], BF16)
    xn_o = fw.dram("xn_o", [128, KC, HALO + QT], BF16)
    raw_f = fw.dram("raw_f", [48, 128, S], F32)
    vtok_f = fw.dram("vtok_f", [S, 2048], BF16)
    ba_f = fw.dram("ba_f", [S, 32], F32)
    raw_o = fw.dram("raw_o", [80, 128, HALO + QT], F32)
    ba_o = fw.dram("ba_o", [HALO + QT, 32], F32)
    kdn_f = fw.dram("kdn_f", [16, 128, S], F32)
    vdn_f = fw.dram("vdn_f", [16, 128, S], F32)
    kdf_f = fw.dram("kdf_f", [16, 128, S], BF16)
    qdn_o = fw.dram("qdn_o", [16, 128, QT], F32)
    kdn_o = fw.dram("kdn_o", [16, 128, QT], F32)
    vdn_o = fw.dram("vdn_o", [16, 128, QT], F32)
    zdn_o = fw.dram("zdn_o", [16, 128, QT], BF16)
    qdf_o = fw.dram("qdf_o", [16, 128, QT], BF16)
    mixT = fw.dram("mixT", [128, KC, QT], BF16)
    x1T = fw.dram("x1T", [128, KC, QT], F32)
    x2T = fw.dram("x2T", [128, KC, QT], F32)
    dbg_o = None
    if dbg is not None:
        dbg_o = [fw.dram(f"dbg{i}", list(d[1]), d[2], kind="ExternalOutput") for i, d in enumerate(dbg)]

    evq = ["sp", "act"]
    cnt = {"e": 0, "q": 0}

    def nextq():
        cnt["q"] += 1
        return evq[cnt["q"] % 2]

    def evac_eng():
        cnt["e"] += 1
        return "act" if cnt["e"] % 2 else "dve"

    def copy_op(ek, out_o, out_ap, in_o, in_ap):
        if ek == "act":
            fw.op("act", lambda e: e.activation(out_ap, in_ap, AF.Copy), reads=[in_o], writes=[out_o])
        else:
            fw.op(ek, lambda e: e.tensor_copy(out_ap, in_ap), reads=[in_o], writes=[out_o])

    cst = ExitStack()
    ones_b = fw.sbuf(cst, "ones_b", [128, 128], BF16)
    ones_f = fw.sbuf(cst, "ones_f", [128, 128], F32)
    lnw = fw.sbuf(cst, "lnw_s", [128, 3, KC], F32, dma=True)
    cm = fw.sbuf(cst, "cm_s", [128, 5, 128], F32, dma=True)
    idb = fw.sbuf(cst, "idb", [128, 128], BF16)
    fw.op("dve", lambda e: e.memset(ones_b[:], 1.0), writes=[ones_b])
    fw.op("dve", lambda e: e.memset(ones_f[:], 1.0), writes=[ones_f])
    fw.dma("sp", lnw, lnw[:], lnw_d, lnw_d[:])
    fw.dma("sp", cm, cm[:], cm_d, cm_d[:])
    fw.op("dve", lambda e: e.tensor_copy(idb[:], cm[:, 3, :]), reads=[cm], writes=[idb])
    trib = fw.sbuf(cst, "trib", [128, 128], BF16)
    fw.op("dve", lambda e: e.tensor_copy(trib[:], cm[:, 0, :]), reads=[cm], writes=[trib])

    def p0_norm(src, dst, ntok_total, which):
        T0 = min(256, QT)
        with ExitStack() as st:
            xt = [fw.sbuf(st, f"p0x{i}", [128, KC, T0], F32, dma=True) for i in range(2)]
            sq = fw.sbuf(st, "p0sq", [128, KC, T0], BF16)
            xo = [fw.sbuf(st, f"p0o{i}", [128, KC, T0], BF16, dma=True) for i in range(2)]
            rs = [fw.sbuf(st, f"p0r{i}", [128, T0], F32) for i in range(2)]
            ps = [fw.psum(st, f"p0ps{i}", [128, 512]) for i in range(2)]
            tiles = []
            t0 = 0
            while t0 < ntok_total:
                n = min(T0, ntok_total - t0)
                tiles.append((t0, n))
                t0 += n
            for it, (t0, n) in enumerate(tiles):
                x_, o_, r_, p_ = xt[it % 2], xo[it % 2], rs[it % 2], ps[it % 2]
                fw.dma(nextq(), x_, x_[:, :, 0:n], src, src[:, :, t0:t0 + n])
                fw.op("act", lambda e, x_=x_, n=n: e.activation(sq[:, :, 0:n], x_[:, :, 0:n], AF.Square),
                      reads=[x_], writes=[sq])

                def mm(e, p_=p_, n=n):
                    r = None
                    for kc in range(KC):
                        r = e.matmul(p_[:, 0:n], ones_b[:], sq[:, kc, 0:n], start=(kc == 0), stop=(kc == KC - 1))
                    return r
                fw.op("pe", mm, reads=[sq, ones_b], writes=[p_])
                fw.op("dve", lambda e, r_=r_, p_=p_, n=n: e.tensor_scalar(r_[:, 0:n], p_[:, 0:n], 1.0 / D, EPS, ALU.mult, ALU.add),
                      reads=[p_], writes=[r_])
                fw.op("act", lambda e, r_=r_, n=n: e.activation(r_[:, 0:n], r_[:, 0:n], AF.Sqrt), reads=[r_], writes=[r_])
                fw.op("dve", lambda e, r_=r_, n=n: e.reciprocal(r_[:, 0:n], r_[:, 0:n]), reads=[r_], writes=[r_])
                ek = "dve"

                def sc(e, x_=x_, o_=o_, r_=r_, n=n):
                    r = None
                    for kc in range(KC):
                        r = e.scalar_tensor_tensor(o_[:, kc, 0:n], x_[:, kc, 0:n], lnw[:, which, kc:kc + 1],
                                                   r_[:, 0:n], ALU.mult, ALU.mult)
                    return r
                fw.op(ek, sc, reads=[x_, r_, lnw], writes=[o_])
                fw.dma(nextq(), dst, dst[:, :, t0:t0 + n], o_, o_[:, :, 0:n])
            fw.barrier()
            fw.flush()

    p0_norm(xTf, xn_f, S, 0)
    p0_norm(xTo, xn_o, HALO + QT, 0)

    if phases < 1:
        return finish(nc, fw, top, cst, dbg, dbg_o, locals())

    w_in_v = w_in.t.rearrange("(kc p) n -> p kc n", p=128)

    def proj(xn, tok_tiles, jobs):
        GW = 1024
        with ExitStack() as st:
            wt = fw.sbuf(st, "p1w", [128, KC, GW], BF16, dma="sw")
            xt = [fw.sbuf(st, f"p1x{i}", [128, KC, TT], BF16, dma=True) for i in range(2)]
            evf = [fw.sbuf(st, f"p1ef{i}", [128, 512], F32, dma=True) for i in range(4)]
            evb = [fw.sbuf(st, f"p1eb{i}", [128, 512], BF16, dma=True) for i in range(4)]
            ps = [fw.psum(st, f"p1ps{i}", [128, 512]) for i in range(4)]
            k = {"x": 0, "p": 0}
            for (col_lo, ncols, mode, dst, d0) in jobs:
                for g0 in range(0, ncols, GW):
                    gw = min(GW, ncols - g0)
                    fw.dma("pool", wt, wt[:, :, 0:gw], w_in, w_in_v[:, :, col_lo + g0: col_lo + g0 + gw])
                    for (t0, n) in tok_tiles:
                        x_ = xt[k["x"] % 2]
                        k["x"] += 1
                        fw.dma(nextq(), x_, x_[:, :, 0:n], xn, xn[:, :, t0:t0 + n])
                        if mode == "cm":
                            for b0 in range(0, gw, 128):
                                p_ = ps[k["p"] % 4]
                                e_ = evf[k["p"] % 4]
                                k["p"] += 1

                                def mm(e, p_=p_, x_=x_, b0=b0, n=n):
                                    r = None
                                    for kc in range(KC):
                                        r = e.matmul(p_[:, 0:n], wt[:, kc, b0:b0 + 128], x_[:, kc, 0:n],
                                                     start=(kc == 0), stop=(kc == KC - 1))
                                    return r
                                fw.op("pe", mm, reads=[wt, x_], writes=[p_])
                                copy_op(evac_eng(), e_, e_[:, 0:n], p_, p_[:, 0:n])
                                blk = d0 + (g0 + b0) // 128
                                fw.dma(nextq(), dst, dst[blk, :, t0:t0 + n], e_, e_[:, 0:n])
                        else:
                            for s0 in range(0, n, 128):
                                m = min(128, n - s0)
                                for c0 in range(0, gw, 512):
                                    cw = min(512, gw - c0)
                                    p_ = ps[k["p"] % 4]
                                    e_ = (evb if mode == "tmb" else evf)[k["p"] % 4]
                                    k["p"] += 1

                                    def mm(e, p_=p_, x_=x_, s0=s0, m=m, c0=c0, cw=cw):
                                        r = None
                                        for kc in range(KC):
                                            r = e.matmul(p_[0:m, 0:cw], x_[:, kc, s0:s0 + m], wt[:, kc, c0:c0 + cw],
                                                         start=(kc == 0), stop=(kc == KC - 1))
                                        return r
                                    fw.op("pe", mm, reads=[wt, x_], writes=[p_])
                                    copy_op(evac_eng(), e_, e_[0:m, 0:cw], p_, p_[0:m, 0:cw])
                                    fw.dma(nextq(), dst, dst[t0 + s0:t0 + s0 + m, d0 + g0 + c0:d0 + g0 + c0 + cw],
                                           e_, e_[0:m, 0:cw])
            fw.barrier()
            fw.flush()

    full_tiles = [(i * TT, TT) for i in range(S // TT)]
    own_tiles = [(0, HALO)] + [(HALO + i * TT, TT) for i in range(QT // TT)]
    proj(xn_f, full_tiles, [
        (OFF_DK, 2048, "cm", raw_f, 0),
        (OFF_DV, 2048, "cm", raw_f, 16),
        (OFF_FK, 2048, "cm", raw_f, 32),
        (OFF_FV, 2048, "tmb", vtok_f, 0),
        (OFF_DB, 32, "tmf", ba_f, 0),
    ])
    proj(xn_o, own_tiles, [
        (OFF_DQ, 2048, "cm", raw_o, 0),
        (OFF_DK, 2048, "cm", raw_o, 16),
        (OFF_DV, 2048, "cm", raw_o, 32),
        (OFF_DZ, 2048, "cm", raw_o, 48),
        (OFF_FQ, 2048, "cm", raw_o, 64),
        (OFF_DB, 32, "tmf", ba_o, 0),
    ])

    if phases < 2:
        return finish(nc, fw, top, cst, dbg, dbg_o, locals())

    cst2 = ExitStack()
    convw = fw.sbuf(cst2, "convw_s", [128, 48, 4], F32, dma=True)
    fw.dma("sp", convw, convw[:], convw_d, convw_d[:])
    invf = fw.sbuf(cst2, "invf_s", [128, 1], F32, dma=True)
    fw.dma("sp", invf, invf[:], invf_d, invf_d[:])
    prot = fw.sbuf(cst2, "prot", [128, 128], BF16)
    fw.op("dve", lambda e: e.tensor_copy(prot[:], cm[:, 4, :]), reads=[cm], writes=[prot])

    def p2_conv(src, src_blk0, cw_blk0, nblk, tiles, col0, dst, kind):
        with ExitStack() as st:
            xin = [fw.sbuf(st, f"p2i{i}", [128, 3 + TT], F32, dma=True) for i in range(2)]
            y = [fw.sbuf(st, f"p2y{i}", [128, TT], F32) for i in range(2)]
            sl = [fw.sbuf(st, f"p2s{i}", [128, TT], F32) for i in range(2)]
            sq = [fw.sbuf(st, f"p2q{i}", [128, TT], BF16) for i in range(2)]
            rr = [fw.sbuf(st, f"p2r{i}", [128, TT], F32) for i in range(2)]
            ob = [fw.sbuf(st, f"p2o{i}", [128, TT], BF16 if kind == "z" else F32, dma=True) for i in range(2)]
            ps = [fw.psum(st, f"p2ps{i}", [128, 512]) for i in range(2)]
            it = 0
            for b in range(nblk):
                for (t0, n) in tiles:
                    i_, y_, s_, q_, r_, o_, p_ = (a[it % 2] for a in (xin, y, sl, sq, rr, ob, ps))
                    it += 1
                    lo = col0 + t0 - 3
                    if lo < 0:
                        fw.op("dve", lambda e, i_=i_: e.memset(i_[:, 0:3], 0.0), writes=[i_])
                        fw.dma(nextq(), i_, i_[:, 3:3 + n], src, src[src_blk0 + b, :, col0 + t0:col0 + t0 + n], group=True)
                    else:
                        fw.dma(nextq(), i_, i_[:, 0:3 + n], src, src[src_blk0 + b, :, lo:lo + 3 + n])
                    if kind == "z":
                        fw.op("act", lambda e, i_=i_, o_=o_, n=n: e.activation(o_[:, 0:n], i_[:, 3:3 + n], AF.Silu),
                              reads=[i_], writes=[o_])
                    else:
                        cb = cw_blk0 + b

                        fw.op("dve", lambda e, i_=i_, y_=y_, n=n, cb=cb: e.tensor_scalar(y_[:, 0:n], i_[:, 0:n], convw[:, cb, 0:1], None, ALU.mult),
                              reads=[i_, convw], writes=[y_])
                        for j in range(1, 4):
                            fw.op("dve", lambda e, i_=i_, y_=y_, n=n, cb=cb, j=j: e.scalar_tensor_tensor(
                                y_[:, 0:n], i_[:, j:j + n], convw[:, cb, j:j + 1], y_[:, 0:n], ALU.mult, ALU.add),
                                reads=[i_, convw, y_], writes=[y_])
                        if kind == "v":
                            fw.op("act", lambda e, y_=y_, o_=o_, n=n: e.activation(o_[:, 0:n], y_[:, 0:n], AF.Silu),
                                  reads=[y_], writes=[o_])
                        else:
                            fw.op("act", lambda e, y_=y_, s_=s_, n=n: e.activation(s_[:, 0:n], y_[:, 0:n], AF.Silu),
                                  reads=[y_], writes=[s_])
                            fw.op("act", lambda e, s_=s_, q_=q_, n=n: e.activation(q_[:, 0:n], s_[:, 0:n], AF.Square),
                                  reads=[s_], writes=[q_])
                            fw.op("pe", lambda e, p_=p_, q_=q_, n=n: e.matmul(p_[:, 0:n], ones_b[:], q_[:, 0:n], start=True, stop=True),
                                  reads=[q_, ones_b], writes=[p_])
                            m_ = 128.0 if kind == "q" else 1.0
                            fw.op("dve", lambda e, r_=r_, p_=p_, n=n, m_=m_: e.tensor_scalar(r_[:, 0:n], p_[:, 0:n], m_, m_ * EPS, ALU.mult, ALU.add),
                                  reads=[p_], writes=[r_])
                            fw.op("act", lambda e, r_=r_, n=n: e.activation(r_[:, 0:n], r_[:, 0:n], AF.Sqrt), reads=[r_], writes=[r_])
                            fw.op("dve", lambda e, r_=r_, n=n: e.reciprocal(r_[:, 0:n], r_[:, 0:n]), reads=[r_], writes=[r_])
                            fw.op("dve", lambda e, o_=o_, s_=s_, r_=r_, n=n: e.tensor_tensor(o_[:, 0:n], s_[:, 0:n], r_[:, 0:n], ALU.mult),
                                  reads=[s_, r_], writes=[o_])
                    fw.dma(nextq(), dst, dst[b, :, t0:t0 + n], o_, o_[:, 0:n])
            fw.barrier()
            fw.flush()

    full_t = [(i * TT, TT) for i in range(S // TT)]
    own_t = [(i * TT, TT) for i in range(QT // TT)]
    p2_conv(raw_f, 0, 16, 16, full_t, 0, kdn_f, "k")
    p2_conv(raw_f, 16, 32, 16, full_t, 0, vdn_f, "v")
    p2_conv(raw_o, 0, 0, 16, own_t, HALO, qdn_o, "q")
    p2_conv(raw_o, 16, 16, 16, own_t, HALO, kdn_o, "k")
    p2_conv(raw_o, 32, 32, 16, own_t, HALO, vdn_o, "v")
    p2_conv(raw_o, 48, 0, 16, own_t, HALO, zdn_o, "z")

    def p2_rot(src, src_blk0, tiles, col0, pos_d, dst):
        with ExitStack() as st:
            pt = fw.sbuf(st, "p2pos", [128, TT], F32, dma=True)
            ca = fw.sbuf(st, "p2ca", [128, TT], F32)
            ti = fw.sbuf(st, "p2ti", [128, TT], mybir.dt.int32)
            sa = fw.sbuf(st, "p2sa", [128, TT], F32)
            cs = fw.sbuf(st, "p2cs", [128, TT], F32)
            sn = fw.sbuf(st, "p2sn", [128, TT], F32)
            xin = [fw.sbuf(st, f"p2x{i}", [128, TT], F32, dma=True) for i in range(2)]
            xb = [fw.sbuf(st, f"p2xb{i}", [128, TT], BF16) for i in range(2)]
            t1 = [fw.sbuf(st, f"p2t{i}", [128, TT], F32) for i in range(2)]
            t2 = [fw.sbuf(st, f"p2u{i}", [128, TT], F32) for i in range(2)]
            ob = [fw.sbuf(st, f"p2ro{i}", [128, TT], BF16, dma=True) for i in range(2)]
            ps = [fw.psum(st, f"p2rp{i}", [128, 512]) for i in range(2)]
            it = 0
            for (t0, n) in tiles:
                fw.dma("sp", pt, pt[:, 0:n], pos_d, pos_d[:, col0 + t0:col0 + t0 + n])
                def trig(dst, shift, n=n):
                    fw.op("dve", lambda e: e.tensor_scalar(sa[:, 0:n], pt[:, 0:n], invf[:, 0:1], shift, ALU.mult, ALU.add),
                          reads=[pt, invf], writes=[sa])
                    fw.op("dve", lambda e: e.tensor_scalar(ca[:, 0:n], sa[:, 0:n], 1.0 / (2 * PI), None, ALU.mult), reads=[sa], writes=[ca])
                    fw.op("dve", lambda e: e.tensor_copy(ti[:, 0:n], ca[:, 0:n]), reads=[ca], writes=[ti])
                    fw.op("dve", lambda e: e.tensor_copy(ca[:, 0:n], ti[:, 0:n]), reads=[ti], writes=[ca])
                    fw.op("dve", lambda e: e.scalar_tensor_tensor(sa[:, 0:n], ca[:, 0:n], -2 * PI, sa[:, 0:n], ALU.mult, ALU.add),
                          reads=[ca, sa], writes=[sa])
                    fw.op("dve", lambda e: e.tensor_scalar(ca[:, 0:n], sa[:, 0:n], PI, -2 * PI, ALU.is_gt, ALU.mult), reads=[sa], writes=[ca])
                    fw.op("dve", lambda e: e.tensor_tensor(sa[:, 0:n], sa[:, 0:n], ca[:, 0:n], ALU.add), reads=[sa, ca], writes=[sa])
                    fw.op("dve", lambda e: e.tensor_scalar(ca[:, 0:n], sa[:, 0:n], -PI, 2 * PI, ALU.is_lt, ALU.mult), reads=[sa], writes=[ca])
                    fw.op("dve", lambda e: e.tensor_tensor(sa[:, 0:n], sa[:, 0:n], ca[:, 0:n], ALU.add), reads=[sa, ca], writes=[sa])
                    fw.op("act", lambda e: e.activation(dst[:, 0:n], sa[:, 0:n], AF.Sin), reads=[sa], writes=[dst])
                trig(sn, 0.0)
                trig(cs, 0.5 * PI)
                for b in range(16):
                    i_, b_, a_, u_, o_, p_ = (a[it % 2] for a in (xin, xb, t1, t2, ob, ps))
                    it += 1
                    fw.dma(nextq(), i_, i_[:, 0:n], src, src[src_blk0 + b, :, col0 + t0:col0 + t0 + n])
                    fw.op("act", lambda e, i_=i_, b_=b_, n=n: e.activation(b_[:, 0:n], i_[:, 0:n], AF.Copy), reads=[i_], writes=[b_])
                    fw.op("pe", lambda e, p_=p_, b_=b_, n=n: e.matmul(p_[:, 0:n], prot[:], b_[:, 0:n], start=True, stop=True),
                          reads=[b_, prot], writes=[p_])
                    fw.op("dve", lambda e, a_=a_, i_=i_, n=n: e.tensor_tensor(a_[:, 0:n], i_[:, 0:n], cs[:, 0:n], ALU.mult),
                          reads=[i_, cs], writes=[a_])
                    fw.op("dve", lambda e, u_=u_, p_=p_, n=n: e.tensor_tensor(u_[:, 0:n], p_[:, 0:n], sn[:, 0:n], ALU.mult),
                          reads=[p_, sn], writes=[u_])
                    fw.op("dve", lambda e, o_=o_, a_=a_, u_=u_, n=n: e.tensor_tensor(o_[:, 0:n], a_[:, 0:n], u_[:, 0:n], ALU.add),
                          reads=[a_, u_], writes=[o_])
                    fw.dma(nextq(), dst, dst[b, :, t0:t0 + n], o_, o_[:, 0:n])
            fw.barrier()
            fw.flush()

    p2_rot(raw_f, 32, full_t, 0, posf_d, kdf_f)
    p2_rot(raw_o, 64, own_t, HALO, qpos_d, qdf_o)

    if phases < 2.5:
        return finish(nc, fw, top, cst, dbg, dbg_o, locals())

    class RR:
        def __init__(self, objs):
            self.objs = objs
            self.i = 0

        def get(self):
            o = self.objs[self.i % len(self.objs)]
            self.i += 1
            return o

    hv = fw.sbuf(cst2, "hv_s", [128, 32], F32, dma=True)
    nw = fw.sbuf(cst2, "nw_s", [128, 3], F32, dma=True)
    flg = fw.sbuf(cst2, "flg_s", [128, NCH], F32, dma=True)
    negea = fw.sbuf(cst2, "negea", [128, 16], F32)
    Sf = fw.sbuf(cst2, "Sf", [128, 16, 128], F32)
    fw.dma("sp", hv, hv[:], hv_d, hv_d[:])
    fw.dma("sp", nw, nw[:], nw_d, nw_d[:])
    fw.dma("sp", flg, flg[:], flags_d, flags_d[:])
    fw.op("act", lambda e: e.activation(negea[:], hv[:, 0:16], AF.Exp), reads=[hv], writes=[negea])
    fw.op("dve", lambda e: e.tensor_scalar(negea[:], negea[:], -1.0, None, ALU.mult), reads=[negea], writes=[negea])
    fw.op("dve", lambda e: e.memset(Sf[:], 0.0), writes=[Sf])
    Sfo = [Obj(Sf.t[:, h, :], f"Sf{h}") for h in range(16)]
    for h in range(16):
        Sfo[h].writers = dict(Sf.writers)

    def p3_dn(kd, vd, qd, zd, ba, ba_row0, nch, masked, with_out):
        with ExitStack() as st:
            pbank = [fw.psum(st, f"dnpf{i}", [128, 512]) for i in range(8)]
            PF = RR([b.view(b.t[:, 0:128]) for b in pbank])
            TF = RR([fw.sbuf(st, f"dntf{i}", [128, 128], F32) for i in range(56)])
            OB = RR([fw.sbuf(st, f"dnob{i}", [128, 128], BF16, dma=True) for i in range(4)])
            bat = [fw.sbuf(st, f"dnba{i}", [128, 32], F32, dma=True) for i in range(2)]
            kt = [fw.sbuf(st, f"dnk{i}", [128, 16, 128], F32, dma=True) for i in range(2)]
            vt = [fw.sbuf(st, f"dnv{i}", [128, 16, 128], F32, dma=True) for i in range(2)]
            qt = [fw.sbuf(st, f"dnq{i}", [128, 16, 128], F32, dma=True) for i in range(2)]
            zt = [fw.sbuf(st, f"dnz{i}", [128, 16, 128], BF16, dma=True) for i in range(2)]
            SC = RR([fw.sbuf(st, f"dnsc{i}", [128, 16], F32) for i in range(32)])
            tri = cm.t[:, 0, :]
            m_il = cm.t[:, 1, :]
            m_sl = cm.t[:, 2, :]
            idf = cm.t[:, 3, :]

            def ew(ek, fn, reads, writes):
                fw.op(ek, fn, reads=reads, writes=writes)

            for n in range(nch):
                c0 = n * 128
                ba_, k_, v_, q_, z_ = bat[n % 2], kt[n % 2], vt[n % 2], qt[n % 2], zt[n % 2]
                fw.dma("sp", ba_, ba_[:], ba, ba[ba_row0 + c0:ba_row0 + c0 + 128, :])
                fw.dma("sp", k_, k_[:], kd, kd[:, :, c0:c0 + 128].rearrange("h p t -> p h t"))
                fw.dma("act", v_, v_[:], vd, vd[:, :, c0:c0 + 128].rearrange("h p t -> p h t"))
                if with_out:
                    fw.dma("sp", q_, q_[:], qd, qd[:, :, c0:c0 + 128].rearrange("h p t -> p h t"))
                    fw.dma("act", z_, z_[:], zd, zd[:, :, c0:c0 + 128].rearrange("h p t -> p h t"))
                beta, negb, xg, ax, ex, g = (SC.get() for _ in range(6))
                ew("act", lambda e, beta=beta, ba_=ba_: e.activation(beta[:], ba_[:, 0:16], AF.Sigmoid), [ba_], [beta])
                ew("dve", lambda e, negb=negb, beta=beta: e.tensor_scalar(negb[:], beta[:], -1.0, None, ALU.mult), [beta], [negb])
                ew("dve", lambda e, xg=xg, ba_=ba_: e.tensor_tensor(xg[:], ba_[:, 16:32], hv[:, 16:32], ALU.add), [ba_, hv], [xg])
                ew("act", lambda e, ax=ax, xg=xg: e.activation(ax[:], xg[:], AF.Abs), [xg], [ax])
                ew("act", lambda e, ex=ex, ax=ax: e.activation(ex[:], ax[:], AF.Exp, scale=-1.0), [ax], [ex])
                ew("act", lambda e, ex=ex: e.activation(ex[:], ex[:], AF.Ln, bias=1.0), [ex], [ex])
                ew("dve", lambda e, xg=xg: e.tensor_scalar(xg[:], xg[:], 0.0, None, ALU.max), [xg], [xg])
                ew("dve", lambda e, xg=xg, ex=ex: e.tensor_tensor(xg[:], xg[:], ex[:], ALU.add), [xg, ex], [xg])
                ew("dve", lambda e, g=g, xg=xg: e.tensor_tensor(g[:], xg[:], negea[:], ALU.mult), [xg, negea], [g])
                pg = PF.get()
                pl = PF.get()
                ew("pe", lambda e, pg=pg, g=g: e.matmul(pg[:, 0:16], tri, g[:], start=True, stop=True), [g, cm], [pg])
                ew("pe", lambda e, pl=pl, g=g: e.matmul(pl[:, 0:16], ones_f[:], g[:], start=True, stop=True), [g, ones_f], [pl])
                gcol, egc, begc, ekt, adec = (SC.get() for _ in range(5))
                ew("act", lambda e, gcol=gcol, pg=pg: e.activation(gcol[:], pg[:, 0:16], AF.Copy), [pg], [gcol])
                ew("act", lambda e, egc=egc, pg=pg: e.activation(egc[:], pg[:, 0:16], AF.Exp), [pg], [egc])
                ew("dve", lambda e, begc=begc, egc=egc, beta=beta: e.tensor_tensor(begc[:], egc[:], beta[:], ALU.mult), [egc, beta], [begc])
                ew("dve", lambda e, ekt=ekt, pl=pl, gcol=gcol: e.tensor_tensor(ekt[:], pl[:, 0:16], gcol[:], ALU.subtract), [pl, gcol], [ekt])
                ew("act", lambda e, ekt=ekt: e.activation(ekt[:], ekt[:], AF.Exp), [ekt], [ekt])
                ew("act", lambda e, adec=adec, pl=pl: e.activation(adec[:], pl[:, 0:16], AF.Exp), [pl], [adec])
                if masked:
                    ew("dve", lambda e, ekt=ekt, n=n: e.tensor_scalar(ekt[:], ekt[:], flg[:, n:n + 1], None, ALU.mult), [ekt, flg], [ekt])
                    ew("dve", lambda e, adec=adec, n=n: e.tensor_scalar(adec[:], adec[:], -1.0, flg[:, n:n + 1], ALU.add, ALU.mult), [adec, flg], [adec])
                    ew("dve", lambda e, adec=adec: e.tensor_scalar(adec[:], adec[:], 1.0, None, ALU.add), [adec], [adec])
                for h in range(16):
                    kT = k_.t[:, h, :]
                    vT = v_.t[:, h, :]
                    gmat = TF.get()
                    ew("dve", lambda e, gmat=gmat, g=g, h=h: e.tensor_scalar(gmat[:], ones_f[:], g[:, h:h + 1], None, ALU.mult), [g, ones_f], [gmat])
                    pgr = PF.get()
                    ew("pe", lambda e, pgr=pgr, gmat=gmat: e.matmul(pgr[:], gmat[:], tri, start=True, stop=True), [gmat, cm], [pgr])
                    if with_out:
                        eg = TF.get()
                        ew("act", lambda e, eg=eg, pgr=pgr: e.activation(eg[:], pgr[:], AF.Exp), [pgr], [eg])
                    dm = TF.get()
                    ew("dve", lambda e, dm=dm, pgr=pgr, gcol=gcol, h=h: e.tensor_scalar(dm[:], pgr[:], gcol[:, h:h + 1], 0.0, ALU.subtract, ALU.max), [pgr, gcol], [dm])
                    ew("act", lambda e, dm=dm: e.activation(dm[:], dm[:], AF.Exp, scale=-1.0), [dm], [dm])
                    ls = TF.get()
                    ew("dve", lambda e, ls=ls, dm=dm: e.tensor_tensor(ls[:], dm[:], m_sl, ALU.mult), [dm, cm], [ls])
                    pkk = PF.get()
                    ew("pe", lambda e, pkk=pkk, kT=kT: e.matmul(pkk[:], kT, kT, start=True, stop=True), [k_], [pkk])
                    N = TF.get()
                    ew("dve", lambda e, N=N, pkk=pkk, negb=negb, ls=ls, h=h: e.scalar_tensor_tensor(N[:], pkk[:], negb[:, h:h + 1], ls[:], ALU.mult, ALU.mult), [pkk, negb, ls], [N])
                    pbt = PF.get()
                    ew("pe", lambda e, pbt=pbt, N=N: e.transpose(pbt[:], N[:], idf), [N, cm], [pbt])
                    B = TF.get()
                    ew("act", lambda e, B=B, pbt=pbt: e.activation(B[:], pbt[:], AF.Copy), [pbt], [B])
                    Pf_ = TF.get()
                    ew("dve", lambda e, Pf_=Pf_, B=B: e.tensor_tensor(Pf_[:], B[:], idf, ALU.add), [B, cm], [Pf_])
                    for lev in range(6):
                        p1, p2 = PF.get(), PF.get()
                        ew("pe", lambda e, p1=p1, N=N, B=B: e.matmul(p1[:], N[:], B[:], start=True, stop=True), [N, B], [p1])
                        ew("pe", lambda e, p2=p2, N=N, B=B: e.matmul(p2[:], B[:], N[:], start=True, stop=True), [N, B], [p2])
                        B2, N2 = TF.get(), TF.get()
                        ew("act", lambda e, B2=B2, p1=p1: e.activation(B2[:], p1[:], AF.Copy), [p1], [B2])
                        ew("dve", lambda e, N2=N2, p2=p2: e.tensor_copy(N2[:], p2[:]), [p2], [N2])
                        B, N = B2, N2
                        p3 = PF.get()
                        ew("pe", lambda e, p3=p3, N=N, Pf_=Pf_: e.matmul(p3[:], N[:], Pf_[:], start=True, stop=True), [N, Pf_], [p3])
                        Pn = TF.get()
                        ew("dve", lambda e, Pn=Pn, Pf_=Pf_, p3=p3: e.tensor_tensor(Pn[:], Pf_[:], p3[:], ALU.add), [Pf_, p3], [Pn])
                        Pf_ = Pn
                    pkt, pvt = PF.get(), PF.get()
                    ew("pe", lambda e, pkt=pkt, kT=kT: e.transpose(pkt[:], kT, idf), [k_, cm], [pkt])
                    ew("pe", lambda e, pvt=pvt, vT=vT: e.transpose(pvt[:], vT, idf), [v_, cm], [pvt])
                    Xk, ktil, Xv = TF.get(), TF.get(), TF.get()
                    ew("dve", lambda e, Xk=Xk, pkt=pkt, begc=begc, h=h: e.tensor_scalar(Xk[:], pkt[:], begc[:, h:h + 1], None, ALU.mult), [pkt, begc], [Xk])
                    ew("dve", lambda e, ktil=ktil, pkt=pkt, ekt=ekt, h=h: e.tensor_scalar(ktil[:], pkt[:], ekt[:, h:h + 1], None, ALU.mult), [pkt, ekt], [ktil])
                    ew("act", lambda e, Xv=Xv, pvt=pvt, beta=beta, h=h: e.activation(Xv[:], pvt[:], AF.Copy, scale=beta[:, h:h + 1]), [pvt, beta], [Xv])
                    pu, pw = PF.get(), PF.get()
                    ew("pe", lambda e, pu=pu, Pf_=Pf_, Xv=Xv: e.matmul(pu[:], Pf_[:], Xv[:], start=True, stop=True), [Pf_, Xv], [pu])
                    ew("pe", lambda e, pw=pw, Pf_=Pf_, Xk=Xk: e.matmul(pw[:], Xk[:], Pf_[:], start=True, stop=True), [Pf_, Xk], [pw])
                    Uf, WT = TF.get(), TF.get()
                    ew("act", lambda e, Uf=Uf, pu=pu: e.activation(Uf[:], pu[:], AF.Copy), [pu], [Uf])
                    ew("dve", lambda e, WT=WT, pw=pw: e.tensor_copy(WT[:], pw[:]), [pw], [WT])
                    pr = PF.get()
                    ew("pe", lambda e, pr=pr, WT=WT, h=h: e.matmul(pr[:], WT[:], Sfo[h][:], start=True, stop=True), [WT, Sfo[h]], [pr])
                    vnew = TF.get()
                    ew("dve", lambda e, vnew=vnew, Uf=Uf, pr=pr: e.tensor_tensor(vnew[:], Uf[:], pr[:], ALU.subtract), [Uf, pr], [vnew])
                    if with_out:
                        qT = q_.t[:, h, :]
                        qtil = TF.get()
                        ew("dve", lambda e, qtil=qtil, qT=qT, eg=eg: e.tensor_tensor(qtil[:], qT, eg[:], ALU.mult), [q_, eg], [qtil])
                        lm = TF.get()
                        ew("dve", lambda e, lm=lm, dm=dm: e.tensor_tensor(lm[:], dm[:], m_il, ALU.mult), [dm, cm], [lm])
                        pqk = PF.get()
                        ew("pe", lambda e, pqk=pqk, qT=qT, kT=kT: e.matmul(pqk[:], qT, kT, start=True, stop=True), [q_, k_], [pqk])
                        attn = TF.get()
                        ew("dve", lambda e, attn=attn, pqk=pqk, lm=lm: e.tensor_tensor(attn[:], pqk[:], lm[:], ALU.mult), [pqk, lm], [attn])
                        pat = PF.get()
                        ew("pe", lambda e, pat=pat, attn=attn: e.transpose(pat[:], attn[:], idf), [attn, cm], [pat])
                        attnT = TF.get()
                        ew("act", lambda e, attnT=attnT, pat=pat: e.activation(attnT[:], pat[:], AF.Copy), [pat], [attnT])
                        po = PF.get()

                        def omm(e, po=po, h=h, qtil=qtil, vnew=vnew, attnT=attnT):
                            e.matmul(po[:], Sfo[h][:], qtil[:], start=True, stop=False)
                            return e.matmul(po[:], vnew[:], attnT[:], start=False, stop=True)
                        ew("pe", omm, [Sfo[h], qtil, vnew, attnT], [po])
                        of, osq = TF.get(), TF.get()
                        ew("act", lambda e, of=of, po=po: e.activation(of[:], po[:], AF.Copy), [po], [of])
                        ew("act", lambda e, osq=osq, of=of: e.activation(osq[:], of[:], AF.Square), [of], [osq])
                        pss = PF.get()
                        ew("pe", lambda e, pss=pss, osq=osq: e.matmul(pss[:], ones_f[:], osq[:], start=True, stop=True), [osq, ones_f], [pss])
                        rr_ = TF.get()
                        ew("dve", lambda e, rr_=rr_, pss=pss: e.tensor_scalar(rr_[:], pss[:], 1.0 / 128, EPS, ALU.mult, ALU.add), [pss], [rr_])
                        ew("act", lambda e, rr_=rr_: e.activation(rr_[:], rr_[:], AF.Sqrt), [rr_], [rr_])
                        ew("dve", lambda e, rr_=rr_: e.reciprocal(rr_[:], rr_[:]), [rr_], [rr_])
                        of2 = TF.get()
                        ew("dve", lambda e, of2=of2, of=of, rr_=rr_: e.tensor_tensor(of2[:], of[:], rr_[:], ALU.mult), [of, rr_], [of2])
                        ob_ = OB.get()
                        ew("dve", lambda e, ob_=ob_, of2=of2, z_=z_, h=h: e.scalar_tensor_tensor(ob_[:], of2[:], nw[:, 0:1], z_[:, h, :], ALU.mult, ALU.mult), [of2, nw, z_], [ob_])
                        fw.dma(nextq(), mixT, mixT[:, h, c0:c0 + 128], ob_, ob_[:])
                    pds = PF.get()
                    ew("pe", lambda e, pds=pds, ktil=ktil, vnew=vnew: e.matmul(pds[:], ktil[:], vnew[:], start=True, stop=True), [ktil, vnew], [pds])
                    ew("dve", lambda e, h=h, adec=adec, pds=pds: e.scalar_tensor_tensor(Sfo[h][:], Sfo[h][:], adec[:, h:h + 1], pds[:], ALU.mult, ALU.add), [Sfo[h], adec, pds], [Sfo[h]])
            fw.barrier()
            fw.flush()

    p3_dn(kdn_f, vdn_f, None, None, ba_f, 0, NCH, True, False)
    if phases == 2.5:
        return finish(nc, fw, top, cst, dbg, dbg_o, locals())
    p3_dn(kdn_o, vdn_o, qdn_o, zdn_o, ba_o, HALO, NCO, False, True)

    if phases < 4:
        return finish(nc, fw, top, cst, dbg, dbg_o, locals())

    def rsqrt_ops(o, ap):
        fw.op("act", lambda e: e.activation(ap, ap, AF.Sqrt), reads=[o], writes=[o])
        fw.op("dve", lambda e: e.reciprocal(ap, ap), reads=[o], writes=[o])

    lamv = fw.sbuf(cst2, "lamv_s", [128, 4], F32, dma=True)
    lam2 = fw.sbuf(cst2, "lam2", [128, 2], F32)
    neglam = fw.sbuf(cst2, "neglam", [128, 1], F32)
    nwdf = fw.sbuf(cst2, "nwdf", [128, 2], F32)
    qpos = fw.sbuf(cst2, "qpos_s", [128, HALO + QT], F32, dma=True)
    kpos = fw.sbuf(cst2, "kpos_s", [128, NCH], F32, dma=True)
    fw.dma("sp", lamv, lamv[:], lamv_d, lamv_d[:])
    fw.dma("sp", qpos, qpos[:], qpos_d, qpos_d[:])
    fw.dma("sp", kpos, kpos[:], kpos_d, kpos_d[:])
    fw.op("dve", lambda e: e.tensor_tensor(lam2[:, 0:1], lamv[:, 0:1], lamv[:, 1:2], ALU.mult), reads=[lamv], writes=[lam2])
    fw.op("dve", lambda e: e.tensor_tensor(lam2[:, 1:2], lamv[:, 2:3], lamv[:, 3:4], ALU.mult), reads=[lamv, lam2], writes=[lam2])
    fw.op("dve", lambda e: e.tensor_scalar(nwdf[:], nw[:, 1:3], 1.0 - LAMBDA_INIT, None, ALU.mult), reads=[nw], writes=[nwdf])

    def p4_attn():
        SCALE = 128.0 ** -0.5
        with ExitStack() as st:
            pst = [fw.psum(st, f"ap{i}", [128, 512]) for i in range(2)]
            pacc = [[fw.psum(st, f"aa{i}_{j}", [128, 512]) for j in range(3)] for i in range(2)]
            vt = fw.sbuf(st, "avt", [128, NCH, 256], BF16, dma=True)
            kt = [fw.sbuf(st, f"akt{i}", [128, S], BF16, dma=True) for i in range(2)]
            qt = [fw.sbuf(st, f"aqt{i}", [128, QT], BF16, dma=True) for i in range(2)]
            O1 = fw.sbuf(st, "aO1", [128, 2, QT], F32)
            pe_t = [fw.sbuf(st, f"ape{i}", [128, TT], F32) for i in range(3)]
            pm_t = [fw.sbuf(st, f"apm{i}", [128, TT], BF16) for i in range(3)]
            rs_t = fw.sbuf(st, "ars", [128, TT], F32)
            cb_t = [fw.sbuf(st, f"acb{i}", [128, TT], F32) for i in range(2)]
            sq_t = [fw.sbuf(st, f"asq{i}", [128, TT], BF16) for i in range(2)]
            ob_t = [fw.sbuf(st, f"aob{i}", [128, TT], BF16, dma=True) for i in range(4)]
            fw.op("pe", lambda e: e.matmul(pst[0][:, 0:2], ones_f[:], lam2[:], start=True, stop=True), reads=[lam2, ones_f], writes=[pst[0]])
            fw.op("act", lambda e: e.activation(lam2[:], pst[0][:, 0:2], AF.Exp), reads=[pst[0]], writes=[lam2])
            fw.op("dve", lambda e: e.tensor_tensor(neglam[:], lam2[:, 1:2], lam2[:, 0:1], ALU.subtract), reads=[lam2], writes=[neglam])
            fw.op("dve", lambda e: e.tensor_scalar(neglam[:], neglam[:], -LAMBDA_INIT, None, ALU.add), reads=[neglam], writes=[neglam])
            it = 0
            ia = 0
            io = 0
            for h in range(NH_DF):
                fw.dma("sp", vt, vt[:], vtok_f, vtok_f[:, h * 256:(h + 1) * 256].rearrange("(kb p) c -> p kb c", p=128))
                for s_ in range(2):
                    sub = 2 * h + s_
                    k_, q_ = kt[sub % 2], qt[sub % 2]
                    fw.dma("act", k_, k_[:], kdf_f, kdf_f[sub])
                    fw.dma("sp", q_, q_[:], qdf_o, qdf_o[sub])
                    for q0 in range(0, QT, TT):
                        n = TT
                        acc = pacc[ia % 2]
                        ia += 1
                        for kb in range(NCH):
                            p_ = pst[it % 2]
                            e_ = pe_t[it % 3]
                            m_ = pm_t[it % 3]
                            it += 1
                            fw.op("pe", lambda e, p_=p_, k_=k_, q_=q_, kb=kb, q0=q0: e.matmul(p_[:, 0:n], k_[:, kb * 128:(kb + 1) * 128], q_[:, q0:q0 + n], start=True, stop=True),
                                  reads=[k_, q_], writes=[p_])
                            fw.op("act", lambda e, e_=e_, p_=p_: e.activation(e_[:, 0:n], p_[:, 0:n], AF.Exp, scale=SCALE), reads=[p_], writes=[e_])
                            fw.op("dve", lambda e, m_=m_, e_=e_, kb=kb, q0=q0: e.scalar_tensor_tensor(m_[:, 0:n], qpos[:, HALO + q0:HALO + q0 + n], kpos[:, kb:kb + 1], e_[:, 0:n], ALU.is_ge, ALU.mult),
                                  reads=[e_, qpos, kpos], writes=[m_])

                            def pv(e, acc=acc, m_=m_, kb=kb):
                                e.matmul(acc[0][:, 0:n], vt[:, kb, 0:128], m_[:, 0:n], start=(kb == 0), stop=(kb == NCH - 1))
                                e.matmul(acc[1][:, 0:n], vt[:, kb, 128:256], m_[:, 0:n], start=(kb == 0), stop=(kb == NCH - 1))
                                return e.matmul(acc[2][:, 0:n], ones_b[:], m_[:, 0:n], start=(kb == 0), stop=(kb == NCH - 1))
                            fw.op("pe", pv, reads=[vt, m_, ones_b], writes=[acc[0], acc[1], acc[2]])
                        fw.op("dve", lambda e, acc=acc: e.reciprocal(rs_t[:, 0:n], acc[2][:, 0:n]), reads=[acc[2]], writes=[rs_t])
                        for c in range(2):
                            if s_ == 0:
                                fw.op("dve", lambda e, acc=acc, c=c, q0=q0: e.tensor_tensor(O1[:, c, q0:q0 + n], acc[c][:, 0:n], rs_t[:, 0:n], ALU.mult),
                                      reads=[acc[c], rs_t], writes=[O1])
                            else:
                                cb_ = cb_t[c]
                                fw.op("dve", lambda e, acc=acc, c=c, cb_=cb_: e.tensor_tensor(cb_[:, 0:n], acc[c][:, 0:n], rs_t[:, 0:n], ALU.mult),
                                      reads=[acc[c], rs_t], writes=[cb_])
                                fw.op("dve", lambda e, c=c, cb_=cb_, q0=q0: e.scalar_tensor_tensor(cb_[:, 0:n], cb_[:, 0:n], neglam[:, 0:1], O1[:, c, q0:q0 + n], ALU.mult, ALU.add),
                                      reads=[cb_, neglam, O1], writes=[cb_])
                                fw.op("act", lambda e, c=c, cb_=cb_: e.activation(sq_t[c][:, 0:n], cb_[:, 0:n], AF.Square), reads=[cb_], writes=[sq_t[c]])
                        if s_ == 1:
                            p_ = pst[it % 2]
                            it += 1

                            def ssm(e, p_=p_):
                                e.matmul(p_[:, 0:n], ones_b[:], sq_t[0][:, 0:n], start=True, stop=False)
                                return e.matmul(p_[:, 0:n], ones_b[:], sq_t[1][:, 0:n], start=False, stop=True)
                            fw.op("pe", ssm, reads=[sq_t[0], sq_t[1], ones_b], writes=[p_])
                            fw.op("dve", lambda e, p_=p_: e.tensor_scalar(rs_t[:, 0:n], p_[:, 0:n], 1.0 / 256, EPS, ALU.mult, ALU.add), reads=[p_], writes=[rs_t])
                            rsqrt_ops(rs_t, rs_t[:, 0:n])
                            for c in range(2):
                                o_ = ob_t[io % 4]
                                io += 1
                                fw.op("dve", lambda e, o_=o_, c=c: e.scalar_tensor_tensor(o_[:, 0:n], cb_t[c][:, 0:n], nwdf[:, c:c + 1], rs_t[:, 0:n], ALU.mult, ALU.mult),
                                      reads=[cb_t[c], nwdf, rs_t], writes=[o_])
                                fw.dma(nextq(), mixT, mixT[:, 16 + 2 * h + c, q0:q0 + n], o_, o_[:, 0:n])
            fw.barrier()
            fw.flush()

    p4_attn()

    if phases < 5:
        return finish(nc, fw, top, cst, dbg, dbg_o, locals())

    cst2.close()
    rstd1 = fw.sbuf(cst, "rstd1", [128, QT], F32)
    w_out_v = w_out.t.rearrange("(kc p) n -> p kc n", p=128)
    w_gate_v = w_gate.t.rearrange("(kc p) n -> p kc n", p=128)
    w_up_v = w_up.t.rearrange("(kc p) n -> p kc n", p=128)
    w_down_v = w_down.t.rearrange("(hb p) n -> p hb n", p=128)

    def p5a():
        with ExitStack() as st:
            mx = fw.sbuf(st, "omx", [128, KC, TT], BF16, dma=True)
            wo = [fw.sbuf(st, f"owo{i}", [128, KC, 512], BF16, dma="sw") for i in range(2)]
            xr = [fw.sbuf(st, f"oxr{i}", [128, TT], F32, dma=True) for i in range(3)]
            sq = [fw.sbuf(st, f"osq{i}", [128, TT], BF16) for i in range(2)]
            ps = [fw.psum(st, f"ops{i}", [128, 512]) for i in range(3)]
            pss = fw.psum(st, "opss", [128, 512])
            it = 0
            ig = 0
            for t0 in range(0, QT, TT):
                n = TT
                fw.dma("sp", mx, mx[:], mixT, mixT[:, :, t0:t0 + n])
                for cg in range(8):
                    w_ = wo[ig % 2]
                    ig += 1
                    fw.dma("pool", w_, w_[:], w_out, w_out_v[:, :, cg * 512:(cg + 1) * 512])
                    for cb in range(4):
                        blk = cg * 4 + cb
                        p_, x_, s_ = ps[it % 3], xr[it % 3], sq[it % 2]
                        it += 1
                        fw.dma(nextq(), x_, x_[:, 0:n], xTo, xTo[:, blk, HALO + t0:HALO + t0 + n])

                        def mm(e, p_=p_, w_=w_, cb=cb):
                            r = None
                            for kc in range(KC):
                                r = e.matmul(p_[:, 0:n], w_[:, kc, cb * 128:(cb + 1) * 128], mx[:, kc, 0:n], start=(kc == 0), stop=(kc == KC - 1))
                            return r
                        fw.op("pe", mm, reads=[w_, mx], writes=[p_])
                        fw.op("dve", lambda e, x_=x_, p_=p_: e.tensor_tensor(x_[:, 0:n], x_[:, 0:n], p_[:, 0:n], ALU.add), reads=[x_, p_], writes=[x_])
                        fw.op("act", lambda e, s_=s_, x_=x_: e.activation(s_[:, 0:n], x_[:, 0:n], AF.Square), reads=[x_], writes=[s_])
                        fw.op("pe", lambda e, s_=s_, blk=blk: e.matmul(pss[:, 0:n], ones_b[:], s_[:, 0:n], start=(blk == 0), stop=(blk == KC - 1)),
                              reads=[s_, ones_b], writes=[pss])
                        fw.dma(nextq(), x1T, x1T[:, blk, t0:t0 + n], x_, x_[:, 0:n])
                fw.op("dve", lambda e, t0=t0: e.tensor_scalar(rstd1[:, t0:t0 + n], pss[:, 0:n], 1.0 / D, EPS, ALU.mult, ALU.add), reads=[pss], writes=[rstd1])
                rsqrt_ops(rstd1, rstd1[:, t0:t0 + n])
            fw.barrier()
            fw.flush()

    p5a()

    def p5b():
        HH = HB // 2
        with ExitStack() as st:
            hT = fw.sbuf(st, "fh", [128, KC, TT], BF16)
            act = fw.sbuf(st, "fact", [128, HB, TT], BF16)
            wg = [fw.sbuf(st, f"fwg{i}", [128, KC, 128], BF16, dma="sw") for i in range(2)]
            wu = [fw.sbuf(st, f"fwu{i}", [128, KC, 128], BF16, dma="sw") for i in range(2)]
            wd = [fw.sbuf(st, f"fwd{i}", [128, HH, 128], BF16, dma="sw") for i in range(2)]
            xr = [fw.sbuf(st, f"fxr{i}", [128, TT], F32, dma=True) for i in range(3)]
            sg = [fw.sbuf(st, f"fsg{i}", [128, TT], F32) for i in range(2)]
            sq = [fw.sbuf(st, f"fsq{i}", [128, TT], BF16) for i in range(2)]
            r2 = fw.sbuf(st, "fr2", [128, TT], F32)
            pg = [fw.psum(st, f"fpg{i}", [128, 512]) for i in range(2)]
            pu = [fw.psum(st, f"fpu{i}", [128, 512]) for i in range(2)]
            pd = [fw.psum(st, f"fpd{i}", [128, 512]) for i in range(2)]
            pss = fw.psum(st, "fpss", [128, 512])
            ix = 0
            iw = 0
            for t0 in range(0, QT, TT):
                n = TT
                for kc in range(KC):
                    x_ = xr[ix % 3]
                    ix += 1
                    fw.dma(nextq(), x_, x_[:, 0:n], x1T, x1T[:, kc, t0:t0 + n])
                    fw.op("dve", lambda e, x_=x_, kc=kc, t0=t0: e.scalar_tensor_tensor(hT[:, kc, 0:n], x_[:, 0:n], lnw[:, 1, kc:kc + 1], rstd1[:, t0:t0 + n], ALU.mult, ALU.mult),
                          reads=[x_, lnw, rstd1], writes=[hT])
                for hb in range(HB):
                    g_, u_, pg_, pu_, sg_ = wg[hb % 2], wu[hb % 2], pg[hb % 2], pu[hb % 2], sg[hb % 2]
                    fw.dma("pool", g_, g_[:], w_gate, w_gate_v[:, :, hb * 128:(hb + 1) * 128])
                    fw.dma("pool", u_, u_[:], w_up, w_up_v[:, :, hb * 128:(hb + 1) * 128])

                    def mmg(e, p_=pg_, w_=g_):
                        r = None
                        for kc in range(KC):
                            r = e.matmul(p_[:, 0:n], w_[:, kc, :], hT[:, kc, 0:n], start=(kc == 0), stop=(kc == KC - 1))
                        return r

                    def mmu(e, p_=pu_, w_=u_):
                        r = None
                        for kc in range(KC):
                            r = e.matmul(p_[:, 0:n], w_[:, kc, :], hT[:, kc, 0:n], start=(kc == 0), stop=(kc == KC - 1))
                        return r
                    fw.op("pe", mmg, reads=[g_, hT], writes=[pg_])
                    fw.op("pe", mmu, reads=[u_, hT], writes=[pu_])
                    fw.op("act", lambda e, sg_=sg_, pg_=pg_: e.activation(sg_[:, 0:n], pg_[:, 0:n], AF.Silu), reads=[pg_], writes=[sg_])
                    fw.op("dve", lambda e, sg_=sg_, pu_=pu_, hb=hb: e.tensor_tensor(act[:, hb, 0:n], sg_[:, 0:n], pu_[:, 0:n], ALU.mult),
                          reads=[sg_, pu_], writes=[act])
                for cb in range(KC):
                    p_ = pd[cb % 2]
                    x_ = xr[ix % 3]
                    ix += 1
                    fw.dma(nextq(), x_, x_[:, 0:n], x1T, x1T[:, cb, t0:t0 + n])
                    for half in range(2):
                        w_ = wd[iw % 2]
                        iw += 1
                        fw.dma("pool", w_, w_[:], w_down, w_down_v[:, half * HH:(half + 1) * HH, cb * 128:(cb + 1) * 128])

                        def mmd(e, p_=p_, w_=w_, half=half):
                            r = None
                            for j in range(HH):
                                hb = half * HH + j
                                r = e.matmul(p_[:, 0:n], w_[:, j, :], act[:, hb, 0:n], start=(hb == 0), stop=(hb == HB - 1))
                            return r
                        fw.op("pe", mmd, reads=[w_, act], writes=[p_])
                    s_ = sq[cb % 2]
                    fw.op("dve", lambda e, x_=x_, p_=p_: e.tensor_tensor(x_[:, 0:n], x_[:, 0:n], p_[:, 0:n], ALU.add), reads=[x_, p_], writes=[x_])
                    fw.op("act", lambda e, s_=s_, x_=x_: e.activation(s_[:, 0:n], x_[:, 0:n], AF.Square), reads=[x_], writes=[s_])
                    fw.op("pe", lambda e, s_=s_, cb=cb: e.matmul(pss[:, 0:n], ones_b[:], s_[:, 0:n], start=(cb == 0), stop=(cb == KC - 1)),
                          reads=[s_, ones_b], writes=[pss])
                    fw.dma(nextq(), x2T, x2T[:, cb, t0:t0 + n], x_, x_[:, 0:n])
                fw.op("dve", lambda e: e.tensor_scalar(r2[:, 0:n], pss[:, 0:n], 1.0 / D, EPS, ALU.mult, ALU.add), reads=[pss], writes=[r2])
                rsqrt_ops(r2, r2[:, 0:n])
                for cb in range(KC):
                    x_ = xr[ix % 3]
                    ix += 1
                    fw.dma(nextq(), x_, x_[:, 0:n], x2T, x2T[:, cb, t0:t0 + n])
                    fw.op("dve", lambda e, x_=x_, cb=cb: e.scalar_tensor_tensor(x_[:, 0:n], x_[:, 0:n], lnw[:, 2, cb:cb + 1], r2[:, 0:n], ALU.mult, ALU.mult),
                          reads=[x_, lnw, r2], writes=[x_])
                    fw.dma(nextq(), outT, outT[:, cb, t0:t0 + n], x_, x_[:, 0:n])
            fw.barrier()
            fw.flush()

    p5b()
    return finish(nc, fw, top, cst, dbg, dbg_o, locals())


def _prep(inputs, S):
    x = np.asarray(inputs["x"], np.float32)
    QT = S // 4
    NCH = S // 128
    g = lambda k: np.asarray(inputs[k], np.float32)
    w_in = np.ascontiguousarray(g("w_in")[0])
    w_out = np.ascontiguousarray(g("w_out")[0])
    w_gate = np.ascontiguousarray(g("w_gate")[0])
    w_up = np.ascontiguousarray(g("w_up")[0])
    w_down = np.ascontiguousarray(g("w_down")[0])
    lnw = np.stack([g("ln_mix_w")[0], g("ln_ffn_w")[0], g("ln_final_w")], 0).reshape(3, 32, 128).transpose(2, 0, 1).copy()
    convw = g("conv_w")[0].reshape(4, 48, 128).transpose(2, 1, 0).copy()
    hv = np.broadcast_to(np.concatenate([g("a_log")[0], g("dt_bias")[0]])[None, :], (128, 32)).copy()
    dfw = g("df_norm_w")[0]
    nw = np.stack([g("dn_norm_w")[0], dfw[:128], dfw[128:]], 1).copy()
    lamv = np.stack([g("lambda_q1")[0], g("lambda_k1")[0], g("lambda_q2")[0], g("lambda_k2")[0]], 1).copy()
    j = np.arange(128)[:, None]
    i = np.arange(128)[None, :]
    prot = np.zeros((128, 128), np.float32)
    for m in range(16):
        prot[m + 16, m] = -1.0
        prot[m, m + 16] = 1.0
    cmask = np.stack([(j <= i), (i <= j), (i < j), (i == j), prot], 0).astype(np.float32).transpose(1, 0, 2).copy()
    kpos = (np.arange(NCH)[None, :] * 128 + np.arange(128)[:, None]).astype(np.float32)
    posf = np.broadcast_to(np.arange(S, dtype=np.float32)[None, :], (128, S)).copy()
    invf = np.zeros((128, 1), np.float32)
    fr = np.array([ROPE_THETA ** (-(2 * k) / 32.0) for k in range(16)], np.float32)
    invf[0:16, 0] = fr
    invf[16:32, 0] = fr
    maps = []
    for c in range(8):
        b, tq = c // 4, c % 4
        xT = x[b].T
        xTf = np.ascontiguousarray(xT.reshape(32, 128, S).transpose(1, 0, 2))
        lo = tq * QT
        own = np.zeros((4096, HALO + QT), np.float32)
        own[:, HALO:] = xT[:, lo:lo + QT]
        if tq > 0:
            own[:, :HALO] = xT[:, lo - HALO:lo]
        xTo = np.ascontiguousarray(own.reshape(32, 128, HALO + QT).transpose(1, 0, 2))
        flags = np.broadcast_to((np.arange(NCH) < lo // 128).astype(np.float32)[None, :], (128, NCH)).copy()
        qp = np.arange(lo - HALO, lo + QT, dtype=np.float32)
        qpos = np.broadcast_to(qp[None, :], (128, HALO + QT)).copy()
        maps.append(dict(xTf=xTf, xTo=xTo, w_in=w_in, w_out=w_out, w_gate=w_gate, w_up=w_up, w_down=w_down,
                         lnw=lnw, convw=convw, hv=hv, nw=nw, lamv=lamv, cmask=cmask, flags=flags, qpos=qpos,
                         kpos=kpos, posf=posf, invf=invf))
    return maps


def kernel(**inputs):
    x = np.asarray(inputs["x"])
    B, S, _ = x.shape
    QT = S // 4
    nc = build_program(S)
    maps = _prep(inputs, S)
    res = run_bass_kernel_spmd(nc, maps, core_ids=list(range(8)))
    out = np.empty((B, S, D), np.float32)
    for c in range(8):
        b, tq = c // 4, c % 4
        o = np.asarray(res.results[c]["outT"], np.float32)
        out[b, tq * QT:(tq + 1) * QT, :] = o.transpose(2, 1, 0).reshape(QT, D)
    return out
```

```python
import math
from contextlib import ExitStack
import numpy as np
import concourse.bass as bass
import concourse.mybir as mybir
from concourse.bass_utils import run_bass_kernel_spmd

F32 = mybir.dt.float32
BF16 = mybir.dt.bfloat16
AF = mybir.ActivationFunctionType
ALU = mybir.AluOpType
AX = mybir.AxisListType

SAME_ENGINE_SYNC = True
import os
DNSTOP = int(os.environ.get("DNSTOP", "0"))
DNHEADS = int(os.environ.get("DNHEADS", "16"))


class Sem:
    def __init__(self, h):
        self.h = h
        self.cnt = 0


class Track:
    def __init__(self):
        self.writers = {}
        self.readers = {}


class Obj:
    def __init__(self, t, name, dsem=None, tr=None, psum=False):
        self.t = t
        self.name = name
        self.tr = tr if tr is not None else Track()
        self.dsem = dsem
        self.psum = psum

    @property
    def writers(self):
        return self.tr.writers

    @writers.setter
    def writers(self, v):
        self.tr.writers = v

    @property
    def readers(self):
        return self.tr.readers

    @readers.setter
    def readers(self, v):
        self.tr.readers = v

    def view(self, ap, name=None):
        return Obj(ap, name or self.name, self.dsem, self.tr, self.psum)

    def __getitem__(self, k):
        return self.t[k]


class FW:
    def __init__(self, nc, stack):
        self.nc = nc
        self.stack = stack
        self.engs = ["pe", "act", "dve", "pool", "sp"]
        self.esem = {k: Sem(stack.enter_context(nc.semaphore("es_" + k))) for k in self.engs}
        self.thunks = {k: [] for k in self.engs}
        self.waited = {k: {} for k in self.engs}
        self.allsems = list(self.esem.values())
        self.ninst = 0

    def new_sem(self, name):
        s = Sem(self.stack.enter_context(self.nc.semaphore(name)))
        self.allsems.append(s)
        return s

    def sbuf(self, st, name, shape, dt, dma=False):
        self.uid = getattr(self, "uid", 0) + 1
        name = f"{name}_{self.uid}"
        t = st.enter_context(self.nc.sbuf_tensor(name, shape, dt))
        sem = None
        if dma:
            pool = self.__dict__.setdefault("sem_pool_" + str(dma), [])
            sem = pool.pop() if pool else self.new_sem("d_" + name)
            st.callback(lambda: pool.append(sem))
        return Obj(t, name, sem)

    def psum(self, st, name, shape, dt=F32):
        self.uid = getattr(self, "uid", 0) + 1
        name = f"{name}_{self.uid}"
        t = st.enter_context(self.nc.psum_tensor(name, shape, dt))
        return Obj(t, name, psum=True)

    def dram(self, name, shape, dt, kind="Internal"):
        t = self.nc.dram_tensor(name, shape, dt, kind=kind).ap()
        return Obj(t, name)

    def _deps(self, reads, writes):
        deps = {}
        for o in reads:
            for s, v in o.writers.items():
                if deps.get(s, 0) < v:
                    deps[s] = v
            if o.psum:
                for s, v in o.readers.items():
                    if deps.get(s, 0) < v:
                        deps[s] = v
        for o in writes:
            for d in (o.writers, o.readers):
                for s, v in d.items():
                    if deps.get(s, 0) < v:
                        deps[s] = v
        return deps

    def _waits(self, ek, deps):
        w = self.waited[ek]
        own = self.esem[ek]
        out = []
        for s, v in deps.items():
            if s is own and not SAME_ENGINE_SYNC:
                continue
            if w.get(s, 0) >= v:
                continue
            w[s] = v
            out.append((s.h, v))
        return out

    def op(self, ek, fn, reads=(), writes=()):
        waits = self._waits(ek, self._deps(reads, writes))
        sem = self.esem[ek]
        sem.cnt += 1
        val = sem.cnt
        h = sem.h

        def thunk(e):
            for sh, v in waits:
                e.wait_ge(sh, v)
            r = fn(e)
            if isinstance(r, (list, tuple)):
                r = r[-1]
            r.then_inc(h, 1)

        self.thunks[ek].append(thunk)
        self.ninst += 1
        for o in reads:
            if o.readers.get(sem, 0) < val:
                o.readers[sem] = val
        for o in writes:
            o.writers = {sem: val}
            o.readers = {}

    def dma(self, qk, out_o, out_ap, in_o, in_ap, sem_obj=None, group=False, **kw):
        if sem_obj is None:
            sem_obj = out_o if out_o.dsem is not None else in_o
        sem = sem_obj.dsem
        assert sem is not None, (out_o.name, in_o.name)
        if group:
            deps = self._deps([in_o], [])
            for s, v in out_o.readers.items():
                deps[s] = max(deps.get(s, 0), v)
            for s, v in out_o.writers.items():
                if s is not sem:
                    deps[s] = max(deps.get(s, 0), v)
        else:
            deps = self._deps([in_o], [out_o])
        waits = self._waits(qk, deps)
        sem.cnt += 16
        val = sem.cnt
        h = sem.h

        def thunk(e):
            for sh, v in waits:
                e.wait_ge(sh, v)
            e.dma_start(out=out_ap, in_=in_ap, **kw).then_inc(h, 16)

        self.thunks[qk].append(thunk)
        self.ninst += 1
        if in_o.readers.get(sem, 0) < val:
            in_o.readers[sem] = val
        if group:
            out_o.writers[sem] = val
        else:
            out_o.writers = {sem: val}
        out_o.readers = {}

    def barrier(self):
        snap = [(s, s.cnt) for s in self.allsems if s.cnt > 0]
        for ek in self.engs:
            waits = self._waits(ek, dict(snap))

            def thunk(e, waits=waits):
                for sh, v in waits:
                    e.wait_ge(sh, v)

            self.thunks[ek].append(thunk)

    def flush(self):
        lists = self.thunks
        with self.nc.Block() as block:
            @block.tensor
            def _(e):
                for t in lists["pe"]:
                    t(e)

            @block.scalar
            def _(e):
                for t in lists["act"]:
                    t(e)

            @block.vector
            def _(e):
                for t in lists["dve"]:
                    t(e)

            @block.gpsimd
            def _(e):
                for t in lists["pool"]:
                    t(e)

            @block.sync
            def _(e):
                for t in lists["sp"]:
                    t(e)
        self.thunks = {k: [] for k in self.engs}
        if max(s.cnt for s in self.esem.values()) > 20000:
            self.epoch = getattr(self, "epoch", 0) + 1
            for k in self.engs:
                ns = Sem(self.stack.enter_context(self.nc.semaphore(f"es_{k}_{self.epoch}")))
                self.esem[k] = ns
                self.allsems.append(ns)

D = 4096
KC = 32
NH_DN = 16
NSUB = 16
NH_DF = 8
FFN = 11008
HB = FFN // 128
PROJ = 14368
OFF_DQ, OFF_DK, OFF_DV, OFF_DZ, OFF_DB, OFF_DA, OFF_FQ, OFF_FK, OFF_FV = 0, 2048, 4096, 6144, 8192, 8208, 8224, 10272, 12320
HALO = 4
EPS = 1e-6
ROPE_THETA = 500000.0
LAMBDA_INIT = 0.8 - 0.6 * math.exp(-0.3 * 0)
PI = math.pi


def finish(nc, fw, top, cst, dbg, dbg_o, env):
    if dbg is not None:
        for i, d in enumerate(dbg):
            src = env[d[0]]
            dbg_o[i].dsem = fw.new_sem(f"dbgsem{i}")
            fw.dma("sp", dbg_o[i], dbg_o[i][:], src, d[3](src), sem_obj=dbg_o[i])
    fw.barrier()
    fw.flush()
    c2 = env.get("cst2")
    if c2 is not None:
        c2.close()
    cst.close()
    top.close()
    return nc


def build_program(S, phases=99, dbg=None):
    QT = S // 4
    TT = min(512, QT)
    NCH = S // 128
    NCO = QT // 128
    nc = bass.Bass("TRN2", target_bir_lowering=False)
    top = ExitStack()
    fw = FW(nc, top)

    def ein(name, shape, dt=F32):
        return fw.dram(name, shape, dt, kind="ExternalInput")

    xTf = ein("xTf", [128, KC, S])
    xTo = ein("xTo", [128, KC, HALO + QT])
    w_in = ein("w_in", [D, PROJ])
    if phases >= 5:
        w_out = ein("w_out", [D, D])
        w_gate = ein("w_gate", [D, FFN])
        w_up = ein("w_up", [D, FFN])
        w_down = ein("w_down", [FFN, D])
    lnw_d = ein("lnw", [128, 3, KC])
    convw_d = ein("convw", [128, 48, 4])
    hv_d = ein("hv", [128, 32])
    nw_d = ein("nw", [128, 3])
    lamv_d = ein("lamv", [128, 4])
    cm_d = ein("cmask", [128, 5, 128])
    flags_d = ein("flags", [128, NCH])
    qpos_d = ein("qpos", [128, HALO + QT])
    kpos_d = ein("kpos", [128, NCH])
    posf_d = ein("posf", [128, S])
    invf_d = ein("invf", [128, 1])
    outT = fw.dram("outT", [128, KC, QT], F32, kind="ExternalOutput")

    xn_f = fw.dram("xn_f", [128, KC, S], BF16)
    xn_o = fw.dram("xn_o", [128, KC, HALO + QT], BF16)
    raw_f = fw.dram("raw_f", [48, 128, S], F32)
    vtok_f = fw.dram("vtok_f", [S, 2048], BF16)
    ba_f = fw.dram("ba_f", [S, 32], F32)
    raw_o = fw.dram("raw_o", [80, 128, HALO + QT], F32)
    ba_o = fw.dram("ba_o", [HALO + QT, 32], F32)
    kdn_f = fw.dram("kdn_f", [16, 128, S], F32)
    vdn_f = fw.dram("vdn_f", [16, 128, S], F32)
    kdf_f = fw.dram("kdf_f", [16, 128, S], BF16)
    qdn_o = fw.dram("qdn_o", [16, 128, QT], F32)
    kdn_o = fw.dram("kdn_o", [16, 128, QT], F32)
    vdn_o = fw.dram("vdn_o", [16, 128, QT], F32)
    zdn_o = fw.dram("zdn_o", [16, 128, QT], BF16)
    qdf_o = fw.dram("qdf_o", [16, 128, QT], BF16)
    mixT = fw.dram("mixT", [128, KC, QT], BF16)
    x1T = fw.dram("x1T", [128, KC, QT], F32)
    x2T = fw.dram("x2T", [128, KC, QT], F32)
    dbg_o = None
    if dbg is not None:
        dbg_o = [fw.dram(f"dbg{i}", list(d[1]), d[2], kind="ExternalOutput") for i, d in enumerate(dbg)]

    evq = ["sp", "act"]
    cnt = {"e": 0, "q": 0}

    def nextq():
        cnt["q"] += 1
        return evq[cnt["q"] % 2]

    def evac_eng():
        cnt["e"] += 1
        return "act" if cnt["e"] % 2 else "dve"

    def interleave(gens):
        gens = list(gens)
        while gens:
            nxt = []
            for g_ in gens:
                try:
                    next(g_)
                    nxt.append(g_)
                except StopIteration:
                    pass
            gens = nxt

    def copy_op(ek, out_o, out_ap, in_o, in_ap):
        if ek == "act":
            fw.op("act", lambda e: e.activation(out_ap, in_ap, AF.Copy), reads=[in_o], writes=[out_o])
        else:
            fw.op(ek, lambda e: e.tensor_copy(out_ap, in_ap), reads=[in_o], writes=[out_o])

    cst = ExitStack()
    ones_b = fw.sbuf(cst, "ones_b", [128, 128], BF16)
    ones_f = fw.sbuf(cst, "ones_f", [128, 128], F32)
    lnw = fw.sbuf(cst, "lnw_s", [128, 3, KC], F32, dma=True)
    cm = fw.sbuf(cst, "cm_s", [128, 5, 128], F32, dma=True)
    idb = fw.sbuf(cst, "idb", [128, 128], BF16)
    fw.op("dve", lambda e: e.memset(ones_b[:], 1.0), writes=[ones_b])
    fw.op("dve", lambda e: e.memset(ones_f[:], 1.0), writes=[ones_f])
    fw.dma("sp", lnw, lnw[:], lnw_d, lnw_d[:])
    fw.dma("sp", cm, cm[:], cm_d, cm_d[:])
    fw.op("dve", lambda e: e.tensor_copy(idb[:], cm[:, 3, :]), reads=[cm], writes=[idb])
    trib = fw.sbuf(cst, "trib", [128, 128], BF16)
    fw.op("dve", lambda e: e.tensor_copy(trib[:], cm[:, 0, :]), reads=[cm], writes=[trib])

    def p0_norm(src, dst, ntok_total, which):
        T0 = min(256, QT)
        with ExitStack() as st:
            xt = [fw.sbuf(st, f"p0x{i}", [128, KC, T0], F32, dma=True) for i in range(2)]
            sq = fw.sbuf(st, "p0sq", [128, KC, T0], BF16)
            xo = [fw.sbuf(st, f"p0o{i}", [128, KC, T0], BF16, dma=True) for i in range(2)]
            rs = [fw.sbuf(st, f"p0r{i}", [128, T0], F32) for i in range(2)]
            ps = [fw.psum(st, f"p0ps{i}", [128, 512]) for i in range(2)]
            tiles = []
            t0 = 0
            while t0 < ntok_total:
                n = min(T0, ntok_total - t0)
                tiles.append((t0, n))
                t0 += n
            for it, (t0, n) in enumerate(tiles):
                x_, o_, r_, p_ = xt[it % 2], xo[it % 2], rs[it % 2], ps[it % 2]
                fw.dma(nextq(), x_, x_[:, :, 0:n], src, src[:, :, t0:t0 + n])
                fw.op("act", lambda e, x_=x_, n=n: e.activation(sq[:, :, 0:n], x_[:, :, 0:n], AF.Square),
                      reads=[x_], writes=[sq])

                def mm(e, p_=p_, n=n):
                    r = None
                    for kc in range(KC):
                        r = e.matmul(p_[:, 0:n], ones_b[:], sq[:, kc, 0:n], start=(kc == 0), stop=(kc == KC - 1))
                    return r
                fw.op("pe", mm, reads=[sq, ones_b], writes=[p_])
                fw.op("dve", lambda e, r_=r_, p_=p_, n=n: e.tensor_scalar(r_[:, 0:n], p_[:, 0:n], 1.0 / D, EPS, ALU.mult, ALU.add),
                      reads=[p_], writes=[r_])
                fw.op("act", lambda e, r_=r_, n=n: e.activation(r_[:, 0:n], r_[:, 0:n], AF.Sqrt), reads=[r_], writes=[r_])
                fw.op("dve", lambda e, r_=r_, n=n: e.reciprocal(r_[:, 0:n], r_[:, 0:n]), reads=[r_], writes=[r_])
                ek = "dve"

                def sc(e, x_=x_, o_=o_, r_=r_, n=n):
                    r = None
                    for kc in range(KC):
                        r = e.scalar_tensor_tensor(o_[:, kc, 0:n], x_[:, kc, 0:n], lnw[:, which, kc:kc + 1],
                                                   r_[:, 0:n], ALU.mult, ALU.mult)
                    return r
                fw.op(ek, sc, reads=[x_, r_, lnw], writes=[o_])
                fw.dma(nextq(), dst, dst[:, :, t0:t0 + n], o_, o_[:, :, 0:n])
            fw.barrier()
            fw.flush()

    p0_norm(xTf, xn_f, S, 0)
    p0_norm(xTo, xn_o, HALO + QT, 0)

    if phases < 1:
        return finish(nc, fw, top, cst, dbg, dbg_o, locals())

    w_in_v = w_in.t.rearrange("(kc p) n -> p kc n", p=128)

    def proj(xn, tok_tiles, jobs):
        GW = 1024
        with ExitStack() as st:
            wt = fw.sbuf(st, "p1w", [128, KC, GW], BF16, dma="sw")
            xt = [fw.sbuf(st, f"p1x{i}", [128, KC, TT], BF16, dma=True) for i in range(2)]
            evf = [fw.sbuf(st, f"p1ef{i}", [128, 512], F32, dma=True) for i in range(4)]
            evb = [fw.sbuf(st, f"p1eb{i}", [128, 512], BF16, dma=True) for i in range(4)]
            ps = [fw.psum(st, f"p1ps{i}", [128, 512]) for i in range(4)]
            k = {"x": 0, "p": 0}
            for (col_lo, ncols, mode, dst, d0) in jobs:
                for g0 in range(0, ncols, GW):
                    gw = min(GW, ncols - g0)
                    fw.dma("pool", wt, wt[:, :, 0:gw], w_in, w_in_v[:, :, col_lo + g0: col_lo + g0 + gw])
                    for (t0, n) in tok_tiles:
                        x_ = xt[k["x"] % 2]
                        k["x"] += 1
                        fw.dma(nextq(), x_, x_[:, :, 0:n], xn, xn[:, :, t0:t0 + n])
                        if mode == "cm":
                            for b0 in range(0, gw, 128):
                                p_ = ps[k["p"] % 4]
                                e_ = evf[k["p"] % 4]
                                k["p"] += 1

                                def mm(e, p_=p_, x_=x_, b0=b0, n=n):
                                    r = None
                                    for kc in range(KC):
                                        r = e.matmul(p_[:, 0:n], wt[:, kc, b0:b0 + 128], x_[:, kc, 0:n],
                                                     start=(kc == 0), stop=(kc == KC - 1))
                                    return r
                                fw.op("pe", mm, reads=[wt, x_], writes=[p_])
                                copy_op(evac_eng(), e_, e_[:, 0:n], p_, p_[:, 0:n])
                                blk = d0 + (g0 + b0) // 128
                                fw.dma(nextq(), dst, dst[blk, :, t0:t0 + n], e_, e_[:, 0:n])
                        else:
                            for s0 in range(0, n, 128):
                                m = min(128, n - s0)
                                for c0 in range(0, gw, 512):
                                    cw = min(512, gw - c0)
                                    p_ = ps[k["p"] % 4]
                                    e_ = (evb if mode == "tmb" else evf)[k["p"] % 4]
                                    k["p"] += 1

                                    def mm(e, p_=p_, x_=x_, s0=s0, m=m, c0=c0, cw=cw):
                                        r = None
                                        for kc in range(KC):
                                            r = e.matmul(p_[0:m, 0:cw], x_[:, kc, s0:s0 + m], wt[:, kc, c0:c0 + cw],
                                                         start=(kc == 0), stop=(kc == KC - 1))
                                        return r
                                    fw.op("pe", mm, reads=[wt, x_], writes=[p_])
                                    copy_op(evac_eng(), e_, e_[0:m, 0:cw], p_, p_[0:m, 0:cw])
                                    fw.dma(nextq(), dst, dst[t0 + s0:t0 + s0 + m, d0 + g0 + c0:d0 + g0 + c0 + cw],
                                           e_, e_[0:m, 0:cw])
            fw.barrier()
            fw.flush()

    full_tiles = [(i * TT, TT) for i in range(S // TT)]
    own_tiles = [(0, HALO)] + [(HALO + i * TT, TT) for i in range(QT // TT)]
    proj(xn_f, full_tiles, [
        (OFF_DK, 2048, "cm", raw_f, 0),
        (OFF_DV, 2048, "cm", raw_f, 16),
        (OFF_FK, 2048, "cm", raw_f, 32),
        (OFF_FV, 2048, "tmb", vtok_f, 0),
        (OFF_DB, 32, "tmf", ba_f, 0),
    ])
    proj(xn_o, own_tiles, [
        (OFF_DQ, 2048, "cm", raw_o, 0),
        (OFF_DK, 2048, "cm", raw_o, 16),
        (OFF_DV, 2048, "cm", raw_o, 32),
        (OFF_DZ, 2048, "cm", raw_o, 48),
        (OFF_FQ, 2048, "cm", raw_o, 64),
        (OFF_DB, 32, "tmf", ba_o, 0),
    ])

    if phases < 2:
        return finish(nc, fw, top, cst, dbg, dbg_o, locals())

    cst2 = ExitStack()
    convw = fw.sbuf(cst2, "convw_s", [128, 48, 4], F32, dma=True)
    fw.dma("sp", convw, convw[:], convw_d, convw_d[:])
    invf = fw.sbuf(cst2, "invf_s", [128, 1], F32, dma=True)
    fw.dma("sp", invf, invf[:], invf_d, invf_d[:])
    prot = fw.sbuf(cst2, "prot", [128, 128], BF16)
    fw.op("dve", lambda e: e.tensor_copy(prot[:], cm[:, 4, :]), reads=[cm], writes=[prot])

    def p2_conv(src, src_blk0, cw_blk0, nblk, tiles, col0, dst, kind):
        NS = 4
        with ExitStack() as st:
            xin = [fw.sbuf(st, f"p2i{i}", [128, 3 + TT], F32, dma=True) for i in range(NS)]
            y = [fw.sbuf(st, f"p2y{i}", [128, TT], F32) for i in range(NS)]
            sl = [fw.sbuf(st, f"p2s{i}", [128, TT], F32) for i in range(NS)]
            sq = [fw.sbuf(st, f"p2q{i}", [128, TT], BF16) for i in range(NS)]
            rr = [fw.sbuf(st, f"p2r{i}", [128, TT], F32) for i in range(NS)]
            ob = [fw.sbuf(st, f"p2o{i}", [128, TT], BF16 if kind == "z" else F32, dma=True) for i in range(NS)]
            ps = [fw.psum(st, f"p2ps{i}", [128, 512]) for i in range(NS)]

            def tile_gen(b, t0, n, slot):
                i_, y_, s_, q_, r_, o_, p_ = (a[slot] for a in (xin, y, sl, sq, rr, ob, ps))
                lo = col0 + t0 - 3
                if lo < 0:
                    fw.op("dve", lambda e: e.memset(i_[:, 0:3], 0.0), writes=[i_])
                    yield
                    fw.dma(nextq(), i_, i_[:, 3:3 + n], src, src[src_blk0 + b, :, col0 + t0:col0 + t0 + n], group=True)
                else:
                    fw.dma(nextq(), i_, i_[:, 0:3 + n], src, src[src_blk0 + b, :, lo:lo + 3 + n])
                yield
                if kind == "z":
                    fw.op("act", lambda e: e.activation(o_[:, 0:n], i_[:, 3:3 + n], AF.Silu), reads=[i_], writes=[o_])
                    yield
                else:
                    cb = cw_blk0 + b
                    fw.op("dve", lambda e: e.tensor_scalar(y_[:, 0:n], i_[:, 0:n], convw[:, cb, 0:1], None, ALU.mult),
                          reads=[i_, convw], writes=[y_])
                    yield
                    for j in range(1, 4):
                        fw.op("dve", lambda e, j=j: e.scalar_tensor_tensor(
                            y_[:, 0:n], i_[:, j:j + n], convw[:, cb, j:j + 1], y_[:, 0:n], ALU.mult, ALU.add),
                            reads=[i_, convw, y_], writes=[y_])
                        yield
                    if kind == "v":
                        fw.op("act", lambda e: e.activation(o_[:, 0:n], y_[:, 0:n], AF.Silu), reads=[y_], writes=[o_])
                        yield
                    else:
                        fw.op("act", lambda e: e.activation(s_[:, 0:n], y_[:, 0:n], AF.Silu), reads=[y_], writes=[s_])
                        yield
                        fw.op("act", lambda e: e.activation(q_[:, 0:n], s_[:, 0:n], AF.Square), reads=[s_], writes=[q_])
                        yield
                        fw.op("pe", lambda e: e.matmul(p_[:, 0:n], ones_b[:], q_[:, 0:n], start=True, stop=True),
                              reads=[q_, ones_b], writes=[p_])
                        yield
                        m_ = 128.0 if kind == "q" else 1.0
                        fw.op("dve", lambda e: e.tensor_scalar(r_[:, 0:n], p_[:, 0:n], m_, m_ * EPS, ALU.mult, ALU.add),
                              reads=[p_], writes=[r_])
                        yield
                        fw.op("act", lambda e: e.activation(r_[:, 0:n], r_[:, 0:n], AF.Sqrt), reads=[r_], writes=[r_])
                        yield
                        fw.op("dve", lambda e: e.reciprocal(r_[:, 0:n], r_[:, 0:n]), reads=[r_], writes=[r_])
                        yield
                        fw.op("dve", lambda e: e.tensor_tensor(o_[:, 0:n], s_[:, 0:n], r_[:, 0:n], ALU.mult),
                              reads=[s_, r_], writes=[o_])
                        yield
                fw.dma(nextq(), dst, dst[b, :, t0:t0 + n], o_, o_[:, 0:n])
                yield

            work = [(b, t0, n) for b in range(nblk) for (t0, n) in tiles]
            for w0 in range(0, len(work), NS):
                interleave(tile_gen(b, t0, n, j) for j, (b, t0, n) in enumerate(work[w0:w0 + NS]))
            fw.barrier()
            fw.flush()

    full_t = [(i * TT, TT) for i in range(S // TT)]
    own_t = [(i * TT, TT) for i in range(QT // TT)]
    p2_conv(raw_f, 0, 16, 16, full_t, 0, kdn_f, "k")
    p2_conv(raw_f, 16, 32, 16, full_t, 0, vdn_f, "v")
    p2_conv(raw_o, 0, 0, 16, own_t, HALO, qdn_o, "q")
    p2_conv(raw_o, 16, 16, 16, own_t, HALO, kdn_o, "k")
    p2_conv(raw_o, 32, 32, 16, own_t, HALO, vdn_o, "v")
    p2_conv(raw_o, 48, 0, 16, own_t, HALO, zdn_o, "z")

    def p2_rot(src, src_blk0, tiles, col0, pos_d, dst):
        with ExitStack() as st:
            pt = fw.sbuf(st, "p2pos", [128, TT], F32, dma=True)
            ca = fw.sbuf(st, "p2ca", [128, TT], F32)
            ti = fw.sbuf(st, "p2ti", [128, TT], mybir.dt.int32)
            sa = fw.sbuf(st, "p2sa", [128, TT], F32)
            cs = fw.sbuf(st, "p2cs", [128, TT], F32)
            sn = fw.sbuf(st, "p2sn", [128, TT], F32)
            xin = [fw.sbuf(st, f"p2x{i}", [128, TT], F32, dma=True) for i in range(4)]
            xb = [fw.sbuf(st, f"p2xb{i}", [128, TT], BF16) for i in range(4)]
            t1 = [fw.sbuf(st, f"p2t{i}", [128, TT], F32) for i in range(4)]
            t2 = [fw.sbuf(st, f"p2u{i}", [128, TT], F32) for i in range(4)]
            ob = [fw.sbuf(st, f"p2ro{i}", [128, TT], BF16, dma=True) for i in range(4)]
            ps = [fw.psum(st, f"p2rp{i}", [128, 512]) for i in range(4)]
            it = 0
            for (t0, n) in tiles:
                fw.dma("sp", pt, pt[:, 0:n], pos_d, pos_d[:, col0 + t0:col0 + t0 + n])
                def trig(dst, shift, n=n):
                    fw.op("dve", lambda e: e.tensor_scalar(sa[:, 0:n], pt[:, 0:n], invf[:, 0:1], shift, ALU.mult, ALU.add),
                          reads=[pt, invf], writes=[sa])
                    fw.op("dve", lambda e: e.tensor_scalar(ca[:, 0:n], sa[:, 0:n], 1.0 / (2 * PI), None, ALU.mult), reads=[sa], writes=[ca])
                    fw.op("dve", lambda e: e.tensor_copy(ti[:, 0:n], ca[:, 0:n]), reads=[ca], writes=[ti])
                    fw.op("dve", lambda e: e.tensor_copy(ca[:, 0:n], ti[:, 0:n]), reads=[ti], writes=[ca])
                    fw.op("dve", lambda e: e.scalar_tensor_tensor(sa[:, 0:n], ca[:, 0:n], -2 * PI, sa[:, 0:n], ALU.mult, ALU.add),
                          reads=[ca, sa], writes=[sa])
                    fw.op("dve", lambda e: e.tensor_scalar(ca[:, 0:n], sa[:, 0:n], PI, -2 * PI, ALU.is_gt, ALU.mult), reads=[sa], writes=[ca])
                    fw.op("dve", lambda e: e.tensor_tensor(sa[:, 0:n], sa[:, 0:n], ca[:, 0:n], ALU.add), reads=[sa, ca], writes=[sa])
                    fw.op("dve", lambda e: e.tensor_scalar(ca[:, 0:n], sa[:, 0:n], -PI, 2 * PI, ALU.is_lt, ALU.mult), reads=[sa], writes=[ca])
                    fw.op("dve", lambda e: e.tensor_tensor(sa[:, 0:n], sa[:, 0:n], ca[:, 0:n], ALU.add), reads=[sa, ca], writes=[sa])
                    fw.op("act", lambda e: e.activation(dst[:, 0:n], sa[:, 0:n], AF.Sin), reads=[sa], writes=[dst])
                trig(sn, 0.0)
                trig(cs, 0.5 * PI)
                def blk_gen(b, slot, t0=t0, n=n):
                    i_, b_, a_, u_, o_, p_ = (a[slot] for a in (xin, xb, t1, t2, ob, ps))
                    fw.dma(nextq(), i_, i_[:, 0:n], src, src[src_blk0 + b, :, col0 + t0:col0 + t0 + n])
                    yield
                    fw.op("act", lambda e: e.activation(b_[:, 0:n], i_[:, 0:n], AF.Copy), reads=[i_], writes=[b_])
                    yield
                    fw.op("pe", lambda e: e.matmul(p_[:, 0:n], prot[:], b_[:, 0:n], start=True, stop=True),
                          reads=[b_, prot], writes=[p_])
                    yield
                    fw.op("dve", lambda e: e.tensor_tensor(a_[:, 0:n], i_[:, 0:n], cs[:, 0:n], ALU.mult),
                          reads=[i_, cs], writes=[a_])
                    yield
                    fw.op("dve", lambda e: e.tensor_tensor(u_[:, 0:n], p_[:, 0:n], sn[:, 0:n], ALU.mult),
                          reads=[p_, sn], writes=[u_])
                    yield
                    fw.op("dve", lambda e: e.tensor_tensor(o_[:, 0:n], a_[:, 0:n], u_[:, 0:n], ALU.add),
                          reads=[a_, u_], writes=[o_])
                    yield
                    fw.dma(nextq(), dst, dst[b, :, t0:t0 + n], o_, o_[:, 0:n])
                    yield
                for b0 in range(0, 16, 4):
                    interleave(blk_gen(b0 + j, j) for j in range(4))
            fw.barrier()
            fw.flush()

    p2_rot(raw_f, 32, full_t, 0, posf_d, kdf_f)
    p2_rot(raw_o, 64, own_t, HALO, qpos_d, qdf_o)

    if phases < 2.5:
        return finish(nc, fw, top, cst, dbg, dbg_o, locals())

    class RR:
        def __init__(self, objs):
            self.objs = objs
            self.i = 0

        def get(self):
            o = self.objs[self.i % len(self.objs)]
            self.i += 1
            return o

    hv = fw.sbuf(cst2, "hv_s", [128, 32], F32, dma=True)
    nw = fw.sbuf(cst2, "nw_s", [128, 3], F32, dma=True)
    flg = fw.sbuf(cst2, "flg_s", [128, NCH], F32, dma=True)
    negea = fw.sbuf(cst2, "negea", [128, 16], F32)
    Sf = fw.sbuf(cst2, "Sf", [128, 16, 128], F32)
    fw.dma("sp", hv, hv[:], hv_d, hv_d[:])
    fw.dma("sp", nw, nw[:], nw_d, nw_d[:])
    fw.dma("sp", flg, flg[:], flags_d, flags_d[:])
    fw.op("act", lambda e: e.activation(negea[:], hv[:, 0:16], AF.Exp), reads=[hv], writes=[negea])
    fw.op("dve", lambda e: e.tensor_scalar(negea[:], negea[:], -1.0, None, ALU.mult), reads=[negea], writes=[negea])
    fw.op("dve", lambda e: e.memset(Sf[:], 0.0), writes=[Sf])
    Sfo = [Obj(Sf.t[:, h, :], f"Sf{h}") for h in range(16)]
    for h in range(16):
        Sfo[h].writers = dict(Sf.writers)

    def p3_dn(kd, vd, qd, zd, ba, ba_row0, nch, masked, with_out):
        with ExitStack() as st:
            pbank = [fw.psum(st, f"dnpf{i}", [128, 512]) for i in range(8)]
            PFs = [RR([b.view(b.t[:, 0:128]) for b in pbank[2 * j:2 * j + 2]]) for j in range(4)]
            PF = PFs[0]
            TFs = [RR([fw.sbuf(st, f"dntf{j}_{i}", [128, 128], F32) for i in range(48)]) for j in range(4)]
            OB = RR([fw.sbuf(st, f"dnob{i}", [128, 128], BF16, dma=True) for i in range(8)])
            bat = [fw.sbuf(st, f"dnba{i}", [128, 32], F32, dma=True) for i in range(2)]
            kt = [fw.sbuf(st, f"dnk{i}", [128, 16, 128], F32, dma=True) for i in range(2)]
            vt = [fw.sbuf(st, f"dnv{i}", [128, 16, 128], F32, dma=True) for i in range(2)]
            qt = [fw.sbuf(st, f"dnq{i}", [128, 16, 128], F32, dma=True) for i in range(2)]
            zt = [fw.sbuf(st, f"dnz{i}", [128, 16, 128], BF16, dma=True) for i in range(2)]
            SC = RR([fw.sbuf(st, f"dnsc{i}", [128, 16], F32) for i in range(32)])
            tri = cm.t[:, 0, :]
            m_il = cm.t[:, 1, :]
            m_sl = cm.t[:, 2, :]
            idf = cm.t[:, 3, :]

            def ew(ek, fn, reads, writes):
                fw.op(ek, fn, reads=reads, writes=writes)

            for n in range(nch):
                c0 = n * 128
                ba_, k_, v_, q_, z_ = bat[n % 2], kt[n % 2], vt[n % 2], qt[n % 2], zt[n % 2]
                fw.dma("sp", ba_, ba_[:], ba, ba[ba_row0 + c0:ba_row0 + c0 + 128, :])
                fw.dma("sp", k_, k_[:], kd, kd[:, :, c0:c0 + 128].rearrange("h p t -> p h t"))
                fw.dma("act", v_, v_[:], vd, vd[:, :, c0:c0 + 128].rearrange("h p t -> p h t"))
                if with_out:
                    fw.dma("sp", q_, q_[:], qd, qd[:, :, c0:c0 + 128].rearrange("h p t -> p h t"))
                    fw.dma("act", z_, z_[:], zd, zd[:, :, c0:c0 + 128].rearrange("h p t -> p h t"))
                beta, negb, xg, ax, ex, g = (SC.get() for _ in range(6))
                ew("act", lambda e, beta=beta, ba_=ba_: e.activation(beta[:], ba_[:, 0:16], AF.Sigmoid), [ba_], [beta])
                ew("dve", lambda e, negb=negb, beta=beta: e.tensor_scalar(negb[:], beta[:], -1.0, None, ALU.mult), [beta], [negb])
                ew("dve", lambda e, xg=xg, ba_=ba_: e.tensor_tensor(xg[:], ba_[:, 16:32], hv[:, 16:32], ALU.add), [ba_, hv], [xg])
                ew("act", lambda e, ax=ax, xg=xg: e.activation(ax[:], xg[:], AF.Abs), [xg], [ax])
                ew("act", lambda e, ex=ex, ax=ax: e.activation(ex[:], ax[:], AF.Exp, scale=-1.0), [ax], [ex])
                ew("act", lambda e, ex=ex: e.activation(ex[:], ex[:], AF.Ln, bias=1.0), [ex], [ex])
                ew("dve", lambda e, xg=xg: e.tensor_scalar(xg[:], xg[:], 0.0, None, ALU.max), [xg], [xg])
                ew("dve", lambda e, xg=xg, ex=ex: e.tensor_tensor(xg[:], xg[:], ex[:], ALU.add), [xg, ex], [xg])
                ew("dve", lambda e, g=g, xg=xg: e.tensor_tensor(g[:], xg[:], negea[:], ALU.mult), [xg, negea], [g])
                pg = PF.get()
                pl = PF.get()
                ew("pe", lambda e, pg=pg, g=g: e.matmul(pg[:, 0:16], tri, g[:], start=True, stop=True), [g, cm], [pg])
                ew("pe", lambda e, pl=pl, g=g: e.matmul(pl[:, 0:16], ones_f[:], g[:], start=True, stop=True), [g, ones_f], [pl])
                gcol, egc, begc, ekt, adec = (SC.get() for _ in range(5))
                ew("act", lambda e, gcol=gcol, pg=pg: e.activation(gcol[:], pg[:, 0:16], AF.Copy), [pg], [gcol])
                ew("act", lambda e, egc=egc, pg=pg: e.activation(egc[:], pg[:, 0:16], AF.Exp), [pg], [egc])
                ew("dve", lambda e, begc=begc, egc=egc, beta=beta: e.tensor_tensor(begc[:], egc[:], beta[:], ALU.mult), [egc, beta], [begc])
                ew("dve", lambda e, ekt=ekt, pl=pl, gcol=gcol: e.tensor_tensor(ekt[:], pl[:, 0:16], gcol[:], ALU.subtract), [pl, gcol], [ekt])
                ew("act", lambda e, ekt=ekt: e.activation(ekt[:], ekt[:], AF.Exp), [ekt], [ekt])
                ew("act", lambda e, adec=adec, pl=pl: e.activation(adec[:], pl[:, 0:16], AF.Exp), [pl], [adec])
                if masked:
                    ew("dve", lambda e, ekt=ekt, n=n: e.tensor_scalar(ekt[:], ekt[:], flg[:, n:n + 1], None, ALU.mult), [ekt, flg], [ekt])
                    ew("dve", lambda e, adec=adec, n=n: e.tensor_scalar(adec[:], adec[:], -1.0, flg[:, n:n + 1], ALU.add, ALU.mult), [adec, flg], [adec])
                    ew("dve", lambda e, adec=adec: e.tensor_scalar(adec[:], adec[:], 1.0, None, ALU.add), [adec], [adec])
                def head_gen(h, slot, k_=k_, v_=v_, q_=q_, z_=z_, g=g, gcol=gcol, negb=negb, begc=begc, ekt=ekt, beta=beta, adec=adec, c0=c0):
                    TF = TFs[slot]
                    PF = PFs[slot]
                    kT = k_.t[:, h, :]
                    vT = v_.t[:, h, :]
                    gmat = TF.get()
                    ew("dve", lambda e, gmat=gmat, g=g, h=h: e.tensor_scalar(gmat[:], ones_f[:], g[:, h:h + 1], None, ALU.mult), [g, ones_f], [gmat])
                    yield
                    pgr = PF.get()
                    ew("pe", lambda e, pgr=pgr, gmat=gmat: e.matmul(pgr[:], gmat[:], tri, start=True, stop=True), [gmat, cm], [pgr])
                    yield
                    if with_out:
                        eg = TF.get()
                        ew("act", lambda e, eg=eg, pgr=pgr: e.activation(eg[:], pgr[:], AF.Exp), [pgr], [eg])
                        yield
                    dm = TF.get()
                    ew("dve", lambda e, dm=dm, pgr=pgr, gcol=gcol, h=h: e.tensor_scalar(dm[:], pgr[:], gcol[:, h:h + 1], 0.0, ALU.subtract, ALU.max), [pgr, gcol], [dm])
                    yield
                    ew("act", lambda e, dm=dm: e.activation(dm[:], dm[:], AF.Exp, scale=-1.0), [dm], [dm])
                    yield
                    ls = TF.get()
                    ew("dve", lambda e, ls=ls, dm=dm: e.tensor_tensor(ls[:], dm[:], m_sl, ALU.mult), [dm, cm], [ls])
                    yield
                    pkk = PF.get()
                    ew("pe", lambda e, pkk=pkk, kT=kT: e.matmul(pkk[:], kT, kT, start=True, stop=True), [k_], [pkk])
                    yield
                    N = TF.get()
                    ew("dve", lambda e, N=N, pkk=pkk, negb=negb, ls=ls, h=h: e.scalar_tensor_tensor(N[:], pkk[:], negb[:, h:h + 1], ls[:], ALU.mult, ALU.mult), [pkk, negb, ls], [N])
                    yield
                    pbt = PF.get()
                    ew("pe", lambda e, pbt=pbt, N=N: e.transpose(pbt[:], N[:], idf), [N, cm], [pbt])
                    yield
                    B = TF.get()
                    ew("act", lambda e, B=B, pbt=pbt: e.activation(B[:], pbt[:], AF.Copy), [pbt], [B])
                    yield
                    Pf_ = TF.get()
                    ew("dve", lambda e, Pf_=Pf_, B=B: e.tensor_tensor(Pf_[:], B[:], idf, ALU.add), [B, cm], [Pf_])
                    yield
                    for lev in range(6):
                        p1, p2 = PF.get(), PF.get()
                        ew("pe", lambda e, p1=p1, N=N, B=B: e.matmul(p1[:], N[:], B[:], start=True, stop=True), [N, B], [p1])
                        yield
                        ew("pe", lambda e, p2=p2, N=N, B=B: e.matmul(p2[:], B[:], N[:], start=True, stop=True), [N, B], [p2])
                        yield
                        B2, N2 = TF.get(), TF.get()
                        ew("act", lambda e, B2=B2, p1=p1: e.activation(B2[:], p1[:], AF.Copy), [p1], [B2])
                        yield
                        ew("dve", lambda e, N2=N2, p2=p2: e.tensor_copy(N2[:], p2[:]), [p2], [N2])
                        yield
                        B, N = B2, N2
                        p3 = PF.get()
                        ew("pe", lambda e, p3=p3, N=N, Pf_=Pf_: e.matmul(p3[:], N[:], Pf_[:], start=True, stop=True), [N, Pf_], [p3])
                        yield
                        Pn = TF.get()
                        ew("dve", lambda e, Pn=Pn, Pf_=Pf_, p3=p3: e.tensor_tensor(Pn[:], Pf_[:], p3[:], ALU.add), [Pf_, p3], [Pn])
                        yield
                        Pf_ = Pn
                    pkt, pvt = PF.get(), PF.get()
                    ew("pe", lambda e, pkt=pkt, kT=kT: e.transpose(pkt[:], kT, idf), [k_, cm], [pkt])
                    yield
                    ew("pe", lambda e, pvt=pvt, vT=vT: e.transpose(pvt[:], vT, idf), [v_, cm], [pvt])
                    yield
                    Xk, ktil, Xv = TF.get(), TF.get(), TF.get()
                    ew("dve", lambda e, Xk=Xk, pkt=pkt, begc=begc, h=h: e.tensor_scalar(Xk[:], pkt[:], begc[:, h:h + 1], None, ALU.mult), [pkt, begc], [Xk])
                    yield
                    ew("dve", lambda e, ktil=ktil, pkt=pkt, ekt=ekt, h=h: e.tensor_scalar(ktil[:], pkt[:], ekt[:, h:h + 1], None, ALU.mult), [pkt, ekt], [ktil])
                    yield
                    ew("act", lambda e, Xv=Xv, pvt=pvt, beta=beta, h=h: e.activation(Xv[:], pvt[:], AF.Copy, scale=beta[:, h:h + 1]), [pvt, beta], [Xv])
                    yield
                    pu, pw = PF.get(), PF.get()
                    ew("pe", lambda e, pu=pu, Pf_=Pf_, Xv=Xv: e.matmul(pu[:], Pf_[:], Xv[:], start=True, stop=True), [Pf_, Xv], [pu])
                    yield
                    ew("pe", lambda e, pw=pw, Pf_=Pf_, Xk=Xk: e.matmul(pw[:], Xk[:], Pf_[:], start=True, stop=True), [Pf_, Xk], [pw])
                    yield
                    Uf, WT = TF.get(), TF.get()
                    ew("act", lambda e, Uf=Uf, pu=pu: e.activation(Uf[:], pu[:], AF.Copy), [pu], [Uf])
                    yield
                    ew("dve", lambda e, WT=WT, pw=pw: e.tensor_copy(WT[:], pw[:]), [pw], [WT])
                    yield
                    pr = PF.get()
                    ew("pe", lambda e, pr=pr, WT=WT, h=h: e.matmul(pr[:], WT[:], Sfo[h][:], start=True, stop=True), [WT, Sfo[h]], [pr])
                    yield
                    vnew = TF.get()
                    ew("dve", lambda e, vnew=vnew, Uf=Uf, pr=pr: e.tensor_tensor(vnew[:], Uf[:], pr[:], ALU.subtract), [Uf, pr], [vnew])
                    yield
                    if with_out:
                        qT = q_.t[:, h, :]
                        qtil = TF.get()
                        ew("dve", lambda e, qtil=qtil, qT=qT, eg=eg: e.tensor_tensor(qtil[:], qT, eg[:], ALU.mult), [q_, eg], [qtil])
                        yield
                        lm = TF.get()
                        ew("dve", lambda e, lm=lm, dm=dm: e.tensor_tensor(lm[:], dm[:], m_il, ALU.mult), [dm, cm], [lm])
                        yield
                        pqk = PF.get()
                        ew("pe", lambda e, pqk=pqk, qT=qT, kT=kT: e.matmul(pqk[:], qT, kT, start=True, stop=True), [q_, k_], [pqk])
                        yield
                        attn = TF.get()
                        ew("dve", lambda e, attn=attn, pqk=pqk, lm=lm: e.tensor_tensor(attn[:], pqk[:], lm[:], ALU.mult), [pqk, lm], [attn])
                        yield
                        pat = PF.get()
                        ew("pe", lambda e, pat=pat, attn=attn: e.transpose(pat[:], attn[:], idf), [attn, cm], [pat])
                        yield
                        attnT = TF.get()
                        ew("act", lambda e, attnT=attnT, pat=pat: e.activation(attnT[:], pat[:], AF.Copy), [pat], [attnT])
                        yield
                        po = PF.get()

                        def omm(e, po=po, h=h, qtil=qtil, vnew=vnew, attnT=attnT):
                            e.matmul(po[:], Sfo[h][:], qtil[:], start=True, stop=False)
                            return e.matmul(po[:], vnew[:], attnT[:], start=False, stop=True)
                        ew("pe", omm, [Sfo[h], qtil, vnew, attnT], [po])
                        yield
                        of, osq = TF.get(), TF.get()
                        ew("act", lambda e, of=of, po=po: e.activation(of[:], po[:], AF.Copy), [po], [of])
                        yield
                        ew("act", lambda e, osq=osq, of=of: e.activation(osq[:], of[:], AF.Square), [of], [osq])
                        yield
                        pss = PF.get()
                        ew("pe", lambda e, pss=pss, osq=osq: e.matmul(pss[:], ones_f[:], osq[:], start=True, stop=True), [osq, ones_f], [pss])
                        yield
                        rr_ = TF.get()
                        ew("dve", lambda e, rr_=rr_, pss=pss: e.tensor_scalar(rr_[:], pss[:], 1.0 / 128, EPS, ALU.mult, ALU.add), [pss], [rr_])
                        yield
                        ew("act", lambda e, rr_=rr_: e.activation(rr_[:], rr_[:], AF.Sqrt), [rr_], [rr_])
                        yield
                        ew("dve", lambda e, rr_=rr_: e.reciprocal(rr_[:], rr_[:]), [rr_], [rr_])
                        yield
                        of2 = TF.get()
                        ew("dve", lambda e, of2=of2, of=of, rr_=rr_: e.tensor_tensor(of2[:], of[:], rr_[:], ALU.mult), [of, rr_], [of2])
                        yield
                        ob_ = OB.get()
                        ew("dve", lambda e, ob_=ob_, of2=of2, z_=z_, h=h: e.scalar_tensor_tensor(ob_[:], of2[:], nw[:, 0:1], z_[:, h, :], ALU.mult, ALU.mult), [of2, nw, z_], [ob_])
                        yield
                        fw.dma(nextq(), mixT, mixT[:, h, c0:c0 + 128], ob_, ob_[:])
                        yield
                    pds = PF.get()
                    ew("pe", lambda e, pds=pds, ktil=ktil, vnew=vnew: e.matmul(pds[:], ktil[:], vnew[:], start=True, stop=True), [ktil, vnew], [pds])
                    yield
                    ew("dve", lambda e, h=h, adec=adec, pds=pds: e.scalar_tensor_tensor(Sfo[h][:], Sfo[h][:], adec[:, h:h + 1], pds[:], ALU.mult, ALU.add), [Sfo[h], adec, pds], [Sfo[h]])
                    yield

                for h0 in range(0, 16, 4):
                    interleave(head_gen(h0 + j, j) for j in range(4))
            fw.barrier()
            fw.flush()

    p3_dn(kdn_f, vdn_f, None, None, ba_f, 0, NCH, True, False)
    if phases == 2.5:
        return finish(nc, fw, top, cst, dbg, dbg_o, locals())
    p3_dn(kdn_o, vdn_o, qdn_o, zdn_o, ba_o, HALO, NCO, False, True)

    if phases < 4:
        return finish(nc, fw, top, cst, dbg, dbg_o, locals())

    def rsqrt_ops(o, ap):
        fw.op("act", lambda e: e.activation(ap, ap, AF.Sqrt), reads=[o], writes=[o])
        fw.op("dve", lambda e: e.reciprocal(ap, ap), reads=[o], writes=[o])

    lamv = fw.sbuf(cst2, "lamv_s", [128, 4], F32, dma=True)
    lam2 = fw.sbuf(cst2, "lam2", [128, 2], F32)
    neglam = fw.sbuf(cst2, "neglam", [128, 1], F32)
    nwdf = fw.sbuf(cst2, "nwdf", [128, 2], F32)
    qpos = fw.sbuf(cst2, "qpos_s", [128, HALO + QT], F32, dma=True)
    kpos = fw.sbuf(cst2, "kpos_s", [128, NCH], F32, dma=True)
    fw.dma("sp", lamv, lamv[:], lamv_d, lamv_d[:])
    fw.dma("sp", qpos, qpos[:], qpos_d, qpos_d[:])
    fw.dma("sp", kpos, kpos[:], kpos_d, kpos_d[:])
    fw.op("dve", lambda e: e.tensor_tensor(lam2[:, 0:1], lamv[:, 0:1], lamv[:, 1:2], ALU.mult), reads=[lamv], writes=[lam2])
    fw.op("dve", lambda e: e.tensor_tensor(lam2[:, 1:2], lamv[:, 2:3], lamv[:, 3:4], ALU.mult), reads=[lamv, lam2], writes=[lam2])
    fw.op("dve", lambda e: e.tensor_scalar(nwdf[:], nw[:, 1:3], 1.0 - LAMBDA_INIT, None, ALU.mult), reads=[nw], writes=[nwdf])

    def p4_attn():
        SCALE = 128.0 ** -0.5
        with ExitStack() as st:
            pst = [fw.psum(st, f"ap{i}", [128, 512]) for i in range(2)]
            pacc = [[fw.psum(st, f"aa{i}_{j}", [128, 512]) for j in range(3)] for i in range(2)]
            vt = fw.sbuf(st, "avt", [128, NCH, 256], BF16, dma=True)
            kt = [fw.sbuf(st, f"akt{i}", [128, S], BF16, dma=True) for i in range(2)]
            qt = [fw.sbuf(st, f"aqt{i}", [128, QT], BF16, dma=True) for i in range(2)]
            O1 = fw.sbuf(st, "aO1", [128, 2, QT], F32)
            pe_t = [fw.sbuf(st, f"ape{i}", [128, TT], F32) for i in range(3)]
            pm_t = [fw.sbuf(st, f"apm{i}", [128, TT], BF16) for i in range(3)]
            rs_t = fw.sbuf(st, "ars", [128, TT], F32)
            cb_t = [fw.sbuf(st, f"acb{i}", [128, TT], F32) for i in range(2)]
            sq_t = [fw.sbuf(st, f"asq{i}", [128, TT], BF16) for i in range(2)]
            ob_t = [fw.sbuf(st, f"aob{i}", [128, TT], BF16, dma=True) for i in range(4)]
            fw.op("pe", lambda e: e.matmul(pst[0][:, 0:2], ones_f[:], lam2[:], start=True, stop=True), reads=[lam2, ones_f], writes=[pst[0]])
            fw.op("act", lambda e: e.activation(lam2[:], pst[0][:, 0:2], AF.Exp), reads=[pst[0]], writes=[lam2])
            fw.op("dve", lambda e: e.tensor_tensor(neglam[:], lam2[:, 1:2], lam2[:, 0:1], ALU.subtract), reads=[lam2], writes=[neglam])
            fw.op("dve", lambda e: e.tensor_scalar(neglam[:], neglam[:], -LAMBDA_INIT, None, ALU.add), reads=[neglam], writes=[neglam])
            it = 0
            ia = 0
            io = 0
            for h in range(NH_DF):
                fw.dma("sp", vt, vt[:], vtok_f, vtok_f[:, h * 256:(h + 1) * 256].rearrange("(kb p) c -> p kb c", p=128))
                for s_ in range(2):
                    sub = 2 * h + s_
                    k_, q_ = kt[sub % 2], qt[sub % 2]
                    fw.dma("act", k_, k_[:], kdf_f, kdf_f[sub])
                    fw.dma("sp", q_, q_[:], qdf_o, qdf_o[sub])
                    for q0 in range(0, QT, TT):
                        n = TT
                        acc = pacc[ia % 2]
                        ia += 1
                        for kb in range(NCH):
                            p_ = pst[it % 2]
                            e_ = pe_t[it % 3]
                            m_ = pm_t[it % 3]
                            it += 1
                            fw.op("pe", lambda e, p_=p_, k_=k_, q_=q_, kb=kb, q0=q0: e.matmul(p_[:, 0:n], k_[:, kb * 128:(kb + 1) * 128], q_[:, q0:q0 + n], start=True, stop=True),
                                  reads=[k_, q_], writes=[p_])
                            fw.op("act", lambda e, e_=e_, p_=p_: e.activation(e_[:, 0:n], p_[:, 0:n], AF.Exp, scale=SCALE), reads=[p_], writes=[e_])
                            fw.op("dve", lambda e, m_=m_, e_=e_, kb=kb, q0=q0: e.scalar_tensor_tensor(m_[:, 0:n], qpos[:, HALO + q0:HALO + q0 + n], kpos[:, kb:kb + 1], e_[:, 0:n], ALU.is_ge, ALU.mult),
                                  reads=[e_, qpos, kpos], writes=[m_])

                            def pv(e, acc=acc, m_=m_, kb=kb):
                                e.matmul(acc[0][:, 0:n], vt[:, kb, 0:128], m_[:, 0:n], start=(kb == 0), stop=(kb == NCH - 1))
                                e.matmul(acc[1][:, 0:n], vt[:, kb, 128:256], m_[:, 0:n], start=(kb == 0), stop=(kb == NCH - 1))
                                return e.matmul(acc[2][:, 0:n], ones_b[:], m_[:, 0:n], start=(kb == 0), stop=(kb == NCH - 1))
                            fw.op("pe", pv, reads=[vt, m_, ones_b], writes=[acc[0], acc[1], acc[2]])
                        fw.op("dve", lambda e, acc=acc: e.reciprocal(rs_t[:, 0:n], acc[2][:, 0:n]), reads=[acc[2]], writes=[rs_t])
                        for c in range(2):
                            if s_ == 0:
                                fw.op("dve", lambda e, acc=acc, c=c, q0=q0: e.tensor_tensor(O1[:, c, q0:q0 + n], acc[c][:, 0:n], rs_t[:, 0:n], ALU.mult),
                                      reads=[acc[c], rs_t], writes=[O1])
                            else:
                                cb_ = cb_t[c]
                                fw.op("dve", lambda e, acc=acc, c=c, cb_=cb_: e.tensor_tensor(cb_[:, 0:n], acc[c][:, 0:n], rs_t[:, 0:n], ALU.mult),
                                      reads=[acc[c], rs_t], writes=[cb_])
                                fw.op("dve", lambda e, c=c, cb_=cb_, q0=q0: e.scalar_tensor_tensor(cb_[:, 0:n], cb_[:, 0:n], neglam[:, 0:1], O1[:, c, q0:q0 + n], ALU.mult, ALU.add),
                                      reads=[cb_, neglam, O1], writes=[cb_])
                                fw.op("act", lambda e, c=c, cb_=cb_: e.activation(sq_t[c][:, 0:n], cb_[:, 0:n], AF.Square), reads=[cb_], writes=[sq_t[c]])
                        if s_ == 1:
                            p_ = pst[it % 2]
                            it += 1

                            def ssm(e, p_=p_):
                                e.matmul(p_[:, 0:n], ones_b[:], sq_t[0][:, 0:n], start=True, stop=False)
                                return e.matmul(p_[:, 0:n], ones_b[:], sq_t[1][:, 0:n], start=False, stop=True)
                            fw.op("pe", ssm, reads=[sq_t[0], sq_t[1], ones_b], writes=[p_])
                            fw.op("dve", lambda e, p_=p_: e.tensor_scalar(rs_t[:, 0:n], p_[:, 0:n], 1.0 / 256, EPS, ALU.mult, ALU.add), reads=[p_], writes=[rs_t])
                            rsqrt_ops(rs_t, rs_t[:, 0:n])
                            for c in range(2):
                                o_ = ob_t[io % 4]
                                io += 1
                                fw.op("dve", lambda e, o_=o_, c=c: e.scalar_tensor_tensor(o_[:, 0:n], cb_t[c][:, 0:n], nwdf[:, c:c + 1], rs_t[:, 0:n], ALU.mult, ALU.mult),
                                      reads=[cb_t[c], nwdf, rs_t], writes=[o_])
                                fw.dma(nextq(), mixT, mixT[:, 16 + 2 * h + c, q0:q0 + n], o_, o_[:, 0:n])
            fw.barrier()
            fw.flush()

    p4_attn()

    if phases < 5:
        return finish(nc, fw, top, cst, dbg, dbg_o, locals())

    cst2.close()
    rstd1 = fw.sbuf(cst, "rstd1", [128, QT], F32)
    w_out_v = w_out.t.rearrange("(kc p) n -> p kc n", p=128)
    w_gate_v = w_gate.t.rearrange("(kc p) n -> p kc n", p=128)
    w_up_v = w_up.t.rearrange("(kc p) n -> p kc n", p=128)
    w_down_v = w_down.t.rearrange("(hb p) n -> p hb n", p=128)

    def p5a():
        with ExitStack() as st:
            mx = fw.sbuf(st, "omx", [128, KC, TT], BF16, dma=True)
            wo = [fw.sbuf(st, f"owo{i}", [128, KC, 512], BF16, dma="sw") for i in range(2)]
            xr = [fw.sbuf(st, f"oxr{i}", [128, TT], F32, dma=True) for i in range(3)]
            sq = [fw.sbuf(st, f"osq{i}", [128, TT], BF16) for i in range(2)]
            ps = [fw.psum(st, f"ops{i}", [128, 512]) for i in range(3)]
            pss = fw.psum(st, "opss", [128, 512])
            it = 0
            ig = 0
            for t0 in range(0, QT, TT):
                n = TT
                fw.dma("sp", mx, mx[:], mixT, mixT[:, :, t0:t0 + n])
                for cg in range(8):
                    w_ = wo[ig % 2]
                    ig += 1
                    fw.dma("pool", w_, w_[:], w_out, w_out_v[:, :, cg * 512:(cg + 1) * 512])
                    for cb in range(4):
                        blk = cg * 4 + cb
                        p_, x_, s_ = ps[it % 3], xr[it % 3], sq[it % 2]
                        it += 1
                        fw.dma(nextq(), x_, x_[:, 0:n], xTo, xTo[:, blk, HALO + t0:HALO + t0 + n])

                        def mm(e, p_=p_, w_=w_, cb=cb):
                            r = None
                            for kc in range(KC):
                                r = e.matmul(p_[:, 0:n], w_[:, kc, cb * 128:(cb + 1) * 128], mx[:, kc, 0:n], start=(kc == 0), stop=(kc == KC - 1))
                            return r
                        fw.op("pe", mm, reads=[w_, mx], writes=[p_])
                        fw.op("dve", lambda e, x_=x_, p_=p_: e.tensor_tensor(x_[:, 0:n], x_[:, 0:n], p_[:, 0:n], ALU.add), reads=[x_, p_], writes=[x_])
                        fw.op("act", lambda e, s_=s_, x_=x_: e.activation(s_[:, 0:n], x_[:, 0:n], AF.Square), reads=[x_], writes=[s_])
                        fw.op("pe", lambda e, s_=s_, blk=blk: e.matmul(pss[:, 0:n], ones_b[:], s_[:, 0:n], start=(blk == 0), stop=(blk == KC - 1)),
                              reads=[s_, ones_b], writes=[pss])
                        fw.dma(nextq(), x1T, x1T[:, blk, t0:t0 + n], x_, x_[:, 0:n])
                fw.op("dve", lambda e, t0=t0: e.tensor_scalar(rstd1[:, t0:t0 + n], pss[:, 0:n], 1.0 / D, EPS, ALU.mult, ALU.add), reads=[pss], writes=[rstd1])
                rsqrt_ops(rstd1, rstd1[:, t0:t0 + n])
            fw.barrier()
            fw.flush()

    p5a()

    def p5b():
        HH = HB // 2
        with ExitStack() as st:
            hT = fw.sbuf(st, "fh", [128, KC, TT], BF16)
            act = fw.sbuf(st, "fact", [128, HB, TT], BF16)
            wg = [fw.sbuf(st, f"fwg{i}", [128, KC, 128], BF16, dma="sw") for i in range(2)]
            wu = [fw.sbuf(st, f"fwu{i}", [128, KC, 128], BF16, dma="sw") for i in range(2)]
            wd = [fw.sbuf(st, f"fwd{i}", [128, HH, 128], BF16, dma="sw") for i in range(2)]
            xr = [fw.sbuf(st, f"fxr{i}", [128, TT], F32, dma=True) for i in range(3)]
            sg = [fw.sbuf(st, f"fsg{i}", [128, TT], F32) for i in range(2)]
            sq = [fw.sbuf(st, f"fsq{i}", [128, TT], BF16) for i in range(2)]
            r2 = fw.sbuf(st, "fr2", [128, TT], F32)
            pg = [fw.psum(st, f"fpg{i}", [128, 512]) for i in range(2)]
            pu = [fw.psum(st, f"fpu{i}", [128, 512]) for i in range(2)]
            pd = [fw.psum(st, f"fpd{i}", [128, 512]) for i in range(2)]
            pss = fw.psum(st, "fpss", [128, 512])
            ix = 0
            iw = 0
            for t0 in range(0, QT, TT):
                n = TT
                for kc in range(KC):
                    x_ = xr[ix % 3]
                    ix += 1
                    fw.dma(nextq(), x_, x_[:, 0:n], x1T, x1T[:, kc, t0:t0 + n])
                    fw.op("dve", lambda e, x_=x_, kc=kc, t0=t0: e.scalar_tensor_tensor(hT[:, kc, 0:n], x_[:, 0:n], lnw[:, 1, kc:kc + 1], rstd1[:, t0:t0 + n], ALU.mult, ALU.mult),
                          reads=[x_, lnw, rstd1], writes=[hT])
                for hb in range(HB):
                    g_, u_, pg_, pu_, sg_ = wg[hb % 2], wu[hb % 2], pg[hb % 2], pu[hb % 2], sg[hb % 2]
                    fw.dma("pool", g_, g_[:], w_gate, w_gate_v[:, :, hb * 128:(hb + 1) * 128])
                    fw.dma("pool", u_, u_[:], w_up, w_up_v[:, :, hb * 128:(hb + 1) * 128])

                    def mmg(e, p_=pg_, w_=g_):
                        r = None
                        for kc in range(KC):
                            r = e.matmul(p_[:, 0:n], w_[:, kc, :], hT[:, kc, 0:n], start=(kc == 0), stop=(kc == KC - 1))
                        return r

                    def mmu(e, p_=pu_, w_=u_):
                        r = None
                        for kc in range(KC):
                            r = e.matmul(p_[:, 0:n], w_[:, kc, :], hT[:, kc, 0:n], start=(kc == 0), stop=(kc == KC - 1))
                        return r
                    fw.op("pe", mmg, reads=[g_, hT], writes=[pg_])
                    fw.op("pe", mmu, reads=[u_, hT], writes=[pu_])
                    fw.op("act", lambda e, sg_=sg_, pg_=pg_: e.activation(sg_[:, 0:n], pg_[:, 0:n], AF.Silu), reads=[pg_], writes=[sg_])
                    fw.op("dve", lambda e, sg_=sg_, pu_=pu_, hb=hb: e.tensor_tensor(act[:, hb, 0:n], sg_[:, 0:n], pu_[:, 0:n], ALU.mult),
                          reads=[sg_, pu_], writes=[act])
                for cb in range(KC):
                    p_ = pd[cb % 2]
                    x_ = xr[ix % 3]
                    ix += 1
                    fw.dma(nextq(), x_, x_[:, 0:n], x1T, x1T[:, cb, t0:t0 + n])
                    for half in range(2):
                        w_ = wd[iw % 2]
                        iw += 1
                        fw.dma("pool", w_, w_[:], w_down, w_down_v[:, half * HH:(half + 1) * HH, cb * 128:(cb + 1) * 128])

                        def mmd(e, p_=p_, w_=w_, half=half):
                            r = None
                            for j in range(HH):
                                hb = half * HH + j
                                r = e.matmul(p_[:, 0:n], w_[:, j, :], act[:, hb, 0:n], start=(hb == 0), stop=(hb == HB - 1))
                            return r
                        fw.op("pe", mmd, reads=[w_, act], writes=[p_])
                    s_ = sq[cb % 2]
                    fw.op("dve", lambda e, x_=x_, p_=p_: e.tensor_tensor(x_[:, 0:n], x_[:, 0:n], p_[:, 0:n], ALU.add), reads=[x_, p_], writes=[x_])
                    fw.op("act", lambda e, s_=s_, x_=x_: e.activation(s_[:, 0:n], x_[:, 0:n], AF.Square), reads=[x_], writes=[s_])
                    fw.op("pe", lambda e, s_=s_, cb=cb: e.matmul(pss[:, 0:n], ones_b[:], s_[:, 0:n], start=(cb == 0), stop=(cb == KC - 1)),
                          reads=[s_, ones_b], writes=[pss])
                    fw.dma(nextq(), x2T, x2T[:, cb, t0:t0 + n], x_, x_[:, 0:n])
                fw.op("dve", lambda e: e.tensor_scalar(r2[:, 0:n], pss[:, 0:n], 1.0 / D, EPS, ALU.mult, ALU.add), reads=[pss], writes=[r2])
                rsqrt_ops(r2, r2[:, 0:n])
                for cb in range(KC):
                    x_ = xr[ix % 3]
                    ix += 1
                    fw.dma(nextq(), x_, x_[:, 0:n], x2T, x2T[:, cb, t0:t0 + n])
                    fw.op("dve", lambda e, x_=x_, cb=cb: e.scalar_tensor_tensor(x_[:, 0:n], x_[:, 0:n], lnw[:, 2, cb:cb + 1], r2[:, 0:n], ALU.mult, ALU.mult),
                          reads=[x_, lnw, r2], writes=[x_])
                    fw.dma(nextq(), outT, outT[:, cb, t0:t0 + n], x_, x_[:, 0:n])
            fw.barrier()
            fw.flush()

    p5b()
    return finish(nc, fw, top, cst, dbg, dbg_o, locals())


def _prep(inputs, S):
    x = np.asarray(inputs["x"], np.float32)
    QT = S // 4
    NCH = S // 128
    g = lambda k: np.asarray(inputs[k], np.float32)
    w_in = np.ascontiguousarray(g("w_in")[0])
    w_out = np.ascontiguousarray(g("w_out")[0])
    w_gate = np.ascontiguousarray(g("w_gate")[0])
    w_up = np.ascontiguousarray(g("w_up")[0])
    w_down = np.ascontiguousarray(g("w_down")[0])
    lnw = np.stack([g("ln_mix_w")[0], g("ln_ffn_w")[0], g("ln_final_w")], 0).reshape(3, 32, 128).transpose(2, 0, 1).copy()
    convw = g("conv_w")[0].reshape(4, 48, 128).transpose(2, 1, 0).copy()
    hv = np.broadcast_to(np.concatenate([g("a_log")[0], g("dt_bias")[0]])[None, :], (128, 32)).copy()
    dfw = g("df_norm_w")[0]
    nw = np.stack([g("dn_norm_w")[0], dfw[:128], dfw[128:]], 1).copy()
    lamv = np.stack([g("lambda_q1")[0], g("lambda_k1")[0], g("lambda_q2")[0], g("lambda_k2")[0]], 1).copy()
    j = np.arange(128)[:, None]
    i = np.arange(128)[None, :]
    prot = np.zeros((128, 128), np.float32)
    for m in range(16):
        prot[m + 16, m] = -1.0
        prot[m, m + 16] = 1.0
    cmask = np.stack([(j <= i), (i <= j), (i < j), (i == j), prot], 0).astype(np.float32).transpose(1, 0, 2).copy()
    kpos = (np.arange(NCH)[None, :] * 128 + np.arange(128)[:, None]).astype(np.float32)
    posf = np.broadcast_to(np.arange(S, dtype=np.float32)[None, :], (128, S)).copy()
    invf = np.zeros((128, 1), np.float32)
    fr = np.array([ROPE_THETA ** (-(2 * k) / 32.0) for k in range(16)], np.float32)
    invf[0:16, 0] = fr
    invf[16:32, 0] = fr
    maps = []
    for c in range(8):
        b, tq = c // 4, c % 4
        xT = x[b].T
        xTf = np.ascontiguousarray(xT.reshape(32, 128, S).transpose(1, 0, 2))
        lo = tq * QT
        own = np.zeros((4096, HALO + QT), np.float32)
        own[:, HALO:] = xT[:, lo:lo + QT]
        if tq > 0:
            own[:, :HALO] = xT[:, lo - HALO:lo]
        xTo = np.ascontiguousarray(own.reshape(32, 128, HALO + QT).transpose(1, 0, 2))
        flags = np.broadcast_to((np.arange(NCH) < lo // 128).astype(np.float32)[None, :], (128, NCH)).copy()
        qp = np.arange(lo - HALO, lo + QT, dtype=np.float32)
        qpos = np.broadcast_to(qp[None, :], (128, HALO + QT)).copy()
        maps.append(dict(xTf=xTf, xTo=xTo, w_in=w_in, w_out=w_out, w_gate=w_gate, w_up=w_up, w_down=w_down,
                         lnw=lnw, convw=convw, hv=hv, nw=nw, lamv=lamv, cmask=cmask, flags=flags, qpos=qpos,
                         kpos=kpos, posf=posf, invf=invf))
    return maps


def kernel(**inputs):
    x = np.asarray(inputs["x"])
    B, S, _ = x.shape
    QT = S // 4
    nc = build_program(S)
    maps = _prep(inputs, S)
    res = run_bass_kernel_spmd(nc, maps, core_ids=list(range(8)))
    out = np.empty((B, S, D), np.float32)
    for c in range(8):
        b, tq = c // 4, c % 4
        o = np.asarray(res.results[c]["outT"], np.float32)
        out[b, tq * QT:(tq + 1) * QT, :] = o.transpose(2, 1, 0).reshape(QT, D)
    return out
```

```python
import math
from contextlib import ExitStack
import numpy as np
import concourse.bass as bass
import concourse.mybir as mybir
from concourse.bass_utils import run_bass_kernel_spmd

F32 = mybir.dt.float32
BF16 = mybir.dt.bfloat16
AF = mybir.ActivationFunctionType
ALU = mybir.AluOpType
AX = mybir.AxisListType

SAME_ENGINE_SYNC = True


class Sem:
    def __init__(self, h):
        self.h = h
        self.cnt = 0


class Track:
    def __init__(self):
        self.writers = {}
        self.readers = {}


class Obj:
    def __init__(self, t, name, dsem=None, tr=None, psum=False):
        self.t = t
        self.name = name
        self.tr = tr if tr is not None else Track()
        self.dsem = dsem
        self.psum = psum

    @property
    def writers(self):
        return self.tr.writers

    @writers.setter
    def writers(self, v):
        self.tr.writers = v

    @property
    def readers(self):
        return self.tr.readers

    @readers.setter
    def readers(self, v):
        self.tr.readers = v

    def view(self, ap, name=None):
        return Obj(ap, name or self.name, self.dsem, self.tr, self.psum)

    def __getitem__(self, k):
        return self.t[k]


class FW:
    def __init__(self, nc, stack):
        self.nc = nc
        self.stack = stack
        self.engs = ["pe", "act", "dve", "pool", "sp"]
        self.esem = {k: Sem(stack.enter_context(nc.semaphore("es_" + k))) for k in self.engs}
        self.thunks = {k: [] for k in self.engs}
        self.waited = {k: {} for k in self.engs}
        self.allsems = list(self.esem.values())
        self.ninst = 0

    def new_sem(self, name):
        s = Sem(self.stack.enter_context(self.nc.semaphore(name)))
        self.allsems.append(s)
        return s

    def sbuf(self, st, name, shape, dt, dma=False):
        self.uid = getattr(self, "uid", 0) + 1
        name = f"{name}_{self.uid}"
        t = st.enter_context(self.nc.sbuf_tensor(name, shape, dt))
        sem = None
        if dma:
            pool = self.__dict__.setdefault("sem_pool_" + str(dma), [])
            sem = pool.pop() if pool else self.new_sem("d_" + name)
            st.callback(lambda: pool.append(sem))
        return Obj(t, name, sem)

    def psum(self, st, name, shape, dt=F32):
        self.uid = getattr(self, "uid", 0) + 1
        name = f"{name}_{self.uid}"
        t = st.enter_context(self.nc.psum_tensor(name, shape, dt))
        return Obj(t, name, psum=True)

    def dram(self, name, shape, dt, kind="Internal"):
        t = self.nc.dram_tensor(name, shape, dt, kind=kind).ap()
        return Obj(t, name)

    def _deps(self, reads, writes):
        deps = {}
        for o in reads:
            for s, v in o.writers.items():
                if deps.get(s, 0) < v:
                    deps[s] = v
            if o.psum:
                for s, v in o.readers.items():
                    if deps.get(s, 0) < v:
                        deps[s] = v
        for o in writes:
            for d in (o.writers, o.readers):
                for s, v in d.items():
                    if deps.get(s, 0) < v:
                        deps[s] = v
        return deps

    def _waits(self, ek, deps):
        w = self.waited[ek]
        own = self.esem[ek]
        out = []
        for s, v in deps.items():
            if s is own and not SAME_ENGINE_SYNC:
                continue
            if w.get(s, 0) >= v:
                continue
            w[s] = v
            out.append((s.h, v))
        return out

    def op(self, ek, fn, reads=(), writes=()):
        waits = self._waits(ek, self._deps(reads, writes))
        sem = self.esem[ek]
        sem.cnt += 1
        val = sem.cnt
        h = sem.h

        def thunk(e):
            for sh, v in waits:
                e.wait_ge(sh, v)
            r = fn(e)
            if isinstance(r, (list, tuple)):
                r = r[-1]
            r.then_inc(h, 1)

        self.thunks[ek].append(thunk)
        self.ninst += 1
        for o in reads:
            if o.readers.get(sem, 0) < val:
                o.readers[sem] = val
        for o in writes:
            o.writers = {sem: val}
            o.readers = {}

    def dma(self, qk, out_o, out_ap, in_o, in_ap, sem_obj=None, group=False, **kw):
        if sem_obj is None:
            sem_obj = out_o if out_o.dsem is not None else in_o
        sem = sem_obj.dsem
        assert sem is not None, (out_o.name, in_o.name)
        if group:
            deps = self._deps([in_o], [])
            for s, v in out_o.readers.items():
                deps[s] = max(deps.get(s, 0), v)
            for s, v in out_o.writers.items():
                if s is not sem:
                    deps[s] = max(deps.get(s, 0), v)
        else:
            deps = self._deps([in_o], [out_o])
        waits = self._waits(qk, deps)
        sem.cnt += 16
        val = sem.cnt
        h = sem.h

        def thunk(e):
            for sh, v in waits:
                e.wait_ge(sh, v)
            e.dma_start(out=out_ap, in_=in_ap, **kw).then_inc(h, 16)

        self.thunks[qk].append(thunk)
        self.ninst += 1
        if in_o.readers.get(sem, 0) < val:
            in_o.readers[sem] = val
        if group:
            out_o.writers[sem] = val
        else:
            out_o.writers = {sem: val}
        out_o.readers = {}

    def barrier(self):
        snap = [(s, s.cnt) for s in self.allsems if s.cnt > 0]
        for ek in self.engs:
            waits = self._waits(ek, dict(snap))

            def thunk(e, waits=waits):
                for sh, v in waits:
                    e.wait_ge(sh, v)

            self.thunks[ek].append(thunk)

    def flush(self):
        lists = self.thunks
        with self.nc.Block() as block:
            @block.tensor
            def _(e):
                for t in lists["pe"]:
                    t(e)

            @block.scalar
            def _(e):
                for t in lists["act"]:
                    t(e)

            @block.vector
            def _(e):
                for t in lists["dve"]:
                    t(e)

            @block.gpsimd
            def _(e):
                for t in lists["pool"]:
                    t(e)

            @block.sync
            def _(e):
                for t in lists["sp"]:
                    t(e)
        self.thunks = {k: [] for k in self.engs}
        if max(s.cnt for s in self.esem.values()) > 20000:
            self.epoch = getattr(self, "epoch", 0) + 1
            for k in self.engs:
                ns = Sem(self.stack.enter_context(self.nc.semaphore(f"es_{k}_{self.epoch}")))
                self.esem[k] = ns
                self.allsems.append(ns)

D = 4096
KC = 32
NH_DN = 16
NSUB = 16
NH_DF = 8
FFN = 11008
HB = FFN // 128
PROJ = 14368
OFF_DQ, OFF_DK, OFF_DV, OFF_DZ, OFF_DB, OFF_DA, OFF_FQ, OFF_FK, OFF_FV = 0, 2048, 4096, 6144, 8192, 8208, 8224, 10272, 12320
HALO = 4
EPS = 1e-6
ROPE_THETA = 500000.0
LAMBDA_INIT = 0.8 - 0.6 * math.exp(-0.3 * 0)
PI = math.pi


def finish(nc, fw, top, cst, dbg, dbg_o, env):
    if dbg is not None:
        for i, d in enumerate(dbg):
            src = env[d[0]]
            dbg_o[i].dsem = fw.new_sem(f"dbgsem{i}")
            fw.dma("sp", dbg_o[i], dbg_o[i][:], src, d[3](src), sem_obj=dbg_o[i])
    fw.barrier()
    fw.flush()
    c2 = env.get("cst2")
    if c2 is not None:
        c2.close()
    cst.close()
    top.close()
    return nc


def build_program(S, phases=99, dbg=None):
    QT = S // 4
    TT = min(512, QT)
    NCH = S // 128
    NCO = QT // 128
    nc = bass.Bass("TRN2", target_bir_lowering=False)
    top = ExitStack()
    fw = FW(nc, top)

    def ein(name, shape, dt=F32):
        return fw.dram(name, shape, dt, kind="ExternalInput")

    xTf = ein("xTf", [128, KC, S])
    xTo = ein("xTo", [128, KC, HALO + QT])
    w_in = ein("w_in", [D, PROJ])
    if phases >= 5:
        w_out = ein("w_out", [D, D])
        w_gate = ein("w_gate", [D, FFN])
        w_up = ein("w_up", [D, FFN])
        w_down = ein("w_down", [FFN, D])
    lnw_d = ein("lnw", [128, 3, KC])
    convw_d = ein("convw", [128, 48, 4])
    hv_d = ein("hv", [128, 32])
    nw_d = ein("nw", [128, 3])
    lamv_d = ein("lamv", [128, 4])
    cm_d = ein("cmask", [128, 5, 128])
    flags_d = ein("flags", [128, NCH])
    qpos_d = ein("qpos", [128, HALO + QT])
    kpos_d = ein("kpos", [128, NCH])
    posf_d = ein("posf", [128, S])
    invf_d = ein("invf", [128, 1])
    outT = fw.dram("outT", [128, KC, QT], F32, kind="ExternalOutput")

    xn_f = fw.dram("xn_f", [128, KC, S], BF16)
    xn_o = fw.dram("xn_o", [128, KC, HALO + QT], BF16)
    raw_f = fw.dram("raw_f", [48, 128, S], F32)
    vtok_f = fw.dram("vtok_f", [S, 2048], BF16)
    ba_f = fw.dram("ba_f", [S, 32], F32)
    raw_o = fw.dram("raw_o", [80, 128, HALO + QT], F32)
    ba_o = fw.dram("ba_o", [HALO + QT, 32], F32)
    kdn_f = fw.dram("kdn_f", [16, 128, S], F32)
    vdn_f = fw.dram("vdn_f", [16, 128, S], F32)
    kdf_f = fw.dram("kdf_f", [16, 128, S], BF16)
    qdn_o = fw.dram("qdn_o", [16, 128, QT], F32)
    kdn_o = fw.dram("kdn_o", [16, 128, QT], F32)
    vdn_o = fw.dram("vdn_o", [16, 128, QT], F32)
    zdn_o = fw.dram("zdn_o", [16, 128, QT], BF16)
    qdf_o = fw.dram("qdf_o", [16, 128, QT], BF16)
    mixT = fw.dram("mixT", [128, KC, QT], BF16)
    x1T = fw.dram("x1T", [128, KC, QT], F32)
    x2T = fw.dram("x2T", [128, KC, QT], F32)
    dbg_o = None
    if dbg is not None:
        dbg_o = [fw.dram(f"dbg{i}", list(d[1]), d[2], kind="ExternalOutput") for i, d in enumerate(dbg)]

    evq = ["sp", "act"]
    cnt = {"e": 0, "q": 0}

    def nextq():
        cnt["q"] += 1
        return evq[cnt["q"] % 2]

    def evac_eng():
        cnt["e"] += 1
        return "act" if cnt["e"] % 2 else "dve"

    def interleave(gens):
        gens = list(gens)
        while gens:
            nxt = []
            for g_ in gens:
                try:
                    next(g_)
                    nxt.append(g_)
                except StopIteration:
                    pass
            gens = nxt

    def copy_op(ek, out_o, out_ap, in_o, in_ap):
        if ek == "act":
            fw.op("act", lambda e: e.activation(out_ap, in_ap, AF.Copy), reads=[in_o], writes=[out_o])
        else:
            fw.op(ek, lambda e: e.tensor_copy(out_ap, in_ap), reads=[in_o], writes=[out_o])

    cst = ExitStack()
    ones_b = fw.sbuf(cst, "ones_b", [128, 128], BF16)
    ones_f = fw.sbuf(cst, "ones_f", [128, 128], F32)
    lnw = fw.sbuf(cst, "lnw_s", [128, 3, KC], F32, dma=True)
    cm = fw.sbuf(cst, "cm_s", [128, 5, 128], F32, dma=True)
    idb = fw.sbuf(cst, "idb", [128, 128], BF16)
    fw.op("dve", lambda e: e.memset(ones_b[:], 1.0), writes=[ones_b])
    fw.op("dve", lambda e: e.memset(ones_f[:], 1.0), writes=[ones_f])
    fw.dma("sp", lnw, lnw[:], lnw_d, lnw_d[:])
    fw.dma("sp", cm, cm[:], cm_d, cm_d[:])
    fw.op("dve", lambda e: e.tensor_copy(idb[:], cm[:, 3, :]), reads=[cm], writes=[idb])
    trib = fw.sbuf(cst, "trib", [128, 128], BF16)
    fw.op("dve", lambda e: e.tensor_copy(trib[:], cm[:, 0, :]), reads=[cm], writes=[trib])

    def p0_norm(src, dst, ntok_total, which):
        T0 = min(256, QT)
        with ExitStack() as st:
            xt = [fw.sbuf(st, f"p0x{i}", [128, KC, T0], F32, dma=True) for i in range(2)]
            sq = fw.sbuf(st, "p0sq", [128, KC, T0], BF16)
            xo = [fw.sbuf(st, f"p0o{i}", [128, KC, T0], BF16, dma=True) for i in range(2)]
            rs = [fw.sbuf(st, f"p0r{i}", [128, T0], F32) for i in range(2)]
            ps = [fw.psum(st, f"p0ps{i}", [128, 512]) for i in range(2)]
            tiles = []
            t0 = 0
            while t0 < ntok_total:
                n = min(T0, ntok_total - t0)
                tiles.append((t0, n))
                t0 += n
            for it, (t0, n) in enumerate(tiles):
                x_, o_, r_, p_ = xt[it % 2], xo[it % 2], rs[it % 2], ps[it % 2]
                fw.dma(nextq(), x_, x_[:, :, 0:n], src, src[:, :, t0:t0 + n])
                fw.op("act", lambda e, x_=x_, n=n: e.activation(sq[:, :, 0:n], x_[:, :, 0:n], AF.Square),
                      reads=[x_], writes=[sq])

                def mm(e, p_=p_, n=n):
                    r = None
                    for kc in range(KC):
                        r = e.matmul(p_[:, 0:n], ones_b[:], sq[:, kc, 0:n], start=(kc == 0), stop=(kc == KC - 1))
                    return r
                fw.op("pe", mm, reads=[sq, ones_b], writes=[p_])
                fw.op("dve", lambda e, r_=r_, p_=p_, n=n: e.tensor_scalar(r_[:, 0:n], p_[:, 0:n], 1.0 / D, EPS, ALU.mult, ALU.add),
                      reads=[p_], writes=[r_])
                fw.op("act", lambda e, r_=r_, n=n: e.activation(r_[:, 0:n], r_[:, 0:n], AF.Sqrt), reads=[r_], writes=[r_])
                fw.op("dve", lambda e, r_=r_, n=n: e.reciprocal(r_[:, 0:n], r_[:, 0:n]), reads=[r_], writes=[r_])
                ek = "dve"

                def sc(e, x_=x_, o_=o_, r_=r_, n=n):
                    r = None
                    for kc in range(KC):
                        r = e.scalar_tensor_tensor(o_[:, kc, 0:n], x_[:, kc, 0:n], lnw[:, which, kc:kc + 1],
                                                   r_[:, 0:n], ALU.mult, ALU.mult)
                    return r
                fw.op(ek, sc, reads=[x_, r_, lnw], writes=[o_])
                fw.dma(nextq(), dst, dst[:, :, t0:t0 + n], o_, o_[:, :, 0:n])
            fw.barrier()
            fw.flush()

    p0_norm(xTf, xn_f, S, 0)
    p0_norm(xTo, xn_o, HALO + QT, 0)

    if phases < 1:
        return finish(nc, fw, top, cst, dbg, dbg_o, locals())

    w_in_v = w_in.t.rearrange("(kc p) n -> p kc n", p=128)

    def proj(xn, tok_tiles, jobs):
        GW = 1024
        with ExitStack() as st:
            wt = fw.sbuf(st, "p1w", [128, KC, GW], BF16, dma="sw")
            xt = [fw.sbuf(st, f"p1x{i}", [128, KC, TT], BF16, dma=True) for i in range(2)]
            evf = [fw.sbuf(st, f"p1ef{i}", [128, 512], F32, dma=True) for i in range(4)]
            evb = [fw.sbuf(st, f"p1eb{i}", [128, 512], BF16, dma=True) for i in range(4)]
            ps = [fw.psum(st, f"p1ps{i}", [128, 512]) for i in range(4)]
            k = {"x": 0, "p": 0}
            for (col_lo, ncols, mode, dst, d0) in jobs:
                for g0 in range(0, ncols, GW):
                    gw = min(GW, ncols - g0)
                    fw.dma("pool", wt, wt[:, :, 0:gw], w_in, w_in_v[:, :, col_lo + g0: col_lo + g0 + gw])
                    for (t0, n) in tok_tiles:
                        x_ = xt[k["x"] % 2]
                        k["x"] += 1
                        fw.dma(nextq(), x_, x_[:, :, 0:n], xn, xn[:, :, t0:t0 + n])
                        if mode == "cm":
                            for b0 in range(0, gw, 128):
                                p_ = ps[k["p"] % 4]
                                e_ = evf[k["p"] % 4]
                                k["p"] += 1

                                def mm(e, p_=p_, x_=x_, b0=b0, n=n):
                                    r = None
                                    for kc in range(KC):
                                        r = e.matmul(p_[:, 0:n], wt[:, kc, b0:b0 + 128], x_[:, kc, 0:n],
                                                     start=(kc == 0), stop=(kc == KC - 1))
                                    return r
                                fw.op("pe", mm, reads=[wt, x_], writes=[p_])
                                copy_op(evac_eng(), e_, e_[:, 0:n], p_, p_[:, 0:n])
                                blk = d0 + (g0 + b0) // 128
                                fw.dma(nextq(), dst, dst[blk, :, t0:t0 + n], e_, e_[:, 0:n])
                        else:
                            for s0 in range(0, n, 128):
                                m = min(128, n - s0)
                                for c0 in range(0, gw, 512):
                                    cw = min(512, gw - c0)
                                    p_ = ps[k["p"] % 4]
                                    e_ = (evb if mode == "tmb" else evf)[k["p"] % 4]
                                    k["p"] += 1

                                    def mm(e, p_=p_, x_=x_, s0=s0, m=m, c0=c0, cw=cw):
                                        r = None
                                        for kc in range(KC):
                                            r = e.matmul(p_[0:m, 0:cw], x_[:, kc, s0:s0 + m], wt[:, kc, c0:c0 + cw],
                                                         start=(kc == 0), stop=(kc == KC - 1))
                                        return r
                                    fw.op("pe", mm, reads=[wt, x_], writes=[p_])
                                    copy_op(evac_eng(), e_, e_[0:m, 0:cw], p_, p_[0:m, 0:cw])
                                    fw.dma(nextq(), dst, dst[t0 + s0:t0 + s0 + m, d0 + g0 + c0:d0 + g0 + c0 + cw],
                                           e_, e_[0:m, 0:cw])
            fw.barrier()
            fw.flush()

    full_tiles = [(i * TT, TT) for i in range(S // TT)]
    own_tiles = [(0, HALO)] + [(HALO + i * TT, TT) for i in range(QT // TT)]
    pre_tiles = [t for t in full_tiles if t[0] < S - QT]
    proj(xn_f, pre_tiles, [
        (OFF_DK, 2048, "cm", raw_f, 0),
        (OFF_DV, 2048, "cm", raw_f, 16),
        (OFF_DB, 32, "tmf", ba_f, 0),
    ])
    proj(xn_f, full_tiles, [
        (OFF_FK, 2048, "cm", raw_f, 32),
        (OFF_FV, 2048, "tmb", vtok_f, 0),
    ])
    proj(xn_o, own_tiles, [
        (OFF_DQ, 2048, "cm", raw_o, 0),
        (OFF_DK, 2048, "cm", raw_o, 16),
        (OFF_DV, 2048, "cm", raw_o, 32),
        (OFF_DZ, 2048, "cm", raw_o, 48),
        (OFF_FQ, 2048, "cm", raw_o, 64),
        (OFF_DB, 32, "tmf", ba_o, 0),
    ])

    if phases < 2:
        return finish(nc, fw, top, cst, dbg, dbg_o, locals())

    cst2 = ExitStack()
    convw = fw.sbuf(cst2, "convw_s", [128, 48, 4], F32, dma=True)
    fw.dma("sp", convw, convw[:], convw_d, convw_d[:])
    invf = fw.sbuf(cst2, "invf_s", [128, 1], F32, dma=True)
    fw.dma("sp", invf, invf[:], invf_d, invf_d[:])
    prot = fw.sbuf(cst2, "prot", [128, 128], BF16)
    fw.op("dve", lambda e: e.tensor_copy(prot[:], cm[:, 4, :]), reads=[cm], writes=[prot])

    def p2_conv(src, src_blk0, cw_blk0, nblk, tiles, col0, dst, kind):
        NS = 4
        with ExitStack() as st:
            xin = [fw.sbuf(st, f"p2i{i}", [128, 3 + TT], F32, dma=True) for i in range(NS)]
            y = [fw.sbuf(st, f"p2y{i}", [128, TT], F32) for i in range(NS)]
            sl = [fw.sbuf(st, f"p2s{i}", [128, TT], F32) for i in range(NS)]
            sq = [fw.sbuf(st, f"p2q{i}", [128, TT], BF16) for i in range(NS)]
            rr = [fw.sbuf(st, f"p2r{i}", [128, TT], F32) for i in range(NS)]
            ob = [fw.sbuf(st, f"p2o{i}", [128, TT], BF16 if kind == "z" else F32, dma=True) for i in range(NS)]
            ps = [fw.psum(st, f"p2ps{i}", [128, 512]) for i in range(NS)]

            def tile_gen(b, t0, n, slot):
                i_, y_, s_, q_, r_, o_, p_ = (a[slot] for a in (xin, y, sl, sq, rr, ob, ps))
                lo = col0 + t0 - 3
                if lo < 0:
                    fw.op("dve", lambda e: e.memset(i_[:, 0:3], 0.0), writes=[i_])
                    yield
                    fw.dma(nextq(), i_, i_[:, 3:3 + n], src, src[src_blk0 + b, :, col0 + t0:col0 + t0 + n], group=True)
                else:
                    fw.dma(nextq(), i_, i_[:, 0:3 + n], src, src[src_blk0 + b, :, lo:lo + 3 + n])
                yield
                if kind == "z":
                    fw.op("act", lambda e: e.activation(o_[:, 0:n], i_[:, 3:3 + n], AF.Silu), reads=[i_], writes=[o_])
                    yield
                else:
                    cb = cw_blk0 + b
                    fw.op("dve", lambda e: e.tensor_scalar(y_[:, 0:n], i_[:, 0:n], convw[:, cb, 0:1], None, ALU.mult),
                          reads=[i_, convw], writes=[y_])
                    yield
                    for j in range(1, 4):
                        fw.op("dve", lambda e, j=j: e.scalar_tensor_tensor(
                            y_[:, 0:n], i_[:, j:j + n], convw[:, cb, j:j + 1], y_[:, 0:n], ALU.mult, ALU.add),
                            reads=[i_, convw, y_], writes=[y_])
                        yield
                    if kind == "v":
                        fw.op("act", lambda e: e.activation(o_[:, 0:n], y_[:, 0:n], AF.Silu), reads=[y_], writes=[o_])
                        yield
                    else:
                        fw.op("act", lambda e: e.activation(s_[:, 0:n], y_[:, 0:n], AF.Silu), reads=[y_], writes=[s_])
                        yield
                        fw.op("act", lambda e: e.activation(q_[:, 0:n], s_[:, 0:n], AF.Square), reads=[s_], writes=[q_])
                        yield
                        fw.op("pe", lambda e: e.matmul(p_[:, 0:n], ones_b[:], q_[:, 0:n], start=True, stop=True),
                              reads=[q_, ones_b], writes=[p_])
                        yield
                        m_ = 128.0 if kind == "q" else 1.0
                        fw.op("act", lambda e: e.activation(r_[:, 0:n], p_[:, 0:n], AF.Sqrt, bias=m_ * EPS, scale=m_),
                              reads=[p_], writes=[r_])
                        yield
                        fw.op("dve", lambda e: e.reciprocal(r_[:, 0:n], r_[:, 0:n]), reads=[r_], writes=[r_])
                        yield
                        fw.op("dve", lambda e: e.tensor_tensor(o_[:, 0:n], s_[:, 0:n], r_[:, 0:n], ALU.mult),
                              reads=[s_, r_], writes=[o_])
                        yield
                fw.dma(nextq(), dst, dst[b, :, t0:t0 + n], o_, o_[:, 0:n])
                yield

            work = [(b, t0, n) for b in range(nblk) for (t0, n) in tiles]
            for w0 in range(0, len(work), NS):
                interleave(tile_gen(b, t0, n, j) for j, (b, t0, n) in enumerate(work[w0:w0 + NS]))
            fw.barrier()
            fw.flush()

    full_t = [(i * TT, TT) for i in range(S // TT)]
    own_t = [(i * TT, TT) for i in range(QT // TT)]
    pre_t = [t for t in full_t if t[0] < S - QT]
    p2_conv(raw_f, 0, 16, 16, pre_t, 0, kdn_f, "k")
    p2_conv(raw_f, 16, 32, 16, pre_t, 0, vdn_f, "v")
    p2_conv(raw_o, 0, 0, 16, own_t, HALO, qdn_o, "q")
    p2_conv(raw_o, 16, 16, 16, own_t, HALO, kdn_o, "k")
    p2_conv(raw_o, 32, 32, 16, own_t, HALO, vdn_o, "v")
    p2_conv(raw_o, 48, 0, 16, own_t, HALO, zdn_o, "z")

    def p2_rot(src, src_blk0, tiles, col0, pos_d, dst):
        with ExitStack() as st:
            pt = fw.sbuf(st, "p2pos", [128, TT], F32, dma=True)
            ca = fw.sbuf(st, "p2ca", [128, TT], F32)
            ti = fw.sbuf(st, "p2ti", [128, TT], mybir.dt.int32)
            sa = fw.sbuf(st, "p2sa", [128, TT], F32)
            cs = fw.sbuf(st, "p2cs", [128, TT], F32)
            sn = fw.sbuf(st, "p2sn", [128, TT], F32)
            xin = [fw.sbuf(st, f"p2x{i}", [128, TT], F32, dma=True) for i in range(4)]
            xb = [fw.sbuf(st, f"p2xb{i}", [128, TT], BF16) for i in range(4)]
            t1 = [fw.sbuf(st, f"p2t{i}", [128, TT], F32) for i in range(4)]
            t2 = [fw.sbuf(st, f"p2u{i}", [128, TT], F32) for i in range(4)]
            ob = [fw.sbuf(st, f"p2ro{i}", [128, TT], BF16, dma=True) for i in range(4)]
            ps = [fw.psum(st, f"p2rp{i}", [128, 512]) for i in range(4)]
            it = 0
            for (t0, n) in tiles:
                fw.dma("sp", pt, pt[:, 0:n], pos_d, pos_d[:, col0 + t0:col0 + t0 + n])
                def trig(dst, shift, n=n):
                    fw.op("dve", lambda e: e.tensor_scalar(sa[:, 0:n], pt[:, 0:n], invf[:, 0:1], shift, ALU.mult, ALU.add),
                          reads=[pt, invf], writes=[sa])
                    fw.op("dve", lambda e: e.tensor_scalar(ca[:, 0:n], sa[:, 0:n], 1.0 / (2 * PI), None, ALU.mult), reads=[sa], writes=[ca])
                    fw.op("dve", lambda e: e.tensor_copy(ti[:, 0:n], ca[:, 0:n]), reads=[ca], writes=[ti])
                    fw.op("dve", lambda e: e.tensor_copy(ca[:, 0:n], ti[:, 0:n]), reads=[ti], writes=[ca])
                    fw.op("dve", lambda e: e.scalar_tensor_tensor(sa[:, 0:n], ca[:, 0:n], -2 * PI, sa[:, 0:n], ALU.mult, ALU.add),
                          reads=[ca, sa], writes=[sa])
                    fw.op("dve", lambda e: e.tensor_scalar(ca[:, 0:n], sa[:, 0:n], PI, -2 * PI, ALU.is_gt, ALU.mult), reads=[sa], writes=[ca])
                    fw.op("dve", lambda e: e.tensor_tensor(sa[:, 0:n], sa[:, 0:n], ca[:, 0:n], ALU.add), reads=[sa, ca], writes=[sa])
                    fw.op("dve", lambda e: e.tensor_scalar(ca[:, 0:n], sa[:, 0:n], -PI, 2 * PI, ALU.is_lt, ALU.mult), reads=[sa], writes=[ca])
                    fw.op("dve", lambda e: e.tensor_tensor(sa[:, 0:n], sa[:, 0:n], ca[:, 0:n], ALU.add), reads=[sa, ca], writes=[sa])
                    fw.op("act", lambda e: e.activation(dst[:, 0:n], sa[:, 0:n], AF.Sin), reads=[sa], writes=[dst])
                trig(sn, 0.0)
                trig(cs, 0.5 * PI)
                def blk_gen(b, slot, t0=t0, n=n):
                    i_, b_, a_, u_, o_, p_ = (a[slot] for a in (xin, xb, t1, t2, ob, ps))
                    fw.dma(nextq(), i_, i_[:, 0:n], src, src[src_blk0 + b, :, col0 + t0:col0 + t0 + n])
                    yield
                    fw.op("act", lambda e: e.activation(b_[:, 0:n], i_[:, 0:n], AF.Copy), reads=[i_], writes=[b_])
                    yield
                    fw.op("pe", lambda e: e.matmul(p_[:, 0:n], prot[:], b_[:, 0:n], start=True, stop=True),
                          reads=[b_, prot], writes=[p_])
                    yield
                    fw.op("dve", lambda e: e.tensor_tensor(a_[:, 0:n], i_[:, 0:n], cs[:, 0:n], ALU.mult),
                          reads=[i_, cs], writes=[a_])
                    yield
                    fw.op("dve", lambda e: e.tensor_tensor(u_[:, 0:n], p_[:, 0:n], sn[:, 0:n], ALU.mult),
                          reads=[p_, sn], writes=[u_])
                    yield
                    fw.op("dve", lambda e: e.tensor_tensor(o_[:, 0:n], a_[:, 0:n], u_[:, 0:n], ALU.add),
                          reads=[a_, u_], writes=[o_])
                    yield
                    fw.dma(nextq(), dst, dst[b, :, t0:t0 + n], o_, o_[:, 0:n])
                    yield
                for b0 in range(0, 16, 4):
                    interleave(blk_gen(b0 + j, j) for j in range(4))
            fw.barrier()
            fw.flush()

    p2_rot(raw_f, 32, full_t, 0, posf_d, kdf_f)
    p2_rot(raw_o, 64, own_t, HALO, qpos_d, qdf_o)

    if phases < 2.5:
        return finish(nc, fw, top, cst, dbg, dbg_o, locals())

    class RR:
        def __init__(self, objs):
            self.objs = objs
            self.i = 0

        def get(self):
            o = self.objs[self.i % len(self.objs)]
            self.i += 1
            return o

    hv = fw.sbuf(cst2, "hv_s", [128, 32], F32, dma=True)
    nw = fw.sbuf(cst2, "nw_s", [128, 3], F32, dma=True)
    flg = fw.sbuf(cst2, "flg_s", [128, NCH], F32, dma=True)
    negea = fw.sbuf(cst2, "negea", [128, 16], F32)
    Sf = fw.sbuf(cst2, "Sf", [128, 16, 128], F32)
    fw.dma("sp", hv, hv[:], hv_d, hv_d[:])
    fw.dma("sp", nw, nw[:], nw_d, nw_d[:])
    fw.dma("sp", flg, flg[:], flags_d, flags_d[:])
    fw.op("act", lambda e: e.activation(negea[:], hv[:, 0:16], AF.Exp), reads=[hv], writes=[negea])
    fw.op("dve", lambda e: e.tensor_scalar(negea[:], negea[:], -1.0, None, ALU.mult), reads=[negea], writes=[negea])
    fw.op("dve", lambda e: e.memset(Sf[:], 0.0), writes=[Sf])
    Sfo = [Obj(Sf.t[:, h, :], f"Sf{h}") for h in range(16)]
    for h in range(16):
        Sfo[h].writers = dict(Sf.writers)

    def p3_dn(kd, vd, qd, zd, ba, ba_row0, nch, masked, with_out):
        with ExitStack() as st:
            pbank = [fw.psum(st, f"dnpf{i}", [128, 512]) for i in range(8)]
            NSLOT = 4 if with_out else 8
            NB = 8 // NSLOT
            PFs = [RR([b.view(b.t[:, 0:128]) for b in pbank[NB * j:NB * j + NB]]) for j in range(NSLOT)]
            PF = RR([b.view(b.t[:, 0:128]) for b in pbank[0:2]])
            TFs = [RR([fw.sbuf(st, f"dntf{j}_{i}", [128, 128], F32) for i in range(48 if with_out else 24)]) for j in range(NSLOT)]
            OB = RR([fw.sbuf(st, f"dnob{i}", [128, 128], BF16, dma=True) for i in range(8)])
            bat = [fw.sbuf(st, f"dnba{i}", [128, 32], F32, dma=True) for i in range(2)]
            kt = [fw.sbuf(st, f"dnk{i}", [128, 16, 128], F32, dma=True) for i in range(2)]
            vt = [fw.sbuf(st, f"dnv{i}", [128, 16, 128], F32, dma=True) for i in range(2)]
            qt = [fw.sbuf(st, f"dnq{i}", [128, 16, 128], F32, dma=True) for i in range(2)]
            zt = [fw.sbuf(st, f"dnz{i}", [128, 16, 128], BF16, dma=True) for i in range(2)]
            SC = RR([fw.sbuf(st, f"dnsc{i}", [128, 16], F32) for i in range(32)])
            tri = cm.t[:, 0, :]
            m_il = cm.t[:, 1, :]
            m_sl = cm.t[:, 2, :]
            idf = cm.t[:, 3, :]

            def ew(ek, fn, reads, writes):
                fw.op(ek, fn, reads=reads, writes=writes)

            for n in range(nch):
                c0 = n * 128
                ba_, k_, v_, q_, z_ = bat[n % 2], kt[n % 2], vt[n % 2], qt[n % 2], zt[n % 2]
                fw.dma("sp", ba_, ba_[:], ba, ba[ba_row0 + c0:ba_row0 + c0 + 128, :])
                fw.dma("sp", k_, k_[:], kd, kd[:, :, c0:c0 + 128].rearrange("h p t -> p h t"))
                fw.dma("act", v_, v_[:], vd, vd[:, :, c0:c0 + 128].rearrange("h p t -> p h t"))
                if with_out:
                    fw.dma("sp", q_, q_[:], qd, qd[:, :, c0:c0 + 128].rearrange("h p t -> p h t"))
                    fw.dma("act", z_, z_[:], zd, zd[:, :, c0:c0 + 128].rearrange("h p t -> p h t"))
                beta, negb, xg, ax, ex, g = (SC.get() for _ in range(6))
                ew("act", lambda e, beta=beta, ba_=ba_: e.activation(beta[:], ba_[:, 0:16], AF.Sigmoid), [ba_], [beta])
                ew("dve", lambda e, negb=negb, beta=beta: e.tensor_scalar(negb[:], beta[:], -1.0, None, ALU.mult), [beta], [negb])
                ew("dve", lambda e, xg=xg, ba_=ba_: e.tensor_tensor(xg[:], ba_[:, 16:32], hv[:, 16:32], ALU.add), [ba_, hv], [xg])
                ew("act", lambda e, ax=ax, xg=xg: e.activation(ax[:], xg[:], AF.Abs), [xg], [ax])
                ew("act", lambda e, ex=ex, ax=ax: e.activation(ex[:], ax[:], AF.Exp, scale=-1.0), [ax], [ex])
                ew("act", lambda e, ex=ex: e.activation(ex[:], ex[:], AF.Ln, bias=1.0), [ex], [ex])
                ew("dve", lambda e, xg=xg: e.tensor_scalar(xg[:], xg[:], 0.0, None, ALU.max), [xg], [xg])
                ew("dve", lambda e, xg=xg, ex=ex: e.tensor_tensor(xg[:], xg[:], ex[:], ALU.add), [xg, ex], [xg])
                ew("dve", lambda e, g=g, xg=xg: e.tensor_tensor(g[:], xg[:], negea[:], ALU.mult), [xg, negea], [g])
                pg = PF.get()
                pl = PF.get()
                ew("pe", lambda e, pg=pg, g=g: e.matmul(pg[:, 0:16], tri, g[:], start=True, stop=True), [g, cm], [pg])
                ew("pe", lambda e, pl=pl, g=g: e.matmul(pl[:, 0:16], ones_f[:], g[:], start=True, stop=True), [g, ones_f], [pl])
                gcol, egc, begc, ekt, adec = (SC.get() for _ in range(5))
                ew("act", lambda e, gcol=gcol, pg=pg: e.activation(gcol[:], pg[:, 0:16], AF.Copy), [pg], [gcol])
                ew("act", lambda e, egc=egc, pg=pg: e.activation(egc[:], pg[:, 0:16], AF.Exp), [pg], [egc])
                ew("dve", lambda e, begc=begc, egc=egc, beta=beta: e.tensor_tensor(begc[:], egc[:], beta[:], ALU.mult), [egc, beta], [begc])
                ew("dve", lambda e, ekt=ekt, pl=pl, gcol=gcol: e.tensor_tensor(ekt[:], pl[:, 0:16], gcol[:], ALU.subtract), [pl, gcol], [ekt])
                ew("act", lambda e, ekt=ekt: e.activation(ekt[:], ekt[:], AF.Exp), [ekt], [ekt])
                ew("act", lambda e, adec=adec, pl=pl: e.activation(adec[:], pl[:, 0:16], AF.Exp), [pl], [adec])
                if masked:
                    ew("dve", lambda e, ekt=ekt, n=n: e.tensor_scalar(ekt[:], ekt[:], flg[:, n:n + 1], None, ALU.mult), [ekt, flg], [ekt])
                    ew("dve", lambda e, adec=adec, n=n: e.tensor_scalar(adec[:], adec[:], -1.0, flg[:, n:n + 1], ALU.add, ALU.mult), [adec, flg], [adec])
                    ew("dve", lambda e, adec=adec: e.tensor_scalar(adec[:], adec[:], 1.0, None, ALU.add), [adec], [adec])
                def head_gen(h, slot, k_=k_, v_=v_, q_=q_, z_=z_, g=g, gcol=gcol, negb=negb, begc=begc, ekt=ekt, beta=beta, adec=adec, c0=c0):
                    TF = TFs[slot]
                    PF = PFs[slot]
                    kT = k_.t[:, h, :]
                    vT = v_.t[:, h, :]
                    gmat = TF.get()
                    ew("dve", lambda e, gmat=gmat, g=g, h=h: e.tensor_scalar(gmat[:], ones_f[:], g[:, h:h + 1], None, ALU.mult), [g, ones_f], [gmat])
                    yield
                    pgr = PF.get()
                    ew("pe", lambda e, pgr=pgr, gmat=gmat: e.matmul(pgr[:], gmat[:], tri, start=True, stop=True), [gmat, cm], [pgr])
                    yield
                    if with_out:
                        eg = TF.get()
                        ew("act", lambda e, eg=eg, pgr=pgr: e.activation(eg[:], pgr[:], AF.Exp), [pgr], [eg])
                        yield
                    dm = TF.get()
                    ew("dve", lambda e, dm=dm, pgr=pgr, gcol=gcol, h=h: e.tensor_scalar(dm[:], pgr[:], gcol[:, h:h + 1], 0.0, ALU.subtract, ALU.max), [pgr, gcol], [dm])
                    yield
                    ew("act", lambda e, dm=dm: e.activation(dm[:], dm[:], AF.Exp, scale=-1.0), [dm], [dm])
                    yield
                    ls = TF.get()
                    ew("dve", lambda e, ls=ls, dm=dm: e.tensor_tensor(ls[:], dm[:], m_sl, ALU.mult), [dm, cm], [ls])
                    yield
                    pkk = PF.get()
                    ew("pe", lambda e, pkk=pkk, kT=kT: e.matmul(pkk[:], kT, kT, start=True, stop=True), [k_], [pkk])
                    yield
                    N = TF.get()
                    ew("dve", lambda e, N=N, pkk=pkk, negb=negb, ls=ls, h=h: e.scalar_tensor_tensor(N[:], pkk[:], negb[:, h:h + 1], ls[:], ALU.mult, ALU.mult), [pkk, negb, ls], [N])
                    yield
                    pbt = PF.get()
                    ew("pe", lambda e, pbt=pbt, N=N: e.transpose(pbt[:], N[:], idf), [N, cm], [pbt])
                    yield
                    B = TF.get()
                    ew("act", lambda e, B=B, pbt=pbt: e.activation(B[:], pbt[:], AF.Copy), [pbt], [B])
                    yield
                    Pf_ = TF.get()
                    ew("dve", lambda e, Pf_=Pf_, B=B: e.tensor_tensor(Pf_[:], B[:], idf, ALU.add), [B, cm], [Pf_])
                    yield
                    for lev in range(6):
                        B2, N2 = TF.get(), TF.get()
                        if lev < 5:
                            p1 = PF.get()
                            ew("pe", lambda e, p1=p1, N=N, B=B: e.matmul(p1[:], N[:], B[:], start=True, stop=True), [N, B], [p1])
                            yield
                            ew("act", lambda e, B2=B2, p1=p1: e.activation(B2[:], p1[:], AF.Copy), [p1], [B2])
                            yield
                        p2 = PF.get()
                        ew("pe", lambda e, p2=p2, N=N, B=B: e.matmul(p2[:], B[:], N[:], start=True, stop=True), [N, B], [p2])
                        yield
                        ew("act", lambda e, N2=N2, p2=p2: e.activation(N2[:], p2[:], AF.Copy), [p2], [N2])
                        yield
                        B, N = B2, N2
                        p3 = PF.get()
                        ew("pe", lambda e, p3=p3, N=N, Pf_=Pf_: e.matmul(p3[:], N[:], Pf_[:], start=True, stop=True), [N, Pf_], [p3])
                        yield
                        Pn = TF.get()
                        ew("dve", lambda e, Pn=Pn, Pf_=Pf_, p3=p3: e.tensor_tensor(Pn[:], Pf_[:], p3[:], ALU.add), [Pf_, p3], [Pn])
                        yield
                        Pf_ = Pn
                    Xk, ktil, Xv = TF.get(), TF.get(), TF.get()
                    pkt = PF.get()
                    ew("pe", lambda e, pkt=pkt, kT=kT: e.transpose(pkt[:], kT, idf), [k_, cm], [pkt])
                    yield
                    ew("dve", lambda e, Xk=Xk, pkt=pkt, begc=begc, h=h: e.tensor_scalar(Xk[:], pkt[:], begc[:, h:h + 1], None, ALU.mult), [pkt, begc], [Xk])
                    yield
                    ew("dve", lambda e, ktil=ktil, pkt=pkt, ekt=ekt, h=h: e.tensor_scalar(ktil[:], pkt[:], ekt[:, h:h + 1], None, ALU.mult), [pkt, ekt], [ktil])
                    yield
                    pvt = PF.get()
                    ew("pe", lambda e, pvt=pvt, vT=vT: e.transpose(pvt[:], vT, idf), [v_, cm], [pvt])
                    yield
                    ew("act", lambda e, Xv=Xv, pvt=pvt, beta=beta, h=h: e.activation(Xv[:], pvt[:], AF.Copy, scale=beta[:, h:h + 1]), [pvt, beta], [Xv])
                    yield
                    Uf, WT = TF.get(), TF.get()
                    pu = PF.get()
                    ew("pe", lambda e, pu=pu, Pf_=Pf_, Xv=Xv: e.matmul(pu[:], Pf_[:], Xv[:], start=True, stop=True), [Pf_, Xv], [pu])
                    yield
                    ew("act", lambda e, Uf=Uf, pu=pu: e.activation(Uf[:], pu[:], AF.Copy), [pu], [Uf])
                    yield
                    pw = PF.get()
                    ew("pe", lambda e, pw=pw, Pf_=Pf_, Xk=Xk: e.matmul(pw[:], Xk[:], Pf_[:], start=True, stop=True), [Pf_, Xk], [pw])
                    yield
                    ew("dve", lambda e, WT=WT, pw=pw: e.tensor_copy(WT[:], pw[:]), [pw], [WT])
                    yield
                    pr = PF.get()
                    ew("pe", lambda e, pr=pr, WT=WT, h=h: e.matmul(pr[:], WT[:], Sfo[h][:], start=True, stop=True), [WT, Sfo[h]], [pr])
                    yield
                    vnew = TF.get()
                    ew("dve", lambda e, vnew=vnew, Uf=Uf, pr=pr: e.tensor_tensor(vnew[:], Uf[:], pr[:], ALU.subtract), [Uf, pr], [vnew])
                    yield
                    if with_out:
                        qT = q_.t[:, h, :]
                        qtil = TF.get()
                        ew("dve", lambda e, qtil=qtil, qT=qT, eg=eg: e.tensor_tensor(qtil[:], qT, eg[:], ALU.mult), [q_, eg], [qtil])
                        yield
                        lm = TF.get()
                        ew("dve", lambda e, lm=lm, dm=dm: e.tensor_tensor(lm[:], dm[:], m_il, ALU.mult), [dm, cm], [lm])
                        yield
                        pqk = PF.get()
                        ew("pe", lambda e, pqk=pqk, qT=qT, kT=kT: e.matmul(pqk[:], qT, kT, start=True, stop=True), [q_, k_], [pqk])
                        yield
                        attn = TF.get()
                        ew("dve", lambda e, attn=attn, pqk=pqk, lm=lm: e.tensor_tensor(attn[:], pqk[:], lm[:], ALU.mult), [pqk, lm], [attn])
                        yield
                        pat = PF.get()
                        ew("pe", lambda e, pat=pat, attn=attn: e.transpose(pat[:], attn[:], idf), [attn, cm], [pat])
                        yield
                        attnT = TF.get()
                        ew("act", lambda e, attnT=attnT, pat=pat: e.activation(attnT[:], pat[:], AF.Copy), [pat], [attnT])
                        yield
                        po = PF.get()

                        def omm(e, po=po, h=h, qtil=qtil, vnew=vnew, attnT=attnT):
                            e.matmul(po[:], Sfo[h][:], qtil[:], start=True, stop=False)
                            return e.matmul(po[:], vnew[:], attnT[:], start=False, stop=True)
                        ew("pe", omm, [Sfo[h], qtil, vnew, attnT], [po])
                        yield
                        of, osq = TF.get(), TF.get()
                        ew("act", lambda e, of=of, po=po: e.activation(of[:], po[:], AF.Copy), [po], [of])
                        yield
                        ew("act", lambda e, osq=osq, of=of: e.activation(osq[:], of[:], AF.Square), [of], [osq])
                        yield
                        pss = PF.get()
                        ew("pe", lambda e, pss=pss, osq=osq: e.matmul(pss[:], ones_f[:], osq[:], start=True, stop=True), [osq, ones_f], [pss])
                        yield
                        rr_ = TF.get()
                        ew("dve", lambda e, rr_=rr_, pss=pss: e.tensor_scalar(rr_[:], pss[:], 1.0 / 128, EPS, ALU.mult, ALU.add), [pss], [rr_])
                        yield
                        ew("act", lambda e, rr_=rr_: e.activation(rr_[:], rr_[:], AF.Sqrt), [rr_], [rr_])
                        yield
                        ew("dve", lambda e, rr_=rr_: e.reciprocal(rr_[:], rr_[:]), [rr_], [rr_])
                        yield
                        of2 = TF.get()
                        ew("dve", lambda e, of2=of2, of=of, rr_=rr_: e.tensor_tensor(of2[:], of[:], rr_[:], ALU.mult), [of, rr_], [of2])
                        yield
                        ob_ = OB.get()
                        ew("dve", lambda e, ob_=ob_, of2=of2, z_=z_, h=h: e.scalar_tensor_tensor(ob_[:], of2[:], nw[:, 0:1], z_[:, h, :], ALU.mult, ALU.mult), [of2, nw, z_], [ob_])
                        yield
                        fw.dma(nextq(), mixT, mixT[:, h, c0:c0 + 128], ob_, ob_[:])
                        yield
                    pds = PF.get()
                    ew("pe", lambda e, pds=pds, ktil=ktil, vnew=vnew: e.matmul(pds[:], ktil[:], vnew[:], start=True, stop=True), [ktil, vnew], [pds])
                    yield
                    ew("dve", lambda e, h=h, adec=adec, pds=pds: e.scalar_tensor_tensor(Sfo[h][:], Sfo[h][:], adec[:, h:h + 1], pds[:], ALU.mult, ALU.add), [Sfo[h], adec, pds], [Sfo[h]])
                    yield

                for h0 in range(0, 16, NSLOT):
                    interleave(head_gen(h0 + j, j) for j in range(NSLOT))
            fw.barrier()
            fw.flush()

    p3_dn(kdn_f, vdn_f, None, None, ba_f, 0, NCH - NCO, True, False)
    if phases == 2.5:
        return finish(nc, fw, top, cst, dbg, dbg_o, locals())
    p3_dn(kdn_o, vdn_o, qdn_o, zdn_o, ba_o, HALO, NCO, False, True)

    if phases < 4:
        return finish(nc, fw, top, cst, dbg, dbg_o, locals())

    def rsqrt_ops(o, ap):
        fw.op("act", lambda e: e.activation(ap, ap, AF.Sqrt), reads=[o], writes=[o])
        fw.op("dve", lambda e: e.reciprocal(ap, ap), reads=[o], writes=[o])

    lamv = fw.sbuf(cst2, "lamv_s", [128, 4], F32, dma=True)
    lam2 = fw.sbuf(cst2, "lam2", [128, 2], F32)
    neglam = fw.sbuf(cst2, "neglam", [128, 1], F32)
    nwdf = fw.sbuf(cst2, "nwdf", [128, 2], F32)
    qpos = fw.sbuf(cst2, "qpos_s", [128, HALO + QT], F32, dma=True)
    kpos = fw.sbuf(cst2, "kpos_s", [128, NCH], F32, dma=True)
    fw.dma("sp", lamv, lamv[:], lamv_d, lamv_d[:])
    fw.dma("sp", qpos, qpos[:], qpos_d, qpos_d[:])
    fw.dma("sp", kpos, kpos[:], kpos_d, kpos_d[:])
    fw.op("dve", lambda e: e.tensor_tensor(lam2[:, 0:1], lamv[:, 0:1], lamv[:, 1:2], ALU.mult), reads=[lamv], writes=[lam2])
    fw.op("dve", lambda e: e.tensor_tensor(lam2[:, 1:2], lamv[:, 2:3], lamv[:, 3:4], ALU.mult), reads=[lamv, lam2], writes=[lam2])
    fw.op("dve", lambda e: e.tensor_scalar(nwdf[:], nw[:, 1:3], 1.0 - LAMBDA_INIT, None, ALU.mult), reads=[nw], writes=[nwdf])

    def p4_attn():
        SCALE = 128.0 ** -0.5
        with ExitStack() as st:
            pst = [fw.psum(st, f"ap{i}", [128, 512]) for i in range(2)]
            pacc = [[fw.psum(st, f"aa{i}_{j}", [128, 512]) for j in range(3)] for i in range(2)]
            vt = fw.sbuf(st, "avt", [128, NCH, 256], BF16, dma=True)
            kt = [fw.sbuf(st, f"akt{i}", [128, S], BF16, dma=True) for i in range(2)]
            qt = [fw.sbuf(st, f"aqt{i}", [128, QT], BF16, dma=True) for i in range(2)]
            O1 = fw.sbuf(st, "aO1", [128, 2, QT], F32)
            pe_t = [fw.sbuf(st, f"ape{i}", [128, TT], F32) for i in range(3)]
            pm_t = [fw.sbuf(st, f"apm{i}", [128, TT], BF16) for i in range(3)]
            rs_t = fw.sbuf(st, "ars", [128, TT], F32)
            cb_t = [fw.sbuf(st, f"acb{i}", [128, TT], F32) for i in range(2)]
            sq_t = [fw.sbuf(st, f"asq{i}", [128, TT], BF16) for i in range(2)]
            ob_t = [fw.sbuf(st, f"aob{i}", [128, TT], BF16, dma=True) for i in range(4)]
            fw.op("pe", lambda e: e.matmul(pst[0][:, 0:2], ones_f[:], lam2[:], start=True, stop=True), reads=[lam2, ones_f], writes=[pst[0]])
            fw.op("act", lambda e: e.activation(lam2[:], pst[0][:, 0:2], AF.Exp), reads=[pst[0]], writes=[lam2])
            fw.op("dve", lambda e: e.tensor_tensor(neglam[:], lam2[:, 1:2], lam2[:, 0:1], ALU.subtract), reads=[lam2], writes=[neglam])
            fw.op("dve", lambda e: e.tensor_scalar(neglam[:], neglam[:], -LAMBDA_INIT, None, ALU.add), reads=[neglam], writes=[neglam])
            it = 0
            ia = 0
            io = 0
            for h in range(NH_DF):
                fw.dma("sp", vt, vt[:], vtok_f, vtok_f[:, h * 256:(h + 1) * 256].rearrange("(kb p) c -> p kb c", p=128))
                for s_ in range(2):
                    sub = 2 * h + s_
                    k_, q_ = kt[sub % 2], qt[sub % 2]
                    fw.dma("act", k_, k_[:], kdf_f, kdf_f[sub])
                    fw.dma("sp", q_, q_[:], qdf_o, qdf_o[sub])
                    for q0 in range(0, QT, TT):
                        n = TT
                        acc = pacc[ia % 2]
                        ia += 1
                        for kb in range(NCH):
                            p_ = pst[it % 2]
                            e_ = pe_t[it % 3]
                            m_ = pm_t[it % 3]
                            it += 1
                            fw.op("pe", lambda e, p_=p_, k_=k_, q_=q_, kb=kb, q0=q0: e.matmul(p_[:, 0:n], k_[:, kb * 128:(kb + 1) * 128], q_[:, q0:q0 + n], start=True, stop=True),
                                  reads=[k_, q_], writes=[p_])
                            fw.op("act", lambda e, e_=e_, p_=p_: e.activation(e_[:, 0:n], p_[:, 0:n], AF.Exp, scale=SCALE), reads=[p_], writes=[e_])
                            fw.op("dve", lambda e, m_=m_, e_=e_, kb=kb, q0=q0: e.scalar_tensor_tensor(m_[:, 0:n], qpos[:, HALO + q0:HALO + q0 + n], kpos[:, kb:kb + 1], e_[:, 0:n], ALU.is_ge, ALU.mult),
                                  reads=[e_, qpos, kpos], writes=[m_])

                            def pv(e, acc=acc, m_=m_, kb=kb):
                                e.matmul(acc[0][:, 0:n], vt[:, kb, 0:128], m_[:, 0:n], start=(kb == 0), stop=(kb == NCH - 1))
                                e.matmul(acc[1][:, 0:n], vt[:, kb, 128:256], m_[:, 0:n], start=(kb == 0), stop=(kb == NCH - 1))
                                return e.matmul(acc[2][:, 0:n], ones_b[:], m_[:, 0:n], start=(kb == 0), stop=(kb == NCH - 1))
                            fw.op("pe", pv, reads=[vt, m_, ones_b], writes=[acc[0], acc[1], acc[2]])
                        fw.op("dve", lambda e, acc=acc: e.reciprocal(rs_t[:, 0:n], acc[2][:, 0:n]), reads=[acc[2]], writes=[rs_t])
                        for c in range(2):
                            if s_ == 0:
                                fw.op("dve", lambda e, acc=acc, c=c, q0=q0: e.tensor_tensor(O1[:, c, q0:q0 + n], acc[c][:, 0:n], rs_t[:, 0:n], ALU.mult),
                                      reads=[acc[c], rs_t], writes=[O1])
                            else:
                                cb_ = cb_t[c]
                                fw.op("dve", lambda e, acc=acc, c=c, cb_=cb_: e.tensor_tensor(cb_[:, 0:n], acc[c][:, 0:n], rs_t[:, 0:n], ALU.mult),
                                      reads=[acc[c], rs_t], writes=[cb_])
                                fw.op("dve", lambda e, c=c, cb_=cb_, q0=q0: e.scalar_tensor_tensor(cb_[:, 0:n], cb_[:, 0:n], neglam[:, 0:1], O1[:, c, q0:q0 + n], ALU.mult, ALU.add),
                                      reads=[cb_, neglam, O1], writes=[cb_])
                                fw.op("act", lambda e, c=c, cb_=cb_: e.activation(sq_t[c][:, 0:n], cb_[:, 0:n], AF.Square), reads=[cb_], writes=[sq_t[c]])
                        if s_ == 1:
                            p_ = pst[it % 2]
                            it += 1

                            def ssm(e, p_=p_):
                                e.matmul(p_[:, 0:n], ones_b[:], sq_t[0][:, 0:n], start=True, stop=False)
                                return e.matmul(p_[:, 0:n], ones_b[:], sq_t[1][:, 0:n], start=False, stop=True)
                            fw.op("pe", ssm, reads=[sq_t[0], sq_t[1], ones_b], writes=[p_])
                            fw.op("dve", lambda e, p_=p_: e.tensor_scalar(rs_t[:, 0:n], p_[:, 0:n], 1.0 / 256, EPS, ALU.mult, ALU.add), reads=[p_], writes=[rs_t])
                            rsqrt_ops(rs_t, rs_t[:, 0:n])
                            for c in range(2):
                                o_ = ob_t[io % 4]
                                io += 1
                                fw.op("dve", lambda e, o_=o_, c=c: e.scalar_tensor_tensor(o_[:, 0:n], cb_t[c][:, 0:n], nwdf[:, c:c + 1], rs_t[:, 0:n], ALU.mult, ALU.mult),
                                      reads=[cb_t[c], nwdf, rs_t], writes=[o_])
                                fw.dma(nextq(), mixT, mixT[:, 16 + 2 * h + c, q0:q0 + n], o_, o_[:, 0:n])
            fw.barrier()
            fw.flush()

    p4_attn()

    if phases < 5:
        return finish(nc, fw, top, cst, dbg, dbg_o, locals())

    cst2.close()
    rstd1 = fw.sbuf(cst, "rstd1", [128, QT], F32)
    w_out_v = w_out.t.rearrange("(kc p) n -> p kc n", p=128)
    w_gate_v = w_gate.t.rearrange("(kc p) n -> p kc n", p=128)
    w_up_v = w_up.t.rearrange("(kc p) n -> p kc n", p=128)
    w_down_v = w_down.t.rearrange("(hb p) n -> p hb n", p=128)

    def p5a():
        with ExitStack() as st:
            mx = fw.sbuf(st, "omx", [128, KC, TT], BF16, dma=True)
            wo = [fw.sbuf(st, f"owo{i}", [128, KC, 512], BF16, dma="sw") for i in range(2)]
            xr = [fw.sbuf(st, f"oxr{i}", [128, TT], F32, dma=True) for i in range(3)]
            sq = [fw.sbuf(st, f"osq{i}", [128, TT], BF16) for i in range(2)]
            ps = [fw.psum(st, f"ops{i}", [128, 512]) for i in range(3)]
            pss = fw.psum(st, "opss", [128, 512])
            it = 0
            ig = 0
            for t0 in range(0, QT, TT):
                n = TT
                fw.dma("sp", mx, mx[:], mixT, mixT[:, :, t0:t0 + n])
                for cg in range(8):
                    w_ = wo[ig % 2]
                    ig += 1
                    fw.dma("pool", w_, w_[:], w_out, w_out_v[:, :, cg * 512:(cg + 1) * 512])
                    for cb in range(4):
                        blk = cg * 4 + cb
                        p_, x_, s_ = ps[it % 3], xr[it % 3], sq[it % 2]
                        it += 1
                        fw.dma(nextq(), x_, x_[:, 0:n], xTo, xTo[:, blk, HALO + t0:HALO + t0 + n])

                        def mm(e, p_=p_, w_=w_, cb=cb):
                            r = None
                            for kc in range(KC):
                                r = e.matmul(p_[:, 0:n], w_[:, kc, cb * 128:(cb + 1) * 128], mx[:, kc, 0:n], start=(kc == 0), stop=(kc == KC - 1))
                            return r
                        fw.op("pe", mm, reads=[w_, mx], writes=[p_])
                        fw.op("dve", lambda e, x_=x_, p_=p_: e.tensor_tensor(x_[:, 0:n], x_[:, 0:n], p_[:, 0:n], ALU.add), reads=[x_, p_], writes=[x_])
                        fw.op("act", lambda e, s_=s_, x_=x_: e.activation(s_[:, 0:n], x_[:, 0:n], AF.Square), reads=[x_], writes=[s_])
                        fw.op("pe", lambda e, s_=s_, blk=blk: e.matmul(pss[:, 0:n], ones_b[:], s_[:, 0:n], start=(blk == 0), stop=(blk == KC - 1)),
                              reads=[s_, ones_b], writes=[pss])
                        fw.dma(nextq(), x1T, x1T[:, blk, t0:t0 + n], x_, x_[:, 0:n])
                fw.op("dve", lambda e, t0=t0: e.tensor_scalar(rstd1[:, t0:t0 + n], pss[:, 0:n], 1.0 / D, EPS, ALU.mult, ALU.add), reads=[pss], writes=[rstd1])
                rsqrt_ops(rstd1, rstd1[:, t0:t0 + n])
            fw.barrier()
            fw.flush()

    p5a()

    def p5b():
        HH = HB // 2
        with ExitStack() as st:
            hT = fw.sbuf(st, "fh", [128, KC, TT], BF16)
            act = fw.sbuf(st, "fact", [128, HB, TT], BF16)
            wg = [fw.sbuf(st, f"fwg{i}", [128, KC, 128], BF16, dma="sw") for i in range(2)]
            wu = [fw.sbuf(st, f"fwu{i}", [128, KC, 128], BF16, dma="sw") for i in range(2)]
            wd = [fw.sbuf(st, f"fwd{i}", [128, HH, 128], BF16, dma="sw") for i in range(2)]
            xr = [fw.sbuf(st, f"fxr{i}", [128, TT], F32, dma=True) for i in range(3)]
            sg = [fw.sbuf(st, f"fsg{i}", [128, TT], F32) for i in range(2)]
            sq = [fw.sbuf(st, f"fsq{i}", [128, TT], BF16) for i in range(2)]
            r2 = fw.sbuf(st, "fr2", [128, TT], F32)
            pg = [fw.psum(st, f"fpg{i}", [128, 512]) for i in range(2)]
            pu = [fw.psum(st, f"fpu{i}", [128, 512]) for i in range(2)]
            pd = [fw.psum(st, f"fpd{i}", [128, 512]) for i in range(2)]
            pss = fw.psum(st, "fpss", [128, 512])
            ix = 0
            iw = 0
            for t0 in range(0, QT, TT):
                n = TT
                for kc in range(KC):
                    x_ = xr[ix % 3]
                    ix += 1
                    fw.dma(nextq(), x_, x_[:, 0:n], x1T, x1T[:, kc, t0:t0 + n])
                    fw.op("dve", lambda e, x_=x_, kc=kc, t0=t0: e.scalar_tensor_tensor(hT[:, kc, 0:n], x_[:, 0:n], lnw[:, 1, kc:kc + 1], rstd1[:, t0:t0 + n], ALU.mult, ALU.mult),
                          reads=[x_, lnw, rstd1], writes=[hT])
                for hb in range(HB):
                    g_, u_, pg_, pu_, sg_ = wg[hb % 2], wu[hb % 2], pg[hb % 2], pu[hb % 2], sg[hb % 2]
                    fw.dma("pool", g_, g_[:], w_gate, w_gate_v[:, :, hb * 128:(hb + 1) * 128])
                    fw.dma("pool", u_, u_[:], w_up, w_up_v[:, :, hb * 128:(hb + 1) * 128])

                    def mmg(e, p_=pg_, w_=g_):
                        r = None
                        for kc in range(KC):
                            r = e.matmul(p_[:, 0:n], w_[:, kc, :], hT[:, kc, 0:n], start=(kc == 0), stop=(kc == KC - 1))
                        return r

                    def mmu(e, p_=pu_, w_=u_):
                        r = None
                        for kc in range(KC):
                            r = e.matmul(p_[:, 0:n], w_[:, kc, :], hT[:, kc, 0:n], start=(kc == 0), stop=(kc == KC - 1))
                        return r
                    fw.op("pe", mmg, reads=[g_, hT], writes=[pg_])
                    fw.op("pe", mmu, reads=[u_, hT], writes=[pu_])
                    fw.op("act", lambda e, sg_=sg_, pg_=pg_: e.activation(sg_[:, 0:n], pg_[:, 0:n], AF.Silu), reads=[pg_], writes=[sg_])
                    fw.op("dve", lambda e, sg_=sg_, pu_=pu_, hb=hb: e.tensor_tensor(act[:, hb, 0:n], sg_[:, 0:n], pu_[:, 0:n], ALU.mult),
                          reads=[sg_, pu_], writes=[act])
                for cb in range(KC):
                    p_ = pd[cb % 2]
                    x_ = xr[ix % 3]
                    ix += 1
                    fw.dma(nextq(), x_, x_[:, 0:n], x1T, x1T[:, cb, t0:t0 + n])
                    for half in range(2):
                        w_ = wd[iw % 2]
                        iw += 1
                        fw.dma("pool", w_, w_[:], w_down, w_down_v[:, half * HH:(half + 1) * HH, cb * 128:(cb + 1) * 128])

                        def mmd(e, p_=p_, w_=w_, half=half):
                            r = None
                            for j in range(HH):
                                hb = half * HH + j
                                r = e.matmul(p_[:, 0:n], w_[:, j, :], act[:, hb, 0:n], start=(hb == 0), stop=(hb == HB - 1))
                            return r
                        fw.op("pe", mmd, reads=[w_, act], writes=[p_])
                    s_ = sq[cb % 2]
                    fw.op("dve", lambda e, x_=x_, p_=p_: e.tensor_tensor(x_[:, 0:n], x_[:, 0:n], p_[:, 0:n], ALU.add), reads=[x_, p_], writes=[x_])
                    fw.op("act", lambda e, s_=s_, x_=x_: e.activation(s_[:, 0:n], x_[:, 0:n], AF.Square), reads=[x_], writes=[s_])
                    fw.op("pe", lambda e, s_=s_, cb=cb: e.matmul(pss[:, 0:n], ones_b[:], s_[:, 0:n], start=(cb == 0), stop=(cb == KC - 1)),
                          reads=[s_, ones_b], writes=[pss])
                    fw.dma(nextq(), x2T, x2T[:, cb, t0:t0 + n], x_, x_[:, 0:n])
                fw.op("dve", lambda e: e.tensor_scalar(r2[:, 0:n], pss[:, 0:n], 1.0 / D, EPS, ALU.mult, ALU.add), reads=[pss], writes=[r2])
                rsqrt_ops(r2, r2[:, 0:n])
                for cb in range(KC):
                    x_ = xr[ix % 3]
                    ix += 1
                    fw.dma(nextq(), x_, x_[:, 0:n], x2T, x2T[:, cb, t0:t0 + n])
                    fw.op("dve", lambda e, x_=x_, cb=cb: e.scalar_tensor_tensor(x_[:, 0:n], x_[:, 0:n], lnw[:, 2, cb:cb + 1], r2[:, 0:n], ALU.mult, ALU.mult),
                          reads=[x_, lnw, r2], writes=[x_])
                    fw.dma(nextq(), outT, outT[:, cb, t0:t0 + n], x_, x_[:, 0:n])
            fw.barrier()
            fw.flush()

    p5b()
    return finish(nc, fw, top, cst, dbg, dbg_o, locals())


def _prep(inputs, S):
    x = np.asarray(inputs["x"], np.float32)
    QT = S // 4
    NCH = S // 128
    g = lambda k: np.asarray(inputs[k], np.float32)
    w_in = np.ascontiguousarray(g("w_in")[0])
    w_out = np.ascontiguousarray(g("w_out")[0])
    w_gate = np.ascontiguousarray(g("w_gate")[0])
    w_up = np.ascontiguousarray(g("w_up")[0])
    w_down = np.ascontiguousarray(g("w_down")[0])
    lnw = np.stack([g("ln_mix_w")[0], g("ln_ffn_w")[0], g("ln_final_w")], 0).reshape(3, 32, 128).transpose(2, 0, 1).copy()
    convw = g("conv_w")[0].reshape(4, 48, 128).transpose(2, 1, 0).copy()
    hv = np.broadcast_to(np.concatenate([g("a_log")[0], g("dt_bias")[0]])[None, :], (128, 32)).copy()
    dfw = g("df_norm_w")[0]
    nw = np.stack([g("dn_norm_w")[0], dfw[:128], dfw[128:]], 1).copy()
    lamv = np.stack([g("lambda_q1")[0], g("lambda_k1")[0], g("lambda_q2")[0], g("lambda_k2")[0]], 1).copy()
    j = np.arange(128)[:, None]
    i = np.arange(128)[None, :]
    prot = np.zeros((128, 128), np.float32)
    for m in range(16):
        prot[m + 16, m] = -1.0
        prot[m, m + 16] = 1.0
    cmask = np.stack([(j <= i), (i <= j), (i < j), (i == j), prot], 0).astype(np.float32).transpose(1, 0, 2).copy()
    kpos = (np.arange(NCH)[None, :] * 128 + np.arange(128)[:, None]).astype(np.float32)
    posf = np.broadcast_to(np.arange(S, dtype=np.float32)[None, :], (128, S)).copy()
    invf = np.zeros((128, 1), np.float32)
    fr = np.array([ROPE_THETA ** (-(2 * k) / 32.0) for k in range(16)], np.float32)
    invf[0:16, 0] = fr
    invf[16:32, 0] = fr
    maps = []
    for c in range(8):
        b, tq = c // 4, c % 4
        xT = x[b].T
        xTf = np.ascontiguousarray(xT.reshape(32, 128, S).transpose(1, 0, 2))
        lo = tq * QT
        own = np.zeros((4096, HALO + QT), np.float32)
        own[:, HALO:] = xT[:, lo:lo + QT]
        if tq > 0:
            own[:, :HALO] = xT[:, lo - HALO:lo]
        xTo = np.ascontiguousarray(own.reshape(32, 128, HALO + QT).transpose(1, 0, 2))
        flags = np.broadcast_to((np.arange(NCH) < lo // 128).astype(np.float32)[None, :], (128, NCH)).copy()
        qp = np.arange(lo - HALO, lo + QT, dtype=np.float32)
        qpos = np.broadcast_to(qp[None, :], (128, HALO + QT)).copy()
        maps.append(dict(xTf=xTf, xTo=xTo, w_in=w_in, w_out=w_out, w_gate=w_gate, w_up=w_up, w_down=w_down,
                         lnw=lnw, convw=convw, hv=hv, nw=nw, lamv=lamv, cmask=cmask, flags=flags, qpos=qpos,
                         kpos=kpos, posf=posf, invf=invf))
    return maps


def kernel(**inputs):
    x = np.asarray(inputs["x"])
    B, S, _ = x.shape
    QT = S // 4
    nc = build_program(S)
    maps = _prep(inputs, S)
    res = run_bass_kernel_spmd(nc, maps, core_ids=list(range(8)))
    out = np.empty((B, S, D), np.float32)
    for c in range(8):
        b, tq = c // 4, c % 4
        o = np.asarray(res.results[c]["outT"], np.float32)
        out[b, tq * QT:(tq + 1) * QT, :] = o.transpose(2, 1, 0).reshape(QT, D)
    return out
```

```python
import math
from contextlib import ExitStack
import numpy as np
import concourse.bass as bass
import concourse.mybir as mybir
from concourse.bass_utils import run_bass_kernel_spmd

F32 = mybir.dt.float32
BF16 = mybir.dt.bfloat16
AF = mybir.ActivationFunctionType
ALU = mybir.AluOpType
AX = mybir.AxisListType

SAME_ENGINE_SYNC = True


class Sem:
    def __init__(self, h):
        self.h = h
        self.cnt = 0


class Track:
    def __init__(self):
        self.writers = {}
        self.readers = {}


class Obj:
    def __init__(self, t, name, dsem=None, tr=None, psum=False):
        self.t = t
        self.name = name
        self.tr = tr if tr is not None else Track()
        self.dsem = dsem
        self.psum = psum

    @property
    def writers(self):
        return self.tr.writers

    @writers.setter
    def writers(self, v):
        self.tr.writers = v

    @property
    def readers(self):
        return self.tr.readers

    @readers.setter
    def readers(self, v):
        self.tr.readers = v

    def view(self, ap, name=None):
        return Obj(ap, name or self.name, self.dsem, self.tr, self.psum)

    def __getitem__(self, k):
        return self.t[k]


class FW:
    def __init__(self, nc, stack):
        self.nc = nc
        self.stack = stack
        self.engs = ["pe", "act", "dve", "pool", "sp"]
        self.esem = {k: Sem(stack.enter_context(nc.semaphore("es_" + k))) for k in self.engs}
        self.thunks = {k: [] for k in self.engs}
        self.waited = {k: {} for k in self.engs}
        self.allsems = list(self.esem.values())
        self.ninst = 0

    def new_sem(self, name):
        s = Sem(self.stack.enter_context(self.nc.semaphore(name)))
        self.allsems.append(s)
        return s

    def sbuf(self, st, name, shape, dt, dma=False):
        self.uid = getattr(self, "uid", 0) + 1
        name = f"{name}_{self.uid}"
        t = st.enter_context(self.nc.sbuf_tensor(name, shape, dt))
        sem = None
        if dma:
            pool = self.__dict__.setdefault("sem_pool_" + str(dma), [])
            sem = pool.pop() if pool else self.new_sem("d_" + name)
            st.callback(lambda: pool.append(sem))
        return Obj(t, name, sem)

    def psum(self, st, name, shape, dt=F32):
        self.uid = getattr(self, "uid", 0) + 1
        name = f"{name}_{self.uid}"
        t = st.enter_context(self.nc.psum_tensor(name, shape, dt))
        return Obj(t, name, psum=True)

    def dram(self, name, shape, dt, kind="Internal"):
        t = self.nc.dram_tensor(name, shape, dt, kind=kind).ap()
        return Obj(t, name)

    def _deps(self, reads, writes):
        deps = {}
        for o in reads:
            for s, v in o.writers.items():
                if deps.get(s, 0) < v:
                    deps[s] = v
            if o.psum:
                for s, v in o.readers.items():
                    if deps.get(s, 0) < v:
                        deps[s] = v
        for o in writes:
            for d in (o.writers, o.readers):
                for s, v in d.items():
                    if deps.get(s, 0) < v:
                        deps[s] = v
        return deps

    def _waits(self, ek, deps):
        w = self.waited[ek]
        own = self.esem[ek]
        out = []
        for s, v in deps.items():
            if s is own and not SAME_ENGINE_SYNC:
                continue
            if w.get(s, 0) >= v:
                continue
            w[s] = v
            out.append((s.h, v))
        return out

    def op(self, ek, fn, reads=(), writes=()):
        waits = self._waits(ek, self._deps(reads, writes))
        sem = self.esem[ek]
        sem.cnt += 1
        val = sem.cnt
        h = sem.h

        def thunk(e):
            for sh, v in waits:
                e.wait_ge(sh, v)
            r = fn(e)
            if isinstance(r, (list, tuple)):
                r = r[-1]
            r.then_inc(h, 1)

        self.thunks[ek].append(thunk)
        self.ninst += 1
        for o in reads:
            if o.readers.get(sem, 0) < val:
                o.readers[sem] = val
        for o in writes:
            o.writers = {sem: val}
            o.readers = {}

    def dma(self, qk, out_o, out_ap, in_o, in_ap, sem_obj=None, group=False, **kw):
        if sem_obj is None:
            sem_obj = out_o if out_o.dsem is not None else in_o
        sem = sem_obj.dsem
        assert sem is not None, (out_o.name, in_o.name)
        if group:
            deps = self._deps([in_o], [])
            for s, v in out_o.readers.items():
                deps[s] = max(deps.get(s, 0), v)
            for s, v in out_o.writers.items():
                if s is not sem:
                    deps[s] = max(deps.get(s, 0), v)
        else:
            deps = self._deps([in_o], [out_o])
        waits = self._waits(qk, deps)
        sem.cnt += 16
        val = sem.cnt
        h = sem.h

        def thunk(e):
            for sh, v in waits:
                e.wait_ge(sh, v)
            e.dma_start(out=out_ap, in_=in_ap, **kw).then_inc(h, 16)

        self.thunks[qk].append(thunk)
        self.ninst += 1
        if in_o.readers.get(sem, 0) < val:
            in_o.readers[sem] = val
        if group:
            out_o.writers[sem] = val
        else:
            out_o.writers = {sem: val}
        out_o.readers = {}

    def barrier(self):
        snap = [(s, s.cnt) for s in self.allsems if s.cnt > 0]
        for ek in self.engs:
            waits = self._waits(ek, dict(snap))

            def thunk(e, waits=waits):
                for sh, v in waits:
                    e.wait_ge(sh, v)

            self.thunks[ek].append(thunk)

    def flush(self):
        lists = self.thunks
        with self.nc.Block() as block:
            @block.tensor
            def _(e):
                for t in lists["pe"]:
                    t(e)

            @block.scalar
            def _(e):
                for t in lists["act"]:
                    t(e)

            @block.vector
            def _(e):
                for t in lists["dve"]:
                    t(e)

            @block.gpsimd
            def _(e):
                for t in lists["pool"]:
                    t(e)

            @block.sync
            def _(e):
                for t in lists["sp"]:
                    t(e)
        self.thunks = {k: [] for k in self.engs}
        if max(s.cnt for s in self.esem.values()) > 20000:
            self.epoch = getattr(self, "epoch", 0) + 1
            for k in self.engs:
                ns = Sem(self.stack.enter_context(self.nc.semaphore(f"es_{k}_{self.epoch}")))
                self.esem[k] = ns
                self.allsems.append(ns)

D = 4096
KC = 32
NH_DN = 16
NSUB = 16
NH_DF = 8
FFN = 11008
HB = FFN // 128
PROJ = 14368
OFF_DQ, OFF_DK, OFF_DV, OFF_DZ, OFF_DB, OFF_DA, OFF_FQ, OFF_FK, OFF_FV = 0, 2048, 4096, 6144, 8192, 8208, 8224, 10272, 12320
HALO = 4
EPS = 1e-6
ROPE_THETA = 500000.0
LAMBDA_INIT = 0.8 - 0.6 * math.exp(-0.3 * 0)
PI = math.pi


def finish(nc, fw, top, cst, dbg, dbg_o, env):
    if dbg is not None:
        for i, d in enumerate(dbg):
            src = env[d[0]]
            dbg_o[i].dsem = fw.new_sem(f"dbgsem{i}")
            fw.dma("sp", dbg_o[i], dbg_o[i][:], src, d[3](src), sem_obj=dbg_o[i])
    fw.barrier()
    fw.flush()
    c2 = env.get("cst2")
    if c2 is not None:
        c2.close()
    cst.close()
    top.close()
    return nc


def build_program(S, phases=99, dbg=None):
    QT = S // 4
    TT = min(512, QT)
    NCH = S // 128
    NCO = QT // 128
    nc = bass.Bass("TRN2", target_bir_lowering=False)
    top = ExitStack()
    fw = FW(nc, top)

    def ein(name, shape, dt=F32):
        return fw.dram(name, shape, dt, kind="ExternalInput")

    xTf = ein("xTf", [128, KC, S])
    xTo = ein("xTo", [128, KC, HALO + QT])
    w_in = ein("w_in", [D, PROJ])
    if phases >= 5:
        w_out = ein("w_out", [D, D])
        w_gate = ein("w_gate", [D, FFN])
        w_up = ein("w_up", [D, FFN])
        w_down = ein("w_down", [FFN, D])
    lnw_d = ein("lnw", [128, 3, KC])
    convw_d = ein("convw", [128, 48, 4])
    hv_d = ein("hv", [128, 32])
    nw_d = ein("nw", [128, 3])
    lamv_d = ein("lamv", [128, 4])
    cm_d = ein("cmask", [128, 5, 128])
    flags_d = ein("flags", [128, NCH])
    qpos_d = ein("qpos", [128, HALO + QT])
    kpos_d = ein("kpos", [128, NCH])
    posf_d = ein("posf", [128, S])
    invf_d = ein("invf", [128, 1])
    outT = fw.dram("outT", [128, KC, QT], F32, kind="ExternalOutput")

    xn_f = fw.dram("xn_f", [128, KC, S], BF16)
    xn_o = fw.dram("xn_o", [128, KC, HALO + QT], BF16)
    raw_f = fw.dram("raw_f", [48, 128, S], F32)
    vtok_f = fw.dram("vtok_f", [S, 2048], BF16)
    ba_f = fw.dram("ba_f", [S, 32], F32)
    raw_o = fw.dram("raw_o", [80, 128, HALO + QT], F32)
    ba_o = fw.dram("ba_o", [HALO + QT, 32], F32)
    kdn_f = fw.dram("kdn_f", [16, 128, S], F32)
    vdn_f = fw.dram("vdn_f", [16, 128, S], F32)
    kdf_f = fw.dram("kdf_f", [16, 128, S], BF16)
    qdn_o = fw.dram("qdn_o", [16, 128, QT], F32)
    kdn_o = fw.dram("kdn_o", [16, 128, QT], F32)
    vdn_o = fw.dram("vdn_o", [16, 128, QT], F32)
    zdn_o = fw.dram("zdn_o", [16, 128, QT], BF16)
    qdf_o = fw.dram("qdf_o", [16, 128, QT], BF16)
    mixT = fw.dram("mixT", [128, KC, QT], BF16)
    x1T = fw.dram("x1T", [128, KC, QT], F32)
    x2T = fw.dram("x2T", [128, KC, QT], F32)
    dbg_o = None
    if dbg is not None:
        dbg_o = [fw.dram(f"dbg{i}", list(d[1]), d[2], kind="ExternalOutput") for i, d in enumerate(dbg)]

    evq = ["sp", "act"]
    cnt = {"e": 0, "q": 0}

    def nextq():
        cnt["q"] += 1
        return evq[cnt["q"] % 2]

    def evac_eng():
        cnt["e"] += 1
        return "act" if cnt["e"] % 2 else "dve"

    def interleave(gens):
        gens = list(gens)
        while gens:
            nxt = []
            for g_ in gens:
                try:
                    next(g_)
                    nxt.append(g_)
                except StopIteration:
                    pass
            gens = nxt

    def copy_op(ek, out_o, out_ap, in_o, in_ap):
        if ek == "act":
            fw.op("act", lambda e: e.activation(out_ap, in_ap, AF.Copy), reads=[in_o], writes=[out_o])
        else:
            fw.op(ek, lambda e: e.tensor_copy(out_ap, in_ap), reads=[in_o], writes=[out_o])

    cst = ExitStack()
    ones_b = fw.sbuf(cst, "ones_b", [128, 128], BF16)
    ones_f = fw.sbuf(cst, "ones_f", [128, 128], F32)
    lnw = fw.sbuf(cst, "lnw_s", [128, 3, KC], F32, dma=True)
    cm = fw.sbuf(cst, "cm_s", [128, 5, 128], F32, dma=True)
    idb = fw.sbuf(cst, "idb", [128, 128], BF16)
    fw.op("dve", lambda e: e.memset(ones_b[:], 1.0), writes=[ones_b])
    fw.op("dve", lambda e: e.memset(ones_f[:], 1.0), writes=[ones_f])
    fw.dma("sp", lnw, lnw[:], lnw_d, lnw_d[:])
    fw.dma("sp", cm, cm[:], cm_d, cm_d[:])
    fw.op("dve", lambda e: e.tensor_copy(idb[:], cm[:, 3, :]), reads=[cm], writes=[idb])
    trib = fw.sbuf(cst, "trib", [128, 128], BF16)
    fw.op("dve", lambda e: e.tensor_copy(trib[:], cm[:, 0, :]), reads=[cm], writes=[trib])

    def p0_norm(src, dst, ntok_total, which):
        T0 = min(256, QT)
        with ExitStack() as st:
            xt = [fw.sbuf(st, f"p0x{i}", [128, KC, T0], F32, dma=True) for i in range(2)]
            sq = fw.sbuf(st, "p0sq", [128, KC, T0], BF16)
            xo = [fw.sbuf(st, f"p0o{i}", [128, KC, T0], BF16, dma=True) for i in range(2)]
            rs = [fw.sbuf(st, f"p0r{i}", [128, T0], F32) for i in range(2)]
            ps = [fw.psum(st, f"p0ps{i}", [128, 512]) for i in range(2)]
            tiles = []
            t0 = 0
            while t0 < ntok_total:
                n = min(T0, ntok_total - t0)
                tiles.append((t0, n))
                t0 += n
            for it, (t0, n) in enumerate(tiles):
                x_, o_, r_, p_ = xt[it % 2], xo[it % 2], rs[it % 2], ps[it % 2]
                fw.dma(nextq(), x_, x_[:, :, 0:n], src, src[:, :, t0:t0 + n])
                fw.op("act", lambda e, x_=x_, n=n: e.activation(sq[:, :, 0:n], x_[:, :, 0:n], AF.Square),
                      reads=[x_], writes=[sq])

                def mm(e, p_=p_, n=n):
                    r = None
                    for kc in range(KC):
                        r = e.matmul(p_[:, 0:n], ones_b[:], sq[:, kc, 0:n], start=(kc == 0), stop=(kc == KC - 1))
                    return r
                fw.op("pe", mm, reads=[sq, ones_b], writes=[p_])
                fw.op("dve", lambda e, r_=r_, p_=p_, n=n: e.tensor_scalar(r_[:, 0:n], p_[:, 0:n], 1.0 / D, EPS, ALU.mult, ALU.add),
                      reads=[p_], writes=[r_])
                fw.op("act", lambda e, r_=r_, n=n: e.activation(r_[:, 0:n], r_[:, 0:n], AF.Sqrt), reads=[r_], writes=[r_])
                fw.op("dve", lambda e, r_=r_, n=n: e.reciprocal(r_[:, 0:n], r_[:, 0:n]), reads=[r_], writes=[r_])
                ek = "dve"

                def sc(e, x_=x_, o_=o_, r_=r_, n=n):
                    r = None
                    for kc in range(KC):
                        r = e.scalar_tensor_tensor(o_[:, kc, 0:n], x_[:, kc, 0:n], lnw[:, which, kc:kc + 1],
                                                   r_[:, 0:n], ALU.mult, ALU.mult)
                    return r
                fw.op(ek, sc, reads=[x_, r_, lnw], writes=[o_])
                fw.dma(nextq(), dst, dst[:, :, t0:t0 + n], o_, o_[:, :, 0:n])
            fw.barrier()
            fw.flush()

    p0_norm(xTf, xn_f, S, 0)
    p0_norm(xTo, xn_o, HALO + QT, 0)

    if phases < 1:
        return finish(nc, fw, top, cst, dbg, dbg_o, locals())

    w_in_v = w_in.t.rearrange("(kc p) n -> p kc n", p=128)

    def proj(xn, tok_tiles, jobs):
        GW = 1024
        with ExitStack() as st:
            wt = fw.sbuf(st, "p1w", [128, KC, GW], BF16, dma="sw")
            xt = [fw.sbuf(st, f"p1x{i}", [128, KC, TT], BF16, dma=True) for i in range(2)]
            evf = [fw.sbuf(st, f"p1ef{i}", [128, 512], F32, dma=True) for i in range(4)]
            evb = [fw.sbuf(st, f"p1eb{i}", [128, 512], BF16, dma=True) for i in range(4)]
            ps = [fw.psum(st, f"p1ps{i}", [128, 512]) for i in range(4)]
            k = {"x": 0, "p": 0}
            for (col_lo, ncols, mode, dst, d0) in jobs:
                for g0 in range(0, ncols, GW):
                    gw = min(GW, ncols - g0)
                    fw.dma("pool", wt, wt[:, :, 0:gw], w_in, w_in_v[:, :, col_lo + g0: col_lo + g0 + gw])
                    for (t0, n) in tok_tiles:
                        x_ = xt[k["x"] % 2]
                        k["x"] += 1
                        fw.dma(nextq(), x_, x_[:, :, 0:n], xn, xn[:, :, t0:t0 + n])
                        if mode == "cm":
                            for b0 in range(0, gw, 128):
                                p_ = ps[k["p"] % 4]
                                e_ = evf[k["p"] % 4]
                                k["p"] += 1

                                def mm(e, p_=p_, x_=x_, b0=b0, n=n):
                                    r = None
                                    for kc in range(KC):
                                        r = e.matmul(p_[:, 0:n], wt[:, kc, b0:b0 + 128], x_[:, kc, 0:n],
                                                     start=(kc == 0), stop=(kc == KC - 1))
                                    return r
                                fw.op("pe", mm, reads=[wt, x_], writes=[p_])
                                copy_op(evac_eng(), e_, e_[:, 0:n], p_, p_[:, 0:n])
                                blk = d0 + (g0 + b0) // 128
                                fw.dma(nextq(), dst, dst[blk, :, t0:t0 + n], e_, e_[:, 0:n])
                        else:
                            for s0 in range(0, n, 128):
                                m = min(128, n - s0)
                                for c0 in range(0, gw, 512):
                                    cw = min(512, gw - c0)
                                    p_ = ps[k["p"] % 4]
                                    e_ = (evb if mode == "tmb" else evf)[k["p"] % 4]
                                    k["p"] += 1

                                    def mm(e, p_=p_, x_=x_, s0=s0, m=m, c0=c0, cw=cw):
                                        r = None
                                        for kc in range(KC):
                                            r = e.matmul(p_[0:m, 0:cw], x_[:, kc, s0:s0 + m], wt[:, kc, c0:c0 + cw],
                                                         start=(kc == 0), stop=(kc == KC - 1))
                                        return r
                                    fw.op("pe", mm, reads=[wt, x_], writes=[p_])
                                    copy_op(evac_eng(), e_, e_[0:m, 0:cw], p_, p_[0:m, 0:cw])
                                    fw.dma(nextq(), dst, dst[t0 + s0:t0 + s0 + m, d0 + g0 + c0:d0 + g0 + c0 + cw],
                                           e_, e_[0:m, 0:cw])
            fw.barrier()
            fw.flush()

    full_tiles = [(i * TT, TT) for i in range(S // TT)]
    own_tiles = [(0, HALO)] + [(HALO + i * TT, TT) for i in range(QT // TT)]
    pre_tiles = [t for t in full_tiles if t[0] < S - QT]
    proj(xn_f, pre_tiles, [
        (OFF_DK, 2048, "cm", raw_f, 0),
        (OFF_DV, 2048, "cm", raw_f, 16),
        (OFF_DB, 32, "tmf", ba_f, 0),
    ])
    proj(xn_f, full_tiles, [
        (OFF_FK, 2048, "cm", raw_f, 32),
        (OFF_FV, 2048, "tmb", vtok_f, 0),
    ])
    proj(xn_o, own_tiles, [
        (OFF_DQ, 2048, "cm", raw_o, 0),
        (OFF_DK, 2048, "cm", raw_o, 16),
        (OFF_DV, 2048, "cm", raw_o, 32),
        (OFF_DZ, 2048, "cm", raw_o, 48),
        (OFF_FQ, 2048, "cm", raw_o, 64),
        (OFF_DB, 32, "tmf", ba_o, 0),
    ])

    if phases < 2:
        return finish(nc, fw, top, cst, dbg, dbg_o, locals())

    cst2 = ExitStack()
    convw = fw.sbuf(cst2, "convw_s", [128, 48, 4], F32, dma=True)
    fw.dma("sp", convw, convw[:], convw_d, convw_d[:])
    invf = fw.sbuf(cst2, "invf_s", [128, 1], F32, dma=True)
    fw.dma("sp", invf, invf[:], invf_d, invf_d[:])
    prot = fw.sbuf(cst2, "prot", [128, 128], BF16)
    fw.op("dve", lambda e: e.tensor_copy(prot[:], cm[:, 4, :]), reads=[cm], writes=[prot])

    def p2_conv(src, src_blk0, cw_blk0, nblk, tiles, col0, dst, kind):
        NS = 4
        with ExitStack() as st:
            xin = [fw.sbuf(st, f"p2i{i}", [128, 3 + TT], F32, dma=True) for i in range(NS)]
            y = [fw.sbuf(st, f"p2y{i}", [128, TT], F32) for i in range(NS)]
            sl = [fw.sbuf(st, f"p2s{i}", [128, TT], F32) for i in range(NS)]
            sq = [fw.sbuf(st, f"p2q{i}", [128, TT], BF16) for i in range(NS)]
            rr = [fw.sbuf(st, f"p2r{i}", [128, TT], F32) for i in range(NS)]
            ob = [fw.sbuf(st, f"p2o{i}", [128, TT], BF16 if kind == "z" else F32, dma=True) for i in range(NS)]
            ps = [fw.psum(st, f"p2ps{i}", [128, 512]) for i in range(NS)]

            def tile_gen(b, t0, n, slot):
                i_, y_, s_, q_, r_, o_, p_ = (a[slot] for a in (xin, y, sl, sq, rr, ob, ps))
                lo = col0 + t0 - 3
                if lo < 0:
                    fw.op("dve", lambda e: e.memset(i_[:, 0:3], 0.0), writes=[i_])
                    yield
                    fw.dma(nextq(), i_, i_[:, 3:3 + n], src, src[src_blk0 + b, :, col0 + t0:col0 + t0 + n], group=True)
                else:
                    fw.dma(nextq(), i_, i_[:, 0:3 + n], src, src[src_blk0 + b, :, lo:lo + 3 + n])
                yield
                if kind == "z":
                    fw.op("act", lambda e: e.activation(o_[:, 0:n], i_[:, 3:3 + n], AF.Silu), reads=[i_], writes=[o_])
                    yield
                else:
                    cb = cw_blk0 + b
                    fw.op("dve", lambda e: e.tensor_scalar(y_[:, 0:n], i_[:, 0:n], convw[:, cb, 0:1], None, ALU.mult),
                          reads=[i_, convw], writes=[y_])
                    yield
                    for j in range(1, 4):
                        fw.op("dve", lambda e, j=j: e.scalar_tensor_tensor(
                            y_[:, 0:n], i_[:, j:j + n], convw[:, cb, j:j + 1], y_[:, 0:n], ALU.mult, ALU.add),
                            reads=[i_, convw, y_], writes=[y_])
                        yield
                    if kind == "v":
                        fw.op("act", lambda e: e.activation(o_[:, 0:n], y_[:, 0:n], AF.Silu), reads=[y_], writes=[o_])
                        yield
                    else:
                        fw.op("act", lambda e: e.activation(s_[:, 0:n], y_[:, 0:n], AF.Silu), reads=[y_], writes=[s_])
                        yield
                        fw.op("act", lambda e: e.activation(q_[:, 0:n], s_[:, 0:n], AF.Square), reads=[s_], writes=[q_])
                        yield
                        fw.op("pe", lambda e: e.matmul(p_[:, 0:n], ones_b[:], q_[:, 0:n], start=True, stop=True),
                              reads=[q_, ones_b], writes=[p_])
                        yield
                        m_ = 128.0 if kind == "q" else 1.0
                        fw.op("act", lambda e: e.activation(r_[:, 0:n], p_[:, 0:n], AF.Sqrt, bias=m_ * EPS, scale=m_),
                              reads=[p_], writes=[r_])
                        yield
                        fw.op("dve", lambda e: e.reciprocal(r_[:, 0:n], r_[:, 0:n]), reads=[r_], writes=[r_])
                        yield
                        fw.op("dve", lambda e: e.tensor_tensor(o_[:, 0:n], s_[:, 0:n], r_[:, 0:n], ALU.mult),
                              reads=[s_, r_], writes=[o_])
                        yield
                fw.dma(nextq(), dst, dst[b, :, t0:t0 + n], o_, o_[:, 0:n])
                yield

            work = [(b, t0, n) for b in range(nblk) for (t0, n) in tiles]
            for w0 in range(0, len(work), NS):
                interleave(tile_gen(b, t0, n, j) for j, (b, t0, n) in enumerate(work[w0:w0 + NS]))
            fw.barrier()
            fw.flush()

    full_t = [(i * TT, TT) for i in range(S // TT)]
    own_t = [(i * TT, TT) for i in range(QT // TT)]
    pre_t = [t for t in full_t if t[0] < S - QT]
    p2_conv(raw_f, 0, 16, 16, pre_t, 0, kdn_f, "k")
    p2_conv(raw_f, 16, 32, 16, pre_t, 0, vdn_f, "v")
    p2_conv(raw_o, 0, 0, 16, own_t, HALO, qdn_o, "q")
    p2_conv(raw_o, 16, 16, 16, own_t, HALO, kdn_o, "k")
    p2_conv(raw_o, 32, 32, 16, own_t, HALO, vdn_o, "v")
    p2_conv(raw_o, 48, 0, 16, own_t, HALO, zdn_o, "z")

    def p2_rot(src, src_blk0, tiles, col0, pos_d, dst):
        with ExitStack() as st:
            pt = fw.sbuf(st, "p2pos", [128, TT], F32, dma=True)
            ca = fw.sbuf(st, "p2ca", [128, TT], F32)
            ti = fw.sbuf(st, "p2ti", [128, TT], mybir.dt.int32)
            sa = fw.sbuf(st, "p2sa", [128, TT], F32)
            cs = fw.sbuf(st, "p2cs", [128, TT], F32)
            sn = fw.sbuf(st, "p2sn", [128, TT], F32)
            xin = [fw.sbuf(st, f"p2x{i}", [128, TT], F32, dma=True) for i in range(4)]
            xb = [fw.sbuf(st, f"p2xb{i}", [128, TT], BF16) for i in range(4)]
            t1 = [fw.sbuf(st, f"p2t{i}", [128, TT], F32) for i in range(4)]
            t2 = [fw.sbuf(st, f"p2u{i}", [128, TT], F32) for i in range(4)]
            ob = [fw.sbuf(st, f"p2ro{i}", [128, TT], BF16, dma=True) for i in range(4)]
            ps = [fw.psum(st, f"p2rp{i}", [128, 512]) for i in range(4)]
            it = 0
            for (t0, n) in tiles:
                fw.dma("sp", pt, pt[:, 0:n], pos_d, pos_d[:, col0 + t0:col0 + t0 + n])
                def trig(dst, shift, n=n):
                    fw.op("dve", lambda e: e.tensor_scalar(sa[:, 0:n], pt[:, 0:n], invf[:, 0:1], shift, ALU.mult, ALU.add),
                          reads=[pt, invf], writes=[sa])
                    fw.op("dve", lambda e: e.tensor_scalar(ca[:, 0:n], sa[:, 0:n], 1.0 / (2 * PI), None, ALU.mult), reads=[sa], writes=[ca])
                    fw.op("dve", lambda e: e.tensor_copy(ti[:, 0:n], ca[:, 0:n]), reads=[ca], writes=[ti])
                    fw.op("dve", lambda e: e.tensor_copy(ca[:, 0:n], ti[:, 0:n]), reads=[ti], writes=[ca])
                    fw.op("dve", lambda e: e.scalar_tensor_tensor(sa[:, 0:n], ca[:, 0:n], -2 * PI, sa[:, 0:n], ALU.mult, ALU.add),
                          reads=[ca, sa], writes=[sa])
                    fw.op("dve", lambda e: e.tensor_scalar(ca[:, 0:n], sa[:, 0:n], PI, -2 * PI, ALU.is_gt, ALU.mult), reads=[sa], writes=[ca])
                    fw.op("dve", lambda e: e.tensor_tensor(sa[:, 0:n], sa[:, 0:n], ca[:, 0:n], ALU.add), reads=[sa, ca], writes=[sa])
                    fw.op("dve", lambda e: e.tensor_scalar(ca[:, 0:n], sa[:, 0:n], -PI, 2 * PI, ALU.is_lt, ALU.mult), reads=[sa], writes=[ca])
                    fw.op("dve", lambda e: e.tensor_tensor(sa[:, 0:n], sa[:, 0:n], ca[:, 0:n], ALU.add), reads=[sa, ca], writes=[sa])
                    fw.op("act", lambda e: e.activation(dst[:, 0:n], sa[:, 0:n], AF.Sin), reads=[sa], writes=[dst])
                trig(sn, 0.0)
                trig(cs, 0.5 * PI)
                def blk_gen(b, slot, t0=t0, n=n):
                    i_, b_, a_, u_, o_, p_ = (a[slot] for a in (xin, xb, t1, t2, ob, ps))
                    fw.dma(nextq(), i_, i_[:, 0:n], src, src[src_blk0 + b, :, col0 + t0:col0 + t0 + n])
                    yield
                    fw.op("act", lambda e: e.activation(b_[:, 0:n], i_[:, 0:n], AF.Copy), reads=[i_], writes=[b_])
                    yield
                    fw.op("pe", lambda e: e.matmul(p_[:, 0:n], prot[:], b_[:, 0:n], start=True, stop=True),
                          reads=[b_, prot], writes=[p_])
                    yield
                    fw.op("dve", lambda e: e.tensor_tensor(a_[:, 0:n], i_[:, 0:n], cs[:, 0:n], ALU.mult),
                          reads=[i_, cs], writes=[a_])
                    yield
                    fw.op("dve", lambda e: e.tensor_tensor(u_[:, 0:n], p_[:, 0:n], sn[:, 0:n], ALU.mult),
                          reads=[p_, sn], writes=[u_])
                    yield
                    fw.op("dve", lambda e: e.tensor_tensor(o_[:, 0:n], a_[:, 0:n], u_[:, 0:n], ALU.add),
                          reads=[a_, u_], writes=[o_])
                    yield
                    fw.dma(nextq(), dst, dst[b, :, t0:t0 + n], o_, o_[:, 0:n])
                    yield
                for b0 in range(0, 16, 4):
                    interleave(blk_gen(b0 + j, j) for j in range(4))
            fw.barrier()
            fw.flush()

    p2_rot(raw_f, 32, full_t, 0, posf_d, kdf_f)
    p2_rot(raw_o, 64, own_t, HALO, qpos_d, qdf_o)

    if phases < 2.5:
        return finish(nc, fw, top, cst, dbg, dbg_o, locals())

    class RR:
        def __init__(self, objs):
            self.objs = objs
            self.i = 0

        def get(self):
            o = self.objs[self.i % len(self.objs)]
            self.i += 1
            return o

    hv = fw.sbuf(cst2, "hv_s", [128, 32], F32, dma=True)
    nw = fw.sbuf(cst2, "nw_s", [128, 3], F32, dma=True)
    flg = fw.sbuf(cst2, "flg_s", [128, NCH], F32, dma=True)
    negea = fw.sbuf(cst2, "negea", [128, 16], F32)
    Sf = fw.sbuf(cst2, "Sf", [128, 16, 128], F32)
    fw.dma("sp", hv, hv[:], hv_d, hv_d[:])
    fw.dma("sp", nw, nw[:], nw_d, nw_d[:])
    fw.dma("sp", flg, flg[:], flags_d, flags_d[:])
    fw.op("act", lambda e: e.activation(negea[:], hv[:, 0:16], AF.Exp), reads=[hv], writes=[negea])
    fw.op("dve", lambda e: e.tensor_scalar(negea[:], negea[:], -1.0, None, ALU.mult), reads=[negea], writes=[negea])
    fw.op("dve", lambda e: e.memset(Sf[:], 0.0), writes=[Sf])
    Sfo = [Obj(Sf.t[:, h, :], f"Sf{h}") for h in range(16)]
    for h in range(16):
        Sfo[h].writers = dict(Sf.writers)

    def p3_dn(kd, vd, qd, zd, ba, ba_row0, nch, masked, with_out):
        with ExitStack() as st:
            pbank = [fw.psum(st, f"dnpf{i}", [128, 512]) for i in range(8)]
            NSLOT = 4 if with_out else 8
            NB = 8 // NSLOT
            PFs = [RR([b.view(b.t[:, 0:128]) for b in pbank[NB * j:NB * j + NB]]) for j in range(NSLOT)]
            PF = RR([b.view(b.t[:, 0:128]) for b in pbank[0:2]])
            TFs = [RR([fw.sbuf(st, f"dntf{j}_{i}", [128, 128], F32) for i in range(48 if with_out else 24)]) for j in range(NSLOT)]
            OB = RR([fw.sbuf(st, f"dnob{i}", [128, 128], BF16, dma=True) for i in range(8)])
            bat = [fw.sbuf(st, f"dnba{i}", [128, 32], F32, dma=True) for i in range(2)]
            kt = [fw.sbuf(st, f"dnk{i}", [128, 16, 128], F32, dma=True) for i in range(2)]
            vt = [fw.sbuf(st, f"dnv{i}", [128, 16, 128], F32, dma=True) for i in range(2)]
            qt = [fw.sbuf(st, f"dnq{i}", [128, 16, 128], F32, dma=True) for i in range(2)]
            zt = [fw.sbuf(st, f"dnz{i}", [128, 16, 128], BF16, dma=True) for i in range(2)]
            SC = RR([fw.sbuf(st, f"dnsc{i}", [128, 16], F32) for i in range(32)])
            tri = cm.t[:, 0, :]
            m_il = cm.t[:, 1, :]
            m_sl = cm.t[:, 2, :]
            idf = cm.t[:, 3, :]

            def ew(ek, fn, reads, writes):
                fw.op(ek, fn, reads=reads, writes=writes)

            for n in range(nch):
                c0 = n * 128
                ba_, k_, v_, q_, z_ = bat[n % 2], kt[n % 2], vt[n % 2], qt[n % 2], zt[n % 2]
                fw.dma("sp", ba_, ba_[:], ba, ba[ba_row0 + c0:ba_row0 + c0 + 128, :])
                fw.dma("sp", k_, k_[:], kd, kd[:, :, c0:c0 + 128].rearrange("h p t -> p h t"))
                fw.dma("act", v_, v_[:], vd, vd[:, :, c0:c0 + 128].rearrange("h p t -> p h t"))
                if with_out:
                    fw.dma("sp", q_, q_[:], qd, qd[:, :, c0:c0 + 128].rearrange("h p t -> p h t"))
                    fw.dma("act", z_, z_[:], zd, zd[:, :, c0:c0 + 128].rearrange("h p t -> p h t"))
                beta, negb, xg, ax, ex, g = (SC.get() for _ in range(6))
                ew("act", lambda e, beta=beta, ba_=ba_: e.activation(beta[:], ba_[:, 0:16], AF.Sigmoid), [ba_], [beta])
                ew("dve", lambda e, negb=negb, beta=beta: e.tensor_scalar(negb[:], beta[:], -1.0, None, ALU.mult), [beta], [negb])
                ew("dve", lambda e, xg=xg, ba_=ba_: e.tensor_tensor(xg[:], ba_[:, 16:32], hv[:, 16:32], ALU.add), [ba_, hv], [xg])
                ew("act", lambda e, ax=ax, xg=xg: e.activation(ax[:], xg[:], AF.Abs), [xg], [ax])
                ew("act", lambda e, ex=ex, ax=ax: e.activation(ex[:], ax[:], AF.Exp, scale=-1.0), [ax], [ex])
                ew("act", lambda e, ex=ex: e.activation(ex[:], ex[:], AF.Ln, bias=1.0), [ex], [ex])
                ew("dve", lambda e, xg=xg: e.tensor_scalar(xg[:], xg[:], 0.0, None, ALU.max), [xg], [xg])
                ew("dve", lambda e, xg=xg, ex=ex: e.tensor_tensor(xg[:], xg[:], ex[:], ALU.add), [xg, ex], [xg])
                ew("dve", lambda e, g=g, xg=xg: e.tensor_tensor(g[:], xg[:], negea[:], ALU.mult), [xg, negea], [g])
                pg = PF.get()
                pl = PF.get()
                ew("pe", lambda e, pg=pg, g=g: e.matmul(pg[:, 0:16], tri, g[:], start=True, stop=True), [g, cm], [pg])
                ew("pe", lambda e, pl=pl, g=g: e.matmul(pl[:, 0:16], ones_f[:], g[:], start=True, stop=True), [g, ones_f], [pl])
                gcol, egc, begc, ekt, adec = (SC.get() for _ in range(5))
                ew("act", lambda e, gcol=gcol, pg=pg: e.activation(gcol[:], pg[:, 0:16], AF.Copy), [pg], [gcol])
                ew("act", lambda e, egc=egc, pg=pg: e.activation(egc[:], pg[:, 0:16], AF.Exp), [pg], [egc])
                ew("dve", lambda e, begc=begc, egc=egc, beta=beta: e.tensor_tensor(begc[:], egc[:], beta[:], ALU.mult), [egc, beta], [begc])
                ew("dve", lambda e, ekt=ekt, pl=pl, gcol=gcol: e.tensor_tensor(ekt[:], pl[:, 0:16], gcol[:], ALU.subtract), [pl, gcol], [ekt])
                ew("act", lambda e, ekt=ekt: e.activation(ekt[:], ekt[:], AF.Exp), [ekt], [ekt])
                ew("act", lambda e, adec=adec, pl=pl: e.activation(adec[:], pl[:, 0:16], AF.Exp), [pl], [adec])
                if masked:
                    ew("dve", lambda e, ekt=ekt, n=n: e.tensor_scalar(ekt[:], ekt[:], flg[:, n:n + 1], None, ALU.mult), [ekt, flg], [ekt])
                    ew("dve", lambda e, adec=adec, n=n: e.tensor_scalar(adec[:], adec[:], -1.0, flg[:, n:n + 1], ALU.add, ALU.mult), [adec, flg], [adec])
                    ew("dve", lambda e, adec=adec: e.tensor_scalar(adec[:], adec[:], 1.0, None, ALU.add), [adec], [adec])
                def head_gen(h, slot, k_=k_, v_=v_, q_=q_, z_=z_, g=g, gcol=gcol, negb=negb, begc=begc, ekt=ekt, beta=beta, adec=adec, c0=c0):
                    TF = TFs[slot]
                    PF = PFs[slot]
                    kT = k_.t[:, h, :]
                    vT = v_.t[:, h, :]
                    gmat = TF.get()
                    ew("dve", lambda e, gmat=gmat, g=g, h=h: e.tensor_scalar(gmat[:], ones_f[:], g[:, h:h + 1], None, ALU.mult), [g, ones_f], [gmat])
                    yield
                    pgr = PF.get()
                    ew("pe", lambda e, pgr=pgr, gmat=gmat: e.matmul(pgr[:], gmat[:], tri, start=True, stop=True), [gmat, cm], [pgr])
                    yield
                    if with_out:
                        eg = TF.get()
                        ew("act", lambda e, eg=eg, pgr=pgr: e.activation(eg[:], pgr[:], AF.Exp), [pgr], [eg])
                        yield
                    dm = TF.get()
                    ew("dve", lambda e, dm=dm, pgr=pgr, gcol=gcol, h=h: e.tensor_scalar(dm[:], pgr[:], gcol[:, h:h + 1], 0.0, ALU.subtract, ALU.max), [pgr, gcol], [dm])
                    yield
                    ew("act", lambda e, dm=dm: e.activation(dm[:], dm[:], AF.Exp, scale=-1.0), [dm], [dm])
                    yield
                    ls = TF.get()
                    ew("dve", lambda e, ls=ls, dm=dm: e.tensor_tensor(ls[:], dm[:], m_sl, ALU.mult), [dm, cm], [ls])
                    yield
                    pkk = PF.get()
                    ew("pe", lambda e, pkk=pkk, kT=kT: e.matmul(pkk[:], kT, kT, start=True, stop=True), [k_], [pkk])
                    yield
                    N = TF.get()
                    ew("dve", lambda e, N=N, pkk=pkk, negb=negb, ls=ls, h=h: e.scalar_tensor_tensor(N[:], pkk[:], negb[:, h:h + 1], ls[:], ALU.mult, ALU.mult), [pkk, negb, ls], [N])
                    yield
                    pbt = PF.get()
                    ew("pe", lambda e, pbt=pbt, N=N: e.transpose(pbt[:], N[:], idf), [N, cm], [pbt])
                    yield
                    B = TF.get()
                    ew("act", lambda e, B=B, pbt=pbt: e.activation(B[:], pbt[:], AF.Copy), [pbt], [B])
                    yield
                    Pf_ = TF.get()
                    ew("dve", lambda e, Pf_=Pf_, B=B: e.tensor_tensor(Pf_[:], B[:], idf, ALU.add), [B, cm], [Pf_])
                    yield
                    for lev in range(6):
                        B2, N2 = TF.get(), TF.get()
                        if lev < 5:
                            p1 = PF.get()
                            ew("pe", lambda e, p1=p1, N=N, B=B: e.matmul(p1[:], N[:], B[:], start=True, stop=True), [N, B], [p1])
                            yield
                            ew("act", lambda e, B2=B2, p1=p1: e.activation(B2[:], p1[:], AF.Copy), [p1], [B2])
                            yield
                        p2 = PF.get()
                        ew("pe", lambda e, p2=p2, N=N, B=B: e.matmul(p2[:], B[:], N[:], start=True, stop=True), [N, B], [p2])
                        yield
                        ew("act", lambda e, N2=N2, p2=p2: e.activation(N2[:], p2[:], AF.Copy), [p2], [N2])
                        yield
                        B, N = B2, N2
                        p3 = PF.get()
                        ew("pe", lambda e, p3=p3, N=N, Pf_=Pf_: e.matmul(p3[:], N[:], Pf_[:], start=True, stop=True), [N, Pf_], [p3])
                        yield
                        Pn = TF.get()
                        ew("dve", lambda e, Pn=Pn, Pf_=Pf_, p3=p3: e.tensor_tensor(Pn[:], Pf_[:], p3[:], ALU.add), [Pf_, p3], [Pn])
                        yield
                        Pf_ = Pn
                    Xk, ktil, Xv = TF.get(), TF.get(), TF.get()
                    pkt = PF.get()
                    ew("pe", lambda e, pkt=pkt, kT=kT: e.transpose(pkt[:], kT, idf), [k_, cm], [pkt])
                    yield
                    ew("dve", lambda e, Xk=Xk, pkt=pkt, begc=begc, h=h: e.tensor_scalar(Xk[:], pkt[:], begc[:, h:h + 1], None, ALU.mult), [pkt, begc], [Xk])
                    yield
                    ew("dve", lambda e, ktil=ktil, pkt=pkt, ekt=ekt, h=h: e.tensor_scalar(ktil[:], pkt[:], ekt[:, h:h + 1], None, ALU.mult), [pkt, ekt], [ktil])
                    yield
                    pvt = PF.get()
                    ew("pe", lambda e, pvt=pvt, vT=vT: e.transpose(pvt[:], vT, idf), [v_, cm], [pvt])
                    yield
                    ew("act", lambda e, Xv=Xv, pvt=pvt, beta=beta, h=h: e.activation(Xv[:], pvt[:], AF.Copy, scale=beta[:, h:h + 1]), [pvt, beta], [Xv])
                    yield
                    Uf, WT = TF.get(), TF.get()
                    pu = PF.get()
                    ew("pe", lambda e, pu=pu, Pf_=Pf_, Xv=Xv: e.matmul(pu[:], Pf_[:], Xv[:], start=True, stop=True), [Pf_, Xv], [pu])
                    yield
                    ew("act", lambda e, Uf=Uf, pu=pu: e.activation(Uf[:], pu[:], AF.Copy), [pu], [Uf])
                    yield
                    pw = PF.get()
                    ew("pe", lambda e, pw=pw, Pf_=Pf_, Xk=Xk: e.matmul(pw[:], Xk[:], Pf_[:], start=True, stop=True), [Pf_, Xk], [pw])
                    yield
                    ew("dve", lambda e, WT=WT, pw=pw: e.tensor_copy(WT[:], pw[:]), [pw], [WT])
                    yield
                    pr = PF.get()
                    ew("pe", lambda e, pr=pr, WT=WT, h=h: e.matmul(pr[:], WT[:], Sfo[h][:], start=True, stop=True), [WT, Sfo[h]], [pr])
                    yield
                    vnew = TF.get()
                    ew("dve", lambda e, vnew=vnew, Uf=Uf, pr=pr: e.tensor_tensor(vnew[:], Uf[:], pr[:], ALU.subtract), [Uf, pr], [vnew])
                    yield
                    if with_out:
                        qT = q_.t[:, h, :]
                        qtil = TF.get()
                        ew("dve", lambda e, qtil=qtil, qT=qT, eg=eg: e.tensor_tensor(qtil[:], qT, eg[:], ALU.mult), [q_, eg], [qtil])
                        yield
                        lm = TF.get()
                        ew("dve", lambda e, lm=lm, dm=dm: e.tensor_tensor(lm[:], dm[:], m_il, ALU.mult), [dm, cm], [lm])
                        yield
                        pqk = PF.get()
                        ew("pe", lambda e, pqk=pqk, qT=qT, kT=kT: e.matmul(pqk[:], qT, kT, start=True, stop=True), [q_, k_], [pqk])
                        yield
                        attn = TF.get()
                        ew("dve", lambda e, attn=attn, pqk=pqk, lm=lm: e.tensor_tensor(attn[:], pqk[:], lm[:], ALU.mult), [pqk, lm], [attn])
                        yield
                        pat = PF.get()
                        ew("pe", lambda e, pat=pat, attn=attn: e.transpose(pat[:], attn[:], idf), [attn, cm], [pat])
                        yield
                        attnT = TF.get()
                        ew("act", lambda e, attnT=attnT, pat=pat: e.activation(attnT[:], pat[:], AF.Copy), [pat], [attnT])
                        yield
                        po = PF.get()

                        def omm(e, po=po, h=h, qtil=qtil, vnew=vnew, attnT=attnT):
                            e.matmul(po[:], Sfo[h][:], qtil[:], start=True, stop=False)
                            return e.matmul(po[:], vnew[:], attnT[:], start=False, stop=True)
                        ew("pe", omm, [Sfo[h], qtil, vnew, attnT], [po])
                        yield
                        of, osq = TF.get(), TF.get()
                        ew("act", lambda e, of=of, po=po: e.activation(of[:], po[:], AF.Copy), [po], [of])
                        yield
                        ew("act", lambda e, osq=osq, of=of: e.activation(osq[:], of[:], AF.Square), [of], [osq])
                        yield
                        pss = PF.get()
                        ew("pe", lambda e, pss=pss, osq=osq: e.matmul(pss[:], ones_f[:], osq[:], start=True, stop=True), [osq, ones_f], [pss])
                        yield
                        rr_ = TF.get()
                        ew("dve", lambda e, rr_=rr_, pss=pss: e.tensor_scalar(rr_[:], pss[:], 1.0 / 128, EPS, ALU.mult, ALU.add), [pss], [rr_])
                        yield
                        ew("act", lambda e, rr_=rr_: e.activation(rr_[:], rr_[:], AF.Sqrt), [rr_], [rr_])
                        yield
                        ew("dve", lambda e, rr_=rr_: e.reciprocal(rr_[:], rr_[:]), [rr_], [rr_])
                        yield
                        of2 = TF.get()
                        ew("dve", lambda e, of2=of2, of=of, rr_=rr_: e.tensor_tensor(of2[:], of[:], rr_[:], ALU.mult), [of, rr_], [of2])
                        yield
                        ob_ = OB.get()
                        ew("dve", lambda e, ob_=ob_, of2=of2, z_=z_, h=h: e.scalar_tensor_tensor(ob_[:], of2[:], nw[:, 0:1], z_[:, h, :], ALU.mult, ALU.mult), [of2, nw, z_], [ob_])
                        yield
                        fw.dma(nextq(), mixT, mixT[:, h, c0:c0 + 128], ob_, ob_[:])
                        yield
                    pds = PF.get()
                    ew("pe", lambda e, pds=pds, ktil=ktil, vnew=vnew: e.matmul(pds[:], ktil[:], vnew[:], start=True, stop=True), [ktil, vnew], [pds])
                    yield
                    ew("dve", lambda e, h=h, adec=adec, pds=pds: e.scalar_tensor_tensor(Sfo[h][:], Sfo[h][:], adec[:, h:h + 1], pds[:], ALU.mult, ALU.add), [Sfo[h], adec, pds], [Sfo[h]])
                    yield

                for h0 in range(0, 16, NSLOT):
                    interleave(head_gen(h0 + j, j) for j in range(NSLOT))
            fw.barrier()
            fw.flush()

    p3_dn(kdn_f, vdn_f, None, None, ba_f, 0, NCH - NCO, True, False)
    if phases == 2.5:
        return finish(nc, fw, top, cst, dbg, dbg_o, locals())
    p3_dn(kdn_o, vdn_o, qdn_o, zdn_o, ba_o, HALO, NCO, False, True)

    if phases < 4:
        return finish(nc, fw, top, cst, dbg, dbg_o, locals())

    def rsqrt_ops(o, ap):
        fw.op("act", lambda e: e.activation(ap, ap, AF.Sqrt), reads=[o], writes=[o])
        fw.op("dve", lambda e: e.reciprocal(ap, ap), reads=[o], writes=[o])

    lamv = fw.sbuf(cst2, "lamv_s", [128, 4], F32, dma=True)
    lam2 = fw.sbuf(cst2, "lam2", [128, 2], F32)
    neglam = fw.sbuf(cst2, "neglam", [128, 1], F32)
    nwdf = fw.sbuf(cst2, "nwdf", [128, 2], F32)
    qpos = fw.sbuf(cst2, "qpos_s", [128, HALO + QT], F32, dma=True)
    kpos = fw.sbuf(cst2, "kpos_s", [128, NCH], F32, dma=True)
    fw.dma("sp", lamv, lamv[:], lamv_d, lamv_d[:])
    fw.dma("sp", qpos, qpos[:], qpos_d, qpos_d[:])
    fw.dma("sp", kpos, kpos[:], kpos_d, kpos_d[:])
    fw.op("dve", lambda e: e.tensor_tensor(lam2[:, 0:1], lamv[:, 0:1], lamv[:, 1:2], ALU.mult), reads=[lamv], writes=[lam2])
    fw.op("dve", lambda e: e.tensor_tensor(lam2[:, 1:2], lamv[:, 2:3], lamv[:, 3:4], ALU.mult), reads=[lamv, lam2], writes=[lam2])
    fw.op("dve", lambda e: e.tensor_scalar(nwdf[:], nw[:, 1:3], 1.0 - LAMBDA_INIT, None, ALU.mult), reads=[nw], writes=[nwdf])

    def p4_attn():
        SCALE = 128.0 ** -0.5
        with ExitStack() as st:
            pst = [fw.psum(st, f"ap{i}", [128, 512]) for i in range(2)]
            pacc = [[fw.psum(st, f"aa{i}_{j}", [128, 512]) for j in range(3)] for i in range(2)]
            vt = fw.sbuf(st, "avt", [128, NCH, 256], BF16, dma=True)
            kt = [fw.sbuf(st, f"akt{i}", [128, S], BF16, dma=True) for i in range(2)]
            qt = [fw.sbuf(st, f"aqt{i}", [128, QT], BF16, dma=True) for i in range(2)]
            O1 = fw.sbuf(st, "aO1", [128, 2, QT], F32)
            pe_t = [fw.sbuf(st, f"ape{i}", [128, TT], F32) for i in range(3)]
            pm_t = [fw.sbuf(st, f"apm{i}", [128, TT], BF16) for i in range(3)]
            rs_t = fw.sbuf(st, "ars", [128, TT], F32)
            cb_t = [fw.sbuf(st, f"acb{i}", [128, TT], F32) for i in range(2)]
            sq_t = [fw.sbuf(st, f"asq{i}", [128, TT], BF16) for i in range(2)]
            ob_t = [fw.sbuf(st, f"aob{i}", [128, TT], BF16, dma=True) for i in range(4)]
            fw.op("pe", lambda e: e.matmul(pst[0][:, 0:2], ones_f[:], lam2[:], start=True, stop=True), reads=[lam2, ones_f], writes=[pst[0]])
            fw.op("act", lambda e: e.activation(lam2[:], pst[0][:, 0:2], AF.Exp), reads=[pst[0]], writes=[lam2])
            fw.op("dve", lambda e: e.tensor_tensor(neglam[:], lam2[:, 1:2], lam2[:, 0:1], ALU.subtract), reads=[lam2], writes=[neglam])
            fw.op("dve", lambda e: e.tensor_scalar(neglam[:], neglam[:], -LAMBDA_INIT, None, ALU.add), reads=[neglam], writes=[neglam])
            it = 0
            ia = 0
            io = 0
            for h in range(NH_DF):
                fw.dma("sp", vt, vt[:], vtok_f, vtok_f[:, h * 256:(h + 1) * 256].rearrange("(kb p) c -> p kb c", p=128))
                for s_ in range(2):
                    sub = 2 * h + s_
                    k_, q_ = kt[sub % 2], qt[sub % 2]
                    fw.dma("act", k_, k_[:], kdf_f, kdf_f[sub])
                    fw.dma("sp", q_, q_[:], qdf_o, qdf_o[sub])
                    for q0 in range(0, QT, TT):
                        n = TT
                        acc = pacc[ia % 2]
                        ia += 1
                        NKB = min(NCH, NCH - NCO + (q0 + TT) // 128)
                        for kb in range(NKB):
                            p_ = pst[it % 2]
                            e_ = pe_t[it % 3]
                            m_ = pm_t[it % 3]
                            it += 1
                            fw.op("pe", lambda e, p_=p_, k_=k_, q_=q_, kb=kb, q0=q0: e.matmul(p_[:, 0:n], k_[:, kb * 128:(kb + 1) * 128], q_[:, q0:q0 + n], start=True, stop=True),
                                  reads=[k_, q_], writes=[p_])
                            fw.op("act", lambda e, e_=e_, p_=p_: e.activation(e_[:, 0:n], p_[:, 0:n], AF.Exp, scale=SCALE), reads=[p_], writes=[e_])
                            fw.op("dve", lambda e, m_=m_, e_=e_, kb=kb, q0=q0: e.scalar_tensor_tensor(m_[:, 0:n], qpos[:, HALO + q0:HALO + q0 + n], kpos[:, kb:kb + 1], e_[:, 0:n], ALU.is_ge, ALU.mult),
                                  reads=[e_, qpos, kpos], writes=[m_])

                            def pv(e, acc=acc, m_=m_, kb=kb, NKB=NKB):
                                e.matmul(acc[0][:, 0:n], vt[:, kb, 0:128], m_[:, 0:n], start=(kb == 0), stop=(kb == NKB - 1))
                                e.matmul(acc[1][:, 0:n], vt[:, kb, 128:256], m_[:, 0:n], start=(kb == 0), stop=(kb == NKB - 1))
                                return e.matmul(acc[2][:, 0:n], ones_b[:], m_[:, 0:n], start=(kb == 0), stop=(kb == NKB - 1))
                            fw.op("pe", pv, reads=[vt, m_, ones_b], writes=[acc[0], acc[1], acc[2]])
                        fw.op("dve", lambda e, acc=acc: e.reciprocal(rs_t[:, 0:n], acc[2][:, 0:n]), reads=[acc[2]], writes=[rs_t])
                        for c in range(2):
                            if s_ == 0:
                                fw.op("dve", lambda e, acc=acc, c=c, q0=q0: e.tensor_tensor(O1[:, c, q0:q0 + n], acc[c][:, 0:n], rs_t[:, 0:n], ALU.mult),
                                      reads=[acc[c], rs_t], writes=[O1])
                            else:
                                cb_ = cb_t[c]
                                fw.op("dve", lambda e, acc=acc, c=c, cb_=cb_: e.tensor_tensor(cb_[:, 0:n], acc[c][:, 0:n], rs_t[:, 0:n], ALU.mult),
                                      reads=[acc[c], rs_t], writes=[cb_])
                                fw.op("dve", lambda e, c=c, cb_=cb_, q0=q0: e.scalar_tensor_tensor(cb_[:, 0:n], cb_[:, 0:n], neglam[:, 0:1], O1[:, c, q0:q0 + n], ALU.mult, ALU.add),
                                      reads=[cb_, neglam, O1], writes=[cb_])
                                fw.op("act", lambda e, c=c, cb_=cb_: e.activation(sq_t[c][:, 0:n], cb_[:, 0:n], AF.Square), reads=[cb_], writes=[sq_t[c]])
                        if s_ == 1:
                            p_ = pst[it % 2]
                            it += 1

                            def ssm(e, p_=p_):
                                e.matmul(p_[:, 0:n], ones_b[:], sq_t[0][:, 0:n], start=True, stop=False)
                                return e.matmul(p_[:, 0:n], ones_b[:], sq_t[1][:, 0:n], start=False, stop=True)
                            fw.op("pe", ssm, reads=[sq_t[0], sq_t[1], ones_b], writes=[p_])
                            fw.op("dve", lambda e, p_=p_: e.tensor_scalar(rs_t[:, 0:n], p_[:, 0:n], 1.0 / 256, EPS, ALU.mult, ALU.add), reads=[p_], writes=[rs_t])
                            rsqrt_ops(rs_t, rs_t[:, 0:n])
                            for c in range(2):
                                o_ = ob_t[io % 4]
                                io += 1
                                fw.op("dve", lambda e, o_=o_, c=c: e.scalar_tensor_tensor(o_[:, 0:n], cb_t[c][:, 0:n], nwdf[:, c:c + 1], rs_t[:, 0:n], ALU.mult, ALU.mult),
                                      reads=[cb_t[c], nwdf, rs_t], writes=[o_])
                                fw.dma(nextq(), mixT, mixT[:, 16 + 2 * h + c, q0:q0 + n], o_, o_[:, 0:n])
            fw.barrier()
            fw.flush()

    p4_attn()

    if phases < 5:
        return finish(nc, fw, top, cst, dbg, dbg_o, locals())

    cst2.close()
    rstd1 = fw.sbuf(cst, "rstd1", [128, QT], F32)
    w_out_v = w_out.t.rearrange("(kc p) n -> p kc n", p=128)
    w_gate_v = w_gate.t.rearrange("(kc p) n -> p kc n", p=128)
    w_up_v = w_up.t.rearrange("(kc p) n -> p kc n", p=128)
    w_down_v = w_down.t.rearrange("(hb p) n -> p hb n", p=128)

    def p5a():
        with ExitStack() as st:
            mx = fw.sbuf(st, "omx", [128, KC, TT], BF16, dma=True)
            wo = [fw.sbuf(st, f"owo{i}", [128, KC, 512], BF16, dma="sw") for i in range(2)]
            xr = [fw.sbuf(st, f"oxr{i}", [128, TT], F32, dma=True) for i in range(3)]
            sq = [fw.sbuf(st, f"osq{i}", [128, TT], BF16) for i in range(2)]
            ps = [fw.psum(st, f"ops{i}", [128, 512]) for i in range(3)]
            pss = fw.psum(st, "opss", [128, 512])
            it = 0
            ig = 0
            for t0 in range(0, QT, TT):
                n = TT
                fw.dma("sp", mx, mx[:], mixT, mixT[:, :, t0:t0 + n])
                for cg in range(8):
                    w_ = wo[ig % 2]
                    ig += 1
                    fw.dma("pool", w_, w_[:], w_out, w_out_v[:, :, cg * 512:(cg + 1) * 512])
                    for cb in range(4):
                        blk = cg * 4 + cb
                        p_, x_, s_ = ps[it % 3], xr[it % 3], sq[it % 2]
                        it += 1
                        fw.dma(nextq(), x_, x_[:, 0:n], xTo, xTo[:, blk, HALO + t0:HALO + t0 + n])

                        def mm(e, p_=p_, w_=w_, cb=cb):
                            r = None
                            for kc in range(KC):
                                r = e.matmul(p_[:, 0:n], w_[:, kc, cb * 128:(cb + 1) * 128], mx[:, kc, 0:n], start=(kc == 0), stop=(kc == KC - 1))
                            return r
                        fw.op("pe", mm, reads=[w_, mx], writes=[p_])
                        fw.op("dve", lambda e, x_=x_, p_=p_: e.tensor_tensor(x_[:, 0:n], x_[:, 0:n], p_[:, 0:n], ALU.add), reads=[x_, p_], writes=[x_])
                        fw.op("act", lambda e, s_=s_, x_=x_: e.activation(s_[:, 0:n], x_[:, 0:n], AF.Square), reads=[x_], writes=[s_])
                        fw.op("pe", lambda e, s_=s_, blk=blk: e.matmul(pss[:, 0:n], ones_b[:], s_[:, 0:n], start=(blk == 0), stop=(blk == KC - 1)),
                              reads=[s_, ones_b], writes=[pss])
                        fw.dma(nextq(), x1T, x1T[:, blk, t0:t0 + n], x_, x_[:, 0:n])
                fw.op("dve", lambda e, t0=t0: e.tensor_scalar(rstd1[:, t0:t0 + n], pss[:, 0:n], 1.0 / D, EPS, ALU.mult, ALU.add), reads=[pss], writes=[rstd1])
                rsqrt_ops(rstd1, rstd1[:, t0:t0 + n])
            fw.barrier()
            fw.flush()

    p5a()

    def p5b():
        HH = HB // 2
        with ExitStack() as st:
            hT = fw.sbuf(st, "fh", [128, KC, TT], BF16)
            act = fw.sbuf(st, "fact", [128, HB, TT], BF16)
            wg = [fw.sbuf(st, f"fwg{i}", [128, KC, 128], BF16, dma="sw") for i in range(2)]
            wu = [fw.sbuf(st, f"fwu{i}", [128, KC, 128], BF16, dma="sw") for i in range(2)]
            wd = [fw.sbuf(st, f"fwd{i}", [128, HH, 128], BF16, dma="sw") for i in range(2)]
            xr = [fw.sbuf(st, f"fxr{i}", [128, TT], F32, dma=True) for i in range(3)]
            sg = [fw.sbuf(st, f"fsg{i}", [128, TT], F32) for i in range(2)]
            sq = [fw.sbuf(st, f"fsq{i}", [128, TT], BF16) for i in range(2)]
            r2 = fw.sbuf(st, "fr2", [128, TT], F32)
            pg = [fw.psum(st, f"fpg{i}", [128, 512]) for i in range(2)]
            pu = [fw.psum(st, f"fpu{i}", [128, 512]) for i in range(2)]
            pd = [fw.psum(st, f"fpd{i}", [128, 512]) for i in range(2)]
            pss = fw.psum(st, "fpss", [128, 512])
            ix = 0
            iw = 0
            for t0 in range(0, QT, TT):
                n = TT
                for kc in range(KC):
                    x_ = xr[ix % 3]
                    ix += 1
                    fw.dma(nextq(), x_, x_[:, 0:n], x1T, x1T[:, kc, t0:t0 + n])
                    fw.op("dve", lambda e, x_=x_, kc=kc, t0=t0: e.scalar_tensor_tensor(hT[:, kc, 0:n], x_[:, 0:n], lnw[:, 1, kc:kc + 1], rstd1[:, t0:t0 + n], ALU.mult, ALU.mult),
                          reads=[x_, lnw, rstd1], writes=[hT])
                for hb in range(HB):
                    g_, u_, pg_, pu_, sg_ = wg[hb % 2], wu[hb % 2], pg[hb % 2], pu[hb % 2], sg[hb % 2]
                    fw.dma("pool", g_, g_[:], w_gate, w_gate_v[:, :, hb * 128:(hb + 1) * 128])
                    fw.dma("pool", u_, u_[:], w_up, w_up_v[:, :, hb * 128:(hb + 1) * 128])

                    def mmg(e, p_=pg_, w_=g_):
                        r = None
                        for kc in range(KC):
                            r = e.matmul(p_[:, 0:n], w_[:, kc, :], hT[:, kc, 0:n], start=(kc == 0), stop=(kc == KC - 1))
                        return r

                    def mmu(e, p_=pu_, w_=u_):
                        r = None
                        for kc in range(KC):
                            r = e.matmul(p_[:, 0:n], w_[:, kc, :], hT[:, kc, 0:n], start=(kc == 0), stop=(kc == KC - 1))
                        return r
                    fw.op("pe", mmg, reads=[g_, hT], writes=[pg_])
                    fw.op("pe", mmu, reads=[u_, hT], writes=[pu_])
                    fw.op("act", lambda e, sg_=sg_, pg_=pg_: e.activation(sg_[:, 0:n], pg_[:, 0:n], AF.Silu), reads=[pg_], writes=[sg_])
                    fw.op("dve", lambda e, sg_=sg_, pu_=pu_, hb=hb: e.tensor_tensor(act[:, hb, 0:n], sg_[:, 0:n], pu_[:, 0:n], ALU.mult),
                          reads=[sg_, pu_], writes=[act])
                for cb in range(KC):
                    p_ = pd[cb % 2]
                    x_ = xr[ix % 3]
                    ix += 1
                    fw.dma(nextq(), x_, x_[:, 0:n], x1T, x1T[:, cb, t0:t0 + n])
                    for half in range(2):
                        w_ = wd[iw % 2]
                        iw += 1
                        fw.dma("pool", w_, w_[:], w_down, w_down_v[:, half * HH:(half + 1) * HH, cb * 128:(cb + 1) * 128])

                        def mmd(e, p_=p_, w_=w_, half=half):
                            r = None
                            for j in range(HH):
                                hb = half * HH + j
                                r = e.matmul(p_[:, 0:n], w_[:, j, :], act[:, hb, 0:n], start=(hb == 0), stop=(hb == HB - 1))
                            return r
                        fw.op("pe", mmd, reads=[w_, act], writes=[p_])
                    s_ = sq[cb % 2]
                    fw.op("dve", lambda e, x_=x_, p_=p_: e.tensor_tensor(x_[:, 0:n], x_[:, 0:n], p_[:, 0:n], ALU.add), reads=[x_, p_], writes=[x_])
                    fw.op("act", lambda e, s_=s_, x_=x_: e.activation(s_[:, 0:n], x_[:, 0:n], AF.Square), reads=[x_], writes=[s_])
                    fw.op("pe", lambda e, s_=s_, cb=cb: e.matmul(pss[:, 0:n], ones_b[:], s_[:, 0:n], start=(cb == 0), stop=(cb == KC - 1)),
                          reads=[s_, ones_b], writes=[pss])
                    fw.dma(nextq(), x2T, x2T[:, cb, t0:t0 + n], x_, x_[:, 0:n])
                fw.op("dve", lambda e: e.tensor_scalar(r2[:, 0:n], pss[:, 0:n], 1.0 / D, EPS, ALU.mult, ALU.add), reads=[pss], writes=[r2])
                rsqrt_ops(r2, r2[:, 0:n])
                for cb in range(KC):
                    x_ = xr[ix % 3]
                    ix += 1
                    fw.dma(nextq(), x_, x_[:, 0:n], x2T, x2T[:, cb, t0:t0 + n])
                    fw.op("dve", lambda e, x_=x_, cb=cb: e.scalar_tensor_tensor(x_[:, 0:n], x_[:, 0:n], lnw[:, 2, cb:cb + 1], r2[:, 0:n], ALU.mult, ALU.mult),
                          reads=[x_, lnw, r2], writes=[x_])
                    fw.dma(nextq(), outT, outT[:, cb, t0:t0 + n], x_, x_[:, 0:n])
            fw.barrier()
            fw.flush()

    p5b()
    return finish(nc, fw, top, cst, dbg, dbg_o, locals())


def _prep(inputs, S):
    x = np.asarray(inputs["x"], np.float32)
    QT = S // 4
    NCH = S // 128
    g = lambda k: np.asarray(inputs[k], np.float32)
    w_in = np.ascontiguousarray(g("w_in")[0])
    w_out = np.ascontiguousarray(g("w_out")[0])
    w_gate = np.ascontiguousarray(g("w_gate")[0])
    w_up = np.ascontiguousarray(g("w_up")[0])
    w_down = np.ascontiguousarray(g("w_down")[0])
    lnw = np.stack([g("ln_mix_w")[0], g("ln_ffn_w")[0], g("ln_final_w")], 0).reshape(3, 32, 128).transpose(2, 0, 1).copy()
    convw = g("conv_w")[0].reshape(4, 48, 128).transpose(2, 1, 0).copy()
    hv = np.broadcast_to(np.concatenate([g("a_log")[0], g("dt_bias")[0]])[None, :], (128, 32)).copy()
    dfw = g("df_norm_w")[0]
    nw = np.stack([g("dn_norm_w")[0], dfw[:128], dfw[128:]], 1).copy()
    lamv = np.stack([g("lambda_q1")[0], g("lambda_k1")[0], g("lambda_q2")[0], g("lambda_k2")[0]], 1).copy()
    j = np.arange(128)[:, None]
    i = np.arange(128)[None, :]
    prot = np.zeros((128, 128), np.float32)
    for m in range(16):
        prot[m + 16, m] = -1.0
        prot[m, m + 16] = 1.0
    cmask = np.stack([(j <= i), (i <= j), (i < j), (i == j), prot], 0).astype(np.float32).transpose(1, 0, 2).copy()
    kpos = (np.arange(NCH)[None, :] * 128 + np.arange(128)[:, None]).astype(np.float32)
    posf = np.broadcast_to(np.arange(S, dtype=np.float32)[None, :], (128, S)).copy()
    invf = np.zeros((128, 1), np.float32)
    fr = np.array([ROPE_THETA ** (-(2 * k) / 32.0) for k in range(16)], np.float32)
    invf[0:16, 0] = fr
    invf[16:32, 0] = fr
    maps = []
    for c in range(8):
        b, tq = c // 4, c % 4
        xT = x[b].T
        xTf = np.ascontiguousarray(xT.reshape(32, 128, S).transpose(1, 0, 2))
        lo = tq * QT
        own = np.zeros((4096, HALO + QT), np.float32)
        own[:, HALO:] = xT[:, lo:lo + QT]
        if tq > 0:
            own[:, :HALO] = xT[:, lo - HALO:lo]
        xTo = np.ascontiguousarray(own.reshape(32, 128, HALO + QT).transpose(1, 0, 2))
        flags = np.broadcast_to((np.arange(NCH) < lo // 128).astype(np.float32)[None, :], (128, NCH)).copy()
        qp = np.arange(lo - HALO, lo + QT, dtype=np.float32)
        qpos = np.broadcast_to(qp[None, :], (128, HALO + QT)).copy()
        maps.append(dict(xTf=xTf, xTo=xTo, w_in=w_in, w_out=w_out, w_gate=w_gate, w_up=w_up, w_down=w_down,
                         lnw=lnw, convw=convw, hv=hv, nw=nw, lamv=lamv, cmask=cmask, flags=flags, qpos=qpos,
                         kpos=kpos, posf=posf, invf=invf))
    return maps


def kernel(**inputs):
    x = np.asarray(inputs["x"])
    B, S, _ = x.shape
    QT = S // 4
    nc = build_program(S)
    maps = _prep(inputs, S)
    res = run_bass_kernel_spmd(nc, maps, core_ids=list(range(8)))
    out = np.empty((B, S, D), np.float32)
    for c in range(8):
        b, tq = c // 4, c % 4
        o = np.asarray(res.results[c]["outT"], np.float32)
        out[b, tq * QT:(tq + 1) * QT, :] = o.transpose(2, 1, 0).reshape(QT, D)
    return out
```

```python
import math
from contextlib import ExitStack
import numpy as np
import concourse.bass as bass
import concourse.mybir as mybir
from concourse.bass_utils import run_bass_kernel_spmd

F32 = mybir.dt.float32
BF16 = mybir.dt.bfloat16
AF = mybir.ActivationFunctionType
ALU = mybir.AluOpType
AX = mybir.AxisListType

SAME_ENGINE_SYNC = True


class Sem:
    def __init__(self, h):
        self.h = h
        self.cnt = 0


class Track:
    def __init__(self):
        self.writers = {}
        self.readers = {}


class Obj:
    def __init__(self, t, name, dsem=None, tr=None, psum=False):
        self.t = t
        self.name = name
        self.tr = tr if tr is not None else Track()
        self.dsem = dsem
        self.psum = psum

    @property
    def writers(self):
        return self.tr.writers

    @writers.setter
    def writers(self, v):
        self.tr.writers = v

    @property
    def readers(self):
        return self.tr.readers

    @readers.setter
    def readers(self, v):
        self.tr.readers = v

    def view(self, ap, name=None):
        return Obj(ap, name or self.name, self.dsem, self.tr, self.psum)

    def __getitem__(self, k):
        return self.t[k]


class FW:
    def __init__(self, nc, stack):
        self.nc = nc
        self.stack = stack
        self.engs = ["pe", "act", "dve", "pool", "sp"]
        self.esem = {k: Sem(stack.enter_context(nc.semaphore("es_" + k))) for k in self.engs}
        self.thunks = {k: [] for k in self.engs}
        self.waited = {k: {} for k in self.engs}
        self.allsems = list(self.esem.values())
        self.ninst = 0

    def new_sem(self, name):
        s = Sem(self.stack.enter_context(self.nc.semaphore(name)))
        self.allsems.append(s)
        return s

    def sbuf(self, st, name, shape, dt, dma=False):
        self.uid = getattr(self, "uid", 0) + 1
        name = f"{name}_{self.uid}"
        t = st.enter_context(self.nc.sbuf_tensor(name, shape, dt))
        sem = None
        if dma:
            pool = self.__dict__.setdefault("sem_pool_" + str(dma), [])
            sem = pool.pop() if pool else self.new_sem("d_" + name)
            st.callback(lambda: pool.append(sem))
        return Obj(t, name, sem)

    def psum(self, st, name, shape, dt=F32):
        self.uid = getattr(self, "uid", 0) + 1
        name = f"{name}_{self.uid}"
        t = st.enter_context(self.nc.psum_tensor(name, shape, dt))
        return Obj(t, name, psum=True)

    def dram(self, name, shape, dt, kind="Internal"):
        t = self.nc.dram_tensor(name, shape, dt, kind=kind).ap()
        return Obj(t, name)

    def _deps(self, reads, writes):
        deps = {}
        for o in reads:
            for s, v in o.writers.items():
                if deps.get(s, 0) < v:
                    deps[s] = v
            if o.psum:
                for s, v in o.readers.items():
                    if deps.get(s, 0) < v:
                        deps[s] = v
        for o in writes:
            for d in (o.writers, o.readers):
                for s, v in d.items():
                    if deps.get(s, 0) < v:
                        deps[s] = v
        return deps

    def _waits(self, ek, deps):
        w = self.waited[ek]
        own = self.esem[ek]
        out = []
        for s, v in deps.items():
            if s is own and not SAME_ENGINE_SYNC:
                continue
            if w.get(s, 0) >= v:
                continue
            w[s] = v
            out.append((s.h, v))
        return out

    def op(self, ek, fn, reads=(), writes=()):
        waits = self._waits(ek, self._deps(reads, writes))
        sem = self.esem[ek]
        sem.cnt += 1
        val = sem.cnt
        h = sem.h

        def thunk(e):
            for sh, v in waits:
                e.wait_ge(sh, v)
            r = fn(e)
            if isinstance(r, (list, tuple)):
                r = r[-1]
            r.then_inc(h, 1)

        self.thunks[ek].append(thunk)
        self.ninst += 1
        for o in reads:
            if o.readers.get(sem, 0) < val:
                o.readers[sem] = val
        for o in writes:
            o.writers = {sem: val}
            o.readers = {}

    def dma(self, qk, out_o, out_ap, in_o, in_ap, sem_obj=None, group=False, **kw):
        if sem_obj is None:
            sem_obj = out_o if out_o.dsem is not None else in_o
        sem = sem_obj.dsem
        assert sem is not None, (out_o.name, in_o.name)
        if group:
            deps = self._deps([in_o], [])
            for s, v in out_o.readers.items():
                deps[s] = max(deps.get(s, 0), v)
            for s, v in out_o.writers.items():
                if s is not sem:
                    deps[s] = max(deps.get(s, 0), v)
        else:
            deps = self._deps([in_o], [out_o])
        waits = self._waits(qk, deps)
        sem.cnt += 16
        val = sem.cnt
        h = sem.h

        def thunk(e):
            for sh, v in waits:
                e.wait_ge(sh, v)
            e.dma_start(out=out_ap, in_=in_ap, **kw).then_inc(h, 16)

        self.thunks[qk].append(thunk)
        self.ninst += 1
        if in_o.readers.get(sem, 0) < val:
            in_o.readers[sem] = val
        if group:
            out_o.writers[sem] = val
        else:
            out_o.writers = {sem: val}
        out_o.readers = {}

    def barrier(self):
        snap = [(s, s.cnt) for s in self.allsems if s.cnt > 0]
        for ek in self.engs:
            waits = self._waits(ek, dict(snap))

            def thunk(e, waits=waits):
                for sh, v in waits:
                    e.wait_ge(sh, v)

            self.thunks[ek].append(thunk)

    def flush(self):
        lists = self.thunks
        with self.nc.Block() as block:
            @block.tensor
            def _(e):
                for t in lists["pe"]:
                    t(e)

            @block.scalar
            def _(e):
                for t in lists["act"]:
                    t(e)

            @block.vector
            def _(e):
                for t in lists["dve"]:
                    t(e)

            @block.gpsimd
            def _(e):
                for t in lists["pool"]:
                    t(e)

            @block.sync
            def _(e):
                for t in lists["sp"]:
                    t(e)
        self.thunks = {k: [] for k in self.engs}
        if max(s.cnt for s in self.esem.values()) > 20000:
            self.epoch = getattr(self, "epoch", 0) + 1
            for k in self.engs:
                ns = Sem(self.stack.enter_context(self.nc.semaphore(f"es_{k}_{self.epoch}")))
                self.esem[k] = ns
                self.allsems.append(ns)

D = 4096
KC = 32
NH_DN = 16
NSUB = 16
NH_DF = 8
FFN = 11008
HB = FFN // 128
PROJ = 14368
OFF_DQ, OFF_DK, OFF_DV, OFF_DZ, OFF_DB, OFF_DA, OFF_FQ, OFF_FK, OFF_FV = 0, 2048, 4096, 6144, 8192, 8208, 8224, 10272, 12320
HALO = 4
EPS = 1e-6
ROPE_THETA = 500000.0
LAMBDA_INIT = 0.8 - 0.6 * math.exp(-0.3 * 0)
PI = math.pi


def finish(nc, fw, top, cst, dbg, dbg_o, env):
    if dbg is not None:
        for i, d in enumerate(dbg):
            src = env[d[0]]
            dbg_o[i].dsem = fw.new_sem(f"dbgsem{i}")
            fw.dma("sp", dbg_o[i], dbg_o[i][:], src, d[3](src), sem_obj=dbg_o[i])
    fw.barrier()
    fw.flush()
    c2 = env.get("cst2")
    if c2 is not None:
        c2.close()
    cst.close()
    top.close()
    return nc


def build_program(S, phases=99, dbg=None):
    QT = S // 4
    TT = min(512, QT)
    NCH = S // 128
    NCO = QT // 128
    nc = bass.Bass("TRN2", target_bir_lowering=False)
    top = ExitStack()
    fw = FW(nc, top)

    def ein(name, shape, dt=F32):
        return fw.dram(name, shape, dt, kind="ExternalInput")

    xTf = ein("xTf", [128, KC, S])
    xTo = ein("xTo", [128, KC, HALO + QT])
    w_in = ein("w_in", [D, PROJ])
    if phases >= 5:
        w_out = ein("w_out", [D, D])
        w_gate = ein("w_gate", [D, FFN])
        w_up = ein("w_up", [D, FFN])
        w_down = ein("w_down", [FFN, D])
    lnw_d = ein("lnw", [128, 3, KC])
    convw_d = ein("convw", [128, 48, 4])
    hv_d = ein("hv", [128, 32])
    nw_d = ein("nw", [128, 3])
    lamv_d = ein("lamv", [128, 4])
    cm_d = ein("cmask", [128, 5, 128])
    flags_d = ein("flags", [128, NCH])
    qpos_d = ein("qpos", [128, HALO + QT])
    kpos_d = ein("kpos", [128, NCH])
    posf_d = ein("posf", [128, S])
    invf_d = ein("invf", [128, 1])
    outT = fw.dram("outT", [128, KC, QT], F32, kind="ExternalOutput")

    xn_f = fw.dram("xn_f", [128, KC, S], BF16)
    xn_o = fw.dram("xn_o", [128, KC, HALO + QT], BF16)
    raw_f = fw.dram("raw_f", [48, 128, S], F32)
    vtok_f = fw.dram("vtok_f", [S, 2048], BF16)
    ba_f = fw.dram("ba_f", [S, 32], F32)
    raw_o = fw.dram("raw_o", [80, 128, HALO + QT], F32)
    ba_o = fw.dram("ba_o", [HALO + QT, 32], F32)
    kdn_f = fw.dram("kdn_f", [16, 128, S], F32)
    vdn_f = fw.dram("vdn_f", [16, 128, S], F32)
    kdf_f = fw.dram("kdf_f", [16, 128, S], BF16)
    qdn_o = fw.dram("qdn_o", [16, 128, QT], F32)
    kdn_o = fw.dram("kdn_o", [16, 128, QT], F32)
    vdn_o = fw.dram("vdn_o", [16, 128, QT], F32)
    zdn_o = fw.dram("zdn_o", [16, 128, QT], BF16)
    qdf_o = fw.dram("qdf_o", [16, 128, QT], BF16)
    mixT = fw.dram("mixT", [128, KC, QT], BF16)
    x1T = fw.dram("x1T", [128, KC, QT], F32)
    x2T = fw.dram("x2T", [128, KC, QT], F32)
    dbg_o = None
    if dbg is not None:
        dbg_o = [fw.dram(f"dbg{i}", list(d[1]), d[2], kind="ExternalOutput") for i, d in enumerate(dbg)]

    evq = ["sp", "act"]
    cnt = {"e": 0, "q": 0}

    def nextq():
        cnt["q"] += 1
        return evq[cnt["q"] % 2]

    def evac_eng():
        cnt["e"] += 1
        return "act" if cnt["e"] % 2 else "dve"

    def interleave(gens):
        gens = list(gens)
        while gens:
            nxt = []
            for g_ in gens:
                try:
                    next(g_)
                    nxt.append(g_)
                except StopIteration:
                    pass
            gens = nxt

    def copy_op(ek, out_o, out_ap, in_o, in_ap):
        if ek == "act":
            fw.op("act", lambda e: e.activation(out_ap, in_ap, AF.Copy), reads=[in_o], writes=[out_o])
        else:
            fw.op(ek, lambda e: e.tensor_copy(out_ap, in_ap), reads=[in_o], writes=[out_o])

    cst = ExitStack()
    ones_b = fw.sbuf(cst, "ones_b", [128, 128], BF16)
    ones_f = fw.sbuf(cst, "ones_f", [128, 128], F32)
    lnw = fw.sbuf(cst, "lnw_s", [128, 3, KC], F32, dma=True)
    cm = fw.sbuf(cst, "cm_s", [128, 5, 128], F32, dma=True)
    idb = fw.sbuf(cst, "idb", [128, 128], BF16)
    fw.op("dve", lambda e: e.memset(ones_b[:], 1.0), writes=[ones_b])
    fw.op("dve", lambda e: e.memset(ones_f[:], 1.0), writes=[ones_f])
    fw.dma("sp", lnw, lnw[:], lnw_d, lnw_d[:])
    fw.dma("sp", cm, cm[:], cm_d, cm_d[:])
    fw.op("dve", lambda e: e.tensor_copy(idb[:], cm[:, 3, :]), reads=[cm], writes=[idb])
    trib = fw.sbuf(cst, "trib", [128, 128], BF16)
    fw.op("dve", lambda e: e.tensor_copy(trib[:], cm[:, 0, :]), reads=[cm], writes=[trib])

    def p0_norm(src, dst, ntok_total, which):
        T0 = min(256, QT)
        with ExitStack() as st:
            xt = [fw.sbuf(st, f"p0x{i}", [128, KC, T0], F32, dma=True) for i in range(2)]
            sq = fw.sbuf(st, "p0sq", [128, KC, T0], BF16)
            xo = [fw.sbuf(st, f"p0o{i}", [128, KC, T0], BF16, dma=True) for i in range(2)]
            rs = [fw.sbuf(st, f"p0r{i}", [128, T0], F32) for i in range(2)]
            ps = [fw.psum(st, f"p0ps{i}", [128, 512]) for i in range(2)]
            tiles = []
            t0 = 0
            while t0 < ntok_total:
                n = min(T0, ntok_total - t0)
                tiles.append((t0, n))
                t0 += n
            for it, (t0, n) in enumerate(tiles):
                x_, o_, r_, p_ = xt[it % 2], xo[it % 2], rs[it % 2], ps[it % 2]
                fw.dma(nextq(), x_, x_[:, :, 0:n], src, src[:, :, t0:t0 + n])
                fw.op("act", lambda e, x_=x_, n=n: e.activation(sq[:, :, 0:n], x_[:, :, 0:n], AF.Square),
                      reads=[x_], writes=[sq])

                def mm(e, p_=p_, n=n):
                    r = None
                    for kc in range(KC):
                        r = e.matmul(p_[:, 0:n], ones_b[:], sq[:, kc, 0:n], start=(kc == 0), stop=(kc == KC - 1))
                    return r
                fw.op("pe", mm, reads=[sq, ones_b], writes=[p_])
                fw.op("dve", lambda e, r_=r_, p_=p_, n=n: e.tensor_scalar(r_[:, 0:n], p_[:, 0:n], 1.0 / D, EPS, ALU.mult, ALU.add),
                      reads=[p_], writes=[r_])
                fw.op("act", lambda e, r_=r_, n=n: e.activation(r_[:, 0:n], r_[:, 0:n], AF.Sqrt), reads=[r_], writes=[r_])
                fw.op("dve", lambda e, r_=r_, n=n: e.reciprocal(r_[:, 0:n], r_[:, 0:n]), reads=[r_], writes=[r_])
                ek = "dve"

                def sc(e, x_=x_, o_=o_, r_=r_, n=n):
                    r = None
                    for kc in range(KC):
                        r = e.scalar_tensor_tensor(o_[:, kc, 0:n], x_[:, kc, 0:n], lnw[:, which, kc:kc + 1],
                                                   r_[:, 0:n], ALU.mult, ALU.mult)
                    return r
                fw.op(ek, sc, reads=[x_, r_, lnw], writes=[o_])
                fw.dma(nextq(), dst, dst[:, :, t0:t0 + n], o_, o_[:, :, 0:n])
            fw.barrier()
            fw.flush()

    p0_norm(xTf, xn_f, S, 0)
    p0_norm(xTo, xn_o, HALO + QT, 0)

    if phases < 1:
        return finish(nc, fw, top, cst, dbg, dbg_o, locals())

    w_in_v = w_in.t.rearrange("(kc p) n -> p kc n", p=128)

    def proj(xn, tok_tiles, jobs):
        GW = 1024
        with ExitStack() as st:
            wt = fw.sbuf(st, "p1w", [128, KC, GW], BF16, dma="sw")
            xt = [fw.sbuf(st, f"p1x{i}", [128, KC, TT], BF16, dma=True) for i in range(2)]
            evf = [fw.sbuf(st, f"p1ef{i}", [128, 512], F32, dma=True) for i in range(4)]
            evb = [fw.sbuf(st, f"p1eb{i}", [128, 512], BF16, dma=True) for i in range(4)]
            ps = [fw.psum(st, f"p1ps{i}", [128, 512]) for i in range(4)]
            k = {"x": 0, "p": 0}
            for (col_lo, ncols, mode, dst, d0) in jobs:
                for g0 in range(0, ncols, GW):
                    gw = min(GW, ncols - g0)
                    fw.dma("pool", wt, wt[:, :, 0:gw], w_in, w_in_v[:, :, col_lo + g0: col_lo + g0 + gw])
                    for (t0, n) in tok_tiles:
                        x_ = xt[k["x"] % 2]
                        k["x"] += 1
                        fw.dma(nextq(), x_, x_[:, :, 0:n], xn, xn[:, :, t0:t0 + n])
                        if mode == "cm":
                            for b0 in range(0, gw, 128):
                                p_ = ps[k["p"] % 4]
                                e_ = evf[k["p"] % 4]
                                k["p"] += 1

                                def mm(e, p_=p_, x_=x_, b0=b0, n=n):
                                    r = None
                                    for kc in range(KC):
                                        r = e.matmul(p_[:, 0:n], wt[:, kc, b0:b0 + 128], x_[:, kc, 0:n],
                                                     start=(kc == 0), stop=(kc == KC - 1))
                                    return r
                                fw.op("pe", mm, reads=[wt, x_], writes=[p_])
                                copy_op(evac_eng(), e_, e_[:, 0:n], p_, p_[:, 0:n])
                                blk = d0 + (g0 + b0) // 128
                                fw.dma(nextq(), dst, dst[blk, :, t0:t0 + n], e_, e_[:, 0:n])
                        else:
                            for s0 in range(0, n, 128):
                                m = min(128, n - s0)
                                for c0 in range(0, gw, 512):
                                    cw = min(512, gw - c0)
                                    p_ = ps[k["p"] % 4]
                                    e_ = (evb if mode == "tmb" else evf)[k["p"] % 4]
                                    k["p"] += 1

                                    def mm(e, p_=p_, x_=x_, s0=s0, m=m, c0=c0, cw=cw):
                                        r = None
                                        for kc in range(KC):
                                            r = e.matmul(p_[0:m, 0:cw], x_[:, kc, s0:s0 + m], wt[:, kc, c0:c0 + cw],
                                                         start=(kc == 0), stop=(kc == KC - 1))
                                        return r
                                    fw.op("pe", mm, reads=[wt, x_], writes=[p_])
                                    copy_op(evac_eng(), e_, e_[0:m, 0:cw], p_, p_[0:m, 0:cw])
                                    fw.dma(nextq(), dst, dst[t0 + s0:t0 + s0 + m, d0 + g0 + c0:d0 + g0 + c0 + cw],
                                           e_, e_[0:m, 0:cw])
            fw.barrier()
            fw.flush()

    full_tiles = [(i * TT, TT) for i in range(S // TT)]
    own_tiles = [(0, HALO)] + [(HALO + i * TT, TT) for i in range(QT // TT)]
    pre_tiles = [t for t in full_tiles if t[0] < S - QT]
    proj(xn_f, pre_tiles, [
        (OFF_DK, 2048, "cm", raw_f, 0),
        (OFF_DV, 2048, "cm", raw_f, 16),
        (OFF_DB, 32, "tmf", ba_f, 0),
    ])
    proj(xn_f, full_tiles, [
        (OFF_FK, 2048, "cm", raw_f, 32),
        (OFF_FV, 2048, "tmb", vtok_f, 0),
    ])
    proj(xn_o, own_tiles, [
        (OFF_DQ, 2048, "cm", raw_o, 0),
        (OFF_DK, 2048, "cm", raw_o, 16),
        (OFF_DV, 2048, "cm", raw_o, 32),
        (OFF_DZ, 2048, "cm", raw_o, 48),
        (OFF_FQ, 2048, "cm", raw_o, 64),
        (OFF_DB, 32, "tmf", ba_o, 0),
    ])

    if phases < 2:
        return finish(nc, fw, top, cst, dbg, dbg_o, locals())

    cst2 = ExitStack()
    convw = fw.sbuf(cst2, "convw_s", [128, 48, 4], F32, dma=True)
    fw.dma("sp", convw, convw[:], convw_d, convw_d[:])
    invf = fw.sbuf(cst2, "invf_s", [128, 1], F32, dma=True)
    fw.dma("sp", invf, invf[:], invf_d, invf_d[:])
    prot = fw.sbuf(cst2, "prot", [128, 128], BF16)
    fw.op("dve", lambda e: e.tensor_copy(prot[:], cm[:, 4, :]), reads=[cm], writes=[prot])

    def p2_conv(src, src_blk0, cw_blk0, nblk, tiles, col0, dst, kind):
        NS = 4
        with ExitStack() as st:
            xin = [fw.sbuf(st, f"p2i{i}", [128, 3 + TT], F32, dma=True) for i in range(NS)]
            y = [fw.sbuf(st, f"p2y{i}", [128, TT], F32) for i in range(NS)]
            sl = [fw.sbuf(st, f"p2s{i}", [128, TT], F32) for i in range(NS)]
            sq = [fw.sbuf(st, f"p2q{i}", [128, TT], BF16) for i in range(NS)]
            rr = [fw.sbuf(st, f"p2r{i}", [128, TT], F32) for i in range(NS)]
            ob = [fw.sbuf(st, f"p2o{i}", [128, TT], BF16 if kind == "z" else F32, dma=True) for i in range(NS)]
            ps = [fw.psum(st, f"p2ps{i}", [128, 512]) for i in range(NS)]

            def tile_gen(b, t0, n, slot):
                i_, y_, s_, q_, r_, o_, p_ = (a[slot] for a in (xin, y, sl, sq, rr, ob, ps))
                lo = col0 + t0 - 3
                if lo < 0:
                    fw.op("dve", lambda e: e.memset(i_[:, 0:3], 0.0), writes=[i_])
                    yield
                    fw.dma(nextq(), i_, i_[:, 3:3 + n], src, src[src_blk0 + b, :, col0 + t0:col0 + t0 + n], group=True)
                else:
                    fw.dma(nextq(), i_, i_[:, 0:3 + n], src, src[src_blk0 + b, :, lo:lo + 3 + n])
                yield
                if kind == "z":
                    fw.op("act", lambda e: e.activation(o_[:, 0:n], i_[:, 3:3 + n], AF.Silu), reads=[i_], writes=[o_])
                    yield
                else:
                    cb = cw_blk0 + b
                    fw.op("dve", lambda e: e.tensor_scalar(y_[:, 0:n], i_[:, 0:n], convw[:, cb, 0:1], None, ALU.mult),
                          reads=[i_, convw], writes=[y_])
                    yield
                    for j in range(1, 4):
                        fw.op("dve", lambda e, j=j: e.scalar_tensor_tensor(
                            y_[:, 0:n], i_[:, j:j + n], convw[:, cb, j:j + 1], y_[:, 0:n], ALU.mult, ALU.add),
                            reads=[i_, convw, y_], writes=[y_])
                        yield
                    if kind == "v":
                        fw.op("act", lambda e: e.activation(o_[:, 0:n], y_[:, 0:n], AF.Silu), reads=[y_], writes=[o_])
                        yield
                    else:
                        fw.op("act", lambda e: e.activation(s_[:, 0:n], y_[:, 0:n], AF.Silu), reads=[y_], writes=[s_])
                        yield
                        fw.op("act", lambda e: e.activation(q_[:, 0:n], s_[:, 0:n], AF.Square), reads=[s_], writes=[q_])
                        yield
                        fw.op("pe", lambda e: e.matmul(p_[:, 0:n], ones_b[:], q_[:, 0:n], start=True, stop=True),
                              reads=[q_, ones_b], writes=[p_])
                        yield
                        m_ = 128.0 if kind == "q" else 1.0
                        fw.op("act", lambda e: e.activation(r_[:, 0:n], p_[:, 0:n], AF.Sqrt, bias=m_ * EPS, scale=m_),
                              reads=[p_], writes=[r_])
                        yield
                        fw.op("dve", lambda e: e.reciprocal(r_[:, 0:n], r_[:, 0:n]), reads=[r_], writes=[r_])
                        yield
                        fw.op("dve", lambda e: e.tensor_tensor(o_[:, 0:n], s_[:, 0:n], r_[:, 0:n], ALU.mult),
                              reads=[s_, r_], writes=[o_])
                        yield
                fw.dma(nextq(), dst, dst[b, :, t0:t0 + n], o_, o_[:, 0:n])
                yield

            work = [(b, t0, n) for b in range(nblk) for (t0, n) in tiles]
            for w0 in range(0, len(work), NS):
                interleave(tile_gen(b, t0, n, j) for j, (b, t0, n) in enumerate(work[w0:w0 + NS]))
            fw.barrier()
            fw.flush()

    full_t = [(i * TT, TT) for i in range(S // TT)]
    own_t = [(i * TT, TT) for i in range(QT // TT)]
    pre_t = [t for t in full_t if t[0] < S - QT]
    p2_conv(raw_f, 0, 16, 16, pre_t, 0, kdn_f, "k")
    p2_conv(raw_f, 16, 32, 16, pre_t, 0, vdn_f, "v")
    p2_conv(raw_o, 0, 0, 16, own_t, HALO, qdn_o, "q")
    p2_conv(raw_o, 16, 16, 16, own_t, HALO, kdn_o, "k")
    p2_conv(raw_o, 32, 32, 16, own_t, HALO, vdn_o, "v")
    p2_conv(raw_o, 48, 0, 16, own_t, HALO, zdn_o, "z")

    def p2_rot(src, src_blk0, tiles, col0, pos_d, dst):
        with ExitStack() as st:
            pt = fw.sbuf(st, "p2pos", [128, TT], F32, dma=True)
            ca = fw.sbuf(st, "p2ca", [128, TT], F32)
            ti = fw.sbuf(st, "p2ti", [128, TT], mybir.dt.int32)
            sa = fw.sbuf(st, "p2sa", [128, TT], F32)
            cs = fw.sbuf(st, "p2cs", [128, TT], F32)
            sn = fw.sbuf(st, "p2sn", [128, TT], F32)
            xin = [fw.sbuf(st, f"p2x{i}", [128, TT], F32, dma=True) for i in range(4)]
            xb = [fw.sbuf(st, f"p2xb{i}", [128, TT], BF16) for i in range(4)]
            t1 = [fw.sbuf(st, f"p2t{i}", [128, TT], F32) for i in range(4)]
            t2 = [fw.sbuf(st, f"p2u{i}", [128, TT], F32) for i in range(4)]
            ob = [fw.sbuf(st, f"p2ro{i}", [128, TT], BF16, dma=True) for i in range(4)]
            ps = [fw.psum(st, f"p2rp{i}", [128, 512]) for i in range(4)]
            it = 0
            for (t0, n) in tiles:
                fw.dma("sp", pt, pt[:, 0:n], pos_d, pos_d[:, col0 + t0:col0 + t0 + n])
                def trig(dst, shift, n=n):
                    fw.op("dve", lambda e: e.tensor_scalar(sa[:, 0:n], pt[:, 0:n], invf[:, 0:1], shift, ALU.mult, ALU.add),
                          reads=[pt, invf], writes=[sa])
                    fw.op("dve", lambda e: e.tensor_scalar(ca[:, 0:n], sa[:, 0:n], 1.0 / (2 * PI), None, ALU.mult), reads=[sa], writes=[ca])
                    fw.op("dve", lambda e: e.tensor_copy(ti[:, 0:n], ca[:, 0:n]), reads=[ca], writes=[ti])
                    fw.op("dve", lambda e: e.tensor_copy(ca[:, 0:n], ti[:, 0:n]), reads=[ti], writes=[ca])
                    fw.op("dve", lambda e: e.scalar_tensor_tensor(sa[:, 0:n], ca[:, 0:n], -2 * PI, sa[:, 0:n], ALU.mult, ALU.add),
                          reads=[ca, sa], writes=[sa])
                    fw.op("dve", lambda e: e.tensor_scalar(ca[:, 0:n], sa[:, 0:n], PI, -2 * PI, ALU.is_gt, ALU.mult), reads=[sa], writes=[ca])
                    fw.op("dve", lambda e: e.tensor_tensor(sa[:, 0:n], sa[:, 0:n], ca[:, 0:n], ALU.add), reads=[sa, ca], writes=[sa])
                    fw.op("dve", lambda e: e.tensor_scalar(ca[:, 0:n], sa[:, 0:n], -PI, 2 * PI, ALU.is_lt, ALU.mult), reads=[sa], writes=[ca])
                    fw.op("dve", lambda e: e.tensor_tensor(sa[:, 0:n], sa[:, 0:n], ca[:, 0:n], ALU.add), reads=[sa, ca], writes=[sa])
                    fw.op("act", lambda e: e.activation(dst[:, 0:n], sa[:, 0:n], AF.Sin), reads=[sa], writes=[dst])
                trig(sn, 0.0)
                trig(cs, 0.5 * PI)
                def blk_gen(b, slot, t0=t0, n=n):
                    i_, b_, a_, u_, o_, p_ = (a[slot] for a in (xin, xb, t1, t2, ob, ps))
                    fw.dma(nextq(), i_, i_[:, 0:n], src, src[src_blk0 + b, :, col0 + t0:col0 + t0 + n])
                    yield
                    fw.op("act", lambda e: e.activation(b_[:, 0:n], i_[:, 0:n], AF.Copy), reads=[i_], writes=[b_])
                    yield
                    fw.op("pe", lambda e: e.matmul(p_[:, 0:n], prot[:], b_[:, 0:n], start=True, stop=True),
                          reads=[b_, prot], writes=[p_])
                    yield
                    fw.op("dve", lambda e: e.tensor_tensor(a_[:, 0:n], i_[:, 0:n], cs[:, 0:n], ALU.mult),
                          reads=[i_, cs], writes=[a_])
                    yield
                    fw.op("dve", lambda e: e.tensor_tensor(u_[:, 0:n], p_[:, 0:n], sn[:, 0:n], ALU.mult),
                          reads=[p_, sn], writes=[u_])
                    yield
                    fw.op("dve", lambda e: e.tensor_tensor(o_[:, 0:n], a_[:, 0:n], u_[:, 0:n], ALU.add),
                          reads=[a_, u_], writes=[o_])
                    yield
                    fw.dma(nextq(), dst, dst[b, :, t0:t0 + n], o_, o_[:, 0:n])
                    yield
                for b0 in range(0, 16, 4):
                    interleave(blk_gen(b0 + j, j) for j in range(4))
            fw.barrier()
            fw.flush()

    p2_rot(raw_f, 32, full_t, 0, posf_d, kdf_f)
    p2_rot(raw_o, 64, own_t, HALO, qpos_d, qdf_o)

    if phases < 2.5:
        return finish(nc, fw, top, cst, dbg, dbg_o, locals())

    class RR:
        def __init__(self, objs):
            self.objs = objs
            self.i = 0

        def get(self):
            o = self.objs[self.i % len(self.objs)]
            self.i += 1
            return o

    hv = fw.sbuf(cst2, "hv_s", [128, 32], F32, dma=True)
    nw = fw.sbuf(cst2, "nw_s", [128, 3], F32, dma=True)
    flg = fw.sbuf(cst2, "flg_s", [128, NCH], F32, dma=True)
    negea = fw.sbuf(cst2, "negea", [128, 16], F32)
    Sf = fw.sbuf(cst2, "Sf", [128, 16, 128], F32)
    fw.dma("sp", hv, hv[:], hv_d, hv_d[:])
    fw.dma("sp", nw, nw[:], nw_d, nw_d[:])
    fw.dma("sp", flg, flg[:], flags_d, flags_d[:])
    fw.op("act", lambda e: e.activation(negea[:], hv[:, 0:16], AF.Exp), reads=[hv], writes=[negea])
    fw.op("dve", lambda e: e.tensor_scalar(negea[:], negea[:], -1.0, None, ALU.mult), reads=[negea], writes=[negea])
    fw.op("dve", lambda e: e.memset(Sf[:], 0.0), writes=[Sf])
    Sfo = [Obj(Sf.t[:, h, :], f"Sf{h}") for h in range(16)]
    for h in range(16):
        Sfo[h].writers = dict(Sf.writers)

    def p3_dn(kd, vd, qd, zd, ba, ba_row0, nch, masked, with_out):
        with ExitStack() as st:
            pbank = [fw.psum(st, f"dnpf{i}", [128, 512]) for i in range(8)]
            NSLOT = 4 if with_out else 8
            NB = 8 // NSLOT
            PFs = [RR([b.view(b.t[:, 0:128]) for b in pbank[NB * j:NB * j + NB]]) for j in range(NSLOT)]
            PF = RR([b.view(b.t[:, 0:128]) for b in pbank[0:2]])
            TFs = [RR([fw.sbuf(st, f"dntf{j}_{i}", [128, 128], F32) for i in range(48 if with_out else 24)]) for j in range(NSLOT)]
            OB = RR([fw.sbuf(st, f"dnob{i}", [128, 128], BF16, dma=True) for i in range(8)])
            bat = [fw.sbuf(st, f"dnba{i}", [128, 32], F32, dma=True) for i in range(2)]
            kt = [fw.sbuf(st, f"dnk{i}", [128, 16, 128], F32, dma=True) for i in range(2)]
            vt = [fw.sbuf(st, f"dnv{i}", [128, 16, 128], F32, dma=True) for i in range(2)]
            qt = [fw.sbuf(st, f"dnq{i}", [128, 16, 128], F32, dma=True) for i in range(2)]
            zt = [fw.sbuf(st, f"dnz{i}", [128, 16, 128], BF16, dma=True) for i in range(2)]
            SC = RR([fw.sbuf(st, f"dnsc{i}", [128, 16], F32) for i in range(32)])
            tri = cm.t[:, 0, :]
            m_il = cm.t[:, 1, :]
            m_sl = cm.t[:, 2, :]
            idf = cm.t[:, 3, :]

            def ew(ek, fn, reads, writes):
                fw.op(ek, fn, reads=reads, writes=writes)

            for n in range(nch):
                c0 = n * 128
                ba_, k_, v_, q_, z_ = bat[n % 2], kt[n % 2], vt[n % 2], qt[n % 2], zt[n % 2]
                fw.dma("sp", ba_, ba_[:], ba, ba[ba_row0 + c0:ba_row0 + c0 + 128, :])
                fw.dma("sp", k_, k_[:], kd, kd[:, :, c0:c0 + 128].rearrange("h p t -> p h t"))
                fw.dma("act", v_, v_[:], vd, vd[:, :, c0:c0 + 128].rearrange("h p t -> p h t"))
                if with_out:
                    fw.dma("sp", q_, q_[:], qd, qd[:, :, c0:c0 + 128].rearrange("h p t -> p h t"))
                    fw.dma("act", z_, z_[:], zd, zd[:, :, c0:c0 + 128].rearrange("h p t -> p h t"))
                beta, negb, xg, ax, ex, g = (SC.get() for _ in range(6))
                ew("act", lambda e, beta=beta, ba_=ba_: e.activation(beta[:], ba_[:, 0:16], AF.Sigmoid), [ba_], [beta])
                ew("dve", lambda e, negb=negb, beta=beta: e.tensor_scalar(negb[:], beta[:], -1.0, None, ALU.mult), [beta], [negb])
                ew("dve", lambda e, xg=xg, ba_=ba_: e.tensor_tensor(xg[:], ba_[:, 16:32], hv[:, 16:32], ALU.add), [ba_, hv], [xg])
                ew("act", lambda e, ax=ax, xg=xg: e.activation(ax[:], xg[:], AF.Abs), [xg], [ax])
                ew("act", lambda e, ex=ex, ax=ax: e.activation(ex[:], ax[:], AF.Exp, scale=-1.0), [ax], [ex])
                ew("act", lambda e, ex=ex: e.activation(ex[:], ex[:], AF.Ln, bias=1.0), [ex], [ex])
                ew("dve", lambda e, xg=xg: e.tensor_scalar(xg[:], xg[:], 0.0, None, ALU.max), [xg], [xg])
                ew("dve", lambda e, xg=xg, ex=ex: e.tensor_tensor(xg[:], xg[:], ex[:], ALU.add), [xg, ex], [xg])
                ew("dve", lambda e, g=g, xg=xg: e.tensor_tensor(g[:], xg[:], negea[:], ALU.mult), [xg, negea], [g])
                pg = PF.get()
                pl = PF.get()
                ew("pe", lambda e, pg=pg, g=g: e.matmul(pg[:, 0:16], tri, g[:], start=True, stop=True), [g, cm], [pg])
                ew("pe", lambda e, pl=pl, g=g: e.matmul(pl[:, 0:16], ones_f[:], g[:], start=True, stop=True), [g, ones_f], [pl])
                gcol, egc, begc, ekt, adec = (SC.get() for _ in range(5))
                ew("act", lambda e, gcol=gcol, pg=pg: e.activation(gcol[:], pg[:, 0:16], AF.Copy), [pg], [gcol])
                ew("act", lambda e, egc=egc, pg=pg: e.activation(egc[:], pg[:, 0:16], AF.Exp), [pg], [egc])
                ew("dve", lambda e, begc=begc, egc=egc, beta=beta: e.tensor_tensor(begc[:], egc[:], beta[:], ALU.mult), [egc, beta], [begc])
                ew("dve", lambda e, ekt=ekt, pl=pl, gcol=gcol: e.tensor_tensor(ekt[:], pl[:, 0:16], gcol[:], ALU.subtract), [pl, gcol], [ekt])
                ew("act", lambda e, ekt=ekt: e.activation(ekt[:], ekt[:], AF.Exp), [ekt], [ekt])
                ew("act", lambda e, adec=adec, pl=pl: e.activation(adec[:], pl[:, 0:16], AF.Exp), [pl], [adec])
                if masked:
                    ew("dve", lambda e, ekt=ekt, n=n: e.tensor_scalar(ekt[:], ekt[:], flg[:, n:n + 1], None, ALU.mult), [ekt, flg], [ekt])
                    ew("dve", lambda e, adec=adec, n=n: e.tensor_scalar(adec[:], adec[:], -1.0, flg[:, n:n + 1], ALU.add, ALU.mult), [adec, flg], [adec])
                    ew("dve", lambda e, adec=adec: e.tensor_scalar(adec[:], adec[:], 1.0, None, ALU.add), [adec], [adec])
                def head_gen(h, slot, k_=k_, v_=v_, q_=q_, z_=z_, g=g, gcol=gcol, negb=negb, begc=begc, ekt=ekt, beta=beta, adec=adec, c0=c0):
                    TF = TFs[slot]
                    PF = PFs[slot]
                    kT = k_.t[:, h, :]
                    vT = v_.t[:, h, :]
                    gmat = TF.get()
                    ew("dve", lambda e, gmat=gmat, g=g, h=h: e.tensor_scalar(gmat[:], ones_f[:], g[:, h:h + 1], None, ALU.mult), [g, ones_f], [gmat])
                    yield
                    pgr = PF.get()
                    ew("pe", lambda e, pgr=pgr, gmat=gmat: e.matmul(pgr[:], gmat[:], tri, start=True, stop=True), [gmat, cm], [pgr])
                    yield
                    if with_out:
                        eg = TF.get()
                        ew("act", lambda e, eg=eg, pgr=pgr: e.activation(eg[:], pgr[:], AF.Exp), [pgr], [eg])
                        yield
                    dm = TF.get()
                    ew("dve", lambda e, dm=dm, pgr=pgr, gcol=gcol, h=h: e.tensor_scalar(dm[:], pgr[:], gcol[:, h:h + 1], 0.0, ALU.subtract, ALU.max), [pgr, gcol], [dm])
                    yield
                    ew("act", lambda e, dm=dm: e.activation(dm[:], dm[:], AF.Exp, scale=-1.0), [dm], [dm])
                    yield
                    ls = TF.get()
                    ew("dve", lambda e, ls=ls, dm=dm: e.tensor_tensor(ls[:], dm[:], m_sl, ALU.mult), [dm, cm], [ls])
                    yield
                    pkk = PF.get()
                    ew("pe", lambda e, pkk=pkk, kT=kT: e.matmul(pkk[:], kT, kT, start=True, stop=True), [k_], [pkk])
                    yield
                    N = TF.get()
                    ew("dve", lambda e, N=N, pkk=pkk, negb=negb, ls=ls, h=h: e.scalar_tensor_tensor(N[:], pkk[:], negb[:, h:h + 1], ls[:], ALU.mult, ALU.mult), [pkk, negb, ls], [N])
                    yield
                    pbt = PF.get()
                    ew("pe", lambda e, pbt=pbt, N=N: e.transpose(pbt[:], N[:], idf), [N, cm], [pbt])
                    yield
                    B = TF.get()
                    ew("act", lambda e, B=B, pbt=pbt: e.activation(B[:], pbt[:], AF.Copy), [pbt], [B])
                    yield
                    Pf_ = TF.get()
                    ew("dve", lambda e, Pf_=Pf_, B=B: e.tensor_tensor(Pf_[:], B[:], idf, ALU.add), [B, cm], [Pf_])
                    yield
                    for lev in range(6):
                        B2, N2 = TF.get(), TF.get()
                        if lev < 5:
                            p1 = PF.get()
                            ew("pe", lambda e, p1=p1, N=N, B=B: e.matmul(p1[:], N[:], B[:], start=True, stop=True), [N, B], [p1])
                            yield
                            ew("act", lambda e, B2=B2, p1=p1: e.activation(B2[:], p1[:], AF.Copy), [p1], [B2])
                            yield
                        p2 = PF.get()
                        ew("pe", lambda e, p2=p2, N=N, B=B: e.matmul(p2[:], B[:], N[:], start=True, stop=True), [N, B], [p2])
                        yield
                        ew("act", lambda e, N2=N2, p2=p2: e.activation(N2[:], p2[:], AF.Copy), [p2], [N2])
                        yield
                        B, N = B2, N2
                        p3 = PF.get()
                        ew("pe", lambda e, p3=p3, N=N, Pf_=Pf_: e.matmul(p3[:], N[:], Pf_[:], start=True, stop=True), [N, Pf_], [p3])
                        yield
                        Pn = TF.get()
                        ew("dve", lambda e, Pn=Pn, Pf_=Pf_, p3=p3: e.tensor_tensor(Pn[:], Pf_[:], p3[:], ALU.add), [Pf_, p3], [Pn])
                        yield
                        Pf_ = Pn
                    Xk, ktil, Xv = TF.get(), TF.get(), TF.get()
                    pkt = PF.get()
                    ew("pe", lambda e, pkt=pkt, kT=kT: e.transpose(pkt[:], kT, idf), [k_, cm], [pkt])
                    yield
                    ew("dve", lambda e, Xk=Xk, pkt=pkt, begc=begc, h=h: e.tensor_scalar(Xk[:], pkt[:], begc[:, h:h + 1], None, ALU.mult), [pkt, begc], [Xk])
                    yield
                    ew("dve", lambda e, ktil=ktil, pkt=pkt, ekt=ekt, h=h: e.tensor_scalar(ktil[:], pkt[:], ekt[:, h:h + 1], None, ALU.mult), [pkt, ekt], [ktil])
                    yield
                    pvt = PF.get()
                    ew("pe", lambda e, pvt=pvt, vT=vT: e.transpose(pvt[:], vT, idf), [v_, cm], [pvt])
                    yield
                    ew("act", lambda e, Xv=Xv, pvt=pvt, beta=beta, h=h: e.activation(Xv[:], pvt[:], AF.Copy, scale=beta[:, h:h + 1]), [pvt, beta], [Xv])
                    yield
                    Uf, WT = TF.get(), TF.get()
                    pu = PF.get()
                    ew("pe", lambda e, pu=pu, Pf_=Pf_, Xv=Xv: e.matmul(pu[:], Pf_[:], Xv[:], start=True, stop=True), [Pf_, Xv], [pu])
                    yield
                    ew("act", lambda e, Uf=Uf, pu=pu: e.activation(Uf[:], pu[:], AF.Copy), [pu], [Uf])
                    yield
                    pw = PF.get()
                    ew("pe", lambda e, pw=pw, Pf_=Pf_, Xk=Xk: e.matmul(pw[:], Xk[:], Pf_[:], start=True, stop=True), [Pf_, Xk], [pw])
                    yield
                    ew("dve", lambda e, WT=WT, pw=pw: e.tensor_copy(WT[:], pw[:]), [pw], [WT])
                    yield
                    pr = PF.get()
                    ew("pe", lambda e, pr=pr, WT=WT, h=h: e.matmul(pr[:], WT[:], Sfo[h][:], start=True, stop=True), [WT, Sfo[h]], [pr])
                    yield
                    vnew = TF.get()
                    ew("dve", lambda e, vnew=vnew, Uf=Uf, pr=pr: e.tensor_tensor(vnew[:], Uf[:], pr[:], ALU.subtract), [Uf, pr], [vnew])
                    yield
                    if with_out:
                        qT = q_.t[:, h, :]
                        qtil = TF.get()
                        ew("dve", lambda e, qtil=qtil, qT=qT, eg=eg: e.tensor_tensor(qtil[:], qT, eg[:], ALU.mult), [q_, eg], [qtil])
                        yield
                        lm = TF.get()
                        ew("dve", lambda e, lm=lm, dm=dm: e.tensor_tensor(lm[:], dm[:], m_il, ALU.mult), [dm, cm], [lm])
                        yield
                        pqk = PF.get()
                        ew("pe", lambda e, pqk=pqk, qT=qT, kT=kT: e.matmul(pqk[:], qT, kT, start=True, stop=True), [q_, k_], [pqk])
                        yield
                        attn = TF.get()
                        ew("dve", lambda e, attn=attn, pqk=pqk, lm=lm: e.tensor_tensor(attn[:], pqk[:], lm[:], ALU.mult), [pqk, lm], [attn])
                        yield
                        pat = PF.get()
                        ew("pe", lambda e, pat=pat, attn=attn: e.transpose(pat[:], attn[:], idf), [attn, cm], [pat])
                        yield
                        attnT = TF.get()
                        ew("act", lambda e, attnT=attnT, pat=pat: e.activation(attnT[:], pat[:], AF.Copy), [pat], [attnT])
                        yield
                        po = PF.get()

                        def omm(e, po=po, h=h, qtil=qtil, vnew=vnew, attnT=attnT):
                            e.matmul(po[:], Sfo[h][:], qtil[:], start=True, stop=False)
                            return e.matmul(po[:], vnew[:], attnT[:], start=False, stop=True)
                        ew("pe", omm, [Sfo[h], qtil, vnew, attnT], [po])
                        yield
                        of, osq = TF.get(), TF.get()
                        ew("act", lambda e, of=of, po=po: e.activation(of[:], po[:], AF.Copy), [po], [of])
                        yield
                        ew("act", lambda e, osq=osq, of=of: e.activation(osq[:], of[:], AF.Square), [of], [osq])
                        yield
                        pss = PF.get()
                        ew("pe", lambda e, pss=pss, osq=osq: e.matmul(pss[:], ones_f[:], osq[:], start=True, stop=True), [osq, ones_f], [pss])
                        yield
                        rr_ = TF.get()
                        ew("dve", lambda e, rr_=rr_, pss=pss: e.tensor_scalar(rr_[:], pss[:], 1.0 / 128, EPS, ALU.mult, ALU.add), [pss], [rr_])
                        yield
                        ew("act", lambda e, rr_=rr_: e.activation(rr_[:], rr_[:], AF.Sqrt), [rr_], [rr_])
                        yield
                        ew("dve", lambda e, rr_=rr_: e.reciprocal(rr_[:], rr_[:]), [rr_], [rr_])
                        yield
                        of2 = TF.get()
                        ew("dve", lambda e, of2=of2, of=of, rr_=rr_: e.tensor_tensor(of2[:], of[:], rr_[:], ALU.mult), [of, rr_], [of2])
                        yield
                        ob_ = OB.get()
                        ew("dve", lambda e, ob_=ob_, of2=of2, z_=z_, h=h: e.scalar_tensor_tensor(ob_[:], of2[:], nw[:, 0:1], z_[:, h, :], ALU.mult, ALU.mult), [of2, nw, z_], [ob_])
                        yield
                        fw.dma(nextq(), mixT, mixT[:, h, c0:c0 + 128], ob_, ob_[:])
                        yield
                    pds = PF.get()
                    ew("pe", lambda e, pds=pds, ktil=ktil, vnew=vnew: e.matmul(pds[:], ktil[:], vnew[:], start=True, stop=True), [ktil, vnew], [pds])
                    yield
                    ew("dve", lambda e, h=h, adec=adec, pds=pds: e.scalar_tensor_tensor(Sfo[h][:], Sfo[h][:], adec[:, h:h + 1], pds[:], ALU.mult, ALU.add), [Sfo[h], adec, pds], [Sfo[h]])
                    yield

                for h0 in range(0, 16, NSLOT):
                    interleave(head_gen(h0 + j, j) for j in range(NSLOT))
            fw.barrier()
            fw.flush()

    p3_dn(kdn_f, vdn_f, None, None, ba_f, 0, NCH - NCO, True, False)
    if phases == 2.5:
        return finish(nc, fw, top, cst, dbg, dbg_o, locals())
    p3_dn(kdn_o, vdn_o, qdn_o, zdn_o, ba_o, HALO, NCO, False, True)

    if phases < 4:
        return finish(nc, fw, top, cst, dbg, dbg_o, locals())

    def rsqrt_ops(o, ap):
        fw.op("act", lambda e: e.activation(ap, ap, AF.Sqrt), reads=[o], writes=[o])
        fw.op("dve", lambda e: e.reciprocal(ap, ap), reads=[o], writes=[o])

    lamv = fw.sbuf(cst2, "lamv_s", [128, 4], F32, dma=True)
    lam2 = fw.sbuf(cst2, "lam2", [128, 2], F32)
    neglam = fw.sbuf(cst2, "neglam", [128, 1], F32)
    nwdf = fw.sbuf(cst2, "nwdf", [128, 2], F32)
    qpos = fw.sbuf(cst2, "qpos_s", [128, HALO + QT], F32, dma=True)
    kpos = fw.sbuf(cst2, "kpos_s", [128, NCH], F32, dma=True)
    fw.dma("sp", lamv, lamv[:], lamv_d, lamv_d[:])
    fw.dma("sp", qpos, qpos[:], qpos_d, qpos_d[:])
    fw.dma("sp", kpos, kpos[:], kpos_d, kpos_d[:])
    fw.op("dve", lambda e: e.tensor_tensor(lam2[:, 0:1], lamv[:, 0:1], lamv[:, 1:2], ALU.mult), reads=[lamv], writes=[lam2])
    fw.op("dve", lambda e: e.tensor_tensor(lam2[:, 1:2], lamv[:, 2:3], lamv[:, 3:4], ALU.mult), reads=[lamv, lam2], writes=[lam2])
    fw.op("dve", lambda e: e.tensor_scalar(nwdf[:], nw[:, 1:3], 1.0 - LAMBDA_INIT, None, ALU.mult), reads=[nw], writes=[nwdf])

    def p4_attn():
        SCALE = 128.0 ** -0.5
        with ExitStack() as st:
            pst = [fw.psum(st, f"ap{i}", [128, 512]) for i in range(2)]
            pacc = [[fw.psum(st, f"aa{i}_{j}", [128, 512]) for j in range(3)] for i in range(2)]
            vt = fw.sbuf(st, "avt", [128, NCH, 256], BF16, dma=True)
            kt = [fw.sbuf(st, f"akt{i}", [128, S], BF16, dma=True) for i in range(2)]
            qt = [fw.sbuf(st, f"aqt{i}", [128, QT], BF16, dma=True) for i in range(2)]
            O1 = fw.sbuf(st, "aO1", [128, 2, QT], F32)
            pe_t = [fw.sbuf(st, f"ape{i}", [128, TT], F32) for i in range(3)]
            pm_t = [fw.sbuf(st, f"apm{i}", [128, TT], BF16) for i in range(3)]
            rs_t = fw.sbuf(st, "ars", [128, TT], F32)
            cb_t = [fw.sbuf(st, f"acb{i}", [128, TT], F32) for i in range(2)]
            sq_t = [fw.sbuf(st, f"asq{i}", [128, TT], BF16) for i in range(2)]
            ob_t = [fw.sbuf(st, f"aob{i}", [128, TT], BF16, dma=True) for i in range(4)]
            fw.op("pe", lambda e: e.matmul(pst[0][:, 0:2], ones_f[:], lam2[:], start=True, stop=True), reads=[lam2, ones_f], writes=[pst[0]])
            fw.op("act", lambda e: e.activation(lam2[:], pst[0][:, 0:2], AF.Exp), reads=[pst[0]], writes=[lam2])
            fw.op("dve", lambda e: e.tensor_tensor(neglam[:], lam2[:, 1:2], lam2[:, 0:1], ALU.subtract), reads=[lam2], writes=[neglam])
            fw.op("dve", lambda e: e.tensor_scalar(neglam[:], neglam[:], -LAMBDA_INIT, None, ALU.add), reads=[neglam], writes=[neglam])
            it = 0
            ia = 0
            io = 0
            for h in range(NH_DF):
                fw.dma("sp", vt, vt[:], vtok_f, vtok_f[:, h * 256:(h + 1) * 256].rearrange("(kb p) c -> p kb c", p=128))
                for s_ in range(2):
                    sub = 2 * h + s_
                    k_, q_ = kt[sub % 2], qt[sub % 2]
                    fw.dma("act", k_, k_[:], kdf_f, kdf_f[sub])
                    fw.dma("sp", q_, q_[:], qdf_o, qdf_o[sub])
                    for q0 in range(0, QT, TT):
                        n = TT
                        acc = pacc[ia % 2]
                        ia += 1
                        NKB = min(NCH, NCH - NCO + (q0 + TT) // 128)
                        def front(kb, acc=acc, k_=k_, q_=q_, q0=q0):
                            nonlocal it
                            p_ = pst[it % 2]
                            e_ = pe_t[it % 3]
                            m_ = pm_t[it % 3]
                            it += 1
                            fw.op("pe", lambda e: e.matmul(p_[:, 0:n], k_[:, kb * 128:(kb + 1) * 128], q_[:, q0:q0 + n], start=True, stop=True),
                                  reads=[k_, q_], writes=[p_])
                            fw.op("act", lambda e: e.activation(e_[:, 0:n], p_[:, 0:n], AF.Exp, scale=SCALE), reads=[p_], writes=[e_])
                            fw.op("dve", lambda e: e.scalar_tensor_tensor(m_[:, 0:n], qpos[:, HALO + q0:HALO + q0 + n], kpos[:, kb:kb + 1], e_[:, 0:n], ALU.is_ge, ALU.mult),
                                  reads=[e_, qpos, kpos], writes=[m_])
                            return m_

                        def back(kb, m_, acc=acc, NKB=NKB):
                            def pv(e):
                                e.matmul(acc[0][:, 0:n], vt[:, kb, 0:128], m_[:, 0:n], start=(kb == 0), stop=(kb == NKB - 1))
                                e.matmul(acc[1][:, 0:n], vt[:, kb, 128:256], m_[:, 0:n], start=(kb == 0), stop=(kb == NKB - 1))
                                return e.matmul(acc[2][:, 0:n], ones_b[:], m_[:, 0:n], start=(kb == 0), stop=(kb == NKB - 1))
                            fw.op("pe", pv, reads=[vt, m_, ones_b], writes=[acc[0], acc[1], acc[2]])

                        pend = []
                        for kb in range(NKB):
                            pend.append((kb, front(kb)))
                            if len(pend) > 1:
                                back(*pend.pop(0))
                        for pb_ in pend:
                            back(*pb_)
                        fw.op("dve", lambda e, acc=acc: e.reciprocal(rs_t[:, 0:n], acc[2][:, 0:n]), reads=[acc[2]], writes=[rs_t])
                        for c in range(2):
                            if s_ == 0:
                                fw.op("dve", lambda e, acc=acc, c=c, q0=q0: e.tensor_tensor(O1[:, c, q0:q0 + n], acc[c][:, 0:n], rs_t[:, 0:n], ALU.mult),
                                      reads=[acc[c], rs_t], writes=[O1])
                            else:
                                cb_ = cb_t[c]
                                fw.op("dve", lambda e, acc=acc, c=c, cb_=cb_: e.tensor_tensor(cb_[:, 0:n], acc[c][:, 0:n], rs_t[:, 0:n], ALU.mult),
                                      reads=[acc[c], rs_t], writes=[cb_])
                                fw.op("dve", lambda e, c=c, cb_=cb_, q0=q0: e.scalar_tensor_tensor(cb_[:, 0:n], cb_[:, 0:n], neglam[:, 0:1], O1[:, c, q0:q0 + n], ALU.mult, ALU.add),
                                      reads=[cb_, neglam, O1], writes=[cb_])
                                fw.op("act", lambda e, c=c, cb_=cb_: e.activation(sq_t[c][:, 0:n], cb_[:, 0:n], AF.Square), reads=[cb_], writes=[sq_t[c]])
                        if s_ == 1:
                            p_ = pst[it % 2]
                            it += 1

                            def ssm(e, p_=p_):
                                e.matmul(p_[:, 0:n], ones_b[:], sq_t[0][:, 0:n], start=True, stop=False)
                                return e.matmul(p_[:, 0:n], ones_b[:], sq_t[1][:, 0:n], start=False, stop=True)
                            fw.op("pe", ssm, reads=[sq_t[0], sq_t[1], ones_b], writes=[p_])
                            fw.op("dve", lambda e, p_=p_: e.tensor_scalar(rs_t[:, 0:n], p_[:, 0:n], 1.0 / 256, EPS, ALU.mult, ALU.add), reads=[p_], writes=[rs_t])
                            rsqrt_ops(rs_t, rs_t[:, 0:n])
                            for c in range(2):
                                o_ = ob_t[io % 4]
                                io += 1
                                fw.op("dve", lambda e, o_=o_, c=c: e.scalar_tensor_tensor(o_[:, 0:n], cb_t[c][:, 0:n], nwdf[:, c:c + 1], rs_t[:, 0:n], ALU.mult, ALU.mult),
                                      reads=[cb_t[c], nwdf, rs_t], writes=[o_])
                                fw.dma(nextq(), mixT, mixT[:, 16 + 2 * h + c, q0:q0 + n], o_, o_[:, 0:n])
            fw.barrier()
            fw.flush()

    p4_attn()

    if phases < 5:
        return finish(nc, fw, top, cst, dbg, dbg_o, locals())

    cst2.close()
    rstd1 = fw.sbuf(cst, "rstd1", [128, QT], F32)
    w_out_v = w_out.t.rearrange("(kc p) n -> p kc n", p=128)
    w_gate_v = w_gate.t.rearrange("(kc p) n -> p kc n", p=128)
    w_up_v = w_up.t.rearrange("(kc p) n -> p kc n", p=128)
    w_down_v = w_down.t.rearrange("(hb p) n -> p hb n", p=128)

    def p5a():
        with ExitStack() as st:
            mx = fw.sbuf(st, "omx", [128, KC, TT], BF16, dma=True)
            wo = [fw.sbuf(st, f"owo{i}", [128, KC, 512], BF16, dma="sw") for i in range(2)]
            xr = [fw.sbuf(st, f"oxr{i}", [128, TT], F32, dma=True) for i in range(3)]
            sq = [fw.sbuf(st, f"osq{i}", [128, TT], BF16) for i in range(2)]
            ps = [fw.psum(st, f"ops{i}", [128, 512]) for i in range(3)]
            pss = fw.psum(st, "opss", [128, 512])
            it = 0
            ig = 0
            for t0 in range(0, QT, TT):
                n = TT
                fw.dma("sp", mx, mx[:], mixT, mixT[:, :, t0:t0 + n])
                for cg in range(8):
                    w_ = wo[ig % 2]
                    ig += 1
                    fw.dma("pool", w_, w_[:], w_out, w_out_v[:, :, cg * 512:(cg + 1) * 512])
                    for cb in range(4):
                        blk = cg * 4 + cb
                        p_, x_, s_ = ps[it % 3], xr[it % 3], sq[it % 2]
                        it += 1
                        fw.dma(nextq(), x_, x_[:, 0:n], xTo, xTo[:, blk, HALO + t0:HALO + t0 + n])

                        def mm(e, p_=p_, w_=w_, cb=cb):
                            r = None
                            for kc in range(KC):
                                r = e.matmul(p_[:, 0:n], w_[:, kc, cb * 128:(cb + 1) * 128], mx[:, kc, 0:n], start=(kc == 0), stop=(kc == KC - 1))
                            return r
                        fw.op("pe", mm, reads=[w_, mx], writes=[p_])
                        fw.op("dve", lambda e, x_=x_, p_=p_: e.tensor_tensor(x_[:, 0:n], x_[:, 0:n], p_[:, 0:n], ALU.add), reads=[x_, p_], writes=[x_])
                        fw.op("act", lambda e, s_=s_, x_=x_: e.activation(s_[:, 0:n], x_[:, 0:n], AF.Square), reads=[x_], writes=[s_])
                        fw.op("pe", lambda e, s_=s_, blk=blk: e.matmul(pss[:, 0:n], ones_b[:], s_[:, 0:n], start=(blk == 0), stop=(blk == KC - 1)),
                              reads=[s_, ones_b], writes=[pss])
                        fw.dma(nextq(), x1T, x1T[:, blk, t0:t0 + n], x_, x_[:, 0:n])
                fw.op("dve", lambda e, t0=t0: e.tensor_scalar(rstd1[:, t0:t0 + n], pss[:, 0:n], 1.0 / D, EPS, ALU.mult, ALU.add), reads=[pss], writes=[rstd1])
                rsqrt_ops(rstd1, rstd1[:, t0:t0 + n])
            fw.barrier()
            fw.flush()

    p5a()

    def p5b():
        HH = HB // 2
        with ExitStack() as st:
            hT = fw.sbuf(st, "fh", [128, KC, TT], BF16)
            act = fw.sbuf(st, "fact", [128, HB, TT], BF16)
            wg = [fw.sbuf(st, f"fwg{i}", [128, KC, 128], BF16, dma="sw") for i in range(2)]
            wu = [fw.sbuf(st, f"fwu{i}", [128, KC, 128], BF16, dma="sw") for i in range(2)]
            wd = [fw.sbuf(st, f"fwd{i}", [128, HH, 128], BF16, dma="sw") for i in range(2)]
            xr = [fw.sbuf(st, f"fxr{i}", [128, TT], F32, dma=True) for i in range(3)]
            sg = [fw.sbuf(st, f"fsg{i}", [128, TT], F32) for i in range(2)]
            sq = [fw.sbuf(st, f"fsq{i}", [128, TT], BF16) for i in range(2)]
            r2 = fw.sbuf(st, "fr2", [128, TT], F32)
            pg = [fw.psum(st, f"fpg{i}", [128, 512]) for i in range(2)]
            pu = [fw.psum(st, f"fpu{i}", [128, 512]) for i in range(2)]
            pd = [fw.psum(st, f"fpd{i}", [128, 512]) for i in range(2)]
            pss = fw.psum(st, "fpss", [128, 512])
            ix = 0
            iw = 0
            for t0 in range(0, QT, TT):
                n = TT
                for kc in range(KC):
                    x_ = xr[ix % 3]
                    ix += 1
                    fw.dma(nextq(), x_, x_[:, 0:n], x1T, x1T[:, kc, t0:t0 + n])
                    fw.op("dve", lambda e, x_=x_, kc=kc, t0=t0: e.scalar_tensor_tensor(hT[:, kc, 0:n], x_[:, 0:n], lnw[:, 1, kc:kc + 1], rstd1[:, t0:t0 + n], ALU.mult, ALU.mult),
                          reads=[x_, lnw, rstd1], writes=[hT])
                for hb in range(HB):
                    g_, u_, pg_, pu_, sg_ = wg[hb % 2], wu[hb % 2], pg[hb % 2], pu[hb % 2], sg[hb % 2]
                    fw.dma("pool", g_, g_[:], w_gate, w_gate_v[:, :, hb * 128:(hb + 1) * 128])
                    fw.dma("pool", u_, u_[:], w_up, w_up_v[:, :, hb * 128:(hb + 1) * 128])

                    def mmg(e, p_=pg_, w_=g_):
                        r = None
                        for kc in range(KC):
                            r = e.matmul(p_[:, 0:n], w_[:, kc, :], hT[:, kc, 0:n], start=(kc == 0), stop=(kc == KC - 1))
                        return r

                    def mmu(e, p_=pu_, w_=u_):
                        r = None
                        for kc in range(KC):
                            r = e.matmul(p_[:, 0:n], w_[:, kc, :], hT[:, kc, 0:n], start=(kc == 0), stop=(kc == KC - 1))
                        return r
                    fw.op("pe", mmg, reads=[g_, hT], writes=[pg_])
                    fw.op("pe", mmu, reads=[u_, hT], writes=[pu_])
                    fw.op("act", lambda e, sg_=sg_, pg_=pg_: e.activation(sg_[:, 0:n], pg_[:, 0:n], AF.Silu), reads=[pg_], writes=[sg_])
                    fw.op("dve", lambda e, sg_=sg_, pu_=pu_, hb=hb: e.tensor_tensor(act[:, hb, 0:n], sg_[:, 0:n], pu_[:, 0:n], ALU.mult),
                          reads=[sg_, pu_], writes=[act])
                for cb in range(KC):
                    p_ = pd[cb % 2]
                    x_ = xr[ix % 3]
                    ix += 1
                    fw.dma(nextq(), x_, x_[:, 0:n], x1T, x1T[:, cb, t0:t0 + n])
                    for half in range(2):
                        w_ = wd[iw % 2]
                        iw += 1
                        fw.dma("pool", w_, w_[:], w_down, w_down_v[:, half * HH:(half + 1) * HH, cb * 128:(cb + 1) * 128])

                        def mmd(e, p_=p_, w_=w_, half=half):
                            r = None
                            for j in range(HH):
                                hb = half * HH + j
                                r = e.matmul(p_[:, 0:n], w_[:, j, :], act[:, hb, 0:n], start=(hb == 0), stop=(hb == HB - 1))
                            return r
                        fw.op("pe", mmd, reads=[w_, act], writes=[p_])
                    s_ = sq[cb % 2]
                    fw.op("dve", lambda e, x_=x_, p_=p_: e.tensor_tensor(x_[:, 0:n], x_[:, 0:n], p_[:, 0:n], ALU.add), reads=[x_, p_], writes=[x_])
                    fw.op("act", lambda e, s_=s_, x_=x_: e.activation(s_[:, 0:n], x_[:, 0:n], AF.Square), reads=[x_], writes=[s_])
                    fw.op("pe", lambda e, s_=s_, cb=cb: e.matmul(pss[:, 0:n], ones_b[:], s_[:, 0:n], start=(cb == 0), stop=(cb == KC - 1)),
                          reads=[s_, ones_b], writes=[pss])
                    fw.dma(nextq(), x2T, x2T[:, cb, t0:t0 + n], x_, x_[:, 0:n])
                fw.op("dve", lambda e: e.tensor_scalar(r2[:, 0:n], pss[:, 0:n], 1.0 / D, EPS, ALU.mult, ALU.add), reads=[pss], writes=[r2])
                rsqrt_ops(r2, r2[:, 0:n])
                for cb in range(KC):
                    x_ = xr[ix % 3]
                    ix += 1
                    fw.dma(nextq(), x_, x_[:, 0:n], x2T, x2T[:, cb, t0:t0 + n])
                    fw.op("dve", lambda e, x_=x_, cb=cb: e.scalar_tensor_tensor(x_[:, 0:n], x_[:, 0:n], lnw[:, 2, cb:cb + 1], r2[:, 0:n], ALU.mult, ALU.mult),
                          reads=[x_, lnw, r2], writes=[x_])
                    fw.dma(nextq(), outT, outT[:, cb, t0:t0 + n], x_, x_[:, 0:n])
            fw.barrier()
            fw.flush()

    p5b()
    return finish(nc, fw, top, cst, dbg, dbg_o, locals())


def _prep(inputs, S):
    x = np.asarray(inputs["x"], np.float32)
    QT = S // 4
    NCH = S // 128
    g = lambda k: np.asarray(inputs[k], np.float32)
    w_in = np.ascontiguousarray(g("w_in")[0])
    w_out = np.ascontiguousarray(g("w_out")[0])
    w_gate = np.ascontiguousarray(g("w_gate")[0])
    w_up = np.ascontiguousarray(g("w_up")[0])
    w_down = np.ascontiguousarray(g("w_down")[0])
    lnw = np.stack([g("ln_mix_w")[0], g("ln_ffn_w")[0], g("ln_final_w")], 0).reshape(3, 32, 128).transpose(2, 0, 1).copy()
    convw = g("conv_w")[0].reshape(4, 48, 128).transpose(2, 1, 0).copy()
    hv = np.broadcast_to(np.concatenate([g("a_log")[0], g("dt_bias")[0]])[None, :], (128, 32)).copy()
    dfw = g("df_norm_w")[0]
    nw = np.stack([g("dn_norm_w")[0], dfw[:128], dfw[128:]], 1).copy()
    lamv = np.stack([g("lambda_q1")[0], g("lambda_k1")[0], g("lambda_q2")[0], g("lambda_k2")[0]], 1).copy()
    j = np.arange(128)[:, None]
    i = np.arange(128)[None, :]
    prot = np.zeros((128, 128), np.float32)
    for m in range(16):
        prot[m + 16, m] = -1.0
        prot[m, m + 16] = 1.0
    cmask = np.stack([(j <= i), (i <= j), (i < j), (i == j), prot], 0).astype(np.float32).transpose(1, 0, 2).copy()
    kpos = (np.arange(NCH)[None, :] * 128 + np.arange(128)[:, None]).astype(np.float32)
    posf = np.broadcast_to(np.arange(S, dtype=np.float32)[None, :], (128, S)).copy()
    invf = np.zeros((128, 1), np.float32)
    fr = np.array([ROPE_THETA ** (-(2 * k) / 32.0) for k in range(16)], np.float32)
    invf[0:16, 0] = fr
    invf[16:32, 0] = fr
    maps = []
    for c in range(8):
        b, tq = c // 4, c % 4
        xT = x[b].T
        xTf = np.ascontiguousarray(xT.reshape(32, 128, S).transpose(1, 0, 2))
        lo = tq * QT
        own = np.zeros((4096, HALO + QT), np.float32)
        own[:, HALO:] = xT[:, lo:lo + QT]
        if tq > 0:
            own[:, :HALO] = xT[:, lo - HALO:lo]
        xTo = np.ascontiguousarray(own.reshape(32, 128, HALO + QT).transpose(1, 0, 2))
        flags = np.broadcast_to((np.arange(NCH) < lo // 128).astype(np.float32)[None, :], (128, NCH)).copy()
        qp = np.arange(lo - HALO, lo + QT, dtype=np.float32)
        qpos = np.broadcast_to(qp[None, :], (128, HALO + QT)).copy()
        maps.append(dict(xTf=xTf, xTo=xTo, w_in=w_in, w_out=w_out, w_gate=w_gate, w_up=w_up, w_down=w_down,
                         lnw=lnw, convw=convw, hv=hv, nw=nw, lamv=lamv, cmask=cmask, flags=flags, qpos=qpos,
                         kpos=kpos, posf=posf, invf=invf))
    return maps


def kernel(**inputs):
    x = np.asarray(inputs["x"])
    B, S, _ = x.shape
    QT = S // 4
    nc = build_program(S)
    maps = _prep(inputs, S)
    res = run_bass_kernel_spmd(nc, maps, core_ids=list(range(8)))
    out = np.empty((B, S, D), np.float32)
    for c in range(8):
        b, tq = c // 4, c % 4
        o = np.asarray(res.results[c]["outT"], np.float32)
        out[b, tq * QT:(tq + 1) * QT, :] = o.transpose(2, 1, 0).reshape(QT, D)
    return out
```

```python
import math
from contextlib import ExitStack
import numpy as np
import concourse.bass as bass
import concourse.mybir as mybir
from concourse.bass_utils import run_bass_kernel_spmd

F32 = mybir.dt.float32
BF16 = mybir.dt.bfloat16
AF = mybir.ActivationFunctionType
ALU = mybir.AluOpType
AX = mybir.AxisListType

SAME_ENGINE_SYNC = True


class Sem:
    def __init__(self, h):
        self.h = h
        self.cnt = 0


class Track:
    def __init__(self):
        self.writers = {}
        self.readers = {}


class Obj:
    def __init__(self, t, name, dsem=None, tr=None, psum=False):
        self.t = t
        self.name = name
        self.tr = tr if tr is not None else Track()
        self.dsem = dsem
        self.psum = psum

    @property
    def writers(self):
        return self.tr.writers

    @writers.setter
    def writers(self, v):
        self.tr.writers = v

    @property
    def readers(self):
        return self.tr.readers

    @readers.setter
    def readers(self, v):
        self.tr.readers = v

    def view(self, ap, name=None):
        return Obj(ap, name or self.name, self.dsem, self.tr, self.psum)

    def __getitem__(self, k):
        return self.t[k]


class FW:
    def __init__(self, nc, stack):
        self.nc = nc
        self.stack = stack
        self.engs = ["pe", "act", "dve", "pool", "sp"]
        self.esem = {k: Sem(stack.enter_context(nc.semaphore("es_" + k))) for k in self.engs}
        self.thunks = {k: [] for k in self.engs}
        self.waited = {k: {} for k in self.engs}
        self.allsems = list(self.esem.values())
        self.ninst = 0

    def new_sem(self, name):
        s = Sem(self.stack.enter_context(self.nc.semaphore(name)))
        self.allsems.append(s)
        return s

    def sbuf(self, st, name, shape, dt, dma=False):
        self.uid = getattr(self, "uid", 0) + 1
        name = f"{name}_{self.uid}"
        t = st.enter_context(self.nc.sbuf_tensor(name, shape, dt))
        sem = None
        if dma:
            pool = self.__dict__.setdefault("sem_pool_" + str(dma), [])
            sem = pool.pop() if pool else self.new_sem("d_" + name)
            st.callback(lambda: pool.append(sem))
        return Obj(t, name, sem)

    def psum(self, st, name, shape, dt=F32):
        self.uid = getattr(self, "uid", 0) + 1
        name = f"{name}_{self.uid}"
        t = st.enter_context(self.nc.psum_tensor(name, shape, dt))
        return Obj(t, name, psum=True)

    def dram(self, name, shape, dt, kind="Internal"):
        t = self.nc.dram_tensor(name, shape, dt, kind=kind).ap()
        return Obj(t, name)

    def _deps(self, reads, writes):
        deps = {}
        for o in reads:
            for s, v in o.writers.items():
                if deps.get(s, 0) < v:
                    deps[s] = v
            if o.psum:
                for s, v in o.readers.items():
                    if deps.get(s, 0) < v:
                        deps[s] = v
        for o in writes:
            for d in (o.writers, o.readers):
                for s, v in d.items():
                    if deps.get(s, 0) < v:
                        deps[s] = v
        return deps

    def _waits(self, ek, deps):
        w = self.waited[ek]
        own = self.esem[ek]
        out = []
        for s, v in deps.items():
            if s is own and not SAME_ENGINE_SYNC:
                continue
            if w.get(s, 0) >= v:
                continue
            w[s] = v
            out.append((s.h, v))
        return out

    def op(self, ek, fn, reads=(), writes=()):
        waits = self._waits(ek, self._deps(reads, writes))
        sem = self.esem[ek]
        sem.cnt += 1
        val = sem.cnt
        h = sem.h

        def thunk(e):
            for sh, v in waits:
                e.wait_ge(sh, v)
            r = fn(e)
            if isinstance(r, (list, tuple)):
                r = r[-1]
            r.then_inc(h, 1)

        self.thunks[ek].append(thunk)
        self.ninst += 1
        for o in reads:
            if o.readers.get(sem, 0) < val:
                o.readers[sem] = val
        for o in writes:
            o.writers = {sem: val}
            o.readers = {}

    def dma(self, qk, out_o, out_ap, in_o, in_ap, sem_obj=None, group=False, **kw):
        if sem_obj is None:
            sem_obj = out_o if out_o.dsem is not None else in_o
        sem = sem_obj.dsem
        assert sem is not None, (out_o.name, in_o.name)
        if group:
            deps = self._deps([in_o], [])
            for s, v in out_o.readers.items():
                deps[s] = max(deps.get(s, 0), v)
            for s, v in out_o.writers.items():
                if s is not sem:
                    deps[s] = max(deps.get(s, 0), v)
        else:
            deps = self._deps([in_o], [out_o])
        waits = self._waits(qk, deps)
        sem.cnt += 16
        val = sem.cnt
        h = sem.h

        def thunk(e):
            for sh, v in waits:
                e.wait_ge(sh, v)
            e.dma_start(out=out_ap, in_=in_ap, **kw).then_inc(h, 16)

        self.thunks[qk].append(thunk)
        self.ninst += 1
        if in_o.readers.get(sem, 0) < val:
            in_o.readers[sem] = val
        if group:
            out_o.writers[sem] = val
        else:
            out_o.writers = {sem: val}
        out_o.readers = {}

    def barrier(self):
        snap = [(s, s.cnt) for s in self.allsems if s.cnt > 0]
        for ek in self.engs:
            waits = self._waits(ek, dict(snap))

            def thunk(e, waits=waits):
                for sh, v in waits:
                    e.wait_ge(sh, v)

            self.thunks[ek].append(thunk)

    def flush(self):
        lists = self.thunks
        with self.nc.Block() as block:
            @block.tensor
            def _(e):
                for t in lists["pe"]:
                    t(e)

            @block.scalar
            def _(e):
                for t in lists["act"]:
                    t(e)

            @block.vector
            def _(e):
                for t in lists["dve"]:
                    t(e)

            @block.gpsimd
            def _(e):
                for t in lists["pool"]:
                    t(e)

            @block.sync
            def _(e):
                for t in lists["sp"]:
                    t(e)
        self.thunks = {k: [] for k in self.engs}
        if max(s.cnt for s in self.esem.values()) > 20000:
            self.epoch = getattr(self, "epoch", 0) + 1
            for k in self.engs:
                ns = Sem(self.stack.enter_context(self.nc.semaphore(f"es_{k}_{self.epoch}")))
                self.esem[k] = ns
                self.allsems.append(ns)

D = 4096
KC = 32
NH_DN = 16
NSUB = 16
NH_DF = 8
FFN = 11008
HB = FFN // 128
PROJ = 14368
OFF_DQ, OFF_DK, OFF_DV, OFF_DZ, OFF_DB, OFF_DA, OFF_FQ, OFF_FK, OFF_FV = 0, 2048, 4096, 6144, 8192, 8208, 8224, 10272, 12320
HALO = 4
EPS = 1e-6
ROPE_THETA = 500000.0
LAMBDA_INIT = 0.8 - 0.6 * math.exp(-0.3 * 0)
PI = math.pi


def finish(nc, fw, top, cst, dbg, dbg_o, env):
    if dbg is not None:
        for i, d in enumerate(dbg):
            src = env[d[0]]
            dbg_o[i].dsem = fw.new_sem(f"dbgsem{i}")
            fw.dma("sp", dbg_o[i], dbg_o[i][:], src, d[3](src), sem_obj=dbg_o[i])
    fw.barrier()
    fw.flush()
    c2 = env.get("cst2")
    if c2 is not None:
        c2.close()
    cst.close()
    top.close()
    return nc


def build_program(S, phases=99, dbg=None):
    QT = S // 4
    TT = min(512, QT)
    NCH = S // 128
    NCO = QT // 128
    nc = bass.Bass("TRN2", target_bir_lowering=False)
    top = ExitStack()
    fw = FW(nc, top)

    def ein(name, shape, dt=F32):
        return fw.dram(name, shape, dt, kind="ExternalInput")

    xTf = ein("xTf", [128, KC, S])
    xTo = ein("xTo", [128, KC, HALO + QT])
    w_in = ein("w_in", [D, PROJ])
    if phases >= 5:
        w_out = ein("w_out", [D, D])
        w_gate = ein("w_gate", [D, FFN])
        w_up = ein("w_up", [D, FFN])
        w_down = ein("w_down", [FFN, D])
    lnw_d = ein("lnw", [128, 3, KC])
    convw_d = ein("convw", [128, 48, 4])
    hv_d = ein("hv", [128, 32])
    nw_d = ein("nw", [128, 3])
    lamv_d = ein("lamv", [128, 4])
    cm_d = ein("cmask", [128, 5, 128])
    flags_d = ein("flags", [128, NCH])
    qpos_d = ein("qpos", [128, HALO + QT])
    kpos_d = ein("kpos", [128, NCH])
    posf_d = ein("posf", [128, S])
    invf_d = ein("invf", [128, 1])
    outT = fw.dram("outT", [128, KC, QT], F32, kind="ExternalOutput")

    xn_f = fw.dram("xn_f", [128, KC, S], BF16)
    xn_o = fw.dram("xn_o", [128, KC, HALO + QT], BF16)
    raw_f = fw.dram("raw_f", [48, 128, S], F32)
    vtok_f = fw.dram("vtok_f", [S, 2048], BF16)
    ba_f = fw.dram("ba_f", [S, 32], F32)
    raw_o = fw.dram("raw_o", [80, 128, HALO + QT], F32)
    ba_o = fw.dram("ba_o", [HALO + QT, 32], F32)
    kdn_f = fw.dram("kdn_f", [16, 128, S], F32)
    vdn_f = fw.dram("vdn_f", [16, 128, S], F32)
    kdf_f = fw.dram("kdf_f", [16, 128, S], BF16)
    qdn_o = fw.dram("qdn_o", [16, 128, QT], F32)
    kdn_o = fw.dram("kdn_o", [16, 128, QT], F32)
    vdn_o = fw.dram("vdn_o", [16, 128, QT], F32)
    zdn_o = fw.dram("zdn_o", [16, 128, QT], BF16)
    qdf_o = fw.dram("qdf_o", [16, 128, QT], BF16)
    mixT = fw.dram("mixT", [128, KC, QT], BF16)
    x1T = fw.dram("x1T", [128, KC, QT], F32)
    x2T = fw.dram("x2T", [128, KC, QT], F32)
    dbg_o = None
    if dbg is not None:
        dbg_o = [fw.dram(f"dbg{i}", list(d[1]), d[2], kind="ExternalOutput") for i, d in enumerate(dbg)]

    evq = ["sp", "act"]
    cnt = {"e": 0, "q": 0}

    def nextq():
        cnt["q"] += 1
        return evq[cnt["q"] % 2]

    def evac_eng():
        cnt["e"] += 1
        return "act" if cnt["e"] % 2 else "dve"

    def interleave(gens):
        gens = list(gens)
        while gens:
            nxt = []
            for g_ in gens:
                try:
                    next(g_)
                    nxt.append(g_)
                except StopIteration:
                    pass
            gens = nxt

    def copy_op(ek, out_o, out_ap, in_o, in_ap):
        if ek == "act":
            fw.op("act", lambda e: e.activation(out_ap, in_ap, AF.Copy), reads=[in_o], writes=[out_o])
        else:
            fw.op(ek, lambda e: e.tensor_copy(out_ap, in_ap), reads=[in_o], writes=[out_o])

    cst = ExitStack()
    ones_b = fw.sbuf(cst, "ones_b", [128, 128], BF16)
    ones_f = fw.sbuf(cst, "ones_f", [128, 128], F32)
    lnw = fw.sbuf(cst, "lnw_s", [128, 3, KC], F32, dma=True)
    cm = fw.sbuf(cst, "cm_s", [128, 5, 128], F32, dma=True)
    idb = fw.sbuf(cst, "idb", [128, 128], BF16)
    fw.op("dve", lambda e: e.memset(ones_b[:], 1.0), writes=[ones_b])
    fw.op("dve", lambda e: e.memset(ones_f[:], 1.0), writes=[ones_f])
    fw.dma("sp", lnw, lnw[:], lnw_d, lnw_d[:])
    fw.dma("sp", cm, cm[:], cm_d, cm_d[:])
    fw.op("dve", lambda e: e.tensor_copy(idb[:], cm[:, 3, :]), reads=[cm], writes=[idb])
    trib = fw.sbuf(cst, "trib", [128, 128], BF16)
    fw.op("dve", lambda e: e.tensor_copy(trib[:], cm[:, 0, :]), reads=[cm], writes=[trib])

    def p0_norm(src, dst, ntok_total, which):
        T0 = min(256, QT)
        with ExitStack() as st:
            xt = [fw.sbuf(st, f"p0x{i}", [128, KC, T0], F32, dma=True) for i in range(2)]
            sq = fw.sbuf(st, "p0sq", [128, KC, T0], BF16)
            xo = [fw.sbuf(st, f"p0o{i}", [128, KC, T0], BF16, dma=True) for i in range(2)]
            rs = [fw.sbuf(st, f"p0r{i}", [128, T0], F32) for i in range(2)]
            ps = [fw.psum(st, f"p0ps{i}", [128, 512]) for i in range(2)]
            tiles = []
            t0 = 0
            while t0 < ntok_total:
                n = min(T0, ntok_total - t0)
                tiles.append((t0, n))
                t0 += n
            for it, (t0, n) in enumerate(tiles):
                x_, o_, r_, p_ = xt[it % 2], xo[it % 2], rs[it % 2], ps[it % 2]
                fw.dma(nextq(), x_, x_[:, :, 0:n], src, src[:, :, t0:t0 + n])
                fw.op("act", lambda e, x_=x_, n=n: e.activation(sq[:, :, 0:n], x_[:, :, 0:n], AF.Square),
                      reads=[x_], writes=[sq])

                def mm(e, p_=p_, n=n):
                    r = None
                    for kc in range(KC):
                        r = e.matmul(p_[:, 0:n], ones_b[:], sq[:, kc, 0:n], start=(kc == 0), stop=(kc == KC - 1))
                    return r
                fw.op("pe", mm, reads=[sq, ones_b], writes=[p_])
                fw.op("dve", lambda e, r_=r_, p_=p_, n=n: e.tensor_scalar(r_[:, 0:n], p_[:, 0:n], 1.0 / D, EPS, ALU.mult, ALU.add),
                      reads=[p_], writes=[r_])
                fw.op("act", lambda e, r_=r_, n=n: e.activation(r_[:, 0:n], r_[:, 0:n], AF.Sqrt), reads=[r_], writes=[r_])
                fw.op("dve", lambda e, r_=r_, n=n: e.reciprocal(r_[:, 0:n], r_[:, 0:n]), reads=[r_], writes=[r_])
                ek = "dve"

                def sc(e, x_=x_, o_=o_, r_=r_, n=n):
                    r = None
                    for kc in range(KC):
                        r = e.scalar_tensor_tensor(o_[:, kc, 0:n], x_[:, kc, 0:n], lnw[:, which, kc:kc + 1],
                                                   r_[:, 0:n], ALU.mult, ALU.mult)
                    return r
                fw.op(ek, sc, reads=[x_, r_, lnw], writes=[o_])
                fw.dma(nextq(), dst, dst[:, :, t0:t0 + n], o_, o_[:, :, 0:n])
            fw.barrier()
            fw.flush()

    p0_norm(xTf, xn_f, S, 0)
    p0_norm(xTo, xn_o, HALO + QT, 0)

    if phases < 1:
        return finish(nc, fw, top, cst, dbg, dbg_o, locals())

    w_in_v = w_in.t.rearrange("(kc p) n -> p kc n", p=128)

    def proj(xn, tok_tiles, jobs):
        GW = 1024
        with ExitStack() as st:
            wt = fw.sbuf(st, "p1w", [128, KC, GW], BF16, dma="sw")
            xt = [fw.sbuf(st, f"p1x{i}", [128, KC, TT], BF16, dma=True) for i in range(2)]
            evf = [fw.sbuf(st, f"p1ef{i}", [128, 512], F32, dma=True) for i in range(4)]
            evb = [fw.sbuf(st, f"p1eb{i}", [128, 512], BF16, dma=True) for i in range(4)]
            ps = [fw.psum(st, f"p1ps{i}", [128, 512]) for i in range(4)]
            k = {"x": 0, "p": 0}
            for (col_lo, ncols, mode, dst, d0) in jobs:
                for g0 in range(0, ncols, GW):
                    gw = min(GW, ncols - g0)
                    fw.dma("pool", wt, wt[:, :, 0:gw], w_in, w_in_v[:, :, col_lo + g0: col_lo + g0 + gw])
                    for (t0, n) in tok_tiles:
                        x_ = xt[k["x"] % 2]
                        k["x"] += 1
                        fw.dma(nextq(), x_, x_[:, :, 0:n], xn, xn[:, :, t0:t0 + n])
                        if mode == "cm":
                            for b0 in range(0, gw, 128):
                                p_ = ps[k["p"] % 4]
                                e_ = evf[k["p"] % 4]
                                k["p"] += 1

                                def mm(e, p_=p_, x_=x_, b0=b0, n=n):
                                    r = None
                                    for kc in range(KC):
                                        r = e.matmul(p_[:, 0:n], wt[:, kc, b0:b0 + 128], x_[:, kc, 0:n],
                                                     start=(kc == 0), stop=(kc == KC - 1))
                                    return r
                                fw.op("pe", mm, reads=[wt, x_], writes=[p_])
                                copy_op(evac_eng(), e_, e_[:, 0:n], p_, p_[:, 0:n])
                                blk = d0 + (g0 + b0) // 128
                                fw.dma(nextq(), dst, dst[blk, :, t0:t0 + n], e_, e_[:, 0:n])
                        else:
                            for s0 in range(0, n, 128):
                                m = min(128, n - s0)
                                for c0 in range(0, gw, 512):
                                    cw = min(512, gw - c0)
                                    p_ = ps[k["p"] % 4]
                                    e_ = (evb if mode == "tmb" else evf)[k["p"] % 4]
                                    k["p"] += 1

                                    def mm(e, p_=p_, x_=x_, s0=s0, m=m, c0=c0, cw=cw):
                                        r = None
                                        for kc in range(KC):
                                            r = e.matmul(p_[0:m, 0:cw], x_[:, kc, s0:s0 + m], wt[:, kc, c0:c0 + cw],
                                                         start=(kc == 0), stop=(kc == KC - 1))
                                        return r
                                    fw.op("pe", mm, reads=[wt, x_], writes=[p_])
                                    copy_op(evac_eng(), e_, e_[0:m, 0:cw], p_, p_[0:m, 0:cw])
                                    fw.dma(nextq(), dst, dst[t0 + s0:t0 + s0 + m, d0 + g0 + c0:d0 + g0 + c0 + cw],
                                           e_, e_[0:m, 0:cw])
            fw.barrier()
            fw.flush()

    full_tiles = [(i * TT, TT) for i in range(S // TT)]
    own_tiles = [(0, HALO)] + [(HALO + i * TT, TT) for i in range(QT // TT)]
    pre_tiles = [t for t in full_tiles if t[0] < S - QT]
    proj(xn_f, pre_tiles, [
        (OFF_DK, 2048, "cm", raw_f, 0),
        (OFF_DV, 2048, "cm", raw_f, 16),
        (OFF_DB, 32, "tmf", ba_f, 0),
    ])
    proj(xn_f, full_tiles, [
        (OFF_FK, 2048, "cm", raw_f, 32),
        (OFF_FV, 2048, "tmb", vtok_f, 0),
    ])
    proj(xn_o, own_tiles, [
        (OFF_DQ, 2048, "cm", raw_o, 0),
        (OFF_DK, 2048, "cm", raw_o, 16),
        (OFF_DV, 2048, "cm", raw_o, 32),
        (OFF_DZ, 2048, "cm", raw_o, 48),
        (OFF_FQ, 2048, "cm", raw_o, 64),
        (OFF_DB, 32, "tmf", ba_o, 0),
    ])

    if phases < 2:
        return finish(nc, fw, top, cst, dbg, dbg_o, locals())

    cst2 = ExitStack()
    convw = fw.sbuf(cst2, "convw_s", [128, 48, 4], F32, dma=True)
    fw.dma("sp", convw, convw[:], convw_d, convw_d[:])
    invf = fw.sbuf(cst2, "invf_s", [128, 1], F32, dma=True)
    fw.dma("sp", invf, invf[:], invf_d, invf_d[:])
    prot = fw.sbuf(cst2, "prot", [128, 128], BF16)
    fw.op("dve", lambda e: e.tensor_copy(prot[:], cm[:, 4, :]), reads=[cm], writes=[prot])

    def p2_conv(src, src_blk0, cw_blk0, nblk, tiles, col0, dst, kind):
        NS = 4
        with ExitStack() as st:
            xin = [fw.sbuf(st, f"p2i{i}", [128, 3 + TT], F32, dma=True) for i in range(NS)]
            y = [fw.sbuf(st, f"p2y{i}", [128, TT], F32) for i in range(NS)]
            sl = [fw.sbuf(st, f"p2s{i}", [128, TT], F32) for i in range(NS)]
            sq = [fw.sbuf(st, f"p2q{i}", [128, TT], BF16) for i in range(NS)]
            rr = [fw.sbuf(st, f"p2r{i}", [128, TT], F32) for i in range(NS)]
            ob = [fw.sbuf(st, f"p2o{i}", [128, TT], BF16 if kind == "z" else F32, dma=True) for i in range(NS)]
            ps = [fw.psum(st, f"p2ps{i}", [128, 512]) for i in range(NS)]

            def tile_gen(b, t0, n, slot):
                i_, y_, s_, q_, r_, o_, p_ = (a[slot] for a in (xin, y, sl, sq, rr, ob, ps))
                lo = col0 + t0 - 3
                if lo < 0:
                    fw.op("dve", lambda e: e.memset(i_[:, 0:3], 0.0), writes=[i_])
                    yield
                    fw.dma(nextq(), i_, i_[:, 3:3 + n], src, src[src_blk0 + b, :, col0 + t0:col0 + t0 + n], group=True)
                else:
                    fw.dma(nextq(), i_, i_[:, 0:3 + n], src, src[src_blk0 + b, :, lo:lo + 3 + n])
                yield
                if kind == "z":
                    fw.op("act", lambda e: e.activation(o_[:, 0:n], i_[:, 3:3 + n], AF.Silu), reads=[i_], writes=[o_])
                    yield
                else:
                    cb = cw_blk0 + b
                    fw.op("dve", lambda e: e.tensor_scalar(y_[:, 0:n], i_[:, 0:n], convw[:, cb, 0:1], None, ALU.mult),
                          reads=[i_, convw], writes=[y_])
                    yield
                    for j in range(1, 4):
                        fw.op("dve", lambda e, j=j: e.scalar_tensor_tensor(
                            y_[:, 0:n], i_[:, j:j + n], convw[:, cb, j:j + 1], y_[:, 0:n], ALU.mult, ALU.add),
                            reads=[i_, convw, y_], writes=[y_])
                        yield
                    if kind == "v":
                        fw.op("act", lambda e: e.activation(o_[:, 0:n], y_[:, 0:n], AF.Silu), reads=[y_], writes=[o_])
                        yield
                    else:
                        fw.op("act", lambda e: e.activation(s_[:, 0:n], y_[:, 0:n], AF.Silu), reads=[y_], writes=[s_])
                        yield
                        fw.op("act", lambda e: e.activation(q_[:, 0:n], s_[:, 0:n], AF.Square), reads=[s_], writes=[q_])
                        yield
                        fw.op("pe", lambda e: e.matmul(p_[:, 0:n], ones_b[:], q_[:, 0:n], start=True, stop=True),
                              reads=[q_, ones_b], writes=[p_])
                        yield
                        m_ = 128.0 if kind == "q" else 1.0
                        fw.op("act", lambda e: e.activation(r_[:, 0:n], p_[:, 0:n], AF.Sqrt, bias=m_ * EPS, scale=m_),
                              reads=[p_], writes=[r_])
                        yield
                        fw.op("dve", lambda e: e.reciprocal(r_[:, 0:n], r_[:, 0:n]), reads=[r_], writes=[r_])
                        yield
                        fw.op("dve", lambda e: e.tensor_tensor(o_[:, 0:n], s_[:, 0:n], r_[:, 0:n], ALU.mult),
                              reads=[s_, r_], writes=[o_])
                        yield
                fw.dma(nextq(), dst, dst[b, :, t0:t0 + n], o_, o_[:, 0:n])
                yield

            work = [(b, t0, n) for b in range(nblk) for (t0, n) in tiles]
            for w0 in range(0, len(work), NS):
                interleave(tile_gen(b, t0, n, j) for j, (b, t0, n) in enumerate(work[w0:w0 + NS]))
            fw.barrier()
            fw.flush()

    full_t = [(i * TT, TT) for i in range(S // TT)]
    own_t = [(i * TT, TT) for i in range(QT // TT)]
    pre_t = [t for t in full_t if t[0] < S - QT]
    p2_conv(raw_f, 0, 16, 16, pre_t, 0, kdn_f, "k")
    p2_conv(raw_f, 16, 32, 16, pre_t, 0, vdn_f, "v")
    p2_conv(raw_o, 0, 0, 16, own_t, HALO, qdn_o, "q")
    p2_conv(raw_o, 16, 16, 16, own_t, HALO, kdn_o, "k")
    p2_conv(raw_o, 32, 32, 16, own_t, HALO, vdn_o, "v")
    p2_conv(raw_o, 48, 0, 16, own_t, HALO, zdn_o, "z")

    def p2_rot(src, src_blk0, tiles, col0, pos_d, dst):
        with ExitStack() as st:
            pt = fw.sbuf(st, "p2pos", [128, TT], F32, dma=True)
            ca = fw.sbuf(st, "p2ca", [128, TT], F32)
            ti = fw.sbuf(st, "p2ti", [128, TT], mybir.dt.int32)
            sa = fw.sbuf(st, "p2sa", [128, TT], F32)
            cs = fw.sbuf(st, "p2cs", [128, TT], F32)
            sn = fw.sbuf(st, "p2sn", [128, TT], F32)
            xin = [fw.sbuf(st, f"p2x{i}", [128, TT], F32, dma=True) for i in range(4)]
            xb = [fw.sbuf(st, f"p2xb{i}", [128, TT], BF16) for i in range(4)]
            t1 = [fw.sbuf(st, f"p2t{i}", [128, TT], F32) for i in range(4)]
            t2 = [fw.sbuf(st, f"p2u{i}", [128, TT], F32) for i in range(4)]
            ob = [fw.sbuf(st, f"p2ro{i}", [128, TT], BF16, dma=True) for i in range(4)]
            ps = [fw.psum(st, f"p2rp{i}", [128, 512]) for i in range(4)]
            it = 0
            for (t0, n) in tiles:
                fw.dma("sp", pt, pt[:, 0:n], pos_d, pos_d[:, col0 + t0:col0 + t0 + n])
                def trig(dst, shift, n=n):
                    fw.op("dve", lambda e: e.tensor_scalar(sa[:, 0:n], pt[:, 0:n], invf[:, 0:1], shift, ALU.mult, ALU.add),
                          reads=[pt, invf], writes=[sa])
                    fw.op("dve", lambda e: e.tensor_scalar(ca[:, 0:n], sa[:, 0:n], 1.0 / (2 * PI), None, ALU.mult), reads=[sa], writes=[ca])
                    fw.op("dve", lambda e: e.tensor_copy(ti[:, 0:n], ca[:, 0:n]), reads=[ca], writes=[ti])
                    fw.op("dve", lambda e: e.tensor_copy(ca[:, 0:n], ti[:, 0:n]), reads=[ti], writes=[ca])
                    fw.op("dve", lambda e: e.scalar_tensor_tensor(sa[:, 0:n], ca[:, 0:n], -2 * PI, sa[:, 0:n], ALU.mult, ALU.add),
                          reads=[ca, sa], writes=[sa])
                    fw.op("dve", lambda e: e.tensor_scalar(ca[:, 0:n], sa[:, 0:n], PI, -2 * PI, ALU.is_gt, ALU.mult), reads=[sa], writes=[ca])
                    fw.op("dve", lambda e: e.tensor_tensor(sa[:, 0:n], sa[:, 0:n], ca[:, 0:n], ALU.add), reads=[sa, ca], writes=[sa])
                    fw.op("dve", lambda e: e.tensor_scalar(ca[:, 0:n], sa[:, 0:n], -PI, 2 * PI, ALU.is_lt, ALU.mult), reads=[sa], writes=[ca])
                    fw.op("dve", lambda e: e.tensor_tensor(sa[:, 0:n], sa[:, 0:n], ca[:, 0:n], ALU.add), reads=[sa, ca], writes=[sa])
                    fw.op("act", lambda e: e.activation(dst[:, 0:n], sa[:, 0:n], AF.Sin), reads=[sa], writes=[dst])
                trig(sn, 0.0)
                trig(cs, 0.5 * PI)
                def blk_gen(b, slot, t0=t0, n=n):
                    i_, b_, a_, u_, o_, p_ = (a[slot] for a in (xin, xb, t1, t2, ob, ps))
                    fw.dma(nextq(), i_, i_[:, 0:n], src, src[src_blk0 + b, :, col0 + t0:col0 + t0 + n])
                    yield
                    fw.op("act", lambda e: e.activation(b_[:, 0:n], i_[:, 0:n], AF.Copy), reads=[i_], writes=[b_])
                    yield
                    fw.op("pe", lambda e: e.matmul(p_[:, 0:n], prot[:], b_[:, 0:n], start=True, stop=True),
                          reads=[b_, prot], writes=[p_])
                    yield
                    fw.op("dve", lambda e: e.tensor_tensor(a_[:, 0:n], i_[:, 0:n], cs[:, 0:n], ALU.mult),
                          reads=[i_, cs], writes=[a_])
                    yield
                    fw.op("dve", lambda e: e.tensor_tensor(u_[:, 0:n], p_[:, 0:n], sn[:, 0:n], ALU.mult),
                          reads=[p_, sn], writes=[u_])
                    yield
                    fw.op("dve", lambda e: e.tensor_tensor(o_[:, 0:n], a_[:, 0:n], u_[:, 0:n], ALU.add),
                          reads=[a_, u_], writes=[o_])
                    yield
                    fw.dma(nextq(), dst, dst[b, :, t0:t0 + n], o_, o_[:, 0:n])
                    yield
                for b0 in range(0, 16, 4):
                    interleave(blk_gen(b0 + j, j) for j in range(4))
            fw.barrier()
            fw.flush()

    p2_rot(raw_f, 32, full_t, 0, posf_d, kdf_f)
    p2_rot(raw_o, 64, own_t, HALO, qpos_d, qdf_o)

    if phases < 2.5:
        return finish(nc, fw, top, cst, dbg, dbg_o, locals())

    class RR:
        def __init__(self, objs):
            self.objs = objs
            self.i = 0

        def get(self):
            o = self.objs[self.i % len(self.objs)]
            self.i += 1
            return o

    hv = fw.sbuf(cst2, "hv_s", [128, 32], F32, dma=True)
    nw = fw.sbuf(cst2, "nw_s", [128, 3], F32, dma=True)
    flg = fw.sbuf(cst2, "flg_s", [128, NCH], F32, dma=True)
    negea = fw.sbuf(cst2, "negea", [128, 16], F32)
    Sf = fw.sbuf(cst2, "Sf", [128, 16, 128], F32)
    fw.dma("sp", hv, hv[:], hv_d, hv_d[:])
    fw.dma("sp", nw, nw[:], nw_d, nw_d[:])
    fw.dma("sp", flg, flg[:], flags_d, flags_d[:])
    fw.op("act", lambda e: e.activation(negea[:], hv[:, 0:16], AF.Exp), reads=[hv], writes=[negea])
    fw.op("dve", lambda e: e.tensor_scalar(negea[:], negea[:], -1.0, None, ALU.mult), reads=[negea], writes=[negea])
    fw.op("dve", lambda e: e.memset(Sf[:], 0.0), writes=[Sf])
    Sfo = [Obj(Sf.t[:, h, :], f"Sf{h}") for h in range(16)]
    for h in range(16):
        Sfo[h].writers = dict(Sf.writers)

    def p3_dn(kd, vd, qd, zd, ba, ba_row0, nch, masked, with_out):
        with ExitStack() as st:
            pbank = [fw.psum(st, f"dnpf{i}", [128, 512]) for i in range(8)]
            NSLOT = 4 if with_out else 8
            NB = 8 // NSLOT
            PFs = [RR([b.view(b.t[:, 0:128]) for b in pbank[NB * j:NB * j + NB]]) for j in range(NSLOT)]
            PF = RR([b.view(b.t[:, 0:128]) for b in pbank[0:2]])
            TFs = [RR([fw.sbuf(st, f"dntf{j}_{i}", [128, 128], F32) for i in range(48 if with_out else 24)]) for j in range(NSLOT)]
            OB = RR([fw.sbuf(st, f"dnob{i}", [128, 128], BF16, dma=True) for i in range(8)])
            bat = [fw.sbuf(st, f"dnba{i}", [128, 32], F32, dma=True) for i in range(2)]
            kt = [fw.sbuf(st, f"dnk{i}", [128, 16, 128], F32, dma=True) for i in range(2)]
            vt = [fw.sbuf(st, f"dnv{i}", [128, 16, 128], F32, dma=True) for i in range(2)]
            qt = [fw.sbuf(st, f"dnq{i}", [128, 16, 128], F32, dma=True) for i in range(2)]
            zt = [fw.sbuf(st, f"dnz{i}", [128, 16, 128], BF16, dma=True) for i in range(2)]
            SC = RR([fw.sbuf(st, f"dnsc{i}", [128, 16], F32) for i in range(32)])
            tri = cm.t[:, 0, :]
            m_il = cm.t[:, 1, :]
            m_sl = cm.t[:, 2, :]
            idf = cm.t[:, 3, :]

            def ew(ek, fn, reads, writes):
                fw.op(ek, fn, reads=reads, writes=writes)

            for n in range(nch):
                c0 = n * 128
                ba_, k_, v_, q_, z_ = bat[n % 2], kt[n % 2], vt[n % 2], qt[n % 2], zt[n % 2]
                fw.dma("sp", ba_, ba_[:], ba, ba[ba_row0 + c0:ba_row0 + c0 + 128, :])
                fw.dma("sp", k_, k_[:], kd, kd[:, :, c0:c0 + 128].rearrange("h p t -> p h t"))
                fw.dma("act", v_, v_[:], vd, vd[:, :, c0:c0 + 128].rearrange("h p t -> p h t"))
                if with_out:
                    fw.dma("sp", q_, q_[:], qd, qd[:, :, c0:c0 + 128].rearrange("h p t -> p h t"))
                    fw.dma("act", z_, z_[:], zd, zd[:, :, c0:c0 + 128].rearrange("h p t -> p h t"))
                beta, negb, xg, ax, ex, g = (SC.get() for _ in range(6))
                ew("act", lambda e, beta=beta, ba_=ba_: e.activation(beta[:], ba_[:, 0:16], AF.Sigmoid), [ba_], [beta])
                ew("dve", lambda e, negb=negb, beta=beta: e.tensor_scalar(negb[:], beta[:], -1.0, None, ALU.mult), [beta], [negb])
                ew("dve", lambda e, xg=xg, ba_=ba_: e.tensor_tensor(xg[:], ba_[:, 16:32], hv[:, 16:32], ALU.add), [ba_, hv], [xg])
                ew("act", lambda e, ax=ax, xg=xg: e.activation(ax[:], xg[:], AF.Abs), [xg], [ax])
                ew("act", lambda e, ex=ex, ax=ax: e.activation(ex[:], ax[:], AF.Exp, scale=-1.0), [ax], [ex])
                ew("act", lambda e, ex=ex: e.activation(ex[:], ex[:], AF.Ln, bias=1.0), [ex], [ex])
                ew("dve", lambda e, xg=xg: e.tensor_scalar(xg[:], xg[:], 0.0, None, ALU.max), [xg], [xg])
                ew("dve", lambda e, xg=xg, ex=ex: e.tensor_tensor(xg[:], xg[:], ex[:], ALU.add), [xg, ex], [xg])
                ew("dve", lambda e, g=g, xg=xg: e.tensor_tensor(g[:], xg[:], negea[:], ALU.mult), [xg, negea], [g])
                pg = PF.get()
                pl = PF.get()
                ew("pe", lambda e, pg=pg, g=g: e.matmul(pg[:, 0:16], tri, g[:], start=True, stop=True), [g, cm], [pg])
                ew("pe", lambda e, pl=pl, g=g: e.matmul(pl[:, 0:16], ones_f[:], g[:], start=True, stop=True), [g, ones_f], [pl])
                gcol, egc, begc, ekt, adec = (SC.get() for _ in range(5))
                ew("act", lambda e, gcol=gcol, pg=pg: e.activation(gcol[:], pg[:, 0:16], AF.Copy), [pg], [gcol])
                ew("act", lambda e, egc=egc, pg=pg: e.activation(egc[:], pg[:, 0:16], AF.Exp), [pg], [egc])
                ew("dve", lambda e, begc=begc, egc=egc, beta=beta: e.tensor_tensor(begc[:], egc[:], beta[:], ALU.mult), [egc, beta], [begc])
                ew("dve", lambda e, ekt=ekt, pl=pl, gcol=gcol: e.tensor_tensor(ekt[:], pl[:, 0:16], gcol[:], ALU.subtract), [pl, gcol], [ekt])
                ew("act", lambda e, ekt=ekt: e.activation(ekt[:], ekt[:], AF.Exp), [ekt], [ekt])
                ew("act", lambda e, adec=adec, pl=pl: e.activation(adec[:], pl[:, 0:16], AF.Exp), [pl], [adec])
                if masked:
                    ew("dve", lambda e, ekt=ekt, n=n: e.tensor_scalar(ekt[:], ekt[:], flg[:, n:n + 1], None, ALU.mult), [ekt, flg], [ekt])
                    ew("dve", lambda e, adec=adec, n=n: e.tensor_scalar(adec[:], adec[:], -1.0, flg[:, n:n + 1], ALU.add, ALU.mult), [adec, flg], [adec])
                    ew("dve", lambda e, adec=adec: e.tensor_scalar(adec[:], adec[:], 1.0, None, ALU.add), [adec], [adec])
                def head_gen(h, slot, k_=k_, v_=v_, q_=q_, z_=z_, g=g, gcol=gcol, negb=negb, begc=begc, ekt=ekt, beta=beta, adec=adec, c0=c0):
                    TF = TFs[slot]
                    PF = PFs[slot]
                    kT = k_.t[:, h, :]
                    vT = v_.t[:, h, :]
                    gmat = TF.get()
                    ew("dve", lambda e, gmat=gmat, g=g, h=h: e.tensor_scalar(gmat[:], ones_f[:], g[:, h:h + 1], None, ALU.mult), [g, ones_f], [gmat])
                    yield
                    pgr = PF.get()
                    ew("pe", lambda e, pgr=pgr, gmat=gmat: e.matmul(pgr[:], gmat[:], tri, start=True, stop=True), [gmat, cm], [pgr])
                    yield
                    if with_out:
                        eg = TF.get()
                        ew("act", lambda e, eg=eg, pgr=pgr: e.activation(eg[:], pgr[:], AF.Exp), [pgr], [eg])
                        yield
                    dm = TF.get()
                    ew("dve", lambda e, dm=dm, pgr=pgr, gcol=gcol, h=h: e.tensor_scalar(dm[:], pgr[:], gcol[:, h:h + 1], 0.0, ALU.subtract, ALU.max), [pgr, gcol], [dm])
                    yield
                    ew("act", lambda e, dm=dm: e.activation(dm[:], dm[:], AF.Exp, scale=-1.0), [dm], [dm])
                    yield
                    ls = TF.get()
                    ew("dve", lambda e, ls=ls, dm=dm: e.tensor_tensor(ls[:], dm[:], m_sl, ALU.mult), [dm, cm], [ls])
                    yield
                    pkk = PF.get()
                    ew("pe", lambda e, pkk=pkk, kT=kT: e.matmul(pkk[:], kT, kT, start=True, stop=True), [k_], [pkk])
                    yield
                    N = TF.get()
                    ew("dve", lambda e, N=N, pkk=pkk, negb=negb, ls=ls, h=h: e.scalar_tensor_tensor(N[:], pkk[:], negb[:, h:h + 1], ls[:], ALU.mult, ALU.mult), [pkk, negb, ls], [N])
                    yield
                    pbt = PF.get()
                    ew("pe", lambda e, pbt=pbt, N=N: e.transpose(pbt[:], N[:], idf), [N, cm], [pbt])
                    yield
                    B = TF.get()
                    ew("act", lambda e, B=B, pbt=pbt: e.activation(B[:], pbt[:], AF.Copy), [pbt], [B])
                    yield
                    Pf_ = TF.get()
                    ew("dve", lambda e, Pf_=Pf_, B=B: e.tensor_tensor(Pf_[:], B[:], idf, ALU.add), [B, cm], [Pf_])
                    yield
                    for lev in range(6):
                        B2, N2 = TF.get(), TF.get()
                        if lev < 5:
                            p1 = PF.get()
                            ew("pe", lambda e, p1=p1, N=N, B=B: e.matmul(p1[:], N[:], B[:], start=True, stop=True), [N, B], [p1])
                            yield
                            ew("act", lambda e, B2=B2, p1=p1: e.activation(B2[:], p1[:], AF.Copy), [p1], [B2])
                            yield
                        p2 = PF.get()
                        ew("pe", lambda e, p2=p2, N=N, B=B: e.matmul(p2[:], B[:], N[:], start=True, stop=True), [N, B], [p2])
                        yield
                        ew("act", lambda e, N2=N2, p2=p2: e.activation(N2[:], p2[:], AF.Copy), [p2], [N2])
                        yield
                        B, N = B2, N2
                        p3 = PF.get()
                        ew("pe", lambda e, p3=p3, N=N, Pf_=Pf_: e.matmul(p3[:], N[:], Pf_[:], start=True, stop=True), [N, Pf_], [p3])
                        yield
                        Pn = TF.get()
                        ew("dve", lambda e, Pn=Pn, Pf_=Pf_, p3=p3: e.tensor_tensor(Pn[:], Pf_[:], p3[:], ALU.add), [Pf_, p3], [Pn])
                        yield
                        Pf_ = Pn
                    Xk, ktil, Xv = TF.get(), TF.get(), TF.get()
                    pkt = PF.get()
                    ew("pe", lambda e, pkt=pkt, kT=kT: e.transpose(pkt[:], kT, idf), [k_, cm], [pkt])
                    yield
                    ew("dve", lambda e, Xk=Xk, pkt=pkt, begc=begc, h=h: e.tensor_scalar(Xk[:], pkt[:], begc[:, h:h + 1], None, ALU.mult), [pkt, begc], [Xk])
                    yield
                    ew("dve", lambda e, ktil=ktil, pkt=pkt, ekt=ekt, h=h: e.tensor_scalar(ktil[:], pkt[:], ekt[:, h:h + 1], None, ALU.mult), [pkt, ekt], [ktil])
                    yield
                    pvt = PF.get()
                    ew("pe", lambda e, pvt=pvt, vT=vT: e.transpose(pvt[:], vT, idf), [v_, cm], [pvt])
                    yield
                    ew("act", lambda e, Xv=Xv, pvt=pvt, beta=beta, h=h: e.activation(Xv[:], pvt[:], AF.Copy, scale=beta[:, h:h + 1]), [pvt, beta], [Xv])
                    yield
                    Uf, WT = TF.get(), TF.get()
                    pu = PF.get()
                    ew("pe", lambda e, pu=pu, Pf_=Pf_, Xv=Xv: e.matmul(pu[:], Pf_[:], Xv[:], start=True, stop=True), [Pf_, Xv], [pu])
                    yield
                    ew("act", lambda e, Uf=Uf, pu=pu: e.activation(Uf[:], pu[:], AF.Copy), [pu], [Uf])
                    yield
                    pw = PF.get()
                    ew("pe", lambda e, pw=pw, Pf_=Pf_, Xk=Xk: e.matmul(pw[:], Xk[:], Pf_[:], start=True, stop=True), [Pf_, Xk], [pw])
                    yield
                    ew("dve", lambda e, WT=WT, pw=pw: e.tensor_copy(WT[:], pw[:]), [pw], [WT])
                    yield
                    pr = PF.get()
                    ew("pe", lambda e, pr=pr, WT=WT, h=h: e.matmul(pr[:], WT[:], Sfo[h][:], start=True, stop=True), [WT, Sfo[h]], [pr])
                    yield
                    vnew = TF.get()
                    ew("dve", lambda e, vnew=vnew, Uf=Uf, pr=pr: e.tensor_tensor(vnew[:], Uf[:], pr[:], ALU.subtract), [Uf, pr], [vnew])
                    yield
                    if with_out:
                        qT = q_.t[:, h, :]
                        qtil = TF.get()
                        ew("dve", lambda e, qtil=qtil, qT=qT, eg=eg: e.tensor_tensor(qtil[:], qT, eg[:], ALU.mult), [q_, eg], [qtil])
                        yield
                        lm = TF.get()
                        ew("dve", lambda e, lm=lm, dm=dm: e.tensor_tensor(lm[:], dm[:], m_il, ALU.mult), [dm, cm], [lm])
                        yield
                        pqk = PF.get()
                        ew("pe", lambda e, pqk=pqk, qT=qT, kT=kT: e.matmul(pqk[:], qT, kT, start=True, stop=True), [q_, k_], [pqk])
                        yield
                        attn = TF.get()
                        ew("dve", lambda e, attn=attn, pqk=pqk, lm=lm: e.tensor_tensor(attn[:], pqk[:], lm[:], ALU.mult), [pqk, lm], [attn])
                        yield
                        pat = PF.get()
                        ew("pe", lambda e, pat=pat, attn=attn: e.transpose(pat[:], attn[:], idf), [attn, cm], [pat])
                        yield
                        attnT = TF.get()
                        ew("act", lambda e, attnT=attnT, pat=pat: e.activation(attnT[:], pat[:], AF.Copy), [pat], [attnT])
                        yield
                        po = PF.get()

                        def omm(e, po=po, h=h, qtil=qtil, vnew=vnew, attnT=attnT):
                            e.matmul(po[:], Sfo[h][:], qtil[:], start=True, stop=False)
                            return e.matmul(po[:], vnew[:], attnT[:], start=False, stop=True)
                        ew("pe", omm, [Sfo[h], qtil, vnew, attnT], [po])
                        yield
                        of, osq = TF.get(), TF.get()
                        ew("act", lambda e, of=of, po=po: e.activation(of[:], po[:], AF.Copy), [po], [of])
                        yield
                        ew("act", lambda e, osq=osq, of=of: e.activation(osq[:], of[:], AF.Square), [of], [osq])
                        yield
                        pss = PF.get()
                        ew("pe", lambda e, pss=pss, osq=osq: e.matmul(pss[:], ones_f[:], osq[:], start=True, stop=True), [osq, ones_f], [pss])
                        yield
                        rr_ = TF.get()
                        ew("dve", lambda e, rr_=rr_, pss=pss: e.tensor_scalar(rr_[:], pss[:], 1.0 / 128, EPS, ALU.mult, ALU.add), [pss], [rr_])
                        yield
                        ew("act", lambda e, rr_=rr_: e.activation(rr_[:], rr_[:], AF.Sqrt), [rr_], [rr_])
                        yield
                        ew("dve", lambda e, rr_=rr_: e.reciprocal(rr_[:], rr_[:]), [rr_], [rr_])
                        yield
                        of2 = TF.get()
                        ew("dve", lambda e, of2=of2, of=of, rr_=rr_: e.tensor_tensor(of2[:], of[:], rr_[:], ALU.mult), [of, rr_], [of2])
                        yield
                        ob_ = OB.get()
                        ew("dve", lambda e, ob_=ob_, of2=of2, z_=z_, h=h: e.scalar_tensor_tensor(ob_[:], of2[:], nw[:, 0:1], z_[:, h, :], ALU.mult, ALU.mult), [of2, nw, z_], [ob_])
                        yield
                        fw.dma(nextq(), mixT, mixT[:, h, c0:c0 + 128], ob_, ob_[:])
                        yield
                    pds = PF.get()
                    ew("pe", lambda e, pds=pds, ktil=ktil, vnew=vnew: e.matmul(pds[:], ktil[:], vnew[:], start=True, stop=True), [ktil, vnew], [pds])
                    yield
                    ew("dve", lambda e, h=h, adec=adec, pds=pds: e.scalar_tensor_tensor(Sfo[h][:], Sfo[h][:], adec[:, h:h + 1], pds[:], ALU.mult, ALU.add), [Sfo[h], adec, pds], [Sfo[h]])
                    yield

                for h0 in range(0, 16, NSLOT):
                    interleave(head_gen(h0 + j, j) for j in range(NSLOT))
            fw.barrier()
            fw.flush()

    p3_dn(kdn_f, vdn_f, None, None, ba_f, 0, NCH - NCO, True, False)
    if phases == 2.5:
        return finish(nc, fw, top, cst, dbg, dbg_o, locals())
    p3_dn(kdn_o, vdn_o, qdn_o, zdn_o, ba_o, HALO, NCO, False, True)

    if phases < 4:
        return finish(nc, fw, top, cst, dbg, dbg_o, locals())

    def rsqrt_ops(o, ap):
        fw.op("act", lambda e: e.activation(ap, ap, AF.Sqrt), reads=[o], writes=[o])
        fw.op("dve", lambda e: e.reciprocal(ap, ap), reads=[o], writes=[o])

    lamv = fw.sbuf(cst2, "lamv_s", [128, 4], F32, dma=True)
    lam2 = fw.sbuf(cst2, "lam2", [128, 2], F32)
    neglam = fw.sbuf(cst2, "neglam", [128, 1], F32)
    nwdf = fw.sbuf(cst2, "nwdf", [128, 2], F32)
    qpos = fw.sbuf(cst2, "qpos_s", [128, HALO + QT], F32, dma=True)
    kpos = fw.sbuf(cst2, "kpos_s", [128, NCH], F32, dma=True)
    fw.dma("sp", lamv, lamv[:], lamv_d, lamv_d[:])
    fw.dma("sp", qpos, qpos[:], qpos_d, qpos_d[:])
    fw.dma("sp", kpos, kpos[:], kpos_d, kpos_d[:])
    fw.op("dve", lambda e: e.tensor_tensor(lam2[:, 0:1], lamv[:, 0:1], lamv[:, 1:2], ALU.mult), reads=[lamv], writes=[lam2])
    fw.op("dve", lambda e: e.tensor_tensor(lam2[:, 1:2], lamv[:, 2:3], lamv[:, 3:4], ALU.mult), reads=[lamv, lam2], writes=[lam2])
    fw.op("dve", lambda e: e.tensor_scalar(nwdf[:], nw[:, 1:3], 1.0 - LAMBDA_INIT, None, ALU.mult), reads=[nw], writes=[nwdf])

    def p4_attn():
        SCALE = 128.0 ** -0.5
        with ExitStack() as st:
            NPST, NBUF, DEPTH = 5, 6, 3
            pst = [fw.psum(st, f"ap{i}", [128, 512]) for i in range(NPST)]
            pacc = [[fw.psum(st, f"aa{i}_{j}", [128, 512]) for j in range(3)] for i in range(1)]
            vt = fw.sbuf(st, "avt", [128, NCH, 256], BF16, dma=True)
            kt = [fw.sbuf(st, f"akt{i}", [128, S], BF16, dma=True) for i in range(2)]
            qt = [fw.sbuf(st, f"aqt{i}", [128, QT], BF16, dma=True) for i in range(2)]
            O1 = fw.sbuf(st, "aO1", [128, 2, QT], F32)
            pe_t = [fw.sbuf(st, f"ape{i}", [128, TT], F32) for i in range(NBUF)]
            pm_t = [fw.sbuf(st, f"apm{i}", [128, TT], BF16) for i in range(NBUF)]
            rs_t = fw.sbuf(st, "ars", [128, TT], F32)
            cb_t = [fw.sbuf(st, f"acb{i}", [128, TT], F32) for i in range(2)]
            sq_t = [fw.sbuf(st, f"asq{i}", [128, TT], BF16) for i in range(2)]
            ob_t = [fw.sbuf(st, f"aob{i}", [128, TT], BF16, dma=True) for i in range(4)]
            fw.op("pe", lambda e: e.matmul(pst[0][:, 0:2], ones_f[:], lam2[:], start=True, stop=True), reads=[lam2, ones_f], writes=[pst[0]])
            fw.op("act", lambda e: e.activation(lam2[:], pst[0][:, 0:2], AF.Exp), reads=[pst[0]], writes=[lam2])
            fw.op("dve", lambda e: e.tensor_tensor(neglam[:], lam2[:, 1:2], lam2[:, 0:1], ALU.subtract), reads=[lam2], writes=[neglam])
            fw.op("dve", lambda e: e.tensor_scalar(neglam[:], neglam[:], -LAMBDA_INIT, None, ALU.add), reads=[neglam], writes=[neglam])
            it = 0
            ia = 0
            io = 0
            for h in range(NH_DF):
                fw.dma("sp", vt, vt[:], vtok_f, vtok_f[:, h * 256:(h + 1) * 256].rearrange("(kb p) c -> p kb c", p=128))
                for s_ in range(2):
                    sub = 2 * h + s_
                    k_, q_ = kt[sub % 2], qt[sub % 2]
                    fw.dma("act", k_, k_[:], kdf_f, kdf_f[sub])
                    fw.dma("sp", q_, q_[:], qdf_o, qdf_o[sub])
                    for q0 in range(0, QT, TT):
                        n = TT
                        acc = pacc[0]
                        ia += 1
                        NKB = min(NCH, NCH - NCO + (q0 + TT) // 128)
                        def front(kb, acc=acc, k_=k_, q_=q_, q0=q0):
                            nonlocal it
                            p_ = pst[it % NPST]
                            e_ = pe_t[it % NBUF]
                            m_ = pm_t[it % NBUF]
                            it += 1
                            fw.op("pe", lambda e: e.matmul(p_[:, 0:n], k_[:, kb * 128:(kb + 1) * 128], q_[:, q0:q0 + n], start=True, stop=True),
                                  reads=[k_, q_], writes=[p_])
                            fw.op("act", lambda e: e.activation(e_[:, 0:n], p_[:, 0:n], AF.Exp, scale=SCALE), reads=[p_], writes=[e_])
                            fw.op("dve", lambda e: e.scalar_tensor_tensor(m_[:, 0:n], qpos[:, HALO + q0:HALO + q0 + n], kpos[:, kb:kb + 1], e_[:, 0:n], ALU.is_ge, ALU.mult),
                                  reads=[e_, qpos, kpos], writes=[m_])
                            return m_

                        def back(kb, m_, acc=acc, NKB=NKB):
                            def pv(e):
                                e.matmul(acc[0][:, 0:n], vt[:, kb, 0:128], m_[:, 0:n], start=(kb == 0), stop=(kb == NKB - 1))
                                e.matmul(acc[1][:, 0:n], vt[:, kb, 128:256], m_[:, 0:n], start=(kb == 0), stop=(kb == NKB - 1))
                                return e.matmul(acc[2][:, 0:n], ones_b[:], m_[:, 0:n], start=(kb == 0), stop=(kb == NKB - 1))
                            fw.op("pe", pv, reads=[vt, m_, ones_b], writes=[acc[0], acc[1], acc[2]])

                        pend = []
                        for kb in range(NKB):
                            pend.append((kb, front(kb)))
                            if len(pend) > DEPTH:
                                back(*pend.pop(0))
                        for pb_ in pend:
                            back(*pb_)
                        fw.op("dve", lambda e, acc=acc: e.reciprocal(rs_t[:, 0:n], acc[2][:, 0:n]), reads=[acc[2]], writes=[rs_t])
                        for c in range(2):
                            if s_ == 0:
                                fw.op("dve", lambda e, acc=acc, c=c, q0=q0: e.tensor_tensor(O1[:, c, q0:q0 + n], acc[c][:, 0:n], rs_t[:, 0:n], ALU.mult),
                                      reads=[acc[c], rs_t], writes=[O1])
                            else:
                                cb_ = cb_t[c]
                                fw.op("dve", lambda e, acc=acc, c=c, cb_=cb_: e.tensor_tensor(cb_[:, 0:n], acc[c][:, 0:n], rs_t[:, 0:n], ALU.mult),
                                      reads=[acc[c], rs_t], writes=[cb_])
                                fw.op("dve", lambda e, c=c, cb_=cb_, q0=q0: e.scalar_tensor_tensor(cb_[:, 0:n], cb_[:, 0:n], neglam[:, 0:1], O1[:, c, q0:q0 + n], ALU.mult, ALU.add),
                                      reads=[cb_, neglam, O1], writes=[cb_])
                                fw.op("act", lambda e, c=c, cb_=cb_: e.activation(sq_t[c][:, 0:n], cb_[:, 0:n], AF.Square), reads=[cb_], writes=[sq_t[c]])
                        if s_ == 1:
                            p_ = pst[it % NPST]
                            it += 1

                            def ssm(e, p_=p_):
                                e.matmul(p_[:, 0:n], ones_b[:], sq_t[0][:, 0:n], start=True, stop=False)
                                return e.matmul(p_[:, 0:n], ones_b[:], sq_t[1][:, 0:n], start=False, stop=True)
                            fw.op("pe", ssm, reads=[sq_t[0], sq_t[1], ones_b], writes=[p_])
                            fw.op("dve", lambda e, p_=p_: e.tensor_scalar(rs_t[:, 0:n], p_[:, 0:n], 1.0 / 256, EPS, ALU.mult, ALU.add), reads=[p_], writes=[rs_t])
                            rsqrt_ops(rs_t, rs_t[:, 0:n])
                            for c in range(2):
                                o_ = ob_t[io % 4]
                                io += 1
                                fw.op("dve", lambda e, o_=o_, c=c: e.scalar_tensor_tensor(o_[:, 0:n], cb_t[c][:, 0:n], nwdf[:, c:c + 1], rs_t[:, 0:n], ALU.mult, ALU.mult),
                                      reads=[cb_t[c], nwdf, rs_t], writes=[o_])
                                fw.dma(nextq(), mixT, mixT[:, 16 + 2 * h + c, q0:q0 + n], o_, o_[:, 0:n])
            fw.barrier()
            fw.flush()

    p4_attn()

    if phases < 5:
        return finish(nc, fw, top, cst, dbg, dbg_o, locals())

    cst2.close()
    rstd1 = fw.sbuf(cst, "rstd1", [128, QT], F32)
    w_out_v = w_out.t.rearrange("(kc p) n -> p kc n", p=128)
    w_gate_v = w_gate.t.rearrange("(kc p) n -> p kc n", p=128)
    w_up_v = w_up.t.rearrange("(kc p) n -> p kc n", p=128)
    w_down_v = w_down.t.rearrange("(hb p) n -> p hb n", p=128)

    def p5a():
        with ExitStack() as st:
            mx = fw.sbuf(st, "omx", [128, KC, TT], BF16, dma=True)
            wo = [fw.sbuf(st, f"owo{i}", [128, KC, 512], BF16, dma="sw") for i in range(2)]
            xr = [fw.sbuf(st, f"oxr{i}", [128, TT], F32, dma=True) for i in range(3)]
            sq = [fw.sbuf(st, f"osq{i}", [128, TT], BF16) for i in range(2)]
            ps = [fw.psum(st, f"ops{i}", [128, 512]) for i in range(3)]
            pss = fw.psum(st, "opss", [128, 512])
            it = 0
            ig = 0
            for t0 in range(0, QT, TT):
                n = TT
                fw.dma("sp", mx, mx[:], mixT, mixT[:, :, t0:t0 + n])
                for cg in range(8):
                    w_ = wo[ig % 2]
                    ig += 1
                    fw.dma("pool", w_, w_[:], w_out, w_out_v[:, :, cg * 512:(cg + 1) * 512])
                    for cb in range(4):
                        blk = cg * 4 + cb
                        p_, x_, s_ = ps[it % 3], xr[it % 3], sq[it % 2]
                        it += 1
                        fw.dma(nextq(), x_, x_[:, 0:n], xTo, xTo[:, blk, HALO + t0:HALO + t0 + n])

                        def mm(e, p_=p_, w_=w_, cb=cb):
                            r = None
                            for kc in range(KC):
                                r = e.matmul(p_[:, 0:n], w_[:, kc, cb * 128:(cb + 1) * 128], mx[:, kc, 0:n], start=(kc == 0), stop=(kc == KC - 1))
                            return r
                        fw.op("pe", mm, reads=[w_, mx], writes=[p_])
                        fw.op("dve", lambda e, x_=x_, p_=p_: e.tensor_tensor(x_[:, 0:n], x_[:, 0:n], p_[:, 0:n], ALU.add), reads=[x_, p_], writes=[x_])
                        fw.op("act", lambda e, s_=s_, x_=x_: e.activation(s_[:, 0:n], x_[:, 0:n], AF.Square), reads=[x_], writes=[s_])
                        fw.op("pe", lambda e, s_=s_, blk=blk: e.matmul(pss[:, 0:n], ones_b[:], s_[:, 0:n], start=(blk == 0), stop=(blk == KC - 1)),
                              reads=[s_, ones_b], writes=[pss])
                        fw.dma(nextq(), x1T, x1T[:, blk, t0:t0 + n], x_, x_[:, 0:n])
                fw.op("dve", lambda e, t0=t0: e.tensor_scalar(rstd1[:, t0:t0 + n], pss[:, 0:n], 1.0 / D, EPS, ALU.mult, ALU.add), reads=[pss], writes=[rstd1])
                rsqrt_ops(rstd1, rstd1[:, t0:t0 + n])
            fw.barrier()
            fw.flush()

    p5a()

    def p5b():
        HH = HB // 2
        with ExitStack() as st:
            hT = fw.sbuf(st, "fh", [128, KC, TT], BF16)
            act = fw.sbuf(st, "fact", [128, HB, TT], BF16)
            wg = [fw.sbuf(st, f"fwg{i}", [128, KC, 128], BF16, dma="sw") for i in range(2)]
            wu = [fw.sbuf(st, f"fwu{i}", [128, KC, 128], BF16, dma="sw") for i in range(2)]
            wd = [fw.sbuf(st, f"fwd{i}", [128, HH, 128], BF16, dma="sw") for i in range(2)]
            xr = [fw.sbuf(st, f"fxr{i}", [128, TT], F32, dma=True) for i in range(3)]
            sg = [fw.sbuf(st, f"fsg{i}", [128, TT], F32) for i in range(2)]
            sq = [fw.sbuf(st, f"fsq{i}", [128, TT], BF16) for i in range(2)]
            r2 = fw.sbuf(st, "fr2", [128, TT], F32)
            pg = [fw.psum(st, f"fpg{i}", [128, 512]) for i in range(2)]
            pu = [fw.psum(st, f"fpu{i}", [128, 512]) for i in range(2)]
            pd = [fw.psum(st, f"fpd{i}", [128, 512]) for i in range(2)]
            pss = fw.psum(st, "fpss", [128, 512])
            ix = 0
            iw = 0
            for t0 in range(0, QT, TT):
                n = TT
                for kc in range(KC):
                    x_ = xr[ix % 3]
                    ix += 1
                    fw.dma(nextq(), x_, x_[:, 0:n], x1T, x1T[:, kc, t0:t0 + n])
                    fw.op("dve", lambda e, x_=x_, kc=kc, t0=t0: e.scalar_tensor_tensor(hT[:, kc, 0:n], x_[:, 0:n], lnw[:, 1, kc:kc + 1], rstd1[:, t0:t0 + n], ALU.mult, ALU.mult),
                          reads=[x_, lnw, rstd1], writes=[hT])
                for hb in range(HB):
                    g_, u_, pg_, pu_, sg_ = wg[hb % 2], wu[hb % 2], pg[hb % 2], pu[hb % 2], sg[hb % 2]
                    fw.dma("pool", g_, g_[:], w_gate, w_gate_v[:, :, hb * 128:(hb + 1) * 128])
                    fw.dma("pool", u_, u_[:], w_up, w_up_v[:, :, hb * 128:(hb + 1) * 128])

                    def mmg(e, p_=pg_, w_=g_):
                        r = None
                        for kc in range(KC):
                            r = e.matmul(p_[:, 0:n], w_[:, kc, :], hT[:, kc, 0:n], start=(kc == 0), stop=(kc == KC - 1))
                        return r

                    def mmu(e, p_=pu_, w_=u_):
                        r = None
                        for kc in range(KC):
                            r = e.matmul(p_[:, 0:n], w_[:, kc, :], hT[:, kc, 0:n], start=(kc == 0), stop=(kc == KC - 1))
                        return r
                    fw.op("pe", mmg, reads=[g_, hT], writes=[pg_])
                    fw.op("pe", mmu, reads=[u_, hT], writes=[pu_])
                    fw.op("act", lambda e, sg_=sg_, pg_=pg_: e.activation(sg_[:, 0:n], pg_[:, 0:n], AF.Silu), reads=[pg_], writes=[sg_])
                    fw.op("dve", lambda e, sg_=sg_, pu_=pu_, hb=hb: e.tensor_tensor(act[:, hb, 0:n], sg_[:, 0:n], pu_[:, 0:n], ALU.mult),
                          reads=[sg_, pu_], writes=[act])
                for cb in range(KC):
                    p_ = pd[cb % 2]
                    x_ = xr[ix % 3]
                    ix += 1
                    fw.dma(nextq(), x_, x_[:, 0:n], x1T, x1T[:, cb, t0:t0 + n])
                    for half in range(2):
                        w_ = wd[iw % 2]
                        iw += 1
                        fw.dma("pool", w_, w_[:], w_down, w_down_v[:, half * HH:(half + 1) * HH, cb * 128:(cb + 1) * 128])

                        def mmd(e, p_=p_, w_=w_, half=half):
                            r = None
                            for j in range(HH):
                                hb = half * HH + j
                                r = e.matmul(p_[:, 0:n], w_[:, j, :], act[:, hb, 0:n], start=(hb == 0), stop=(hb == HB - 1))
                            return r
                        fw.op("pe", mmd, reads=[w_, act], writes=[p_])
                    s_ = sq[cb % 2]
                    fw.op("dve", lambda e, x_=x_, p_=p_: e.tensor_tensor(x_[:, 0:n], x_[:, 0:n], p_[:, 0:n], ALU.add), reads=[x_, p_], writes=[x_])
                    fw.op("act", lambda e, s_=s_, x_=x_: e.activation(s_[:, 0:n], x_[:, 0:n], AF.Square), reads=[x_], writes=[s_])
                    fw.op("pe", lambda e, s_=s_, cb=cb: e.matmul(pss[:, 0:n], ones_b[:], s_[:, 0:n], start=(cb == 0), stop=(cb == KC - 1)),
                          reads=[s_, ones_b], writes=[pss])
                    fw.dma(nextq(), x2T, x2T[:, cb, t0:t0 + n], x_, x_[:, 0:n])
                fw.op("dve", lambda e: e.tensor_scalar(r2[:, 0:n], pss[:, 0:n], 1.0 / D, EPS, ALU.mult, ALU.add), reads=[pss], writes=[r2])
                rsqrt_ops(r2, r2[:, 0:n])
                for cb in range(KC):
                    x_ = xr[ix % 3]
                    ix += 1
                    fw.dma(nextq(), x_, x_[:, 0:n], x2T, x2T[:, cb, t0:t0 + n])
                    fw.op("dve", lambda e, x_=x_, cb=cb: e.scalar_tensor_tensor(x_[:, 0:n], x_[:, 0:n], lnw[:, 2, cb:cb + 1], r2[:, 0:n], ALU.mult, ALU.mult),
                          reads=[x_, lnw, r2], writes=[x_])
                    fw.dma(nextq(), outT, outT[:, cb, t0:t0 + n], x_, x_[:, 0:n])
            fw.barrier()
            fw.flush()

    p5b()
    return finish(nc, fw, top, cst, dbg, dbg_o, locals())


def _prep(inputs, S):
    x = np.asarray(inputs["x"], np.float32)
    QT = S // 4
    NCH = S // 128
    g = lambda k: np.asarray(inputs[k], np.float32)
    w_in = np.ascontiguousarray(g("w_in")[0])
    w_out = np.ascontiguousarray(g("w_out")[0])
    w_gate = np.ascontiguousarray(g("w_gate")[0])
    w_up = np.ascontiguousarray(g("w_up")[0])
    w_down = np.ascontiguousarray(g("w_down")[0])
    lnw = np.stack([g("ln_mix_w")[0], g("ln_ffn_w")[0], g("ln_final_w")], 0).reshape(3, 32, 128).transpose(2, 0, 1).copy()
    convw = g("conv_w")[0].reshape(4, 48, 128).transpose(2, 1, 0).copy()
    hv = np.broadcast_to(np.concatenate([g("a_log")[0], g("dt_bias")[0]])[None, :], (128, 32)).copy()
    dfw = g("df_norm_w")[0]
    nw = np.stack([g("dn_norm_w")[0], dfw[:128], dfw[128:]], 1).copy()
    lamv = np.stack([g("lambda_q1")[0], g("lambda_k1")[0], g("lambda_q2")[0], g("lambda_k2")[0]], 1).copy()
    j = np.arange(128)[:, None]
    i = np.arange(128)[None, :]
    prot = np.zeros((128, 128), np.float32)
    for m in range(16):
        prot[m + 16, m] = -1.0
        prot[m, m + 16] = 1.0
    cmask = np.stack([(j <= i), (i <= j), (i < j), (i == j), prot], 0).astype(np.float32).transpose(1, 0, 2).copy()
    kpos = (np.arange(NCH)[None, :] * 128 + np.arange(128)[:, None]).astype(np.float32)
    posf = np.broadcast_to(np.arange(S, dtype=np.float32)[None, :], (128, S)).copy()
    invf = np.zeros((128, 1), np.float32)
    fr = np.array([ROPE_THETA ** (-(2 * k) / 32.0) for k in range(16)], np.float32)
    invf[0:16, 0] = fr
    invf[16:32, 0] = fr
    maps = []
    for c in range(8):
        b, tq = c // 4, c % 4
        xT = x[b].T
        xTf = np.ascontiguousarray(xT.reshape(32, 128, S).transpose(1, 0, 2))
        lo = tq * QT
        own = np.zeros((4096, HALO + QT), np.float32)
        own[:, HALO:] = xT[:, lo:lo + QT]
        if tq > 0:
            own[:, :HALO] = xT[:, lo - HALO:lo]
        xTo = np.ascontiguousarray(own.reshape(32, 128, HALO + QT).transpose(1, 0, 2))
        flags = np.broadcast_to((np.arange(NCH) < lo // 128).astype(np.float32)[None, :], (128, NCH)).copy()
        qp = np.arange(lo - HALO, lo + QT, dtype=np.float32)
        qpos = np.broadcast_to(qp[None, :], (128, HALO + QT)).copy()
        maps.append(dict(xTf=xTf, xTo=xTo, w_in=w_in, w_out=w_out, w_gate=w_gate, w_up=w_up, w_down=w_down,
                         lnw=lnw, convw=convw, hv=hv, nw=nw, lamv=lamv, cmask=cmask, flags=flags, qpos=qpos,
                         kpos=kpos, posf=posf, invf=invf))
    return maps


def kernel(**inputs):
    x = np.asarray(inputs["x"])
    B, S, _ = x.shape
    QT = S // 4
    nc = build_program(S)
    maps = _prep(inputs, S)
    res = run_bass_kernel_spmd(nc, maps, core_ids=list(range(8)))
    out = np.empty((B, S, D), np.float32)
    for c in range(8):
        b, tq = c // 4, c % 4
        o = np.asarray(res.results[c]["outT"], np.float32)
        out[b, tq * QT:(tq + 1) * QT, :] = o.transpose(2, 1, 0).reshape(QT, D)
    return out
```
